# Optimizing a Trainium2 kernel written in Bass

```python
import math
import jax, jax.numpy as jnp
from jax import lax
import numpy as np


D_MODEL = 1024
BATCH = 16
SEQ = 2048
DEPTH = 2

D_MIX = D_MODEL
M_WIDTH = D_MIX // 2
M_HEADDIM = 64
M_HEADS = M_WIDTH // M_HEADDIM
M_GROUPS = 2
M_HPG = M_HEADS // M_GROUPS
M_STATE = 128
M_CONV = 4
M_CHUNK = 128
M_CONV_DIM = M_WIDTH + 2 * M_GROUPS * M_STATE
M_PROJ = M_WIDTH + M_CONV_DIM + M_HEADS
S_WIDTH = D_MIX // 4
S_GROUP_CH = 16
S_GROUPS = S_WIDTH // S_GROUP_CH
S_STATE = 64
R_WIDTH = D_MIX - M_WIDTH - S_WIDTH
R_HEADDIM = 64
R_HEADS = R_WIDTH // R_HEADDIM
R_DECAY_LORA = 32
R_AAA_LORA = 32
R_GATE_LORA = 64
R_PROJ = 3 * R_WIDTH + R_DECAY_LORA + R_AAA_LORA + R_GATE_LORA
D_IN = M_PROJ + S_WIDTH + R_PROJ
D_FF = 2816
NORM_EPS = 1e-5
RWKV_GN_EPS = 64e-5
MACARON_WEIGHT = 0.5

kernel_name = 'hybrid_ssd_s5_rwkv7_macaron'


def rmsnorm(x, g, eps=NORM_EPS):
    xf = x.astype(jnp.float32)
    y = xf * lax.rsqrt(jnp.mean(xf * xf, axis=-1, keepdims=True) + eps)
    return y * g.astype(jnp.float32)


def swiglu(h, wg, wu, wd):
    return (jax.nn.silu(h @ wg) * (h @ wu)) @ wd


def token_shift(u):
    return jnp.pad(u[:, :-1], ((0, 0), (1, 0), (0, 0)))


def causal_dwconv(u, w, b):
    L = u.shape[1]
    up = jnp.pad(u, ((0, 0), (M_CONV - 1, 0), (0, 0)))
    out = b
    for j in range(M_CONV):
        out = out + up[:, j:j + L] * w[:, j]
    return out


def segsum(x):
    T = x.shape[-1]
    xr = jnp.broadcast_to(x[..., None], x.shape + (T,))
    xr = jnp.where(jnp.tril(jnp.ones((T, T), dtype=bool), -1), xr, 0.0)
    xs = jnp.cumsum(xr, axis=-2)
    return jnp.where(jnp.tril(jnp.ones((T, T), dtype=bool), 0), xs, -jnp.inf)


def ssd_mixer(u, A_log, dt_bias, conv_w, conv_b, D_skip, norm_w):
    bsz, L, _ = u.shape
    nc = L // M_CHUNK
    z, xbc, dt_raw = jnp.split(u, [M_WIDTH, M_WIDTH + M_CONV_DIM], axis=-1)
    xbc = jax.nn.silu(causal_dwconv(xbc, conv_w, conv_b))
    xs, Bm, Cm = jnp.split(xbc, [M_WIDTH, M_WIDTH + M_GROUPS * M_STATE], axis=-1)
    dt = jax.nn.softplus(dt_raw + dt_bias)
    A = -jnp.exp(A_log.astype(jnp.float32))
    xs = xs.reshape(bsz, nc, M_CHUNK, M_GROUPS, M_HPG, M_HEADDIM)
    Bm = Bm.reshape(bsz, nc, M_CHUNK, M_GROUPS, M_STATE)
    Cm = Cm.reshape(bsz, nc, M_CHUNK, M_GROUPS, M_STATE)
    dt = dt.reshape(bsz, nc, M_CHUNK, M_GROUPS, M_HPG)
    X = xs * dt[..., None]
    dA = jnp.transpose(dt * A.reshape(M_GROUPS, M_HPG), (0, 3, 4, 1, 2))
    A_cs = jnp.cumsum(dA, axis=-1)
    Lmat = jnp.exp(segsum(dA))
    CB = jnp.einsum('bclgn,bcsgn->bgcls', Cm, Bm)
    y_diag = jnp.einsum('bgrcls,bcsgrp->bclgrp', CB[:, :, None] * Lmat, X)
    decay_states = jnp.exp(A_cs[..., -1:] - A_cs)
    states = jnp.einsum('bclgn,bgrcl,bclgrp->bcgrpn', Bm, decay_states, X)
    A_last = jnp.pad(A_cs[..., -1], ((0, 0), (0, 0), (0, 0), (1, 0)))
    decay_chunk = jnp.exp(segsum(A_last))
    states = jnp.concatenate([jnp.zeros_like(states[:, :1]), states], axis=1)
    states = jnp.einsum('bgrzc,bcgrpn->bzgrpn', decay_chunk, states)[:, :-1]
    y_off = jnp.einsum('bclgn,bcgrpn,bgrcl->bclgrp', Cm, states, jnp.exp(A_cs))
    y = y_diag + y_off + xs * D_skip.reshape(M_GROUPS, M_HPG)[:, :, None]
    y = y.reshape(bsz, L, M_WIDTH)
    return rmsnorm(y * jax.nn.silu(z), norm_w)


def s5_mixer(u, A_re, A_im, B_re, B_im, C_re, C_im, log_dt, D_skip, glu_w, glu_b):
    bsz, L, _ = u.shape
    f32 = jnp.float32
    A_re, A_im = A_re.astype(f32), A_im.astype(f32)
    B_re, B_im = B_re.astype(f32), B_im.astype(f32)
    dt = jnp.exp(log_dt.astype(f32))[:, None]
    mag = jnp.exp(A_re * dt)
    abar_re, abar_im = mag * jnp.cos(A_im * dt), mag * jnp.sin(A_im * dt)
    den = A_re * A_re + A_im * A_im
    nr, ni = abar_re - 1.0, abar_im
    coef_re = (nr * A_re + ni * A_im) / den
    coef_im = (ni * A_re - nr * A_im) / den
    Bb_re = coef_re[..., None] * B_re - coef_im[..., None] * B_im
    Bb_im = coef_re[..., None] * B_im + coef_im[..., None] * B_re
    ug = u.reshape(bsz, L, S_GROUPS, S_GROUP_CH)
    bu_re = jnp.einsum('blgh,gph->blgp', ug, Bb_re)
    bu_im = jnp.einsum('blgh,gph->blgp', ug, Bb_im)
    a_re = jnp.broadcast_to(abar_re, bu_re.shape)
    a_im = jnp.broadcast_to(abar_im, bu_im.shape)

    def combine(e1, e2):
        a1r, a1i, b1r, b1i = e1
        a2r, a2i, b2r, b2i = e2
        return (a2r * a1r - a2i * a1i, a2r * a1i + a2i * a1r,
                a2r * b1r - a2i * b1i + b2r, a2r * b1i + a2i * b1r + b2i)

    _, _, xr, xi = lax.associative_scan(combine, (a_re, a_im, bu_re, bu_im), axis=1)
    y = (jnp.einsum('ghp,blgp->blgh', C_re, xr)
         - jnp.einsum('ghp,blgp->blgh', C_im, xi))
    y = y.reshape(bsz, L, S_WIDTH) + D_skip * u
    y = jax.nn.gelu(y)
    return y * jax.nn.sigmoid(y @ glu_w + glu_b)


def rwkv7_mixer(u, mu, w0, w2, a0, a2, g2, k_k, k_a, r_k, gn_w, gn_b):
    bsz, L, _ = u.shape
    u = u + (token_shift(u) - u) * mu
    r, k, v, wl, al, gl = jnp.split(
        u, [R_WIDTH, 2 * R_WIDTH, 3 * R_WIDTH, 3 * R_WIDTH + R_DECAY_LORA,
            3 * R_WIDTH + R_DECAY_LORA + R_AAA_LORA], axis=-1)
    w_log = -jax.nn.softplus(-(w0 + jnp.tanh(wl) @ w2)) - 0.5
    decay = jnp.exp(-jnp.exp(w_log))
    a = jax.nn.sigmoid(a0 + al @ a2)
    g = jax.nn.sigmoid(gl) @ g2
    heads = lambda t: t.reshape(bsz, L, R_HEADS, R_HEADDIM)
    kk = heads(k * k_k)
    kk = kk / jnp.maximum(jnp.sqrt(jnp.sum(kk * kk, axis=-1, keepdims=True)), 1e-12)
    k = k * (1.0 + (a - 1.0) * k_a)
    r_h, k_h, v_h, w_h, a_h = heads(r), heads(k), heads(v), heads(decay), heads(a)

    def step(S, inp):
        rt, wt, kt, vt, kkt, at = inp
        sa = jnp.einsum('bhvk,bhk->bhv', S, -kkt)
        S = (S * wt[:, :, None, :] + sa[..., None] * (kkt * at)[:, :, None, :]
             + vt[..., None] * kt[:, :, None, :])
        return S, jnp.einsum('bhvk,bhk->bhv', S, rt)

    S0 = jnp.zeros((bsz, R_HEADS, R_HEADDIM, R_HEADDIM), jnp.float32)
    xs = tuple(jnp.moveaxis(t, 1, 0) for t in (r_h, w_h, k_h, v_h, kk, a_h))
    _, y = lax.scan(step, S0, xs)
    y = jnp.moveaxis(y, 0, 1)
    mean = jnp.mean(y, axis=-1, keepdims=True)
    var = jnp.mean(jnp.square(y - mean), axis=-1, keepdims=True)
    y = ((y - mean) * lax.rsqrt(var + RWKV_GN_EPS)).reshape(bsz, L, R_WIDTH) * gn_w + gn_b
    bonus = jnp.sum(r_h * k_h * r_k, axis=-1, keepdims=True) * v_h
    y = y + bonus.reshape(bsz, L, R_WIDTH)
    return y * g


def setup_inputs(seed: int = 0) -> dict:
    key = jax.random.key(seed)
    ks = iter(jax.random.split(key, 64))
    f32 = jnp.float32
    nrm = lambda shape, s: s * jax.random.normal(next(ks), shape, f32)
    gain = lambda shape: 1.0 + 0.02 * jax.random.normal(next(ks), shape, f32)
    unif = lambda shape, lo, hi: jax.random.uniform(next(ks), shape, f32, lo, hi)
    Dn = DEPTH
    inp = {}
    inp['x'] = nrm((BATCH, SEQ, D_MODEL), 1.0)
    inp['ffn1_norm'] = gain((Dn, D_MODEL))
    inp['ffn1_wg'] = nrm((Dn, D_MODEL, D_FF), D_MODEL ** -0.5)
    inp['ffn1_wu'] = nrm((Dn, D_MODEL, D_FF), D_MODEL ** -0.5)
    inp['ffn1_wd'] = nrm((Dn, D_FF, D_MODEL), D_FF ** -0.5)
    inp['mix_norm'] = gain((Dn, D_MODEL))
    inp['w_in'] = nrm((Dn, D_MODEL, D_IN), D_MODEL ** -0.5)
    inp['w_out'] = nrm((Dn, D_MIX, D_MODEL), D_MIX ** -0.5)
    inp['m_A_log'] = jnp.log(unif((Dn, M_HEADS), 1.0, 16.0))
    dt0 = jnp.exp(unif((Dn, M_HEADS), math.log(1e-3), math.log(1e-1)))
    inp['m_dt_bias'] = dt0 + jnp.log(-jnp.expm1(-dt0))
    inp['m_conv_w'] = nrm((Dn, M_CONV_DIM, M_CONV), M_CONV ** -0.5)
    inp['m_conv_b'] = nrm((Dn, M_CONV_DIM), 0.02)
    inp['m_D'] = gain((Dn, M_HEADS))
    inp['m_norm_w'] = gain((Dn, M_WIDTH))
    inp['s_A_re'] = -0.5 + nrm((Dn, S_GROUPS, S_STATE), 0.01)
    inp['s_A_im'] = math.pi * jnp.arange(S_STATE, dtype=f32) + nrm((Dn, S_GROUPS, S_STATE), 0.01)
    inp['s_B_re'] = nrm((Dn, S_GROUPS, S_STATE, S_GROUP_CH), (2 * S_GROUP_CH) ** -0.5)
    inp['s_B_im'] = nrm((Dn, S_GROUPS, S_STATE, S_GROUP_CH), (2 * S_GROUP_CH) ** -0.5)
    inp['s_C_re'] = nrm((Dn, S_GROUPS, S_GROUP_CH, S_STATE), S_STATE ** -0.5)
    inp['s_C_im'] = nrm((Dn, S_GROUPS, S_GROUP_CH, S_STATE), S_STATE ** -0.5)
    inp['s_log_dt'] = unif((Dn, S_GROUPS), math.log(1e-3), math.log(1e-1))
    inp['s_D'] = nrm((Dn, S_WIDTH), 1.0)
    inp['s_glu_w'] = nrm((Dn, S_WIDTH, S_WIDTH), S_WIDTH ** -0.5)
    inp['s_glu_b'] = nrm((Dn, S_WIDTH), 0.02)
    inp['r_mu'] = unif((Dn, R_PROJ), 0.0, 1.0)
    inp['r_w0'] = unif((Dn, R_WIDTH), -6.0, 1.0)
    inp['r_w2'] = nrm((Dn, R_DECAY_LORA, R_WIDTH), 0.1 * R_DECAY_LORA ** -0.5)
    inp['r_a0'] = nrm((Dn, R_WIDTH), 0.1)
    inp['r_a2'] = nrm((Dn, R_AAA_LORA, R_WIDTH), 0.1 * R_AAA_LORA ** -0.5)
    inp['r_g2'] = nrm((Dn, R_GATE_LORA, R_WIDTH), R_GATE_LORA ** -0.5)
    inp['r_k_k'] = 0.85 + nrm((Dn, R_WIDTH), 0.02)
    inp['r_k_a'] = gain((Dn, R_WIDTH))
    inp['r_r_k'] = nrm((Dn, R_HEADS, R_HEADDIM), 0.1)
    inp['r_gn_w'] = gain((Dn, R_WIDTH))
    inp['r_gn_b'] = nrm((Dn, R_WIDTH), 0.02)
    inp['ffn2_norm'] = gain((Dn, D_MODEL))
    inp['ffn2_wg'] = nrm((Dn, D_MODEL, D_FF), D_MODEL ** -0.5)
    inp['ffn2_wu'] = nrm((Dn, D_MODEL, D_FF), D_MODEL ** -0.5)
    inp['ffn2_wd'] = nrm((Dn, D_FF, D_MODEL), D_FF ** -0.5)
    inp['final_norm'] = gain((D_MODEL,))
    return inp


def reference(x, ffn1_norm, ffn1_wg, ffn1_wu, ffn1_wd, mix_norm, w_in, w_out,
              m_A_log, m_dt_bias, m_conv_w, m_conv_b, m_D, m_norm_w,
              s_A_re, s_A_im, s_B_re, s_B_im, s_C_re, s_C_im, s_log_dt, s_D, s_glu_w, s_glu_b,
              r_mu, r_w0, r_w2, r_a0, r_a2, r_g2, r_k_k, r_k_a, r_r_k, r_gn_w, r_gn_b,
              ffn2_norm, ffn2_wg, ffn2_wu, ffn2_wd, final_norm):
    for i in range(DEPTH):
        h = rmsnorm(x, ffn1_norm[i])
        x = x + (MACARON_WEIGHT * swiglu(h, ffn1_wg[i], ffn1_wu[i], ffn1_wd[i])).astype(x.dtype)
        h = rmsnorm(x, mix_norm[i])
        u = h @ w_in[i]
        u_m, u_s, u_r = jnp.split(u, [M_PROJ, M_PROJ + S_WIDTH], axis=-1)
        y_m = ssd_mixer(u_m, m_A_log[i], m_dt_bias[i], m_conv_w[i], m_conv_b[i], m_D[i], m_norm_w[i])
        y_s = s5_mixer(u_s, s_A_re[i], s_A_im[i], s_B_re[i], s_B_im[i], s_C_re[i], s_C_im[i],
                       s_log_dt[i], s_D[i], s_glu_w[i], s_glu_b[i])
        y_r = rwkv7_mixer(u_r, r_mu[i], r_w0[i], r_w2[i], r_a0[i], r_a2[i], r_g2[i],
                          r_k_k[i], r_k_a[i], r_r_k[i], r_gn_w[i], r_gn_b[i])
        y = jnp.concatenate([y_m, y_s, y_r], axis=-1)
        x = x + (y @ w_out[i]).astype(x.dtype)
        h = rmsnorm(x, ffn2_norm[i])
        x = x + (MACARON_WEIGHT * swiglu(h, ffn2_wg[i], ffn2_wu[i], ffn2_wd[i])).astype(x.dtype)
    return rmsnorm(x, final_norm).astype(x.dtype)
```

```python
import numpy as np
import concourse.bass as bass
import concourse.mybir as mybir
from concourse.bass_utils import run_bass_kernel_spmd

F32 = mybir.dt.float32
BF16 = mybir.dt.bfloat16
I32 = mybir.dt.int32
ALU = mybir.AluOpType
AF = mybir.ActivationFunctionType
AX = mybir.AxisListType

D = 1024
SEQ = 2048
DFF = 2816
NFT = DFF // 128
DIN = 2696
EPS = 1e-5

ENGS = ("pe", "dve", "act", "pool", "sp")
ROLL = 12000
SELF_SYNC = {"pe": False, "dve": True, "act": True, "pool": True, "sp": True}


class Prog:
    def __init__(self, nc):
        self.nc = nc
        self.ops = {e: [] for e in ENGS}
        self.cnt = {e: 0 for e in ENGS}
        self.gen = {e: 0 for e in ENGS}
        self.seen = {e: {} for e in ENGS}
        self.lastw = {}
        self.readers = {}
        self.dmacnt = {}
        self.semnames = []
        self.fence = {}
        self.fence_done = {e: 0 for e in ENGS}
        self.fence_id = 0

    def _semname(self, n):
        if n not in self.semnames:
            self.semnames.append(n)
        return n

    def _deps(self, eng, reads, writes):
        deps = {}

        def add(tok):
            if tok is None:
                return
            s, v = tok
            if deps.get(s, 0) < v:
                deps[s] = v

        for k in reads:
            add(self.lastw.get(k))
            if isinstance(k, tuple) and k[0] == "ps":
                for tok in self.readers.get(k, ()):
                    if not tok[0].startswith("E_%s_" % eng):
                        add(tok)
        for k in writes:
            add(self.lastw.get(k))
            for tok in self.readers.get(k, ()):
                add(tok)
        waits = []
        seen = self.seen[eng]
        for s, v in deps.items():
            if seen.get(s, 0) >= v:
                continue
            if s.startswith("E_%s_" % eng) and not SELF_SYNC[eng]:
                continue
            seen[s] = v
            waits.append((s, v))
        return waits

    def set_fence(self):
        toks = {}
        for en in ENGS:
            if self.cnt[en] > 0:
                toks["E_%s_%d" % (en, self.gen[en])] = self.cnt[en]
        for sname, v in self.dmacnt.items():
            toks[sname] = v
        self.fence = toks
        self.fence_id += 1

    def _fence_waits(self, eng):
        if self.fence_done[eng] == self.fence_id:
            return []
        self.fence_done[eng] = self.fence_id
        out = []
        seen = self.seen[eng]
        for s, v in self.fence.items():
            if seen.get(s, 0) >= v:
                continue
            if s.startswith("E_%s_" % eng):
                continue
            seen[s] = v
            out.append((s, v))
        return out

    def op(self, eng, fn, r=(), w=(), nofence=False):
        waits = self._deps(eng, r, w)
        if not nofence:
            waits = self._fence_waits(eng) + waits
        if self.cnt[eng] >= ROLL:
            self.gen[eng] += 1
            self.cnt[eng] = 0
        self.cnt[eng] += 1
        sname = self._semname("E_%s_%d" % (eng, self.gen[eng]))
        tok = (sname, self.cnt[eng])
        self.ops[eng].append((waits, fn, sname, 1))
        for k in w:
            self.lastw[k] = tok
            self.readers[k] = []
        for k in r:
            self.readers.setdefault(k, []).append(tok)
        return tok

    def dma(self, q, fn, r=(), w=(), sem=None, nofence=False):
        waits = self._deps(q, r, w)
        if not nofence:
            waits = self._fence_waits(q) + waits
        if sem is None:
            sem = "D_" + str(w[0] if w else r[0])
        sname = self._semname(sem)
        self.dmacnt[sname] = self.dmacnt.get(sname, 0) + 16
        tok = (sname, self.dmacnt[sname])
        self.ops[q].append((waits, fn, sname, 16))
        for k in w:
            self.lastw[k] = tok
            self.readers[k] = []
        for k in r:
            self.readers.setdefault(k, []).append(tok)
        return tok

    def final_wait(self, eng, toks):
        waits = []
        for s, v in toks:
            waits.append((s, v))
        self.ops[eng].append((waits, None, None, 0))

    def emit(self, stack):
        nc = self.nc
        sems = {}
        for n in self.semnames:
            sems[n] = stack.enter_context(nc.semaphore(n))
        block = stack.enter_context(nc.Block())
        engmap = {"pe": block.tensor, "dve": block.vector, "act": block.scalar,
                  "pool": block.gpsimd, "sp": block.sync}

        def mk(elist):
            def body(e):
                for waits, fn, sname, inc in elist:
                    for s, v in waits:
                        e.wait_ge(sems[s], v)
                    if fn is not None:
                        ins = fn(e)
                        ins.then_inc(sems[sname], inc)
            return body

        for en in ENGS:
            if self.ops[en]:
                engmap[en](mk(self.ops[en]))


def build_program(nc, cfg):
    from contextlib import ExitStack
    NS = cfg.get("nseq", 2)
    NL = cfg.get("nlayers", 2)
    do_ffn = cfg.get("ffn", True)
    mixers = cfg.get("mixers", ("ssd", "s5", "rwkv"))
    dbg = cfg.get("dbg", False)

    def din(name, shape):
        return nc.dram_tensor(name, list(shape), F32, kind="ExternalInput").ap()

    x_d = din("x", [NS, SEQ, D])
    W = {}
    for nm, shp in WEIGHT_SHAPES:
        W[nm] = din(nm, shp)
    out_d = nc.dram_tensor("out", [NS, SEQ, D], F32, kind="ExternalOutput").ap()
    if dbg:
        ydbg = nc.dram_tensor("ydbg", [NS, NL, D, SEQ], BF16, kind="ExternalOutput").ap()

    P = Prog(nc)
    stack = ExitStack()

    def sb(name, shape, dt):
        return stack.enter_context(nc.sbuf_tensor(name, list(shape), dt))

    def ps(name, shape, dt):
        return stack.enter_context(nc.psum_tensor(name, list(shape), dt))

    XT = sb("XT", [128, 8, SEQ], F32)
    ident = sb("ident", [128, 128], F32)
    identb = sb("identb", [128, 128], BF16)
    onesf = sb("onesf", [128, 128], F32)
    onesb = sb("onesb", [128, 128], BF16)
    gains = sb("gains", [128, 3 * NL + 1, 8], F32)
    AR_WORDS = 24576
    ARENA = sb("ARENA", [128, AR_WORDS], F32)
    WGU = sb("WGU", [128, 2, 2, 8, 256], BF16)
    STG = sb("STG", [128, 4, 4, 256], F32)
    SQ = sb("SQ", [128, 2, 512], BF16)
    RSTD = sb("RSTD", [128, 512], F32)
    SG = sb("SG", [128, 2, 512], F32)
    CST = sb("CST", [128, 8], F32)
    PSB = [ps("psb%d" % i, [128, 512], F32) for i in range(8)]

    ARMAX = [0]
    cfg['_armax'] = ARMAX

    class Arena:
        def __init__(self):
            self.off = 0

        def alloc(self, shape, dt):
            n = 1
            for d_ in shape:
                n *= d_
            words = (n * (2 if dt == BF16 else 4) + 3) // 4
            words = (words + 7) // 8 * 8
            ARMAX[0] = max(ARMAX[0], self.off + words)
            o_ = self.off if self.off + words <= AR_WORDS else 0
            a = ARENA[:, o_:o_ + words]
            self.off += words
            if dt == BF16:
                a = a.bitcast(BF16)
            a = a[:, 0:n]
            if len(shape) == 2:
                return a.rearrange("p (a b) -> p a b", b=shape[1])
            if len(shape) == 3:
                return a.rearrange("p (a b c) -> p a b c", b=shape[1], c=shape[2])
            return a

    ar = Arena()
    IOB = ar.alloc([2, 1024], F32)
    ar = Arena()
    HT = ar.alloc([8, 1024], BF16)
    ATt = ar.alloc([NFT, 1024], BF16)
    WDb = ar.alloc([2, NFT, 256], BF16)

    P.op("pool", lambda e: e.memset(onesf[:], 1.0), w=["onesf"])
    P.op("pool", lambda e: e.memset(onesb[:], 1.0), w=["onesb"])
    P.op("pool", lambda e: e.memset(CST[:, 0:1], EPS), w=["cst"])
    P.op("pool", lambda e: e.memset(CST[:, 1:2], 1.0), w=["cst"])
    P.op("pool", lambda e: e.memset(CST[:, 2:3], 0.0), w=["cst"])
    P.op("pool", lambda e: e.affine_select(out=ident[:], in_=onesf[:], pattern=[[1, 128]],
                                            compare_op=ALU.is_equal, fill=0.0, base=0,
                                            channel_multiplier=-1),
         r=["onesf"], w=["ident"])
    P.op("pool", lambda e: e.tensor_copy(out=identb[:], in_=ident[:]), r=["ident"], w=["identb"])

    def small_dma(dst, src, key):
        P.dma("sp", (lambda e, dst=dst, src=src: e.dma_start(out=dst, in_=src, allow_slow_non_contiguous=True)),
              w=[key])

    gi = 0
    gidx = {}
    for l in range(NL):
        for nm in ("ffn1_norm", "mix_norm", "ffn2_norm"):
            small_dma(gains[:, gi, :], W[nm][l].rearrange("(kt p) -> p kt", p=128), ("gains", gi))
            gidx[(nm, l)] = gi
            gi += 1
    small_dma(gains[:, gi, :], W["final_norm"].rearrange("(kt p) -> p kt", p=128), ("gains", gi))
    gidx["final"] = gi

    def load_x(s):
        for tb in range(16):
            b = tb % 2
            src = x_d[s, tb * 128:(tb + 1) * 128, :]
            P.dma("sp", (lambda e, b=b, src=src: e.dma_start(out=IOB[:, b, :], in_=src)),
                  w=[("IOB", b)])
            for half in range(2):
                bank = 6 + half
                for j in range(4):
                    kt = half * 4 + j
                    P.op("pe", (lambda e, b=b, kt=kt, j=j, bank=bank: e.transpose(
                        out=PSB[bank][:, j * 128:(j + 1) * 128],
                        in_=IOB[:, b, kt * 128:(kt + 1) * 128], identity=ident[:])),
                        r=[("IOB", b), "ident"], w=[("ps", bank)])
                dst = XT[:, half * 4:(half + 1) * 4, tb * 128:(tb + 1) * 128]
                srcp = PSB[bank][:, :].rearrange("p (a b) -> p a b", b=128)
                if half == 0:
                    P.op("dve", (lambda e, dst=dst, srcp=srcp: e.tensor_copy(out=dst, in_=srcp)),
                         r=[("ps", bank)], w=[("XT", tb // 4)])
                else:
                    P.op("act", (lambda e, dst=dst, srcp=srcp: e.copy(out=dst, in_=srcp)),
                         r=[("ps", bank)], w=[("XT", tb // 4)])

    def rms_stats(tok0, bank=6):
        for kt in range(8):
            b = kt % 2
            P.op("act", (lambda e, b=b, kt=kt: e.activation(
                out=SQ[:, b, :], in_=XT[:, kt, tok0:tok0 + 512], func=AF.Square)),
                r=[("XT", tok0 // 512)], w=[("SQ", b)])
            P.op("pe", (lambda e, b=b, kt=kt: e.matmul(
                PSB[bank][:, :], lhsT=onesb[:], rhs=SQ[:, b, :], start=(kt == 0), stop=(kt == 7))),
                r=[("SQ", b), "onesb"], w=[("ps", bank)])
        P.op("act", (lambda e: e.activation(out=RSTD[:], in_=PSB[bank][:, :], func=AF.Ln,
                                            bias=CST[:, 0:1], scale=1.0 / D)),
             r=[("ps", bank), "cst"], w=["RSTD"])
        P.op("act", (lambda e: e.activation(out=RSTD[:], in_=RSTD[:], func=AF.Exp, scale=-0.5)),
             r=["RSTD"], w=["RSTD"])

    def norm_to(dst, dkey, tok0, g, hoff):
        rms_stats(tok0)
        for kt in range(8):
            P.op("dve", (lambda e, kt=kt: e.scalar_tensor_tensor(
                out=dst[:, kt, hoff:hoff + 512], in0=XT[:, kt, tok0:tok0 + 512],
                scalar=gains[:, g, kt:kt + 1], in1=RSTD[:], op0=ALU.mult, op1=ALU.mult)),
                r=[("XT", tok0 // 512), "RSTD", ("gains", g)], w=[(dkey, kt, hoff // 512)])

    wcount = {"gu": 0, "d": 0, "stg": 0, "slot": 0}

    def wload(dst, src, key, nofence=False, ceng="pool"):
        a = src.shape[1]
        wd_ = src.shape[2]
        sbuf_i = wcount["stg"] % 4
        wcount["stg"] += 1
        P.dma("sp", (lambda e: e.dma_start(out=STG[:, sbuf_i, 0:a, 0:wd_], in_=src)),
              w=[("STG", sbuf_i)], nofence=nofence)
        if ceng == "act":
            P.op("act", (lambda e: e.copy(out=dst, in_=STG[:, sbuf_i, 0:a, 0:wd_])),
                 r=[("STG", sbuf_i)], w=[key], nofence=nofence)
        else:
            P.op("pool", (lambda e: e.tensor_copy(out=dst, in_=STG[:, sbuf_i, 0:a, 0:wd_])),
                 r=[("STG", sbuf_i)], w=[key], nofence=nofence)

    def run_pipelined(gens, depth=2):
        active = []
        it = iter(gens)
        while True:
            while len(active) < depth:
                try:
                    active.append(next(it))
                except StopIteration:
                    break
            if not active:
                break
            for g_ in list(active):
                try:
                    next(g_)
                except StopIteration:
                    active.remove(g_)

    def barrier():
        toks = []
        for en in ENGS:
            if P.cnt[en] > 0:
                toks.append(("E_%s_%d" % (en, P.gen[en]), P.cnt[en]))
        for sname, v in P.dmacnt.items():
            toks.append((sname, v))
        for en in ENGS:
            w_ = []
            for s_, v in toks:
                if P.seen[en].get(s_, 0) < v:
                    P.seen[en][s_] = v
                    w_.append((s_, v))
            if w_:
                P.ops[en].append((w_, None, None, 0))

    def ffn(l, which):
        wg = W[which + "_wg"][l].rearrange("(kt p) f -> p kt f", p=128)
        wu = W[which + "_wu"][l].rearrange("(kt p) f -> p kt f", p=128)
        wd = W[which + "_wd"][l].rearrange("(ft p) d -> p ft d", p=128)
        g = gidx[(which + "_norm", l)]
        for tt in range(SEQ // 1024):
            t0 = tt * 1024
            for half in range(2):
                norm_to(HT, "HT", t0 + half * 512, g, half * 512)
            for fb in range(NFT // 2):
                wb = wcount["gu"] % 2
                wcount["gu"] += 1
                f0 = fb * 256
                for gu, wsrc in ((0, wg), (1, wu)):
                    for kh in range(2):
                        wload(WGU[:, wb, gu, kh * 4:(kh + 1) * 4, :], wsrc[:, kh * 4:(kh + 1) * 4, f0:f0 + 256],
                              ("WGU", wb, gu, kh))
                for fi in range(2):
                    ft = fb * 2 + fi
                    for half in range(2):
                        bg, bu = half * 2, half * 2 + 1
                        for gu, bank in ((0, bg), (1, bu)):
                            for kt in range(8):
                                P.op("pe", (lambda e, wb=wb, gu=gu, kt=kt, fi=fi, half=half, bank=bank: e.matmul(
                                    PSB[bank][:, :], lhsT=WGU[:, wb, gu, kt, fi * 128:(fi + 1) * 128],
                                    rhs=HT[:, kt, half * 512:(half + 1) * 512],
                                    start=(kt == 0), stop=(kt == 7))),
                                    r=[("WGU", wb, gu, kt // 4), ("HT", kt, half)], w=[("ps", bank)])
                        P.op("act", (lambda e, half=half, bg=bg: e.activation(
                            out=SG[:, half, :], in_=PSB[bg][:, :], func=AF.Silu)),
                            r=[("ps", bg)], w=[("SG", half)])
                        P.op("dve", (lambda e, half=half, bu=bu, ft=ft: e.tensor_tensor(
                            out=ATt[:, ft, half * 512:(half + 1) * 512], in0=SG[:, half, :],
                            in1=PSB[bu][:, :], op=ALU.mult)),
                            r=[("SG", half), ("ps", bu)], w=[("AT", ft, half)])
            for db in range(4):
                wb = wcount["d"] % 2
                wcount["d"] += 1
                d0 = db * 256
                for c4 in range(6):
                    a0, a1 = c4 * 4, min(c4 * 4 + 4, NFT)
                    wload(WDb[:, wb, a0:a1, :], wd[:, a0:a1, d0:d0 + 256], ("WD", wb, c4))
                for di in range(2):
                    dt_ = db * 2 + di
                    for half in range(2):
                        bank = 4 + half
                        for ft in range(NFT):
                            P.op("pe", (lambda e, wb=wb, ft=ft, di=di, half=half, bank=bank: e.matmul(
                                PSB[bank][:, :], lhsT=WDb[:, wb, ft, di * 128:(di + 1) * 128],
                                rhs=ATt[:, ft, half * 512:(half + 1) * 512],
                                start=(ft == 0), stop=(ft == NFT - 1))),
                                r=[("WD", wb, ft // 4), ("AT", ft, half)], w=[("ps", bank)])
                        tk = t0 + half * 512
                        P.op("dve", (lambda e, dt_=dt_, tk=tk, bank=bank: e.scalar_tensor_tensor(
                            out=XT[:, dt_, tk:tk + 512], in0=PSB[bank][:, :], scalar=0.5,
                            in1=XT[:, dt_, tk:tk + 512], op0=ALU.mult, op1=ALU.add)),
                            r=[("ps", bank), ("XT", tk // 512)], w=[("XT", tk // 512)])

    out_toks = []

    def final_store(s):
        g = gidx["final"]
        for q in range(4):
            tok0 = q * 512
            rms_stats(tok0)
            for kt in range(8):
                P.op("dve", (lambda e, kt=kt, tok0=tok0: e.scalar_tensor_tensor(
                    out=XT[:, kt, tok0:tok0 + 512], in0=XT[:, kt, tok0:tok0 + 512],
                    scalar=gains[:, g, kt:kt + 1], in1=RSTD[:], op0=ALU.mult, op1=ALU.mult)),
                    r=[("XT", q), "RSTD", ("gains", g)], w=[("XT", q)])
            for tb4 in range(4):
                tb = q * 4 + tb4
                b = tb % 2
                for half in range(2):
                    bank = 6 + half
                    for j in range(4):
                        kt = half * 4 + j
                        P.op("pe", (lambda e, kt=kt, j=j, tb=tb, bank=bank: e.transpose(
                            out=PSB[bank][:, j * 128:(j + 1) * 128],
                            in_=XT[:, kt, tb * 128:(tb + 1) * 128], identity=ident[:])),
                            r=[("XT", q), "ident"], w=[("ps", bank)])
                    if half == 0:
                        P.op("dve", (lambda e, b=b, bank=bank: e.tensor_copy(
                            out=IOB[:, b, 0:512], in_=PSB[bank][:, :])),
                            r=[("ps", bank)], w=[("IOB", b)])
                    else:
                        P.op("act", (lambda e, b=b, bank=bank: e.copy(
                            out=IOB[:, b, 512:1024], in_=PSB[bank][:, :])),
                            r=[("ps", bank)], w=[("IOB", b)])
                dst = out_d[s, tb * 128:(tb + 1) * 128, :]
                tok = P.dma("sp", (lambda e, b=b, dst=dst: e.dma_start(out=dst, in_=IOB[:, b, :])),
                            r=[("IOB", b)], sem="D_out%d" % b)
                out_toks.append(tok)

    CW = sb("CW", [128, NL, 8, 4], F32)
    CBs = sb("CBs", [128, NL, 8], F32)
    DTB = sb("DTB", [8, NL], F32)
    ANEG = sb("ANEG", [8, NL], F32)
    DBC = sb("DBC", [128, NL, 8], F32)
    EH = sb("EH", [8, 8], F32)
    NEH = sb("NEH", [8, 8], F32)
    NEGM = sb("NEGM", [128, 128], F32)
    SEL127 = sb("SEL127", [128, 128], F32)
    ONES8 = sb("ONES8", [8, 128], F32)
    ONESW = sb("ONESW", [128, 132], F32)
    for l in range(NL):
        small_dma(CW[:, l, :, :], W["m_conv_w"][l].rearrange("(t p) k -> p t k", p=128), ("CW", l))
        small_dma(CBs[:, l, :], W["m_conv_b"][l].rearrange("(t p) -> p t", p=128), ("CBs", l))
        small_dma(DTB[:, l:l + 1], W["m_dt_bias"][l].rearrange("(h o) -> h o", o=1), ("DTB", l))
        small_dma(ANEG[:, l:l + 1], W["m_A_log"][l].rearrange("(h o) -> h o", o=1), ("ANEG", l))
        small_dma(DBC[:, l, :], W["m_D"][l:l + 1, :].broadcast_to([128, 8]), ("DBC", l))
        P.op("act", (lambda e, l=l: e.activation(out=ANEG[:, l:l + 1], in_=ANEG[:, l:l + 1], func=AF.Exp)),
             r=[("ANEG", l)], w=[("ANEG", l)])
        P.op("dve", (lambda e, l=l: e.tensor_scalar(out=ANEG[:, l:l + 1], in0=ANEG[:, l:l + 1], scalar1=-1.0,
                                                    scalar2=None, op0=ALU.mult)),
             r=[("ANEG", l)], w=[("ANEG", l)])
    P.op("pool", lambda e: e.memset(ONES8[:], 1.0), w=["ONES8"])
    MASK1 = sb("MASK1", [128, 128], F32)
    MASKL = sb("MASKL", [128, 64], F32)
    OBD = sb("OBD", [128, 128], F32)
    O64BD = sb("O64BD", [128, 128], F32)
    P.op("pool", lambda e: e.memset(OBD[:], 0.0), w=["OBD"])
    P.op("pool", lambda e: e.memset(OBD[0:64, 0:64], 1.0), w=["OBD"])
    P.op("pool", lambda e: e.memset(OBD[64:128, 64:128], 1.0), w=["OBD"])
    P.op("pool", lambda e: e.tensor_scalar(out=O64BD[:], in0=OBD[:], scalar1=1.0 / 64, scalar2=None, op0=ALU.mult),
         r=["OBD"], w=["O64BD"])
    P.op("pool", lambda e: e.memset(CST[:, 3:4], 1e-30), w=["cst"])
    P.op("pool", lambda e: e.memset(CST[:, 4:5], 64e-5), w=["cst"])
    for hb_ in (slice(0, 64), slice(64, 128)):
        P.op("pool", (lambda e, hb_=hb_: e.affine_select(out=MASK1[hb_, 0:64], in_=onesf[hb_, 0:64], pattern=[[1, 64]],
                                                        compare_op=ALU.is_gt, fill=0.0, base=0, channel_multiplier=-1)),
             r=["onesf"], w=["MASK1"])
        P.op("pool", (lambda e, hb_=hb_: e.affine_select(out=MASK1[hb_, 64:128], in_=onesf[hb_, 0:64], pattern=[[1, 64]],
                                                        compare_op=ALU.is_ge, fill=0.0, base=0, channel_multiplier=-1)),
             r=["onesf"], w=["MASK1"])
        P.op("pool", (lambda e, hb_=hb_: e.affine_select(out=MASKL[hb_, :], in_=onesf[hb_, 0:64], pattern=[[-1, 64]],
                                                        compare_op=ALU.is_gt, fill=0.0, base=0, channel_multiplier=1)),
             r=["onesf"], w=["MASKL"])
    P.op("pool", lambda e: e.memset(ONESW[:], 1.0), w=["ONESW"])
    P.op("pool", lambda e: e.memset(EH[:], 1.0), w=["EH"])
    P.op("pool", lambda e: e.affine_select(out=EH[:], in_=EH[:], pattern=[[-1, 8]],
                                            compare_op=ALU.is_equal, fill=0.0, base=0, channel_multiplier=1),
         r=["EH"], w=["EH"])
    P.op("pool", lambda e: e.tensor_scalar(out=NEH[:], in0=EH[:], scalar1=-1.0, scalar2=None, op0=ALU.mult),
         r=["EH"], w=["NEH"])
    P.op("pool", lambda e: e.memset(NEGM[:], 0.0), w=["NEGM"])
    P.op("pool", lambda e: e.affine_select(out=NEGM[:], in_=NEGM[:], pattern=[[1, 128]],
                                            compare_op=ALU.is_ge, fill=-30000.0, base=0, channel_multiplier=-1),
         r=["NEGM"], w=["NEGM"])
    P.op("pool", lambda e: e.affine_select(out=SEL127[:], in_=onesf[:], pattern=[[0, 128]],
                                            compare_op=ALU.is_equal, fill=0.0, base=-127, channel_multiplier=1),
         r=["onesf"], w=["SEL127"])

    win_all = [W["w_in"][l].rearrange("(kt p) c -> p kt c", p=128) for l in range(NL)]
    wout_all = [W["w_out"][l].rearrange("(kt p) c -> p kt c", p=128) for l in range(NL)]

    def wslot():
        i = wcount["slot"] % 4
        wcount["slot"] += 1
        return i, WGU[:, i // 2, i % 2]

    def load_wblock(wsrc, c0, width):
        i, S = wslot()
        for kh in range(2):
            wload(S[:, kh * 4:(kh + 1) * 4, 0:width], wsrc[:, kh * 4:(kh + 1) * 4, c0:c0 + width], ("WS", i, kh), nofence=True, ceng="act")
        return i, S

    def mm_fm(bank, i, S, width, rhs_t, rkey, m0=0):
        for kt in range(8):
            P.op("pe", (lambda e, kt=kt: e.matmul(
                PSB[bank][m0:m0 + width, :], lhsT=S[:, kt, 0:width], rhs=rhs_t[:, kt, :],
                start=(kt == 0), stop=(kt == 7))),
                r=[("WS", i, kt // 4), (rkey, kt, 0)], w=[("ps", bank)], nofence=True)

    def mixer_phase(s, l):
        am = Arena()
        HTB = am.alloc([8, 512], BF16)
        YTB = am.alloc([8, 512], BF16)
        TAIL = am.alloc([8, 4], F32)
        STATE = am.alloc([8, 64], F32)
        STATEB = am.alloc([8, 64], BF16)
        NORMW = am.alloc([1, 512], F32)
        small_dma(NORMW[:, 0, :], W["m_norm_w"][l:l + 1, :].broadcast_to([128, 512]), ("NORMW", l))
        base_off = am.off
        g = gidx[("mix_norm", l)]
        win = win_all[l]
        wout = wout_all[l]

        P.op("pool", lambda e: e.memset(STATE[:], 0.0), w=["STATE"])
        P.op("pool", lambda e: e.memset(STATEB[:], 0.0), w=["STATEB"])
        P.op("pool", lambda e: e.memset(YTB[:], 0.0), w=[("YTB", k_) for k_ in range(8)])

        def ssd_block(tb):
            am.off = base_off
            XS = am.alloc([4, 512], F32)
            BC = am.alloc([4, 512], BF16)
            ACC = am.alloc([2, 512], F32)
            D8 = am.alloc([4, 512], F32)
            SM = am.alloc([2, 64], F32)
            XDT = am.alloc([2, 512], BF16)
            XDS = am.alloc([2, 512], BF16)
            XSD = am.alloc([2, 512], F32)
            BTM = am.alloc([2, 256], BF16)
            LT = am.alloc([2, 512], F32)
            GT = am.alloc([4, 512], BF16)
            Y1 = am.alloc([4, 512], F32)
            SZ = am.alloc([2, 512], F32)
            YTM = am.alloc([2, 512], BF16)
            SS = am.alloc([2, 8], F32)
            for j in range(8):
                bank = j % 2
                i, S = load_wblock(win, 512 + 128 * j, 128)
                mm_fm(bank, i, S, 128, HTB, "HTB")
                a = j % 2
                pb = PSB[bank]
                P.op("dve", (lambda e, a=a, pb=pb, j=j: e.tensor_scalar(
                    out=ACC[:, a, :], in0=pb[:, :], scalar1=CW[:, l, j, 3:4], scalar2=CBs[:, l, j:j + 1],
                    op0=ALU.mult, op1=ALU.add)),
                    r=[("ps", bank), ("CW", l), ("CBs", l)], w=[("ACC", a)])
                for jj in range(3):
                    sh = 3 - jj
                    P.op("dve", (lambda e, a=a, pb=pb, j=j, jj=jj, sh=sh: e.scalar_tensor_tensor(
                        out=ACC[:, a, sh:512], in0=pb[:, 0:512 - sh], scalar=CW[:, l, j, jj:jj + 1],
                        in1=ACC[:, a, sh:512], op0=ALU.mult, op1=ALU.add)),
                        r=[("ps", bank), ("ACC", a)], w=[("ACC", a)])
                    if tb > 0:
                        P.op("dve", (lambda e, a=a, j=j, jj=jj, sh=sh: e.scalar_tensor_tensor(
                            out=ACC[:, a, 0:sh], in0=TAIL[:, j, 3 - sh:3], scalar=CW[:, l, j, jj:jj + 1],
                            in1=ACC[:, a, 0:sh], op0=ALU.mult, op1=ALU.add)),
                            r=[("TAIL", j), ("ACC", a)], w=[("ACC", a)])
                P.op("dve", (lambda e, pb=pb, j=j: e.tensor_copy(out=TAIL[:, j, 0:3], in_=pb[:, 509:512])),
                     r=[("ps", bank), ("ACC", a)], w=[("TAIL", j)])
                dst = XS[:, j, :] if j < 4 else BC[:, j - 4, :]
                dkey = ("XS", j) if j < 4 else ("BC", j - 4)
                P.op("act", (lambda e, a=a, dst=dst: e.activation(out=dst, in_=ACC[:, a, :], func=AF.Silu)),
                     r=[("ACC", a)], w=[dkey])
            i, S = load_wblock(win, 1536, 8)
            mm_fm(2, i, S, 8, HTB, "HTB")
            P.op("act", lambda e: e.activation(out=D8[0:8, 0, :], in_=PSB[2][0:8, :], func=AF.Exp,
                                               bias=DTB[:, l:l + 1], scale=1.0),
                 r=[("ps", 2), ("DTB", l)], w=["DTE"])
            P.op("act", lambda e: e.activation(out=D8[0:8, 1, :], in_=D8[0:8, 0, :], func=AF.Ln,
                                               bias=CST[0:8, 1:2], scale=1.0),
                 r=["DTE", "cst"], w=["DT"])
            P.op("dve", lambda e: e.tensor_scalar(out=D8[0:8, 2, :], in0=D8[0:8, 1, :], scalar1=ANEG[:, l:l + 1],
                                                  scalar2=None, op0=ALU.mult),
                 r=["DT", ("ANEG", l)], w=["DA"])
            for c in range(4):
                P.op("dve", (lambda e, c=c: e.tensor_tensor_scan(
                    out=D8[0:8, 3, c * 128:(c + 1) * 128], data0=ONES8[:, :], data1=D8[0:8, 2, c * 128:(c + 1) * 128],
                    initial=0.0, op0=ALU.mult, op1=ALU.add)),
                    r=["DA", "ONES8"], w=[("ACS", c)])
            iz0, SZ0 = load_wblock(win, 0, 256)
            iz1, SZ1 = load_wblock(win, 256, 256)
            def chunk(c):
                cs = slice(c * 128, (c + 1) * 128)
                b = c % 2
                P.op("pe", (lambda e, cs=cs: e.transpose(out=PSB[2][:, 0:8], in_=D8[0:8, 1, cs], identity=ident[0:8, 0:8])),
                     r=["DT", "ident"], w=[("ps", 2)])
                P.op("pe", (lambda e, cs=cs: e.transpose(out=PSB[2][:, 8:16], in_=D8[0:8, 3, cs], identity=ident[0:8, 0:8])),
                     r=[("ACS", c), "ident"], w=[("ps", 2)])
                P.op("dve", (lambda e, b=b: e.tensor_copy(out=SM[:, b, 0:16], in_=PSB[2][:, 0:16])),
                     r=[("ps", 2)], w=[("SM", b, 0)])
                P.op("pe", (lambda e, b=b: e.matmul(PSB[2][:, 16:24], lhsT=SEL127[:], rhs=SM[:, b, 8:16],
                                                   start=True, stop=True)),
                     r=[("SM", b, 0), "SEL127"], w=[("ps", 2)])
                P.op("dve", (lambda e, b=b: e.tensor_tensor(out=SM[:, b, 16:24], in0=PSB[2][:, 16:24],
                                                           in1=SM[:, b, 8:16], op=ALU.subtract)),
                     r=[("ps", 2), ("SM", b, 0)], w=[("SM", b, 1)])
                P.op("act", (lambda e, b=b: e.activation(out=SM[:, b, 24:32], in_=SM[:, b, 16:24], func=AF.Exp)),
                     r=[("SM", b, 1)], w=[("SM", b, 2)])
                P.op("act", (lambda e, b=b: e.activation(out=SM[:, b, 32:40], in_=PSB[2][:, 16:24], func=AF.Exp)),
                     r=[("ps", 2)], w=[("SM", b, 3)])
                P.op("act", (lambda e, b=b: e.activation(out=SM[:, b, 40:48], in_=SM[:, b, 8:16], func=AF.Exp)),
                     r=[("SM", b, 0)], w=[("SM", b, 4)])
                P.op("dve", (lambda e, b=b: e.tensor_tensor(out=SM[:, b, 48:56], in0=SM[:, b, 0:8],
                                                           in1=SM[:, b, 24:32], op=ALU.mult)),
                     r=[("SM", b, 0), ("SM", b, 2)], w=[("SM", b, 5)])

                def bc8(ap):
                    return ap.unsqueeze(2).broadcast_to([128, 8, 64])

                def v8(ap):
                    return ap.rearrange("p (h d) -> p h d", d=64)

                yield
                for i4 in range(4):
                    P.op("pe", (lambda e, i4=i4, cs=cs: e.transpose(out=PSB[0][:, i4 * 128:(i4 + 1) * 128],
                                                                    in_=XS[:, i4, cs], identity=ident[:])),
                         r=[("XS", i4), "ident"], w=[("ps", 0)])
                P.op("dve", (lambda e, b=b: e.tensor_tensor(out=v8(XDT[:, b, :]), in0=v8(PSB[0][:, :]),
                                                           in1=bc8(SM[:, b, 0:8]), op=ALU.mult)),
                     r=[("ps", 0), ("SM", b, 0)], w=[("XDT", b)])
                P.op("dve", (lambda e, b=b: e.tensor_tensor(out=v8(XDS[:, b, :]), in0=v8(PSB[0][:, :]),
                                                           in1=bc8(SM[:, b, 48:56]), op=ALU.mult)),
                     r=[("ps", 0), ("SM", b, 5)], w=[("XDS", b)])
                P.op("dve", (lambda e, b=b: e.tensor_tensor(out=v8(XSD[:, b, :]), in0=v8(PSB[0][:, :]),
                                                           in1=bc8(DBC[:, l, :]), op=ALU.mult)),
                     r=[("ps", 0), ("DBC", l)], w=[("XSD", b)])
                yield
                pbt = PSB[2][:, 256:384].bitcast(BF16)
                for g2 in range(2):
                    P.op("pe", (lambda e, g2=g2, cs=cs: e.transpose(out=pbt[:, g2 * 128:(g2 + 1) * 128],
                                                                    in_=BC[:, g2, cs], identity=identb[:])),
                         r=[("BC", g2), "identb"], w=[("ps", 2)])
                P.op("act", (lambda e, b=b: e.copy(out=BTM[:, b, :], in_=pbt)),
                     r=[("ps", 2)], w=[("BTM", b)])
                for g2 in range(2):
                    P.op("pe", (lambda e, g2=g2, cs=cs: e.matmul(PSB[3][:, b * 256 + g2 * 128:b * 256 + (g2 + 1) * 128],
                                                                 lhsT=BC[:, g2, cs], rhs=BC[:, 2 + g2, cs],
                                                                 start=True, stop=True)),
                         r=[("BC", g2), ("BC", 2 + g2)], w=[("ps", 3)])
                yield
                for g2 in range(2):
                    P.op("pe", (lambda e, g2=g2, cs=cs: e.matmul(
                        PSB[6][:, g2 * 256:(g2 + 1) * 256], lhsT=BC[:, 2 + g2, cs],
                        rhs=STATEB[:, g2 * 4:(g2 + 1) * 4, :].rearrange("p h d -> p (h d)"),
                        start=True, stop=True)),
                        r=[("BC", 2 + g2), "STATEB"], w=[("ps", 6)])
                P.op("dve", (lambda e, b=b: e.tensor_tensor(out=v8(Y1[:, b * 2, :]), in0=v8(PSB[6][:, :]),
                                                           in1=bc8(SM[:, b, 40:48]), op=ALU.mult)),
                     r=[("ps", 6), ("SM", b, 4)], w=[("Y1", b, 0)])
                for g2 in range(2):
                    P.op("pe", (lambda e, g2=g2, b=b: e.matmul(
                        PSB[7][:, g2 * 256:(g2 + 1) * 256], lhsT=BTM[:, b, g2 * 128:(g2 + 1) * 128],
                        rhs=XDS[:, b, g2 * 256:(g2 + 1) * 256], start=True, stop=True)),
                        r=[("BTM", b), ("XDS", b)], w=[("ps", 7)])
                P.op("dve", (lambda e, b=b: e.tensor_tensor(out=STATE[:], in0=STATE[:], in1=bc8(SM[:, b, 32:40]),
                                                           op=ALU.mult)),
                     r=["STATE", ("SM", b, 3)], w=["STATE"])
                P.op("dve", lambda e: e.tensor_tensor(out=STATE[:], in0=STATE[:], in1=v8(PSB[7][:, :]), op=ALU.add),
                     r=["STATE", ("ps", 7)], w=["STATE"])
                P.op("dve", lambda e: e.tensor_copy(out=STATEB[:], in_=STATE[:]),
                     r=["STATE"], w=["STATEB"])

                yield
                for g2 in range(2):
                    for hh in range(4):
                        h = g2 * 4 + hh
                        o = PSB[4][:, hh * 128:(hh + 1) * 128]
                        P.op("pe", (lambda e, o=o, h=h, cs=cs: e.matmul(o, lhsT=EH[:, h:h + 1].broadcast_to([8, 128]), rhs=D8[0:8, 3, cs],
                                                                        start=True, stop=False)),
                             r=["EH", ("ACS", c)], w=[("ps", 4)])
                        P.op("pe", (lambda e, o=o, h=h, cs=cs: e.matmul(o, lhsT=D8[0:8, 3, cs], rhs=NEH[:, h:h + 1].broadcast_to([8, 128]),
                                                                        start=False, stop=False)),
                             r=["NEH", ("ACS", c)], w=[("ps", 4)])
                        P.op("pe", (lambda e, o=o: e.matmul(o, lhsT=ident[:], rhs=NEGM[:], start=False, stop=True)),
                             r=["ident", "NEGM"], w=[("ps", 4)])
                    P.op("act", (lambda e, g2=g2: e.activation(out=LT[:, g2, :], in_=PSB[4][:, :], func=AF.Exp)),
                         r=[("ps", 4)], w=[("LT", g2)])
                    P.op("dve", (lambda e, g2=g2: e.tensor_tensor(
                        out=GT[:, b * 2 + g2, :].rearrange("p (h l) -> p h l", l=128),
                        in0=LT[:, g2, :].rearrange("p (h l) -> p h l", l=128),
                        in1=PSB[3][:, b * 256 + g2 * 128:b * 256 + (g2 + 1) * 128].unsqueeze(1).broadcast_to([128, 4, 128]),
                        op=ALU.mult)),
                        r=[("LT", g2), ("ps", 3)], w=[("GT", b, g2)])
                yield
                for h in range(8):
                    g2, hh = h // 4, h % 4
                    P.op("pe", (lambda e, h=h, g2=g2, hh=hh, b=b: e.matmul(
                        PSB[5][:, h * 64:(h + 1) * 64], lhsT=GT[:, b * 2 + g2, hh * 128:(hh + 1) * 128],
                        rhs=XDT[:, b, h * 64:(h + 1) * 64], start=True, stop=True)),
                        r=[("GT", b, g2), ("XDT", b)], w=[("ps", 5)])
                P.op("dve", lambda e: e.tensor_tensor(out=Y1[:, b * 2, :], in0=Y1[:, b * 2, :], in1=PSB[5][:, :], op=ALU.add),
                     r=[("ps", 5), ("Y1", b, 0)], w=[("Y1", b, 0)])
                yield
                for half, (iz, Sz) in enumerate(((iz0, SZ0), (iz1, SZ1))):
                    for kt in range(8):
                        P.op("pe", (lambda e, half=half, Sz=Sz, kt=kt, cs=cs: e.matmul(
                            PSB[1][:, half * 256:(half + 1) * 256], lhsT=HTB[:, kt, cs], rhs=Sz[:, kt, 0:256],
                            start=(kt == 0), stop=(kt == 7))),
                            r=[("WS", iz, kt // 4), ("HTB", kt, 0)], w=[("ps", 1)])
                P.op("act", lambda e: e.activation(out=SZ[:, b, :], in_=PSB[1][:, :], func=AF.Silu),
                     r=[("ps", 1)], w=[("SZ", b)])
                yield
                P.op("dve", (lambda e, b=b: e.tensor_tensor(out=Y1[:, b * 2, :], in0=Y1[:, b * 2, :], in1=XSD[:, b, :], op=ALU.add)),
                     r=[("XSD", b), ("Y1", b, 0)], w=[("Y1", b, 0)])
                P.op("dve", lambda e: e.tensor_tensor(out=Y1[:, b * 2, :], in0=Y1[:, b * 2, :], in1=SZ[:, b, :], op=ALU.mult),
                     r=[("SZ", b), ("Y1", b, 0)], w=[("Y1", b, 0)])
                P.op("act", lambda e: e.activation(out=Y1[:, b * 2 + 1, :], in_=Y1[:, b * 2, :], func=AF.Square,
                                                   accum_out=SS[:, b, 0:1]),
                     r=[("Y1", b, 0)], w=[("Y1", b, 1), ("SS", b)])
                P.op("act", lambda e: e.activation(out=SS[:, b, 1:2], in_=SS[:, b, 0:1], func=AF.Ln,
                                                   bias=CST[:, 0:1], scale=1.0 / 512),
                     r=[("SS", b), "cst"], w=[("SS1", b)])
                P.op("act", lambda e: e.activation(out=SS[:, b, 2:3], in_=SS[:, b, 1:2], func=AF.Exp, scale=-0.5),
                     r=[("SS1", b)], w=[("SS2", b)])
                P.op("dve", (lambda e, b=b: e.scalar_tensor_tensor(
                    out=YTM[:, b, :], in0=Y1[:, b * 2, :], scalar=SS[:, b, 2:3], in1=NORMW[:, 0, :],
                    op0=ALU.mult, op1=ALU.mult)),
                    r=[("Y1", b, 0), ("SS2", b), ("NORMW", l)], w=[("YTM", b)])
                pyt = PSB[7][:, 256:512].bitcast(BF16)
                for i4 in range(4):
                    P.op("pe", (lambda e, i4=i4, b=b: e.transpose(out=pyt[:, i4 * 128:(i4 + 1) * 128],
                                                                  in_=YTM[:, b, i4 * 128:(i4 + 1) * 128],
                                                                  identity=identb[:])),
                         r=[("YTM", b), "identb"], w=[("ps", 7)])
                P.op("act", (lambda e, cs=cs: e.copy(out=YTB[:, 0:4, cs], in_=pyt.rearrange("p (a t) -> p a t", t=128))),
                     r=[("ps", 7)], w=[("YTB", k_) for k_ in range(4)])
            run_pipelined([chunk(c_) for c_ in range(4)], 2)

        TWO_PI = 6.283185307179586
        PI = 3.141592653589793

        def s5_setup():
            SC = am.alloc([20, 8], F32)
            MAG = am.alloc([1, 8], F32)
            NS128 = am.alloc([1, 8], F32)
            COST = am.alloc([8, 129], F32)
            SINT = am.alloc([8, 129], F32)
            LBR = am.alloc([8, 128], BF16)
            LBI = am.alloc([8, 128], BF16)
            LCR = am.alloc([8, 64], BF16)
            LCI = am.alloc([8, 64], BF16)
            SDG = am.alloc([1, 4], F32)
            GLW = am.alloc([2, 256], BF16)
            CARR = am.alloc([1, 8], F32)
            CARI = am.alloc([1, 8], F32)
            mark = am.off
            TAU = am.alloc([1, 129], F32)
            ARG = am.alloc([8, 129], F32)
            TMP1 = am.alloc([8, 129], F32)
            TMP2 = am.alloc([8, 129], F32)
            BN2 = am.alloc([2, 8, 64], F32)
            BN2b = am.alloc([2, 8, 64], BF16)
            CNat = am.alloc([2, 8, 128], F32)
            CT = am.alloc([4, 256], F32)
            st = {"n": 0}

            def K_(name):
                return ("s5c", name)

            def col(c):
                return SC[:, c, :]

            log_dt = W["s_log_dt"][l].rearrange("(gp gl) -> gl gp", gl=2)
            for gl in range(2):
                small_dma(SC[gl * 64:(gl + 1) * 64, 0, :], log_dt[gl:gl + 1, :].broadcast_to([64, 8]), K_(("ldt", gl)))
            small_dma(SC[:, 1, :], W["s_A_re"][l].rearrange("(gp gl) p -> (gl p) gp", gl=2), K_("are"))
            small_dma(SC[:, 2, :], W["s_A_im"][l].rearrange("(gp gl) p -> (gl p) gp", gl=2), K_("aim"))
            small_dma(SDG[:, 0, 0:2], W["s_D"][l].rearrange("(t p) -> p t", p=128), K_("sd"))
            small_dma(SDG[:, 0, 2:4], W["s_glu_b"][l].rearrange("(t p) -> p t", p=128), K_("glb"))
            wload(GLW[:, :, :], W["s_glu_w"][l].rearrange("(kt p) c -> p kt c", p=128), K_("glw"))

            def dv(fn, r, w):
                P.op("dve", fn, r=[K_(x) for x in r], w=[K_(x) for x in w])

            def ac(fn, r, w):
                P.op("act", fn, r=[K_(x) for x in r], w=[K_(x) for x in w])

            ac(lambda e: e.activation(out=col(0), in_=col(0), func=AF.Exp), [("ldt", 0), ("ldt", 1)], ["dt"])
            dv(lambda e: e.tensor_tensor(out=col(3), in0=col(1), in1=col(0), op=ALU.mult), ["are", "dt"], ["ar"])
            dv(lambda e: e.tensor_tensor(out=col(4), in0=col(2), in1=col(0), op=ALU.mult), ["aim", "dt"], ["th"])
            ac(lambda e: e.activation(out=MAG[:, 0, :], in_=col(3), func=AF.Exp), ["ar"], ["mag"])

            def sincos(th, out_s, out_c, t1, t2, rk, wk):
                for phase, out in ((0.0, out_s), (PI / 2, out_c)):
                    tag = "s" if phase == 0.0 else "c"
                    dv(lambda e: e.tensor_scalar(out=t1, in0=th, scalar1=1.0 / TWO_PI, scalar2=phase / TWO_PI + 0.5,
                                                 op0=ALU.mult, op1=ALU.add), rk, [wk + "t1"])
                    t2i = t2.bitcast(I32)
                    dv(lambda e: e.tensor_copy(out=t2i, in_=t1), [wk + "t1"], [wk + "t2"])
                    dv(lambda e: e.tensor_copy(out=t1, in_=t2i), [wk + "t2"], [wk + "t1"])
                    dv(lambda e: e.scalar_tensor_tensor(out=t1, in0=t1, scalar=-TWO_PI, in1=th, op0=ALU.mult,
                                                        op1=ALU.add), [wk + "t1"] + rk, [wk + "t1"])
                    if phase != 0.0:
                        dv(lambda e: e.tensor_scalar(out=t1, in0=t1, scalar1=phase, scalar2=None, op0=ALU.add),
                           [wk + "t1"], [wk + "t1"])
                    dv(lambda e: e.tensor_scalar(out=t2, in0=t1, scalar1=PI, scalar2=-TWO_PI, op0=ALU.is_gt,
                                                 op1=ALU.mult), [wk + "t1"], [wk + "t2"])
                    dv(lambda e: e.tensor_tensor(out=t1, in0=t1, in1=t2, op=ALU.add), [wk + "t1", wk + "t2"], [wk + "t1"])
                    dv(lambda e: e.tensor_scalar(out=t2, in0=t1, scalar1=-PI, scalar2=TWO_PI, op0=ALU.is_lt,
                                                 op1=ALU.mult), [wk + "t1"], [wk + "t2"])
                    dv(lambda e: e.tensor_tensor(out=t1, in0=t1, in1=t2, op=ALU.add), [wk + "t1", wk + "t2"], [wk + "t1"])
                    ac(lambda e, out=out: e.activation(out=out, in_=t1, func=AF.Sin), [wk + "t1"], [wk + tag])

            sincos(col(4), col(5), col(6), col(18), col(19), ["th"], "sc0")
            dv(lambda e: e.tensor_tensor(out=col(7), in0=MAG[:, 0, :], in1=col(6), op=ALU.mult), ["mag", "sc0c"], ["lr"])
            dv(lambda e: e.tensor_tensor(out=col(8), in0=MAG[:, 0, :], in1=col(5), op=ALU.mult), ["mag", "sc0s"], ["li"])
            dv(lambda e: e.tensor_tensor(out=col(9), in0=col(1), in1=col(1), op=ALU.mult), ["are"], ["den"])
            dv(lambda e: e.tensor_tensor(out=col(13), in0=col(2), in1=col(2), op=ALU.mult), ["aim"], ["den2"])
            dv(lambda e: e.tensor_tensor(out=col(9), in0=col(9), in1=col(13), op=ALU.add), ["den", "den2"], ["den"])
            dv(lambda e: e.reciprocal(out=col(9), in_=col(9)), ["den"], ["den"])
            dv(lambda e: e.tensor_scalar(out=col(10), in0=col(7), scalar1=-1.0, scalar2=None, op0=ALU.add), ["lr"], ["nr"])
            dv(lambda e: e.tensor_tensor(out=col(11), in0=col(10), in1=col(1), op=ALU.mult), ["nr", "are"], ["cr"])
            dv(lambda e: e.tensor_tensor(out=col(13), in0=col(8), in1=col(2), op=ALU.mult), ["li", "aim", "den2", "den"], ["tmp13"])
            dv(lambda e: e.tensor_tensor(out=col(11), in0=col(11), in1=col(13), op=ALU.add), ["cr", "tmp13"], ["cr"])
            dv(lambda e: e.tensor_tensor(out=col(11), in0=col(11), in1=col(9), op=ALU.mult), ["cr", "den"], ["cr"])
            dv(lambda e: e.tensor_tensor(out=col(12), in0=col(8), in1=col(1), op=ALU.mult), ["li", "are"], ["ci"])
            dv(lambda e: e.tensor_tensor(out=col(13), in0=col(10), in1=col(2), op=ALU.mult), ["nr", "aim", "cr"], ["tmp13"])
            dv(lambda e: e.tensor_tensor(out=col(12), in0=col(12), in1=col(13), op=ALU.subtract), ["ci", "tmp13"], ["ci"])
            dv(lambda e: e.tensor_tensor(out=col(12), in0=col(12), in1=col(9), op=ALU.mult), ["ci", "den"], ["ci"])
            dv(lambda e: e.tensor_tensor_scan(out=TAU[:, 0, :], data0=ONESW[:, 0:129],
                                              data1=ONESW[:, 0:129], initial=-1.0, op0=ALU.mult, op1=ALU.add),
               [], ["tau"])
            dv(lambda e: e.tensor_tensor(out=ARG[:], in0=TAU[:, 0:1, :].broadcast_to([128, 8, 129]),
                                         in1=col(4).unsqueeze(2).broadcast_to([128, 8, 129]), op=ALU.mult),
               ["tau", "th"], ["arg"])
            sincos(ARG[:], SINT[:], COST[:], TMP1[:], TMP2[:], ["arg"], "sc1")
            dv(lambda e: e.tensor_scalar(out=NS128[:, 0, :], in0=SINT[:, :, 128], scalar1=-1.0, scalar2=None, op0=ALU.mult),
               ["sc1s"], ["ns128"])
            for ri, nm in enumerate(("s_B_re", "s_B_im")):
                P.op("pool", (lambda e, ri=ri: e.memset(BN2[:, ri], 0.0)), w=[K_(("bn2", ri))])
                srcb = W[nm][l].rearrange("(gp gl) p h -> gl p gp h", gl=2)
                for gl in range(2):
                    for par in range(2):
                        P.dma("sp", (lambda e, ri=ri, gl=gl, par=par, srcb=srcb: e.dma_start(
                            out=BN2[gl * 64:(gl + 1) * 64, ri, par::2, par * 32 + gl * 16:par * 32 + (gl + 1) * 16],
                            in_=srcb[gl][:, par::2, :], allow_slow_non_contiguous=True)),
                            w=[K_(("bn2", ri))], sem="D_bn2_%d_%d_%d" % (ri, gl, par))
                P.op("pool", (lambda e, ri=ri: e.tensor_copy(out=BN2b[:, ri], in_=BN2[:, ri])),
                     r=[K_(("bn2", ri))], w=[K_(("bn2b", ri))])
                pbf = PSB[ri][:, :].bitcast(BF16)
                for gp in range(8):
                    pr = ((gp % 4) // 2) * 64
                    P.op("pe", (lambda e, ri=ri, gp=gp, pr=pr, pbf=pbf: e.transpose(
                        out=pbf[pr:pr + 64, gp * 128:(gp + 1) * 128], in_=BN2b[:, ri, gp, :], identity=identb[:])),
                        r=[K_(("bn2b", ri)), "identb"], w=[("ps", ri)])
                LB = LBR if ri == 0 else LBI
                for gp in range(8):
                    pr = ((gp % 4) // 2) * 64
                    P.op("act", (lambda e, LB=LB, gp=gp, pr=pr, pbf=pbf: e.copy(
                        out=LB[pr:pr + 64, gp, :], in_=pbf[pr:pr + 64, gp * 128:(gp + 1) * 128])),
                        r=[("ps", ri)], w=[K_(("lb", ri))])
            for ri, nm in enumerate(("s_C_re", "s_C_im")):
                P.op("pool", (lambda e, ri=ri: e.memset(CNat[0:32, ri], 0.0)), w=[K_(("cn", ri))])
                srcc = W[nm][l].rearrange("(gp gl) h p -> gl h gp p", gl=2)
                for gl in range(2):
                    P.dma("sp", (lambda e, ri=ri, gl=gl, srcc=srcc: e.dma_start(
                        out=CNat[gl * 16:(gl + 1) * 16, ri, :, gl * 64:(gl + 1) * 64], in_=srcc[gl],
                        allow_slow_non_contiguous=True)),
                        w=[K_(("cn", ri))], sem="D_cn_%d_%d" % (ri, gl))
                for gp in range(8):
                    P.op("pe", (lambda e, ri=ri, gp=gp: e.transpose(
                        out=PSB[2 + ri][:, gp * 32:(gp + 1) * 32], in_=CNat[0:32, ri, gp, :], identity=ident[0:32, 0:32])),
                        r=[K_(("cn", ri)), "ident"], w=[("ps", 2 + ri)])

            def b32(c):
                return col(c).unsqueeze(2).broadcast_to([128, 8, 32])

            def v32(ap):
                return ap.rearrange("p (g c) -> p g c", c=32)
            crt = v32(PSB[2][:, 0:256])
            cit = v32(PSB[3][:, 0:256])
            P.op("dve", lambda e: e.tensor_tensor(out=v32(CT[:, 0, :]), in0=crt, in1=b32(11), op=ALU.mult),
                 r=[("ps", 2), K_("cr")], w=[K_("ct0")])
            P.op("dve", lambda e: e.tensor_tensor(out=v32(CT[:, 1, :]), in0=cit, in1=b32(12), op=ALU.mult),
                 r=[("ps", 3), K_("ci")], w=[K_("ct1")])
            P.op("dve", lambda e: e.tensor_tensor(out=v32(CT[:, 2, :]), in0=crt, in1=b32(12), op=ALU.mult),
                 r=[("ps", 2), K_("ci")], w=[K_("ct2")])
            P.op("dve", lambda e: e.tensor_tensor(out=v32(CT[:, 3, :]), in0=cit, in1=b32(11), op=ALU.mult),
                 r=[("ps", 3), K_("cr")], w=[K_("ct3")])
            P.op("pool", lambda e: e.memset(LCR[:], 0.0), w=[K_("lcr")])
            P.op("pool", lambda e: e.memset(LCI[:], 0.0), w=[K_("lci")])
            dv(lambda e: e.tensor_tensor(out=CT[:, 2, :], in0=CT[:, 2, :], in1=CT[:, 3, :], op=ALU.add), ["ct2", "ct3"], ["ct2"])
            for par in range(2):
                dv(lambda e, par=par: e.tensor_tensor(
                    out=LCR[:, par::2, par * 32:(par + 1) * 32], in0=v32(CT[:, 0, :])[:, par::2, :],
                    in1=v32(CT[:, 1, :])[:, par::2, :], op=ALU.subtract), ["ct0", "ct1", "lcr"], ["lcr"])
                dv(lambda e, par=par: e.tensor_scalar(
                    out=LCI[:, par::2, par * 32:(par + 1) * 32], in0=v32(CT[:, 2, :])[:, par::2, :], scalar1=-1.0,
                    scalar2=None, op0=ALU.mult), ["ct2", "lci"], ["lci"])
            P.op("pool", lambda e: e.memset(CARR[:], 0.0), w=["CARR"])
            P.op("pool", lambda e: e.memset(CARI[:], 0.0), w=["CARI"])
            am.off = mark
            return dict(MAG=MAG, NS128=NS128, COST=COST, SINT=SINT, LBR=LBR, LBI=LBI, LCR=LCR, LCI=LCI, SDG=SDG,
                        GLW=GLW, CARR=CARR, CARI=CARI, K_=K_)

        def s5_block(tb, C5):
            am.off = base_off
            K_ = C5["K_"]
            MAG, NS128, COST, SINT = C5["MAG"], C5["NS128"], C5["COST"], C5["SINT"]
            LBR, LBI, LCR, LCI, SDG, GLW = C5["LBR"], C5["LBI"], C5["LCR"], C5["LCI"], C5["SDG"], C5["GLW"]
            CARR, CARI = C5["CARR"], C5["CARI"]
            US = am.alloc([2, 512], F32)
            USB = am.alloc([2, 512], BF16)
            T = am.alloc([8, 512], F32)
            WRI = am.alloc([4, 512], F32)
            ZRI = am.alloc([4, 512], F32)
            XRI = am.alloc([4, 512], BF16)
            YS = am.alloc([2, 512], F32)
            YG = am.alloc([2, 512], F32)
            YGB = am.alloc([2, 512], BF16)
            CTMP = am.alloc([1, 4], F32)
            i, S = load_wblock(win, 1544, 256)
            for ct in range(2):
                for kt in range(8):
                    P.op("pe", (lambda e, kt=kt, ct=ct, S=S: e.matmul(
                        PSB[ct][:, :], lhsT=S[:, kt, ct * 128:(ct + 1) * 128], rhs=HTB[:, kt, :],
                        start=(kt == 0), stop=(kt == 7))),
                        r=[("WS", i, kt // 4), ("HTB", kt, 0)], w=[("ps", ct)], nofence=True)
                P.op("act", (lambda e, ct=ct: e.copy(out=US[:, ct, :], in_=PSB[ct][:, :])), r=[("ps", ct)], w=[("US", ct)])
                P.op("pool", (lambda e, ct=ct: e.tensor_copy(out=USB[:, ct, :], in_=US[:, ct, :])),
                     r=[("US", ct)], w=[("USB", ct)])

            def b4(tab, gp):
                return tab[:, gp, 0:128].unsqueeze(1).broadcast_to([128, 4, 128])

            def v4(ap):
                return ap.rearrange("p (c t) -> p c t", t=128)

            def pair(gp):
                pb = gp % 2
                b2, b3 = (2, 3) if pb == 0 else (6, 7)
                ct = gp // 4
                pr = ((gp % 4) // 2) * 64
                P.op("pe", (lambda e, gp=gp, ct=ct, pr=pr: e.matmul(
                    PSB[b2][:, :], lhsT=LBR[pr:pr + 64, gp, :], rhs=USB[pr:pr + 64, ct, :], start=True, stop=True)),
                    r=[K_(("lb", 0)), ("USB", ct)], w=[("ps", b2)])
                P.op("pe", (lambda e, gp=gp, ct=ct, pr=pr: e.matmul(
                    PSB[b3][:, :], lhsT=LBI[pr:pr + 64, gp, :], rhs=USB[pr:pr + 64, ct, :], start=True, stop=True)),
                    r=[K_(("lb", 1)), ("USB", ct)], w=[("ps", b3)])
                cosb, sinb = b4(COST, gp), b4(SINT, gp)
                yield
                rk = [K_("sc1c"), K_("sc1s")]
                P.op("dve", (lambda e, cosb=cosb: e.tensor_tensor(out=v4(T[:, pb * 4 + 0, :]), in0=v4(PSB[b2][:, :]), in1=cosb, op=ALU.mult)),
                     r=[("ps", b2)] + rk, w=[("T", pb, 0)])
                P.op("dve", (lambda e, sinb=sinb: e.tensor_tensor(out=v4(T[:, pb * 4 + 1, :]), in0=v4(PSB[b3][:, :]), in1=sinb, op=ALU.mult)),
                     r=[("ps", b3)] + rk, w=[("T", pb, 1)])
                P.op("dve", (lambda e, cosb=cosb: e.tensor_tensor(out=v4(T[:, pb * 4 + 2, :]), in0=v4(PSB[b3][:, :]), in1=cosb, op=ALU.mult)),
                     r=[("ps", b3)] + rk, w=[("T", pb, 2)])
                P.op("dve", (lambda e, sinb=sinb: e.tensor_tensor(out=v4(T[:, pb * 4 + 3, :]), in0=v4(PSB[b2][:, :]), in1=sinb, op=ALU.mult)),
                     r=[("ps", b2)] + rk, w=[("T", pb, 3)])
                P.op("pool", lambda e: e.tensor_tensor(out=WRI[:, pb * 2 + 0, :], in0=T[:, pb * 4 + 0, :], in1=T[:, pb * 4 + 1, :], op=ALU.add),
                     r=[("T", pb, 0), ("T", pb, 1)], w=[("WRI", pb, 0)])
                P.op("pool", lambda e: e.tensor_tensor(out=WRI[:, pb * 2 + 1, :], in0=T[:, pb * 4 + 2, :], in1=T[:, pb * 4 + 3, :], op=ALU.subtract),
                     r=[("T", pb, 2), ("T", pb, 3)], w=[("WRI", pb, 1)])
                yield
                magb = MAG[:, 0, gp:gp + 1].broadcast_to([128, 128])
                for c in range(4):
                    cs = slice(c * 128, (c + 1) * 128)
                    first = (tb == 0 and c == 0)
                    for ri, CAR in ((0, CARR), (1, CARI)):
                        init = 0.0 if first else CAR[:, 0, gp:gp + 1]
                        P.op("dve", (lambda e, ri=ri, cs=cs, init=init, magb=magb: e.tensor_tensor_scan(
                            out=ZRI[:, pb * 2 + ri, cs], data0=magb, data1=WRI[:, pb * 2 + ri, cs], initial=init,
                            op0=ALU.mult, op1=ALU.add)),
                            r=[("WRI", pb, ri), K_("mag"), ("CAR", ri, gp)], w=[("ZRI", pb, ri, c)])
                    zr = ZRI[:, pb * 2, c * 128 + 127:c * 128 + 128]
                    zi = ZRI[:, pb * 2 + 1, c * 128 + 127:c * 128 + 128]
                    c128 = COST[:, gp, 128:129]
                    s128 = SINT[:, gp, 128:129]
                    ns128 = NS128[:, 0, gp:gp + 1]
                    P.op("act", (lambda e, zi=zi, ns128=ns128: e.activation(out=CTMP[:, 0, pb * 2:pb * 2 + 1], in_=zi, func=AF.Identity,
                                                                            scale=ns128)),
                         r=[("ZRI", pb, 1, c), K_("ns128")], w=[("CTMP0", pb)])
                    P.op("act", (lambda e, zr=zr, c128=c128, gp=gp: e.activation(out=CARR[:, 0, gp:gp + 1], in_=zr, func=AF.Identity,
                                                                                scale=c128, bias=CTMP[:, 0, pb * 2:pb * 2 + 1])),
                         r=[("ZRI", pb, 0, c), ("CTMP0", pb)] + rk, w=[("CAR", 0, gp)])
                    P.op("pool", (lambda e, zr=zr, s128=s128: e.tensor_scalar(out=CTMP[:, 0, pb * 2 + 1:pb * 2 + 2], in0=zr, scalar1=s128,
                                                                              scalar2=None, op0=ALU.mult)),
                         r=[("ZRI", pb, 0, c)] + rk, w=[("CTMP1", pb)])
                    P.op("pool", (lambda e, zi=zi, c128=c128, gp=gp: e.tensor_scalar(
                        out=CARI[:, 0, gp:gp + 1], in0=zi, scalar1=c128, scalar2=CTMP[:, 0, pb * 2 + 1:pb * 2 + 2],
                        op0=ALU.mult, op1=ALU.add)),
                         r=[("ZRI", pb, 1, c), ("CTMP1", pb)] + rk, w=[("CAR", 1, gp)])
                yield
                zk = [("ZRI", pb, 0, c_) for c_ in range(4)]
                zki = [("ZRI", pb, 1, c_) for c_ in range(4)]
                P.op("dve", (lambda e, cosb=cosb: e.tensor_tensor(out=v4(T[:, pb * 4 + 0, :]), in0=v4(ZRI[:, pb * 2, :]), in1=cosb, op=ALU.mult)),
                     r=zk + rk, w=[("T", pb, 0)])
                P.op("dve", (lambda e, sinb=sinb: e.tensor_tensor(out=v4(T[:, pb * 4 + 1, :]), in0=v4(ZRI[:, pb * 2 + 1, :]), in1=sinb, op=ALU.mult)),
                     r=zki + rk, w=[("T", pb, 1)])
                P.op("pool", (lambda e, sinb=sinb: e.tensor_tensor(out=v4(T[:, pb * 4 + 2, :]), in0=v4(ZRI[:, pb * 2, :]), in1=sinb, op=ALU.mult)),
                     r=zk + rk, w=[("T", pb, 2)])
                P.op("pool", (lambda e, cosb=cosb: e.tensor_tensor(out=v4(T[:, pb * 4 + 3, :]), in0=v4(ZRI[:, pb * 2 + 1, :]), in1=cosb, op=ALU.mult)),
                     r=zki + rk, w=[("T", pb, 3)])
                P.op("dve", lambda e: e.tensor_tensor(out=XRI[:, pb * 2 + 0, :], in0=T[:, pb * 4 + 0, :], in1=T[:, pb * 4 + 1, :], op=ALU.subtract),
                     r=[("T", pb, 0), ("T", pb, 1)], w=[("XRI", pb, 0)])
                P.op("pool", lambda e: e.tensor_tensor(out=XRI[:, pb * 2 + 1, :], in0=T[:, pb * 4 + 2, :], in1=T[:, pb * 4 + 3, :], op=ALU.add),
                     r=[("T", pb, 2), ("T", pb, 3)], w=[("XRI", pb, 1)])
                yield
                P.op("pe", (lambda e, gp=gp, ct=ct, pr=pr: e.matmul(
                    PSB[4 + ct][pr:pr + 64, :], lhsT=LCR[:, gp, :], rhs=XRI[:, pb * 2 + 0, :], start=(gp % 2 == 0), stop=False)),
                    r=[K_("lcr"), ("XRI", pb, 0)], w=[("ps", 4 + ct)])
                P.op("pe", (lambda e, gp=gp, ct=ct, pr=pr: e.matmul(
                    PSB[4 + ct][pr:pr + 64, :], lhsT=LCI[:, gp, :], rhs=XRI[:, pb * 2 + 1, :], start=False, stop=(gp % 2 == 1))),
                    r=[K_("lci"), ("XRI", pb, 1)], w=[("ps", 4 + ct)])
            run_pipelined([pair(g_) for g_ in range(8)], 2)
            for ct in range(2):
                P.op("dve", (lambda e, ct=ct: e.scalar_tensor_tensor(
                    out=YS[:, ct, :], in0=US[:, ct, :], scalar=SDG[:, 0, ct:ct + 1], in1=PSB[4 + ct][:, :],
                    op0=ALU.mult, op1=ALU.add)),
                    r=[("US", ct), ("ps", 4 + ct), K_("sd")], w=[("YS", ct)])
                P.op("pool", (lambda e, ct=ct: e.tensor_tensor(out=T[:, ct, :], in0=YS[:, ct, :], in1=YS[:, ct, :], op=ALU.mult)),
                     r=[("YS", ct)], w=[("T", 0, ct)])
                P.op("pool", (lambda e, ct=ct: e.tensor_scalar(out=T[:, ct, :], in0=T[:, ct, :], scalar1=0.044715, scalar2=1.0,
                                                               op0=ALU.mult, op1=ALU.add)),
                     r=[("T", 0, ct)], w=[("T", 0, ct)])
                P.op("pool", (lambda e, ct=ct: e.tensor_tensor(out=T[:, ct, :], in0=T[:, ct, :], in1=YS[:, ct, :], op=ALU.mult)),
                     r=[("T", 0, ct), ("YS", ct)], w=[("T", 0, ct)])
                P.op("act", (lambda e, ct=ct: e.activation(out=T[:, 2 + ct, :], in_=T[:, ct, :], func=AF.Sigmoid,
                                                           scale=1.5957691216057308)),
                     r=[("T", 0, ct)], w=[("T", 0, 2 + ct)])
                P.op("dve", (lambda e, ct=ct: e.tensor_tensor(out=YG[:, ct, :], in0=YS[:, ct, :], in1=T[:, 2 + ct, :], op=ALU.mult)),
                     r=[("YS", ct), ("T", 0, 2 + ct)], w=[("YG", ct)])
                P.op("pool", (lambda e, ct=ct: e.tensor_copy(out=YGB[:, ct, :], in_=YG[:, ct, :])),
                     r=[("YG", ct)], w=[("YGB", ct)])
            for mt in range(2):
                for kt in range(2):
                    P.op("pe", (lambda e, mt=mt, kt=kt: e.matmul(
                        PSB[6 + mt][:, :], lhsT=GLW[:, kt, mt * 128:(mt + 1) * 128], rhs=YGB[:, kt, :],
                        start=(kt == 0), stop=(kt == 1))),
                        r=[K_("glw"), ("YGB", kt)], w=[("ps", 6 + mt)])
                P.op("act", (lambda e, mt=mt: e.activation(out=T[:, mt, :], in_=PSB[6 + mt][:, :], func=AF.Sigmoid,
                                                           bias=SDG[:, 0, 2 + mt:3 + mt], scale=1.0)),
                     r=[("ps", 6 + mt), K_("glb")], w=[("T", 0, mt)])
                P.op("dve", (lambda e, mt=mt: e.tensor_tensor(out=YTB[:, 4 + mt, :], in0=YG[:, mt, :], in1=T[:, mt, :], op=ALU.mult)),
                     r=[("YG", mt), ("T", 0, mt)], w=[("YTB", 4 + mt)])

        R0 = 1800

        def rw_setup():
            RC = am.alloc([1, 32], F32)
            MUL = am.alloc([1, 2], F32)
            LW = am.alloc([1, 256], BF16)
            HST = am.alloc([2, 64], F32)
            HB0 = am.alloc([2, 64], BF16)
            PREV = am.alloc([1, 8], F32)
            mark_ = am.off
            LWF = am.alloc([1, 256], F32)
            am.off = mark_

            def K_(n):
                return ("rwc", n)
            mu = W["r_mu"][l]
            for qi in range(3):
                small_dma(RC[:, 0, qi * 2:(qi + 1) * 2], mu[qi * 256:(qi + 1) * 256].rearrange("(pr p) -> p pr", p=128), K_("mu"))
            small_dma(MUL[:, 0, 0:1], mu[768:896].rearrange("(p o) -> p o", o=1), K_("mul"))
            for nm, c0 in (("r_w0", 12), ("r_a0", 14), ("r_k_k", 16), ("r_k_a", 18), ("r_gn_w", 24), ("r_gn_b", 26)):
                small_dma(RC[:, 0, c0:c0 + 2], W[nm][l].rearrange("(pr p) -> p pr", p=128), K_("cols"))
            small_dma(RC[:, 0, 22:24], W["r_r_k"][l].rearrange("(pr hh) p -> (hh p) pr", hh=2), K_("cols"))
            P.op("dve", lambda e: e.tensor_scalar(out=RC[:, 0, 6:12], in0=RC[:, 0, 0:6], scalar1=-1.0, scalar2=1.0,
                                                  op0=ALU.mult, op1=ALU.add), r=[K_("mu")], w=[K_("omu")])
            P.op("dve", lambda e: e.tensor_scalar(out=RC[:, 0, 20:22], in0=RC[:, 0, 18:20], scalar1=-1.0, scalar2=1.0,
                                                  op0=ALU.mult, op1=ALU.add), r=[K_("cols")], w=[K_("omka")])
            P.op("dve", lambda e: e.tensor_scalar(out=MUL[:, 0, 1:2], in0=MUL[:, 0, 0:1], scalar1=-1.0, scalar2=1.0,
                                                  op0=ALU.mult, op1=ALU.add), r=[K_("mul")], w=[K_("omul")])
            small_dma(LWF[0:32, 0, :], W["r_w2"][l], K_("lwf"))
            small_dma(LWF[32:64, 0, :], W["r_a2"][l], K_("lwf"))
            small_dma(LWF[64:128, 0, :], W["r_g2"][l], K_("lwf"))
            P.op("pool", lambda e: e.tensor_copy(out=LW[:, 0, :], in_=LWF[:, 0, :]), r=[K_("lwf")], w=[K_("lw")])
            P.op("pool", lambda e: e.memset(HST[:], 0.0), w=["HST"])
            P.op("pool", lambda e: e.memset(HB0[:], 0.0), w=["HB0"])
            P.op("pool", lambda e: e.memset(PREV[:], 0.0), w=["PREV"])
            return dict(RC=RC, MUL=MUL, LW=LW, HST=HST, HB0=HB0, PREV=PREV, K_=K_)

        def rw_block(tb, CR):
            am.off = base_off
            K_ = CR["K_"]
            RC, MUL, LW, HST, HB0, PREV = CR["RC"], CR["MUL"], CR["LW"], CR["HST"], CR["HB0"], CR["PREV"]
            Fb = am.alloc([10, 512], F32)
            RAW = am.alloc([1, 516], F32)
            LRAW = am.alloc([1, 516], F32)
            FL = am.alloc([1, 512], F32)
            LB16 = am.alloc([1, 512], BF16)
            LK = am.alloc([8, 128], BF16)
            RB = am.alloc([8, 128], BF16)
            AH = am.alloc([8, 128], BF16)
            VB = am.alloc([1, 512], BF16)
            AB1 = am.alloc([4, 128], BF16)
            AB2 = am.alloc([4, 128], BF16)
            AB3 = am.alloc([4, 64], BF16)
            TM = am.alloc([4, 256], BF16)
            Zr = am.alloc([2, 4, 128], BF16)
            PPr = am.alloc([2, 4, 128], BF16)
            G0TS = am.alloc([8, 64], BF16)
            HINC = am.alloc([8, 64], F32)
            QTS = am.alloc([8, 64], BF16)
            Y0TS = am.alloc([1, 512], F32)
            PTc = am.alloc([1, 8], F32)
            HBs = am.alloc([9, 64], BF16)
            TMPH = am.alloc([1, 64], F32)

            def F(i):
                return Fb[:, i, :]

            def fk(i):
                return ("F", i)

            def col(c):
                return RC[:, 0, c:c + 1]
            cK = [K_("mu"), K_("omu"), K_("cols"), K_("omka")]

            def dve(fn, r, w):
                P.op("dve", fn, r=r, w=w)

            def act(fn, r, w):
                P.op("act", fn, r=r, w=w)

            def pool(fn, r, w):
                P.op("pool", fn, r=r, w=w)

            HB = (slice(0, 64), slice(64, 128))

            i, S = load_wblock(win, R0 + 768, 128)
            mm_fm(0, i, S, 128, HTB, "HTB")
            act(lambda e: e.copy(out=LRAW[:, 0, 1:513], in_=PSB[0][:, :]), [("ps", 0)], ["LRAW"])
            if tb == 0:
                pool(lambda e: e.memset(LRAW[:, 0, 0:1], 0.0), [], ["LRAW0"])
            else:
                pool(lambda e: e.tensor_copy(out=LRAW[:, 0, 0:1], in_=PREV[:, 0, 6:7]), ["PREVL"], ["LRAW0"])
            dve(lambda e: e.tensor_scalar(out=FL[:, 0, :], in0=LRAW[:, 0, 1:513], scalar1=MUL[:, 0, 1:2], scalar2=None,
                                          op0=ALU.mult), ["LRAW", K_("omul")], ["FL"])
            dve(lambda e: e.scalar_tensor_tensor(out=FL[:, 0, :], in0=LRAW[:, 0, 0:512], scalar=MUL[:, 0, 0:1], in1=FL[:, 0, :],
                                                 op0=ALU.mult, op1=ALU.add), ["LRAW", "LRAW0", "FL", K_("mul")], ["FL"])
            pool(lambda e: e.tensor_copy(out=PREV[:, 0, 6:7], in_=LRAW[:, 0, 512:513]), ["LRAW", "LRAW0"], ["PREVL"])
            act(lambda e: e.activation(out=LB16[0:32, 0, :], in_=FL[0:32, 0, :], func=AF.Tanh), ["FL"], ["LB16a"])
            act(lambda e: e.copy(out=LB16[32:64, 0, :], in_=FL[32:64, 0, :]), ["FL"], ["LB16b"])
            act(lambda e: e.activation(out=LB16[64:128, 0, :], in_=FL[64:128, 0, :], func=AF.Sigmoid), ["FL"], ["LB16c"])
            slots = [load_wblock(win, R0 + qi * 256, 256) for qi in range(3)]

            def do_pair(pr):
                ps_ = slice(pr * 128, (pr + 1) * 128)
                for qi in range(3):
                    iq, Sq = slots[qi]
                    bank = qi % 2
                    for kt in range(8):
                        P.op("pe", (lambda e, kt=kt, Sq=Sq, bank=bank: e.matmul(
                            PSB[bank][:, :], lhsT=Sq[:, kt, ps_], rhs=HTB[:, kt, :], start=(kt == 0), stop=(kt == 7))),
                            r=[("WS", iq, kt // 4), ("HTB", kt, 0)], w=[("ps", bank)], nofence=True)
                    act((lambda e, bank=bank: e.copy(out=RAW[:, 0, 1:513], in_=PSB[bank][:, :])), [("ps", bank)], ["RAW"])
                    pc = qi * 2 + pr
                    if tb == 0:
                        pool(lambda e: e.memset(RAW[:, 0, 0:1], 0.0), [], ["RAW0"])
                    else:
                        pool((lambda e, pc=pc: e.tensor_copy(out=RAW[:, 0, 0:1], in_=PREV[:, 0, pc:pc + 1])),
                             [("PREV", pc)], ["RAW0"])
                    dve((lambda e, qi=qi, pc=pc: e.tensor_scalar(out=F(qi), in0=RAW[:, 0, 1:513], scalar1=col(6 + pc),
                                                                 scalar2=None, op0=ALU.mult)), ["RAW"] + cK, [fk(qi)])
                    dve((lambda e, qi=qi, pc=pc: e.scalar_tensor_tensor(out=F(qi), in0=RAW[:, 0, 0:512], scalar=col(pc),
                                                                        in1=F(qi), op0=ALU.mult, op1=ALU.add)),
                        ["RAW", "RAW0", fk(qi)] + cK, [fk(qi)])
                    pool((lambda e, pc=pc: e.tensor_copy(out=PREV[:, 0, pc:pc + 1], in_=RAW[:, 0, 512:513])),
                         ["RAW", "RAW0"], [("PREV", pc)])
                pool(lambda e: e.tensor_copy(out=VB[:, 0, :], in_=F(2)), [fk(2)], ["VB"])
                P.op("pe", lambda e: e.matmul(PSB[2][:, :], lhsT=LW[0:32, 0, ps_], rhs=LB16[0:32, 0, :], start=True, stop=True),
                     r=[K_("lw"), "LB16a"], w=[("ps", 2)])
                act(lambda e: e.activation(out=F(3), in_=PSB[2][:, :], func=AF.Sigmoid, bias=col(12 + pr), scale=1.0),
                    [("ps", 2)] + cK, [fk(3)])
                dve(lambda e: e.tensor_scalar(out=F(3), in0=F(3), scalar1=-0.6065306597126334, scalar2=None, op0=ALU.mult),
                    [fk(3)], [fk(3)])
                P.op("pe", lambda e: e.matmul(PSB[3][:, :], lhsT=LW[32:64, 0, ps_], rhs=LB16[32:64, 0, :], start=True, stop=True),
                     r=[K_("lw"), "LB16b"], w=[("ps", 3)])
                act(lambda e: e.activation(out=F(4), in_=PSB[3][:, :], func=AF.Sigmoid, bias=col(14 + pr), scale=1.0),
                    [("ps", 3)] + cK, [fk(4)])
                P.op("pe", lambda e: e.matmul(PSB[2][:, :], lhsT=LW[64:128, 0, ps_], rhs=LB16[64:128, 0, :], start=True, stop=True),
                     r=[K_("lw"), "LB16c"], w=[("ps", 2)])
                act(lambda e: e.copy(out=F(5), in_=PSB[2][:, :]), [("ps", 2)], [fk(5)])
                dve(lambda e: e.tensor_scalar(out=F(6), in0=F(1), scalar1=col(16 + pr), scalar2=None, op0=ALU.mult),
                    [fk(1)] + cK, [fk(6)])
                pool(lambda e: e.tensor_tensor(out=F(7), in0=F(6), in1=F(6), op=ALU.mult), [fk(6)], [fk(7)])
                P.op("pe", lambda e: e.matmul(PSB[3][:, :], lhsT=OBD[:, :], rhs=F(7), start=True, stop=True),
                     r=["OBD", fk(7)], w=[("ps", 3)])
                act(lambda e: e.activation(out=F(7), in_=PSB[3][:, :], func=AF.Ln, bias=CST[:, 3:4], scale=1.0),
                    [("ps", 3), "cst"], [fk(7)])
                act(lambda e: e.activation(out=F(7), in_=F(7), func=AF.Exp, scale=-0.5), [fk(7)], [fk(7)])
                dve(lambda e: e.tensor_tensor(out=F(6), in0=F(6), in1=F(7), op=ALU.mult), [fk(6), fk(7)], [fk(6)])
                dve(lambda e: e.tensor_scalar(out=F(7), in0=F(4), scalar1=col(18 + pr), scalar2=col(20 + pr), op0=ALU.mult,
                                              op1=ALU.add), [fk(4)] + cK, [fk(7)])
                dve(lambda e: e.tensor_tensor(out=F(1), in0=F(1), in1=F(7), op=ALU.mult), [fk(1), fk(7)], [fk(1)])
                dve(lambda e: e.scalar_tensor_tensor(out=F(7), in0=F(0), scalar=col(22 + pr), in1=F(1), op0=ALU.mult,
                                                     op1=ALU.mult), [fk(0), fk(1)] + cK, [fk(7)])
                P.op("pe", lambda e: e.matmul(PSB[2][:, :], lhsT=OBD[:, :], rhs=F(7), start=True, stop=True),
                     r=["OBD", fk(7)], w=[("ps", 2)])
                dve(lambda e: e.tensor_tensor(out=F(8), in0=PSB[2][:, :], in1=F(2), op=ALU.mult), [("ps", 2), fk(2)], [fk(8)])
                dve(lambda e: e.tensor_tensor(out=F(7), in0=F(6), in1=F(4), op=ALU.mult), [fk(6), fk(4)], [fk(7)])
                for c in range(8):
                    cs = slice(c * 64, (c + 1) * 64)
                    dve((lambda e, cs=cs: e.tensor_tensor_scan(out=Fb[:, 9, cs], data0=ONESW[:, 0:64], data1=Fb[:, 3, cs],
                                                               initial=0.0, op0=ALU.mult, op1=ALU.add)),
                        [fk(3), "ONESW"], [fk(9)])
                dve(lambda e: e.tensor_tensor(out=F(3), in0=F(9), in1=F(3), op=ALU.subtract), [fk(9), fk(3)], [fk(3)])
                act(lambda e: e.activation(out=PTc[:, 0, :], in_=Fb[:, 9, 63::64], func=AF.Exp), [fk(9)], ["PTc"])
                v8c = lambda ap: ap.rearrange("p (c t) -> p c t", t=64)
                act(lambda e: e.activation(out=F(4), in_=F(9), func=AF.Exp), [fk(9), fk(4)], [fk(4)])
                dve(lambda e: e.tensor_tensor(out=RB[:, :, 64:128], in0=v8c(F(0)), in1=v8c(F(4)), op=ALU.mult),
                    [fk(0), fk(4)], ["RBr"])
                act(lambda e: e.activation(out=F(4), in_=F(3), func=AF.Exp), [fk(3), fk(4), "RBr"], [fk(4)])
                dve(lambda e: e.scalar_tensor_tensor(out=RB[:, :, 0:64], in0=v8c(F(6)), scalar=-1.0, in1=v8c(F(4)),
                                                     op0=ALU.mult, op1=ALU.mult), [fk(6), fk(4)], ["RBb"])
                act(lambda e: e.activation(out=F(4), in_=F(9), func=AF.Exp, scale=-1.0), [fk(9), fk(4), "RBb"], [fk(4)])
                dve(lambda e: e.tensor_tensor(out=LK[:, :, 0:64], in0=v8c(F(7)), in1=v8c(F(4)), op=ALU.mult),
                    [fk(7), fk(4)], ["LKa"])
                pool(lambda e: e.tensor_tensor(out=LK[:, :, 64:128], in0=v8c(F(1)), in1=v8c(F(4)), op=ALU.mult),
                     [fk(1), fk(4)], ["LKk"])
                for c in range(8):
                    cs = slice(c * 64, (c + 1) * 64)
                    act((lambda e, c=c, cs=cs: e.activation(out=Fb[:, 3, cs], in_=Fb[:, 9, cs], func=AF.Exp,
                                                            bias=Fb[:, 9, c * 64 + 63:c * 64 + 64], scale=-1.0)),
                        [fk(9), fk(3)], [fk(3)])
                dve(lambda e: e.tensor_tensor(out=AH[:, :, 0:64], in0=v8c(F(7)), in1=v8c(F(3)), op=ALU.mult),
                    [fk(7), fk(3)], ["AHa"])
                pool(lambda e: e.tensor_tensor(out=AH[:, :, 64:128], in0=v8c(F(1)), in1=v8c(F(3)), op=ALU.mult),
                     [fk(1), fk(3)], ["AHk"])

                def do_group(grp):
                    c0 = grp * 4
                    for j in range(4):
                        c = c0 + j
                        for hb in HB:
                            P.op("pe", (lambda e, c=c, j=j, hb=hb: e.matmul(PSB[0][hb, j * 128:(j + 1) * 128], lhsT=LK[hb, c, 0:64],
                                                                            rhs=RB[hb, c, :], start=True, stop=True)),
                                 r=["LKa", "RBr", "RBb"], w=[("ps", 0)])
                            P.op("pe", (lambda e, c=c, j=j, hb=hb: e.matmul(PSB[1][hb, j * 128:(j + 1) * 128], lhsT=LK[hb, c, 64:128],
                                                                            rhs=RB[hb, c, :], start=True, stop=True)),
                                 r=["LKk", "RBr", "RBb"], w=[("ps", 1)])
                            P.op("pe", (lambda e, c=c, j=j, hb=hb: e.matmul(PSB[2][hb, j * 64:(j + 1) * 64], lhsT=RB[hb, c, 0:64],
                                                                            rhs=LK[hb, c, 0:64], start=True, stop=True)),
                                 r=["LKa", "RBb"], w=[("ps", 2)])
                    v4 = lambda ap, w_: ap.rearrange("p (j x) -> p j x", x=w_)
                    m1 = MASK1[:, :].unsqueeze(1).broadcast_to([128, 4, 128])
                    dve(lambda e: e.tensor_tensor(out=AB1[:], in0=v4(PSB[0][:, :], 128), in1=m1, op=ALU.mult),
                        [("ps", 0), "MASK1"], ["AB1"])
                    dve(lambda e: e.tensor_tensor(out=AB2[:], in0=v4(PSB[1][:, :], 128), in1=m1, op=ALU.mult),
                        [("ps", 1), "MASK1"], ["AB2"])
                    dve(lambda e: e.tensor_tensor(out=AB3[:], in0=v4(PSB[2][:, 0:256], 64),
                                                  in1=MASKL[:, :].unsqueeze(1).broadcast_to([128, 4, 64]), op=ALU.mult),
                        [("ps", 2), "MASKL"], ["AB3"])
                    pt3 = PSB[3][:, :].bitcast(BF16)
                    for j in range(4):
                        c = c0 + j
                        for hb in HB:
                            srcs = (RB[hb, c, 0:64], VB[hb, 0, c * 64:(c + 1) * 64], AH[hb, c, 0:64], AH[hb, c, 64:128])
                            for q, src in enumerate(srcs):
                                P.op("pe", (lambda e, j=j, q=q, src=src, hb=hb: e.transpose(
                                    out=pt3[hb, j * 256 + q * 64:j * 256 + (q + 1) * 64], in_=src, identity=identb[hb, hb])),
                                    r=["RBb", "VB", "AHa", "AHk", "identb"], w=[("ps", 3)])
                    act(lambda e: e.copy(out=TM[:], in_=pt3.rearrange("p (j x) -> p j x", x=256)), [("ps", 3)], ["TM"])
                    for j in range(4):
                        for hb in HB:
                            P.op("pe", (lambda e, j=j, hb=hb: e.matmul(PSB[2][hb, 256 + j * 64:256 + (j + 1) * 64], lhsT=AB2[hb, j, 0:64],
                                                                       rhs=TM[hb, j, 64:128], start=True, stop=True)),
                                 r=["AB2", "TM"], w=[("ps", 2)])
                    pool(lambda e: e.tensor_copy(out=Zr[:, 0, :, 0:64], in_=TM[:, :, 0:64]), ["TM"], [("Z", 0)])
                    act(lambda e: e.copy(out=Zr[:, 0, :, 64:128], in_=v4(PSB[2][:, 256:512], 64)), [("ps", 2)], [("Z", 0)])
                    for jj in range(6):
                        zi, zo = jj % 2, (jj + 1) % 2
                        for j in range(4):
                            for hb in HB:
                                if jj == 0:
                                    Pm, PmT, pk = AB3[hb, j, :], AB1[hb, j, 0:64], ["AB3", "AB1"]
                                else:
                                    Pm, PmT = PPr[hb, jj % 2, j, 0:64], PPr[hb, jj % 2, j, 64:128]
                                    pk = [("PP", jj % 2)]
                                P.op("pe", (lambda e, j=j, PmT=PmT, zi=zi, hb=hb: e.matmul(
                                    PSB[5][hb, j * 128:(j + 1) * 128], lhsT=PmT, rhs=Zr[hb, zi, j, :], start=True, stop=True)),
                                    r=pk + [("Z", zi)], w=[("ps", 5)])
                                if jj < 5:
                                    P.op("pe", (lambda e, j=j, Pm=Pm, PmT=PmT, hb=hb: e.matmul(
                                        PSB[4][hb, j * 128:j * 128 + 64], lhsT=PmT, rhs=Pm, start=True, stop=True)),
                                        r=pk, w=[("ps", 4)])
                                    P.op("pe", (lambda e, j=j, Pm=Pm, PmT=PmT, hb=hb: e.matmul(
                                        PSB[4][hb, j * 128 + 64:(j + 1) * 128], lhsT=Pm, rhs=PmT, start=True, stop=True)),
                                        r=pk, w=[("ps", 4)])
                        dve((lambda e, zi=zi, zo=zo: e.tensor_tensor(out=Zr[:, zo], in0=Zr[:, zi],
                                                                   in1=v4(PSB[5][:, :], 128), op=ALU.add)),
                            [("Z", zi), ("ps", 5)], [("Z", zo)])
                        if jj < 5:
                            act((lambda e, jj=jj: e.copy(out=PPr[:, (jj + 1) % 2], in_=v4(PSB[4][:, :], 128))),
                                [("ps", 4)], [("PP", (jj + 1) % 2)])
                    for j in range(4):
                        for hb in HB:
                            ZF = Zr[hb, 0]
                            P.op("pe", (lambda e, j=j, hb=hb, ZF=ZF: e.matmul(PSB[6][hb, j * 64:(j + 1) * 64], lhsT=ZF[:, j, 0:64],
                                                                              rhs=TM[hb, j, 128:192], start=True, stop=True)),
                                 r=[("Z", 0), "TM"], w=[("ps", 6)])
                            P.op("pe", (lambda e, j=j, hb=hb, ZF=ZF: e.matmul(PSB[6][hb, 256 + j * 64:256 + (j + 1) * 64], lhsT=TM[hb, j, 128:192],
                                                                              rhs=ZF[:, j, 64:128], start=True, stop=False)),
                                 r=[("Z", 0), "TM"], w=[("ps", 6)])
                            P.op("pe", (lambda e, j=j, hb=hb: e.matmul(PSB[6][hb, 256 + j * 64:256 + (j + 1) * 64], lhsT=TM[hb, j, 192:256],
                                                                       rhs=TM[hb, j, 64:128], start=False, stop=True)),
                                 r=["TM"], w=[("ps", 6)])
                            P.op("pe", (lambda e, j=j, hb=hb, ZF=ZF: e.matmul(PSB[7][hb, j * 64:(j + 1) * 64], lhsT=ZF[:, j, 0:64],
                                                                              rhs=AB1[hb, j, 64:128], start=True, stop=True)),
                                 r=[("Z", 0), "AB1"], w=[("ps", 7)])
                            P.op("pe", (lambda e, j=j, hb=hb, ZF=ZF: e.matmul(PSB[7][hb, 256 + j * 64:256 + (j + 1) * 64], lhsT=ZF[:, j, 64:128],
                                                                              rhs=AB1[hb, j, 64:128], start=True, stop=False)),
                                 r=[("Z", 0), "AB1"], w=[("ps", 7)])
                            P.op("pe", (lambda e, j=j, hb=hb: e.matmul(PSB[7][hb, 256 + j * 64:256 + (j + 1) * 64], lhsT=TM[hb, j, 64:128],
                                                                       rhs=AB2[hb, j, 64:128], start=False, stop=True)),
                                 r=["TM", "AB2"], w=[("ps", 7)])
                    p6a, p6b = v4(PSB[6][:, 0:256], 64), v4(PSB[6][:, 256:512], 64)
                    p7a, p7b = v4(PSB[7][:, 0:256], 64), v4(PSB[7][:, 256:512], 64)
                    act(lambda e: e.copy(out=G0TS[:, c0:c0 + 4, :], in_=p6a), [("ps", 6)], [("G0TS", grp)])
                    act(lambda e: e.copy(out=HINC[:, c0:c0 + 4, :], in_=p6b), [("ps", 6)], [("HINC", grp)])
                    dve(lambda e: e.tensor_tensor(out=QTS[:, c0:c0 + 4, :], in0=p7a, in1=RB[:, c0:c0 + 4, 64:128],
                                                  op=ALU.add), [("ps", 7), "RBr"], [("QTS", grp)])
                    dve(lambda e: e.tensor_copy(out=v8c(Y0TS[:, 0, :])[:, c0:c0 + 4, :], in_=p7b), [("ps", 7)],
                        [("Y0TS", grp)])
                for grp_ in range(2):
                    do_group(grp_)
                pool(lambda e: e.tensor_copy(out=HBs[:, 0, :], in_=HB0[:, pr, :]), ["HB0"], [("HBs", 0)])
                for c in range(8):
                    grp = c // 4
                    for hb in HB:
                        P.op("pe", (lambda e, c=c, hb=hb: e.matmul(PSB[2][hb, 0:64], lhsT=G0TS[hb, c, :], rhs=HBs[hb, c, :],
                                                                   start=True, stop=True)),
                             r=[("G0TS", grp), ("HBs", c)], w=[("ps", 2)])
                    dve((lambda e, c=c: e.scalar_tensor_tensor(out=TMPH[:, 0, :], in0=HST[:, pr, :], scalar=PTc[:, 0, c:c + 1],
                                                               in1=HINC[:, c, :], op0=ALU.mult, op1=ALU.add)),
                        ["HST", "PTc", ("HINC", grp)], ["TMPH"])
                    dve(lambda e: e.tensor_tensor(out=HST[:, pr, :], in0=TMPH[:, 0, :], in1=PSB[2][:, 0:64], op=ALU.add),
                        ["TMPH", ("ps", 2)], ["HST"])
                    act((lambda e, c=c: e.copy(out=HBs[:, c + 1, :], in_=HST[:, pr, :])), ["HST"], [("HBs", c + 1)])
                pool(lambda e: e.tensor_copy(out=HB0[:, pr, :], in_=HBs[:, 8, :]), [("HBs", 8)], ["HB0"])
                for c in range(8):
                    for hb in HB:
                        P.op("pe", (lambda e, c=c, hb=hb: e.matmul(PSB[3][hb, c * 64:(c + 1) * 64], lhsT=HBs[hb, c, :], rhs=QTS[hb, c, :],
                                                                   start=True, stop=True)),
                             r=[("HBs", c), ("QTS", c // 4)], w=[("ps", 3)])
                dve(lambda e: e.tensor_tensor(out=F(0), in0=PSB[3][:, :], in1=Y0TS[:, 0, :], op=ALU.add),
                    [("ps", 3), ("Y0TS", 0), ("Y0TS", 1), fk(0), "RBr"], [fk(0)])
                P.op("pe", lambda e: e.matmul(PSB[0][:, :], lhsT=O64BD[:, :], rhs=F(0), start=True, stop=True),
                     r=["O64BD", fk(0)], w=[("ps", 0)])
                dve(lambda e: e.tensor_tensor(out=F(0), in0=F(0), in1=PSB[0][:, :], op=ALU.subtract), [fk(0), ("ps", 0)], [fk(0)])
                pool(lambda e: e.tensor_tensor(out=F(4), in0=F(0), in1=F(0), op=ALU.mult), [fk(0), fk(4), "LKa", "LKk"], [fk(4)])
                P.op("pe", lambda e: e.matmul(PSB[1][:, :], lhsT=O64BD[:, :], rhs=F(4), start=True, stop=True),
                     r=["O64BD", fk(4)], w=[("ps", 1)])
                act(lambda e: e.activation(out=F(4), in_=PSB[1][:, :], func=AF.Ln, bias=CST[:, 4:5], scale=1.0),
                    [("ps", 1), "cst"], [fk(4)])
                act(lambda e: e.activation(out=F(4), in_=F(4), func=AF.Exp, scale=-0.5), [fk(4)], [fk(4)])
                dve(lambda e: e.tensor_tensor(out=F(0), in0=F(0), in1=F(4), op=ALU.mult), [fk(0), fk(4)], [fk(0)])
                dve(lambda e: e.tensor_scalar(out=F(0), in0=F(0), scalar1=col(24 + pr), scalar2=col(26 + pr), op0=ALU.mult,
                                              op1=ALU.add), [fk(0)] + cK, [fk(0)])
                dve(lambda e: e.tensor_tensor(out=F(0), in0=F(0), in1=F(8), op=ALU.add), [fk(0), fk(8)], [fk(0)])
                dve(lambda e: e.tensor_tensor(out=YTB[:, 6 + pr, :], in0=F(0), in1=F(5), op=ALU.mult), [fk(0), fk(5)], [("YTB", 6 + pr)])
            for pr_ in range(2):
                do_pair(pr_)

        def outproj(t0):
            for db in range(4):
                i, S = load_wblock(wout, db * 256, 256)
                for di in range(2):
                    dt_ = db * 2 + di
                    bank = 4 + di
                    for kt in range(8):
                        P.op("pe", (lambda e, kt=kt, di=di, bank=bank, S=S: e.matmul(
                            PSB[bank][:, :], lhsT=S[:, kt, di * 128:(di + 1) * 128], rhs=YTB[:, kt, :],
                            start=(kt == 0), stop=(kt == 7))),
                            r=[("WS", i, kt // 4), ("YTB", kt)], w=[("ps", bank)])
                    P.op("dve", (lambda e, dt_=dt_, bank=bank: e.tensor_tensor(
                        out=XT[:, dt_, t0:t0 + 512], in0=XT[:, dt_, t0:t0 + 512], in1=PSB[bank][:, :], op=ALU.add)),
                        r=[("ps", bank), ("XT", t0 // 512)], w=[("XT", t0 // 512)])

        if "s5" in mixers:
            C5 = s5_setup()
            barrier()
        CR = rw_setup() if "rwkv" in mixers else None
        barrier()
        base_off = am.off
        for tb in range(4):
            t0 = tb * 512
            norm_to(HTB, "HTB", t0, g, 0)
            if "ssd" in mixers:
                ssd_block(tb)
            if "s5" in mixers:
                P.set_fence()
                s5_block(tb, C5)
                P.set_fence()
            if "rwkv" in mixers:
                P.set_fence()
                rw_block(tb, CR)
                P.set_fence()
            if dbg:
                dst = ydbg[s, l].rearrange("(kt p) t -> p kt t", p=128)[:, :, t0:t0 + 512]
                P.dma("sp", (lambda e, dst=dst: e.dma_start(out=dst, in_=YTB[:])),
                      r=[("YTB", k_) for k_ in range(8)], sem="D_ydbg")
            outproj(t0)

    for s in range(NS):
        barrier()
        load_x(s)
        barrier()
        for l in range(NL):
            if do_ffn:
                ffn(l, "ffn1")
            if mixers:
                barrier()
                mixer_phase(s, l)
                barrier()
            if do_ffn:
                ffn(l, "ffn2")
        barrier()
        final_store(s)
    last = {}
    for s_, v in out_toks:
        last[s_] = max(last.get(s_, 0), v)
    P.final_wait("sp", list(last.items()))
    assert ARMAX[0] <= AR_WORDS, ("arena overflow: need words", ARMAX[0])
    P.emit(stack)
    stack.close()
    return nc


L_ = 2
WEIGHT_SHAPES = [
    ("ffn1_norm", (L_, 1024)), ("ffn1_wg", (L_, 1024, 2816)), ("ffn1_wu", (L_, 1024, 2816)),
    ("ffn1_wd", (L_, 2816, 1024)), ("mix_norm", (L_, 1024)), ("w_in", (L_, 1024, 2696)),
    ("w_out", (L_, 1024, 1024)), ("m_A_log", (L_, 8)), ("m_dt_bias", (L_, 8)),
    ("m_conv_w", (L_, 1024, 4)), ("m_conv_b", (L_, 1024)), ("m_D", (L_, 8)),
    ("m_norm_w", (L_, 512)), ("s_A_re", (L_, 16, 64)), ("s_A_im", (L_, 16, 64)),
    ("s_B_re", (L_, 16, 64, 16)), ("s_B_im", (L_, 16, 64, 16)), ("s_C_re", (L_, 16, 16, 64)),
    ("s_C_im", (L_, 16, 16, 64)), ("s_log_dt", (L_, 16)), ("s_D", (L_, 256)),
    ("s_glu_w", (L_, 256, 256)), ("s_glu_b", (L_, 256)), ("r_mu", (L_, 896)),
    ("r_w0", (L_, 256)), ("r_w2", (L_, 32, 256)), ("r_a0", (L_, 256)), ("r_a2", (L_, 32, 256)),
    ("r_g2", (L_, 64, 256)), ("r_k_k", (L_, 256)), ("r_k_a", (L_, 256)), ("r_r_k", (L_, 4, 64)),
    ("r_gn_w", (L_, 256)), ("r_gn_b", (L_, 256)), ("ffn2_norm", (L_, 1024)),
    ("ffn2_wg", (L_, 1024, 2816)), ("ffn2_wu", (L_, 1024, 2816)), ("ffn2_wd", (L_, 2816, 1024)),
    ("final_norm", (1024,)),
]

_CFG = {"nseq": 2, "nlayers": 2, "mix": True, "ffn": True}


def kernel(**inputs):
    n = 8
    nc = bass.Bass("TRN2", target_bir_lowering=False)
    build_program(nc, _CFG)
    x = np.ascontiguousarray(np.asarray(inputs["x"], dtype=np.float32))
    wts = {nm: np.ascontiguousarray(np.asarray(inputs[nm], dtype=np.float32)) for nm, _ in WEIGHT_SHAPES}
    in_maps = []
    for c in range(n):
        m = {"x": x[2 * c:2 * c + 2]}
        m.update(wts)
        in_maps.append(m)
    res = run_bass_kernel_spmd(nc, in_maps, core_ids=list(range(n)))
    return np.concatenate([r["out"] for r in res.results], axis=0)
```

```python
import numpy as np
import concourse.bass as bass
import concourse.mybir as mybir
from concourse.bass_utils import run_bass_kernel_spmd

F32 = mybir.dt.float32
BF16 = mybir.dt.bfloat16
I32 = mybir.dt.int32
ALU = mybir.AluOpType
AF = mybir.ActivationFunctionType
AX = mybir.AxisListType

D = 1024
SEQ = 2048
DFF = 2816
NFT = DFF // 128
DIN = 2696
EPS = 1e-5

ENGS = ("pe", "dve", "act", "pool", "sp")
ROLL = 12000
SELF_SYNC = {"pe": False, "dve": True, "act": True, "pool": True, "sp": True}


class Prog:
    def __init__(self, nc):
        self.nc = nc
        self.ops = {e: [] for e in ENGS}
        self.cnt = {e: 0 for e in ENGS}
        self.gen = {e: 0 for e in ENGS}
        self.seen = {e: {} for e in ENGS}
        self.lastw = {}
        self.readers = {}
        self.dmacnt = {}
        self.semnames = []
        self.fence = {}
        self.fence_done = {e: 0 for e in ENGS}
        self.fence_id = 0

    def _semname(self, n):
        if n not in self.semnames:
            self.semnames.append(n)
        return n

    def _deps(self, eng, reads, writes):
        deps = {}

        def add(tok):
            if tok is None:
                return
            s, v = tok
            if deps.get(s, 0) < v:
                deps[s] = v

        for k in reads:
            add(self.lastw.get(k))
            if isinstance(k, tuple) and k[0] == "ps":
                for tok in self.readers.get(k, ()):
                    if not tok[0].startswith("E_%s_" % eng):
                        add(tok)
        for k in writes:
            add(self.lastw.get(k))
            for tok in self.readers.get(k, ()):
                add(tok)
        waits = []
        seen = self.seen[eng]
        for s, v in deps.items():
            if seen.get(s, 0) >= v:
                continue
            if s.startswith("E_%s_" % eng) and not SELF_SYNC[eng]:
                continue
            seen[s] = v
            waits.append((s, v))
        return waits

    def set_fence(self):
        toks = {}
        for en in ENGS:
            if self.cnt[en] > 0:
                toks["E_%s_%d" % (en, self.gen[en])] = self.cnt[en]
        for sname, v in self.dmacnt.items():
            toks[sname] = v
        self.fence = toks
        self.fence_id += 1

    def _fence_waits(self, eng):
        if self.fence_done[eng] == self.fence_id:
            return []
        self.fence_done[eng] = self.fence_id
        out = []
        seen = self.seen[eng]
        for s, v in self.fence.items():
            if seen.get(s, 0) >= v:
                continue
            if s.startswith("E_%s_" % eng):
                continue
            seen[s] = v
            out.append((s, v))
        return out

    def op(self, eng, fn, r=(), w=(), nofence=False):
        waits = self._deps(eng, r, w)
        if not nofence:
            waits = self._fence_waits(eng) + waits
        if self.cnt[eng] >= ROLL:
            self.gen[eng] += 1
            self.cnt[eng] = 0
        self.cnt[eng] += 1
        sname = self._semname("E_%s_%d" % (eng, self.gen[eng]))
        tok = (sname, self.cnt[eng])
        self.ops[eng].append((waits, fn, sname, 1))
        for k in w:
            self.lastw[k] = tok
            self.readers[k] = []
        for k in r:
            self.readers.setdefault(k, []).append(tok)
        return tok

    def dma(self, q, fn, r=(), w=(), sem=None, nofence=False):
        waits = self._deps(q, r, w)
        if not nofence:
            waits = self._fence_waits(q) + waits
        if sem is None:
            sem = "D_" + str(w[0] if w else r[0])
        sname = self._semname(sem)
        self.dmacnt[sname] = self.dmacnt.get(sname, 0) + 16
        tok = (sname, self.dmacnt[sname])
        self.ops[q].append((waits, fn, sname, 16))
        for k in w:
            self.lastw[k] = tok
            self.readers[k] = []
        for k in r:
            self.readers.setdefault(k, []).append(tok)
        return tok

    def final_wait(self, eng, toks):
        waits = []
        for s, v in toks:
            waits.append((s, v))
        self.ops[eng].append((waits, None, None, 0))

    def emit(self, stack):
        nc = self.nc
        sems = {}
        for n in self.semnames:
            sems[n] = stack.enter_context(nc.semaphore(n))
        block = stack.enter_context(nc.Block())
        engmap = {"pe": block.tensor, "dve": block.vector, "act": block.scalar,
                  "pool": block.gpsimd, "sp": block.sync}

        def mk(elist):
            def body(e):
                for waits, fn, sname, inc in elist:
                    for s, v in waits:
                        e.wait_ge(sems[s], v)
                    if fn is not None:
                        ins = fn(e)
                        ins.then_inc(sems[sname], inc)
            return body

        for en in ENGS:
            if self.ops[en]:
                engmap[en](mk(self.ops[en]))


def build_program(nc, cfg):
    from contextlib import ExitStack
    NS = cfg.get("nseq", 2)
    NL = cfg.get("nlayers", 2)
    do_ffn = cfg.get("ffn", True)
    mixers = cfg.get("mixers", ("ssd", "s5", "rwkv"))
    dbg = cfg.get("dbg", False)

    def din(name, shape):
        return nc.dram_tensor(name, list(shape), F32, kind="ExternalInput").ap()

    x_d = din("x", [NS, SEQ, D])
    W = {}
    for nm, shp in WEIGHT_SHAPES:
        W[nm] = din(nm, shp)
    out_d = nc.dram_tensor("out", [NS, SEQ, D], F32, kind="ExternalOutput").ap()
    if dbg:
        ydbg = nc.dram_tensor("ydbg", [NS, NL, D, SEQ], BF16, kind="ExternalOutput").ap()

    P = Prog(nc)
    stack = ExitStack()

    def sb(name, shape, dt):
        return stack.enter_context(nc.sbuf_tensor(name, list(shape), dt))

    def ps(name, shape, dt):
        return stack.enter_context(nc.psum_tensor(name, list(shape), dt))

    XT = sb("XT", [128, 8, SEQ], F32)
    ident = sb("ident", [128, 128], F32)
    identb = sb("identb", [128, 128], BF16)
    onesf = sb("onesf", [128, 128], F32)
    onesb = sb("onesb", [128, 128], BF16)
    gains = sb("gains", [128, 3 * NL + 1, 8], F32)
    AR_WORDS = 24576
    ARENA = sb("ARENA", [128, AR_WORDS], F32)
    WGU = sb("WGU", [128, 2, 2, 8, 256], BF16)
    STG = sb("STG", [128, 4, 4, 256], F32)
    SQ = sb("SQ", [128, 2, 512], BF16)
    RSTD = sb("RSTD", [128, 512], F32)
    SG = sb("SG", [128, 2, 512], F32)
    CST = sb("CST", [128, 8], F32)
    PSB = [ps("psb%d" % i, [128, 512], F32) for i in range(8)]

    ARMAX = [0]
    cfg['_armax'] = ARMAX

    class Arena:
        def __init__(self):
            self.off = 0

        def alloc(self, shape, dt):
            n = 1
            for d_ in shape:
                n *= d_
            words = (n * (2 if dt == BF16 else 4) + 3) // 4
            words = (words + 7) // 8 * 8
            ARMAX[0] = max(ARMAX[0], self.off + words)
            o_ = self.off if self.off + words <= AR_WORDS else 0
            a = ARENA[:, o_:o_ + words]
            self.off += words
            if dt == BF16:
                a = a.bitcast(BF16)
            a = a[:, 0:n]
            if len(shape) == 2:
                return a.rearrange("p (a b) -> p a b", b=shape[1])
            if len(shape) == 3:
                return a.rearrange("p (a b c) -> p a b c", b=shape[1], c=shape[2])
            return a

    ar = Arena()
    IOB = ar.alloc([2, 1024], F32)
    ar = Arena()
    HT = ar.alloc([8, 1024], BF16)
    ATt = ar.alloc([NFT, 1024], BF16)
    WDb = ar.alloc([2, NFT, 256], BF16)

    P.op("pool", lambda e: e.memset(onesf[:], 1.0), w=["onesf"])
    P.op("pool", lambda e: e.memset(onesb[:], 1.0), w=["onesb"])
    P.op("pool", lambda e: e.memset(CST[:, 0:1], EPS), w=["cst"])
    P.op("pool", lambda e: e.memset(CST[:, 1:2], 1.0), w=["cst"])
    P.op("pool", lambda e: e.memset(CST[:, 2:3], 0.0), w=["cst"])
    P.op("pool", lambda e: e.affine_select(out=ident[:], in_=onesf[:], pattern=[[1, 128]],
                                            compare_op=ALU.is_equal, fill=0.0, base=0,
                                            channel_multiplier=-1),
         r=["onesf"], w=["ident"])
    P.op("pool", lambda e: e.tensor_copy(out=identb[:], in_=ident[:]), r=["ident"], w=["identb"])

    def small_dma(dst, src, key):
        P.dma("sp", (lambda e, dst=dst, src=src: e.dma_start(out=dst, in_=src, allow_slow_non_contiguous=True)),
              w=[key])

    gi = 0
    gidx = {}
    for l in range(NL):
        for nm in ("ffn1_norm", "mix_norm", "ffn2_norm"):
            small_dma(gains[:, gi, :], W[nm][l].rearrange("(kt p) -> p kt", p=128), ("gains", gi))
            gidx[(nm, l)] = gi
            gi += 1
    small_dma(gains[:, gi, :], W["final_norm"].rearrange("(kt p) -> p kt", p=128), ("gains", gi))
    gidx["final"] = gi

    def load_x(s):
        for tb in range(16):
            b = tb % 2
            src = x_d[s, tb * 128:(tb + 1) * 128, :]
            P.dma("sp", (lambda e, b=b, src=src: e.dma_start(out=IOB[:, b, :], in_=src)),
                  w=[("IOB", b)])
            for half in range(2):
                bank = 6 + half
                for j in range(4):
                    kt = half * 4 + j
                    P.op("pe", (lambda e, b=b, kt=kt, j=j, bank=bank: e.transpose(
                        out=PSB[bank][:, j * 128:(j + 1) * 128],
                        in_=IOB[:, b, kt * 128:(kt + 1) * 128], identity=ident[:])),
                        r=[("IOB", b), "ident"], w=[("ps", bank)])
                dst = XT[:, half * 4:(half + 1) * 4, tb * 128:(tb + 1) * 128]
                srcp = PSB[bank][:, :].rearrange("p (a b) -> p a b", b=128)
                if half == 0:
                    P.op("dve", (lambda e, dst=dst, srcp=srcp: e.tensor_copy(out=dst, in_=srcp)),
                         r=[("ps", bank)], w=[("XT", tb // 4)])
                else:
                    P.op("act", (lambda e, dst=dst, srcp=srcp: e.copy(out=dst, in_=srcp)),
                         r=[("ps", bank)], w=[("XT", tb // 4)])

    def rms_stats(tok0, bank=6):
        for kt in range(8):
            b = kt % 2
            P.op("act", (lambda e, b=b, kt=kt: e.activation(
                out=SQ[:, b, :], in_=XT[:, kt, tok0:tok0 + 512], func=AF.Square)),
                r=[("XT", tok0 // 512)], w=[("SQ", b)])
            P.op("pe", (lambda e, b=b, kt=kt: e.matmul(
                PSB[bank][:, :], lhsT=onesb[:], rhs=SQ[:, b, :], start=(kt == 0), stop=(kt == 7))),
                r=[("SQ", b), "onesb"], w=[("ps", bank)])
        P.op("act", (lambda e: e.activation(out=RSTD[:], in_=PSB[bank][:, :], func=AF.Ln,
                                            bias=CST[:, 0:1], scale=1.0 / D)),
             r=[("ps", bank), "cst"], w=["RSTD"])
        P.op("act", (lambda e: e.activation(out=RSTD[:], in_=RSTD[:], func=AF.Exp, scale=-0.5)),
             r=["RSTD"], w=["RSTD"])

    def norm_to(dst, dkey, tok0, g, hoff):
        rms_stats(tok0)
        for kt in range(8):
            P.op("dve", (lambda e, kt=kt: e.scalar_tensor_tensor(
                out=dst[:, kt, hoff:hoff + 512], in0=XT[:, kt, tok0:tok0 + 512],
                scalar=gains[:, g, kt:kt + 1], in1=RSTD[:], op0=ALU.mult, op1=ALU.mult)),
                r=[("XT", tok0 // 512), "RSTD", ("gains", g)], w=[(dkey, kt, hoff // 512)])

    wcount = {"gu": 0, "d": 0, "stg": 0, "slot": 0}

    def wload(dst, src, key, nofence=False, ceng="pool"):
        a = src.shape[1]
        wd_ = src.shape[2]
        sbuf_i = wcount["stg"] % 4
        wcount["stg"] += 1
        P.dma("sp", (lambda e: e.dma_start(out=STG[:, sbuf_i, 0:a, 0:wd_], in_=src)),
              w=[("STG", sbuf_i)], nofence=nofence)
        if ceng == "act":
            P.op("act", (lambda e: e.copy(out=dst, in_=STG[:, sbuf_i, 0:a, 0:wd_])),
                 r=[("STG", sbuf_i)], w=[key], nofence=nofence)
        else:
            P.op("pool", (lambda e: e.tensor_copy(out=dst, in_=STG[:, sbuf_i, 0:a, 0:wd_])),
                 r=[("STG", sbuf_i)], w=[key], nofence=nofence)

    def run_pipelined(gens, depth=2):
        active = []
        it = iter(gens)
        while True:
            while len(active) < depth:
                try:
                    active.append(next(it))
                except StopIteration:
                    break
            if not active:
                break
            for g_ in list(active):
                try:
                    next(g_)
                except StopIteration:
                    active.remove(g_)

    def barrier():
        toks = []
        for en in ENGS:
            if P.cnt[en] > 0:
                toks.append(("E_%s_%d" % (en, P.gen[en]), P.cnt[en]))
        for sname, v in P.dmacnt.items():
            toks.append((sname, v))
        for en in ENGS:
            w_ = []
            for s_, v in toks:
                if P.seen[en].get(s_, 0) < v:
                    P.seen[en][s_] = v
                    w_.append((s_, v))
            if w_:
                P.ops[en].append((w_, None, None, 0))

    def ffn(l, which):
        wg = W[which + "_wg"][l].rearrange("(kt p) f -> p kt f", p=128)
        wu = W[which + "_wu"][l].rearrange("(kt p) f -> p kt f", p=128)
        wd = W[which + "_wd"][l].rearrange("(ft p) d -> p ft d", p=128)
        g = gidx[(which + "_norm", l)]
        for tt in range(SEQ // 1024):
            t0 = tt * 1024
            for half in range(2):
                norm_to(HT, "HT", t0 + half * 512, g, half * 512)
            for fb in range(NFT // 2):
                wb = wcount["gu"] % 2
                wcount["gu"] += 1
                f0 = fb * 256
                for gu, wsrc in ((0, wg), (1, wu)):
                    for kh in range(2):
                        wload(WGU[:, wb, gu, kh * 4:(kh + 1) * 4, :], wsrc[:, kh * 4:(kh + 1) * 4, f0:f0 + 256],
                              ("WGU", wb, gu, kh))
                for fi in range(2):
                    ft = fb * 2 + fi
                    for half in range(2):
                        bg, bu = half * 2, half * 2 + 1
                        for gu, bank in ((0, bg), (1, bu)):
                            for kt in range(8):
                                P.op("pe", (lambda e, wb=wb, gu=gu, kt=kt, fi=fi, half=half, bank=bank: e.matmul(
                                    PSB[bank][:, :], lhsT=WGU[:, wb, gu, kt, fi * 128:(fi + 1) * 128],
                                    rhs=HT[:, kt, half * 512:(half + 1) * 512],
                                    start=(kt == 0), stop=(kt == 7))),
                                    r=[("WGU", wb, gu, kt // 4), ("HT", kt, half)], w=[("ps", bank)])
                        P.op("act", (lambda e, half=half, bg=bg: e.activation(
                            out=SG[:, half, :], in_=PSB[bg][:, :], func=AF.Silu)),
                            r=[("ps", bg)], w=[("SG", half)])
                        P.op("dve", (lambda e, half=half, bu=bu, ft=ft: e.tensor_tensor(
                            out=ATt[:, ft, half * 512:(half + 1) * 512], in0=SG[:, half, :],
                            in1=PSB[bu][:, :], op=ALU.mult)),
                            r=[("SG", half), ("ps", bu)], w=[("AT", ft, half)])
            for db in range(4):
                wb = wcount["d"] % 2
                wcount["d"] += 1
                d0 = db * 256
                for c4 in range(6):
                    a0, a1 = c4 * 4, min(c4 * 4 + 4, NFT)
                    wload(WDb[:, wb, a0:a1, :], wd[:, a0:a1, d0:d0 + 256], ("WD", wb, c4))
                for di in range(2):
                    dt_ = db * 2 + di
                    for half in range(2):
                        bank = 4 + half
                        for ft in range(NFT):
                            P.op("pe", (lambda e, wb=wb, ft=ft, di=di, half=half, bank=bank: e.matmul(
                                PSB[bank][:, :], lhsT=WDb[:, wb, ft, di * 128:(di + 1) * 128],
                                rhs=ATt[:, ft, half * 512:(half + 1) * 512],
                                start=(ft == 0), stop=(ft == NFT - 1))),
                                r=[("WD", wb, ft // 4), ("AT", ft, half)], w=[("ps", bank)])
                        tk = t0 + half * 512
                        P.op("dve", (lambda e, dt_=dt_, tk=tk, bank=bank: e.scalar_tensor_tensor(
                            out=XT[:, dt_, tk:tk + 512], in0=PSB[bank][:, :], scalar=0.5,
                            in1=XT[:, dt_, tk:tk + 512], op0=ALU.mult, op1=ALU.add)),
                            r=[("ps", bank), ("XT", tk // 512)], w=[("XT", tk // 512)])

    out_toks = []

    def final_store(s):
        g = gidx["final"]
        for q in range(4):
            tok0 = q * 512
            rms_stats(tok0)
            for kt in range(8):
                P.op("dve", (lambda e, kt=kt, tok0=tok0: e.scalar_tensor_tensor(
                    out=XT[:, kt, tok0:tok0 + 512], in0=XT[:, kt, tok0:tok0 + 512],
                    scalar=gains[:, g, kt:kt + 1], in1=RSTD[:], op0=ALU.mult, op1=ALU.mult)),
                    r=[("XT", q), "RSTD", ("gains", g)], w=[("XT", q)])
            for tb4 in range(4):
                tb = q * 4 + tb4
                b = tb % 2
                for half in range(2):
                    bank = 6 + half
                    for j in range(4):
                        kt = half * 4 + j
                        P.op("pe", (lambda e, kt=kt, j=j, tb=tb, bank=bank: e.transpose(
                            out=PSB[bank][:, j * 128:(j + 1) * 128],
                            in_=XT[:, kt, tb * 128:(tb + 1) * 128], identity=ident[:])),
                            r=[("XT", q), "ident"], w=[("ps", bank)])
                    if half == 0:
                        P.op("dve", (lambda e, b=b, bank=bank: e.tensor_copy(
                            out=IOB[:, b, 0:512], in_=PSB[bank][:, :])),
                            r=[("ps", bank)], w=[("IOB", b)])
                    else:
                        P.op("act", (lambda e, b=b, bank=bank: e.copy(
                            out=IOB[:, b, 512:1024], in_=PSB[bank][:, :])),
                            r=[("ps", bank)], w=[("IOB", b)])
                dst = out_d[s, tb * 128:(tb + 1) * 128, :]
                tok = P.dma("sp", (lambda e, b=b, dst=dst: e.dma_start(out=dst, in_=IOB[:, b, :])),
                            r=[("IOB", b)], sem="D_out%d" % b)
                out_toks.append(tok)

    CW = sb("CW", [128, NL, 8, 4], F32)
    CBs = sb("CBs", [128, NL, 8], F32)
    DTB = sb("DTB", [8, NL], F32)
    ANEG = sb("ANEG", [8, NL], F32)
    DBC = sb("DBC", [128, NL, 8], F32)
    EH = sb("EH", [8, 8], F32)
    NEH = sb("NEH", [8, 8], F32)
    NEGM = sb("NEGM", [128, 128], F32)
    SEL127 = sb("SEL127", [128, 128], F32)
    ONES8 = sb("ONES8", [8, 128], F32)
    ONESW = sb("ONESW", [128, 132], F32)
    for l in range(NL):
        small_dma(CW[:, l, :, :], W["m_conv_w"][l].rearrange("(t p) k -> p t k", p=128), ("CW", l))
        small_dma(CBs[:, l, :], W["m_conv_b"][l].rearrange("(t p) -> p t", p=128), ("CBs", l))
        small_dma(DTB[:, l:l + 1], W["m_dt_bias"][l].rearrange("(h o) -> h o", o=1), ("DTB", l))
        small_dma(ANEG[:, l:l + 1], W["m_A_log"][l].rearrange("(h o) -> h o", o=1), ("ANEG", l))
        small_dma(DBC[:, l, :], W["m_D"][l:l + 1, :].broadcast_to([128, 8]), ("DBC", l))
        P.op("act", (lambda e, l=l: e.activation(out=ANEG[:, l:l + 1], in_=ANEG[:, l:l + 1], func=AF.Exp)),
             r=[("ANEG", l)], w=[("ANEG", l)])
        P.op("dve", (lambda e, l=l: e.tensor_scalar(out=ANEG[:, l:l + 1], in0=ANEG[:, l:l + 1], scalar1=-1.0,
                                                    scalar2=None, op0=ALU.mult)),
             r=[("ANEG", l)], w=[("ANEG", l)])
    P.op("pool", lambda e: e.memset(ONES8[:], 1.0), w=["ONES8"])
    MASK1 = sb("MASK1", [128, 128], F32)
    MASKL = sb("MASKL", [128, 64], F32)
    OBD = sb("OBD", [128, 128], F32)
    O64BD = sb("O64BD", [128, 128], F32)
    P.op("pool", lambda e: e.memset(OBD[:], 0.0), w=["OBD"])
    P.op("pool", lambda e: e.memset(OBD[0:64, 0:64], 1.0), w=["OBD"])
    P.op("pool", lambda e: e.memset(OBD[64:128, 64:128], 1.0), w=["OBD"])
    P.op("pool", lambda e: e.tensor_scalar(out=O64BD[:], in0=OBD[:], scalar1=1.0 / 64, scalar2=None, op0=ALU.mult),
         r=["OBD"], w=["O64BD"])
    P.op("pool", lambda e: e.memset(CST[:, 3:4], 1e-30), w=["cst"])
    P.op("pool", lambda e: e.memset(CST[:, 4:5], 64e-5), w=["cst"])
    for hb_ in (slice(0, 64), slice(64, 128)):
        P.op("pool", (lambda e, hb_=hb_: e.affine_select(out=MASK1[hb_, 0:64], in_=onesf[hb_, 0:64], pattern=[[1, 64]],
                                                        compare_op=ALU.is_gt, fill=0.0, base=0, channel_multiplier=-1)),
             r=["onesf"], w=["MASK1"])
        P.op("pool", (lambda e, hb_=hb_: e.affine_select(out=MASK1[hb_, 64:128], in_=onesf[hb_, 0:64], pattern=[[1, 64]],
                                                        compare_op=ALU.is_ge, fill=0.0, base=0, channel_multiplier=-1)),
             r=["onesf"], w=["MASK1"])
        P.op("pool", (lambda e, hb_=hb_: e.affine_select(out=MASKL[hb_, :], in_=onesf[hb_, 0:64], pattern=[[-1, 64]],
                                                        compare_op=ALU.is_gt, fill=0.0, base=0, channel_multiplier=1)),
             r=["onesf"], w=["MASKL"])
    P.op("pool", lambda e: e.memset(ONESW[:], 1.0), w=["ONESW"])
    P.op("pool", lambda e: e.memset(EH[:], 1.0), w=["EH"])
    P.op("pool", lambda e: e.affine_select(out=EH[:], in_=EH[:], pattern=[[-1, 8]],
                                            compare_op=ALU.is_equal, fill=0.0, base=0, channel_multiplier=1),
         r=["EH"], w=["EH"])
    P.op("pool", lambda e: e.tensor_scalar(out=NEH[:], in0=EH[:], scalar1=-1.0, scalar2=None, op0=ALU.mult),
         r=["EH"], w=["NEH"])
    P.op("pool", lambda e: e.memset(NEGM[:], 0.0), w=["NEGM"])
    P.op("pool", lambda e: e.affine_select(out=NEGM[:], in_=NEGM[:], pattern=[[1, 128]],
                                            compare_op=ALU.is_ge, fill=-30000.0, base=0, channel_multiplier=-1),
         r=["NEGM"], w=["NEGM"])
    P.op("pool", lambda e: e.affine_select(out=SEL127[:], in_=onesf[:], pattern=[[0, 128]],
                                            compare_op=ALU.is_equal, fill=0.0, base=-127, channel_multiplier=1),
         r=["onesf"], w=["SEL127"])

    win_all = [W["w_in"][l].rearrange("(kt p) c -> p kt c", p=128) for l in range(NL)]
    wout_all = [W["w_out"][l].rearrange("(kt p) c -> p kt c", p=128) for l in range(NL)]

    def wslot():
        i = wcount["slot"] % 4
        wcount["slot"] += 1
        return i, WGU[:, i // 2, i % 2]

    def load_wblock(wsrc, c0, width):
        i, S = wslot()
        for kh in range(2):
            wload(S[:, kh * 4:(kh + 1) * 4, 0:width], wsrc[:, kh * 4:(kh + 1) * 4, c0:c0 + width], ("WS", i, kh), nofence=True, ceng="act")
        return i, S

    def mm_fm(bank, i, S, width, rhs_t, rkey, m0=0):
        for kt in range(8):
            P.op("pe", (lambda e, kt=kt: e.matmul(
                PSB[bank][m0:m0 + width, :], lhsT=S[:, kt, 0:width], rhs=rhs_t[:, kt, :],
                start=(kt == 0), stop=(kt == 7))),
                r=[("WS", i, kt // 4), (rkey, kt, 0)], w=[("ps", bank)], nofence=True)

    def mixer_phase(s, l):
        am = Arena()
        HTB = am.alloc([8, 512], BF16)
        YTB = am.alloc([8, 512], BF16)
        TAIL = am.alloc([8, 4], F32)
        STATE = am.alloc([8, 64], F32)
        STATEB = am.alloc([8, 64], BF16)
        NORMW = am.alloc([1, 512], F32)
        small_dma(NORMW[:, 0, :], W["m_norm_w"][l:l + 1, :].broadcast_to([128, 512]), ("NORMW", l))
        base_off = am.off
        g = gidx[("mix_norm", l)]
        win = win_all[l]
        wout = wout_all[l]

        P.op("pool", lambda e: e.memset(STATE[:], 0.0), w=["STATE"])
        P.op("pool", lambda e: e.memset(STATEB[:], 0.0), w=["STATEB"])
        P.op("pool", lambda e: e.memset(YTB[:], 0.0), w=[("YTB", k_) for k_ in range(8)])

        def ssd_block(tb):
            am.off = base_off
            XS = am.alloc([4, 512], F32)
            BC = am.alloc([4, 512], BF16)
            ACC = am.alloc([2, 512], F32)
            D8 = am.alloc([4, 512], F32)
            SM = am.alloc([2, 64], F32)
            XDT = am.alloc([2, 512], BF16)
            XDS = am.alloc([2, 512], BF16)
            XSD = am.alloc([2, 512], F32)
            BTM = am.alloc([2, 256], BF16)
            LT = am.alloc([2, 512], F32)
            GT = am.alloc([4, 512], BF16)
            Y1 = am.alloc([4, 512], F32)
            SZ = am.alloc([2, 512], F32)
            YTM = am.alloc([2, 512], BF16)
            SS = am.alloc([2, 8], F32)
            xslots = {0: load_wblock(win, 512, 128)}
            for j in range(8):
                bank = j % 2
                if j + 1 < 8:
                    xslots[j + 1] = load_wblock(win, 512 + 128 * (j + 1), 128)
                i, S = xslots[j]
                mm_fm(bank, i, S, 128, HTB, "HTB")
                a = j % 2
                pb = PSB[bank]
                P.op("dve", (lambda e, a=a, pb=pb, j=j: e.tensor_scalar(
                    out=ACC[:, a, :], in0=pb[:, :], scalar1=CW[:, l, j, 3:4], scalar2=CBs[:, l, j:j + 1],
                    op0=ALU.mult, op1=ALU.add)),
                    r=[("ps", bank), ("CW", l), ("CBs", l)], w=[("ACC", a)])
                for jj in range(3):
                    sh = 3 - jj
                    P.op("dve", (lambda e, a=a, pb=pb, j=j, jj=jj, sh=sh: e.scalar_tensor_tensor(
                        out=ACC[:, a, sh:512], in0=pb[:, 0:512 - sh], scalar=CW[:, l, j, jj:jj + 1],
                        in1=ACC[:, a, sh:512], op0=ALU.mult, op1=ALU.add)),
                        r=[("ps", bank), ("ACC", a)], w=[("ACC", a)])
                    if tb > 0:
                        P.op("dve", (lambda e, a=a, j=j, jj=jj, sh=sh: e.scalar_tensor_tensor(
                            out=ACC[:, a, 0:sh], in0=TAIL[:, j, 3 - sh:3], scalar=CW[:, l, j, jj:jj + 1],
                            in1=ACC[:, a, 0:sh], op0=ALU.mult, op1=ALU.add)),
                            r=[("TAIL", j), ("ACC", a)], w=[("ACC", a)])
                P.op("dve", (lambda e, pb=pb, j=j: e.tensor_copy(out=TAIL[:, j, 0:3], in_=pb[:, 509:512])),
                     r=[("ps", bank), ("ACC", a)], w=[("TAIL", j)])
                dst = XS[:, j, :] if j < 4 else BC[:, j - 4, :]
                dkey = ("XS", j) if j < 4 else ("BC", j - 4)
                P.op("act", (lambda e, a=a, dst=dst: e.activation(out=dst, in_=ACC[:, a, :], func=AF.Silu)),
                     r=[("ACC", a)], w=[dkey])
            i, S = load_wblock(win, 1536, 8)
            mm_fm(2, i, S, 8, HTB, "HTB")
            P.op("act", lambda e: e.activation(out=D8[0:8, 0, :], in_=PSB[2][0:8, :], func=AF.Exp,
                                               bias=DTB[:, l:l + 1], scale=1.0),
                 r=[("ps", 2), ("DTB", l)], w=["DTE"])
            P.op("act", lambda e: e.activation(out=D8[0:8, 1, :], in_=D8[0:8, 0, :], func=AF.Ln,
                                               bias=CST[0:8, 1:2], scale=1.0),
                 r=["DTE", "cst"], w=["DT"])
            P.op("dve", lambda e: e.tensor_scalar(out=D8[0:8, 2, :], in0=D8[0:8, 1, :], scalar1=ANEG[:, l:l + 1],
                                                  scalar2=None, op0=ALU.mult),
                 r=["DT", ("ANEG", l)], w=["DA"])
            for c in range(4):
                P.op("dve", (lambda e, c=c: e.tensor_tensor_scan(
                    out=D8[0:8, 3, c * 128:(c + 1) * 128], data0=ONES8[:, :], data1=D8[0:8, 2, c * 128:(c + 1) * 128],
                    initial=0.0, op0=ALU.mult, op1=ALU.add)),
                    r=["DA", "ONES8"], w=[("ACS", c)])
            iz0, SZ0 = load_wblock(win, 0, 256)
            iz1, SZ1 = load_wblock(win, 256, 256)
            def chunk(c):
                cs = slice(c * 128, (c + 1) * 128)
                b = c % 2
                P.op("pe", (lambda e, cs=cs: e.transpose(out=PSB[2][:, 0:8], in_=D8[0:8, 1, cs], identity=ident[0:8, 0:8])),
                     r=["DT", "ident"], w=[("ps", 2)])
                P.op("pe", (lambda e, cs=cs: e.transpose(out=PSB[2][:, 8:16], in_=D8[0:8, 3, cs], identity=ident[0:8, 0:8])),
                     r=[("ACS", c), "ident"], w=[("ps", 2)])
                P.op("dve", (lambda e, b=b: e.tensor_copy(out=SM[:, b, 0:16], in_=PSB[2][:, 0:16])),
                     r=[("ps", 2)], w=[("SM", b, 0)])
                P.op("pe", (lambda e, b=b: e.matmul(PSB[2][:, 16:24], lhsT=SEL127[:], rhs=SM[:, b, 8:16],
                                                   start=True, stop=True)),
                     r=[("SM", b, 0), "SEL127"], w=[("ps", 2)])
                P.op("dve", (lambda e, b=b: e.tensor_tensor(out=SM[:, b, 16:24], in0=PSB[2][:, 16:24],
                                                           in1=SM[:, b, 8:16], op=ALU.subtract)),
                     r=[("ps", 2), ("SM", b, 0)], w=[("SM", b, 1)])
                P.op("act", (lambda e, b=b: e.activation(out=SM[:, b, 24:32], in_=SM[:, b, 16:24], func=AF.Exp)),
                     r=[("SM", b, 1)], w=[("SM", b, 2)])
                P.op("act", (lambda e, b=b: e.activation(out=SM[:, b, 32:40], in_=PSB[2][:, 16:24], func=AF.Exp)),
                     r=[("ps", 2)], w=[("SM", b, 3)])
                P.op("act", (lambda e, b=b: e.activation(out=SM[:, b, 40:48], in_=SM[:, b, 8:16], func=AF.Exp)),
                     r=[("SM", b, 0)], w=[("SM", b, 4)])
                P.op("dve", (lambda e, b=b: e.tensor_tensor(out=SM[:, b, 48:56], in0=SM[:, b, 0:8],
                                                           in1=SM[:, b, 24:32], op=ALU.mult)),
                     r=[("SM", b, 0), ("SM", b, 2)], w=[("SM", b, 5)])

                def bc8(ap):
                    return ap.unsqueeze(2).broadcast_to([128, 8, 64])

                def v8(ap):
                    return ap.rearrange("p (h d) -> p h d", d=64)

                yield
                for i4 in range(4):
                    P.op("pe", (lambda e, i4=i4, cs=cs: e.transpose(out=PSB[0][:, i4 * 128:(i4 + 1) * 128],
                                                                    in_=XS[:, i4, cs], identity=ident[:])),
                         r=[("XS", i4), "ident"], w=[("ps", 0)])
                P.op("dve", (lambda e, b=b: e.tensor_tensor(out=v8(XDT[:, b, :]), in0=v8(PSB[0][:, :]),
                                                           in1=bc8(SM[:, b, 0:8]), op=ALU.mult)),
                     r=[("ps", 0), ("SM", b, 0)], w=[("XDT", b)])
                P.op("dve", (lambda e, b=b: e.tensor_tensor(out=v8(XDS[:, b, :]), in0=v8(PSB[0][:, :]),
                                                           in1=bc8(SM[:, b, 48:56]), op=ALU.mult)),
                     r=[("ps", 0), ("SM", b, 5)], w=[("XDS", b)])
                P.op("dve", (lambda e, b=b: e.tensor_tensor(out=v8(XSD[:, b, :]), in0=v8(PSB[0][:, :]),
                                                           in1=bc8(DBC[:, l, :]), op=ALU.mult)),
                     r=[("ps", 0), ("DBC", l)], w=[("XSD", b)])
                yield
                pbt = PSB[2][:, 256:384].bitcast(BF16)
                for g2 in range(2):
                    P.op("pe", (lambda e, g2=g2, cs=cs: e.transpose(out=pbt[:, g2 * 128:(g2 + 1) * 128],
                                                                    in_=BC[:, g2, cs], identity=identb[:])),
                         r=[("BC", g2), "identb"], w=[("ps", 2)])
                P.op("act", (lambda e, b=b: e.copy(out=BTM[:, b, :], in_=pbt)),
                     r=[("ps", 2)], w=[("BTM", b)])
                for g2 in range(2):
                    P.op("pe", (lambda e, g2=g2, cs=cs: e.matmul(PSB[3][:, b * 256 + g2 * 128:b * 256 + (g2 + 1) * 128],
                                                                 lhsT=BC[:, g2, cs], rhs=BC[:, 2 + g2, cs],
                                                                 start=True, stop=True)),
                         r=[("BC", g2), ("BC", 2 + g2)], w=[("ps", 3)])
                yield
                for g2 in range(2):
                    P.op("pe", (lambda e, g2=g2, cs=cs: e.matmul(
                        PSB[6][:, g2 * 256:(g2 + 1) * 256], lhsT=BC[:, 2 + g2, cs],
                        rhs=STATEB[:, g2 * 4:(g2 + 1) * 4, :].rearrange("p h d -> p (h d)"),
                        start=True, stop=True)),
                        r=[("BC", 2 + g2), "STATEB"], w=[("ps", 6)])
                P.op("dve", (lambda e, b=b: e.tensor_tensor(out=v8(Y1[:, b * 2, :]), in0=v8(PSB[6][:, :]),
                                                           in1=bc8(SM[:, b, 40:48]), op=ALU.mult)),
                     r=[("ps", 6), ("SM", b, 4)], w=[("Y1", b, 0)])
                for g2 in range(2):
                    P.op("pe", (lambda e, g2=g2, b=b: e.matmul(
                        PSB[7][:, g2 * 256:(g2 + 1) * 256], lhsT=BTM[:, b, g2 * 128:(g2 + 1) * 128],
                        rhs=XDS[:, b, g2 * 256:(g2 + 1) * 256], start=True, stop=True)),
                        r=[("BTM", b), ("XDS", b)], w=[("ps", 7)])
                P.op("dve", (lambda e, b=b: e.tensor_tensor(out=STATE[:], in0=STATE[:], in1=bc8(SM[:, b, 32:40]),
                                                           op=ALU.mult)),
                     r=["STATE", ("SM", b, 3)], w=["STATE"])
                P.op("dve", lambda e: e.tensor_tensor(out=STATE[:], in0=STATE[:], in1=v8(PSB[7][:, :]), op=ALU.add),
                     r=["STATE", ("ps", 7)], w=["STATE"])
                P.op("dve", lambda e: e.tensor_copy(out=STATEB[:], in_=STATE[:]),
                     r=["STATE"], w=["STATEB"])

                yield
                for g2 in range(2):
                    for hh in range(4):
                        h = g2 * 4 + hh
                        o = PSB[4][:, hh * 128:(hh + 1) * 128]
                        P.op("pe", (lambda e, o=o, h=h, cs=cs: e.matmul(o, lhsT=EH[:, h:h + 1].broadcast_to([8, 128]), rhs=D8[0:8, 3, cs],
                                                                        start=True, stop=False)),
                             r=["EH", ("ACS", c)], w=[("ps", 4)])
                        P.op("pe", (lambda e, o=o, h=h, cs=cs: e.matmul(o, lhsT=D8[0:8, 3, cs], rhs=NEH[:, h:h + 1].broadcast_to([8, 128]),
                                                                        start=False, stop=False)),
                             r=["NEH", ("ACS", c)], w=[("ps", 4)])
                        P.op("pe", (lambda e, o=o: e.matmul(o, lhsT=ident[:], rhs=NEGM[:], start=False, stop=True)),
                             r=["ident", "NEGM"], w=[("ps", 4)])
                    P.op("act", (lambda e, g2=g2: e.activation(out=LT[:, g2, :], in_=PSB[4][:, :], func=AF.Exp)),
                         r=[("ps", 4)], w=[("LT", g2)])
                    P.op("dve", (lambda e, g2=g2: e.tensor_tensor(
                        out=GT[:, b * 2 + g2, :].rearrange("p (h l) -> p h l", l=128),
                        in0=LT[:, g2, :].rearrange("p (h l) -> p h l", l=128),
                        in1=PSB[3][:, b * 256 + g2 * 128:b * 256 + (g2 + 1) * 128].unsqueeze(1).broadcast_to([128, 4, 128]),
                        op=ALU.mult)),
                        r=[("LT", g2), ("ps", 3)], w=[("GT", b, g2)])
                yield
                for h in range(8):
                    g2, hh = h // 4, h % 4
                    P.op("pe", (lambda e, h=h, g2=g2, hh=hh, b=b: e.matmul(
                        PSB[5][:, h * 64:(h + 1) * 64], lhsT=GT[:, b * 2 + g2, hh * 128:(hh + 1) * 128],
                        rhs=XDT[:, b, h * 64:(h + 1) * 64], start=True, stop=True)),
                        r=[("GT", b, g2), ("XDT", b)], w=[("ps", 5)])
                P.op("dve", lambda e: e.tensor_tensor(out=Y1[:, b * 2, :], in0=Y1[:, b * 2, :], in1=PSB[5][:, :], op=ALU.add),
                     r=[("ps", 5), ("Y1", b, 0)], w=[("Y1", b, 0)])
                yield
                for half, (iz, Sz) in enumerate(((iz0, SZ0), (iz1, SZ1))):
                    for kt in range(8):
                        P.op("pe", (lambda e, half=half, Sz=Sz, kt=kt, cs=cs: e.matmul(
                            PSB[1][:, half * 256:(half + 1) * 256], lhsT=HTB[:, kt, cs], rhs=Sz[:, kt, 0:256],
                            start=(kt == 0), stop=(kt == 7))),
                            r=[("WS", iz, kt // 4), ("HTB", kt, 0)], w=[("ps", 1)])
                P.op("act", lambda e: e.activation(out=SZ[:, b, :], in_=PSB[1][:, :], func=AF.Silu),
                     r=[("ps", 1)], w=[("SZ", b)])
                yield
                P.op("dve", (lambda e, b=b: e.tensor_tensor(out=Y1[:, b * 2, :], in0=Y1[:, b * 2, :], in1=XSD[:, b, :], op=ALU.add)),
                     r=[("XSD", b), ("Y1", b, 0)], w=[("Y1", b, 0)])
                P.op("dve", lambda e: e.tensor_tensor(out=Y1[:, b * 2, :], in0=Y1[:, b * 2, :], in1=SZ[:, b, :], op=ALU.mult),
                     r=[("SZ", b), ("Y1", b, 0)], w=[("Y1", b, 0)])
                P.op("act", lambda e: e.activation(out=Y1[:, b * 2 + 1, :], in_=Y1[:, b * 2, :], func=AF.Square,
                                                   accum_out=SS[:, b, 0:1]),
                     r=[("Y1", b, 0)], w=[("Y1", b, 1), ("SS", b)])
                P.op("act", lambda e: e.activation(out=SS[:, b, 1:2], in_=SS[:, b, 0:1], func=AF.Ln,
                                                   bias=CST[:, 0:1], scale=1.0 / 512),
                     r=[("SS", b), "cst"], w=[("SS1", b)])
                P.op("act", lambda e: e.activation(out=SS[:, b, 2:3], in_=SS[:, b, 1:2], func=AF.Exp, scale=-0.5),
                     r=[("SS1", b)], w=[("SS2", b)])
                P.op("dve", (lambda e, b=b: e.scalar_tensor_tensor(
                    out=YTM[:, b, :], in0=Y1[:, b * 2, :], scalar=SS[:, b, 2:3], in1=NORMW[:, 0, :],
                    op0=ALU.mult, op1=ALU.mult)),
                    r=[("Y1", b, 0), ("SS2", b), ("NORMW", l)], w=[("YTM", b)])
                pyt = PSB[7][:, 256:512].bitcast(BF16)
                for i4 in range(4):
                    P.op("pe", (lambda e, i4=i4, b=b: e.transpose(out=pyt[:, i4 * 128:(i4 + 1) * 128],
                                                                  in_=YTM[:, b, i4 * 128:(i4 + 1) * 128],
                                                                  identity=identb[:])),
                         r=[("YTM", b), "identb"], w=[("ps", 7)])
                P.op("act", (lambda e, cs=cs: e.copy(out=YTB[:, 0:4, cs], in_=pyt.rearrange("p (a t) -> p a t", t=128))),
                     r=[("ps", 7)], w=[("YTB", k_) for k_ in range(4)])
            run_pipelined([chunk(c_) for c_ in range(4)], 2)

        TWO_PI = 6.283185307179586
        PI = 3.141592653589793

        def s5_setup():
            SC = am.alloc([20, 8], F32)
            MAG = am.alloc([1, 8], F32)
            NS128 = am.alloc([1, 8], F32)
            COST = am.alloc([8, 129], F32)
            SINT = am.alloc([8, 129], F32)
            LBR = am.alloc([8, 128], BF16)
            LBI = am.alloc([8, 128], BF16)
            LCR = am.alloc([8, 64], BF16)
            LCI = am.alloc([8, 64], BF16)
            SDG = am.alloc([1, 4], F32)
            GLW = am.alloc([2, 256], BF16)
            CARR = am.alloc([1, 8], F32)
            CARI = am.alloc([1, 8], F32)
            mark = am.off
            TAU = am.alloc([1, 129], F32)
            ARG = am.alloc([8, 129], F32)
            TMP1 = am.alloc([8, 129], F32)
            TMP2 = am.alloc([8, 129], F32)
            BN2 = am.alloc([2, 8, 64], F32)
            BN2b = am.alloc([2, 8, 64], BF16)
            CNat = am.alloc([2, 8, 128], F32)
            CT = am.alloc([4, 256], F32)
            st = {"n": 0}

            def K_(name):
                return ("s5c", name)

            def col(c):
                return SC[:, c, :]

            log_dt = W["s_log_dt"][l].rearrange("(gp gl) -> gl gp", gl=2)
            for gl in range(2):
                small_dma(SC[gl * 64:(gl + 1) * 64, 0, :], log_dt[gl:gl + 1, :].broadcast_to([64, 8]), K_(("ldt", gl)))
            small_dma(SC[:, 1, :], W["s_A_re"][l].rearrange("(gp gl) p -> (gl p) gp", gl=2), K_("are"))
            small_dma(SC[:, 2, :], W["s_A_im"][l].rearrange("(gp gl) p -> (gl p) gp", gl=2), K_("aim"))
            small_dma(SDG[:, 0, 0:2], W["s_D"][l].rearrange("(t p) -> p t", p=128), K_("sd"))
            small_dma(SDG[:, 0, 2:4], W["s_glu_b"][l].rearrange("(t p) -> p t", p=128), K_("glb"))
            wload(GLW[:, :, :], W["s_glu_w"][l].rearrange("(kt p) c -> p kt c", p=128), K_("glw"))

            def dv(fn, r, w):
                P.op("dve", fn, r=[K_(x) for x in r], w=[K_(x) for x in w])

            def ac(fn, r, w):
                P.op("act", fn, r=[K_(x) for x in r], w=[K_(x) for x in w])

            ac(lambda e: e.activation(out=col(0), in_=col(0), func=AF.Exp), [("ldt", 0), ("ldt", 1)], ["dt"])
            dv(lambda e: e.tensor_tensor(out=col(3), in0=col(1), in1=col(0), op=ALU.mult), ["are", "dt"], ["ar"])
            dv(lambda e: e.tensor_tensor(out=col(4), in0=col(2), in1=col(0), op=ALU.mult), ["aim", "dt"], ["th"])
            ac(lambda e: e.activation(out=MAG[:, 0, :], in_=col(3), func=AF.Exp), ["ar"], ["mag"])

            def sincos(th, out_s, out_c, t1, t2, rk, wk):
                for phase, out in ((0.0, out_s), (PI / 2, out_c)):
                    tag = "s" if phase == 0.0 else "c"
                    dv(lambda e: e.tensor_scalar(out=t1, in0=th, scalar1=1.0 / TWO_PI, scalar2=phase / TWO_PI + 0.5,
                                                 op0=ALU.mult, op1=ALU.add), rk, [wk + "t1"])
                    t2i = t2.bitcast(I32)
                    dv(lambda e: e.tensor_copy(out=t2i, in_=t1), [wk + "t1"], [wk + "t2"])
                    dv(lambda e: e.tensor_copy(out=t1, in_=t2i), [wk + "t2"], [wk + "t1"])
                    dv(lambda e: e.scalar_tensor_tensor(out=t1, in0=t1, scalar=-TWO_PI, in1=th, op0=ALU.mult,
                                                        op1=ALU.add), [wk + "t1"] + rk, [wk + "t1"])
                    if phase != 0.0:
                        dv(lambda e: e.tensor_scalar(out=t1, in0=t1, scalar1=phase, scalar2=None, op0=ALU.add),
                           [wk + "t1"], [wk + "t1"])
                    dv(lambda e: e.tensor_scalar(out=t2, in0=t1, scalar1=PI, scalar2=-TWO_PI, op0=ALU.is_gt,
                                                 op1=ALU.mult), [wk + "t1"], [wk + "t2"])
                    dv(lambda e: e.tensor_tensor(out=t1, in0=t1, in1=t2, op=ALU.add), [wk + "t1", wk + "t2"], [wk + "t1"])
                    dv(lambda e: e.tensor_scalar(out=t2, in0=t1, scalar1=-PI, scalar2=TWO_PI, op0=ALU.is_lt,
                                                 op1=ALU.mult), [wk + "t1"], [wk + "t2"])
                    dv(lambda e: e.tensor_tensor(out=t1, in0=t1, in1=t2, op=ALU.add), [wk + "t1", wk + "t2"], [wk + "t1"])
                    ac(lambda e, out=out: e.activation(out=out, in_=t1, func=AF.Sin), [wk + "t1"], [wk + tag])

            sincos(col(4), col(5), col(6), col(18), col(19), ["th"], "sc0")
            dv(lambda e: e.tensor_tensor(out=col(7), in0=MAG[:, 0, :], in1=col(6), op=ALU.mult), ["mag", "sc0c"], ["lr"])
            dv(lambda e: e.tensor_tensor(out=col(8), in0=MAG[:, 0, :], in1=col(5), op=ALU.mult), ["mag", "sc0s"], ["li"])
            dv(lambda e: e.tensor_tensor(out=col(9), in0=col(1), in1=col(1), op=ALU.mult), ["are"], ["den"])
            dv(lambda e: e.tensor_tensor(out=col(13), in0=col(2), in1=col(2), op=ALU.mult), ["aim"], ["den2"])
            dv(lambda e: e.tensor_tensor(out=col(9), in0=col(9), in1=col(13), op=ALU.add), ["den", "den2"], ["den"])
            dv(lambda e: e.reciprocal(out=col(9), in_=col(9)), ["den"], ["den"])
            dv(lambda e: e.tensor_scalar(out=col(10), in0=col(7), scalar1=-1.0, scalar2=None, op0=ALU.add), ["lr"], ["nr"])
            dv(lambda e: e.tensor_tensor(out=col(11), in0=col(10), in1=col(1), op=ALU.mult), ["nr", "are"], ["cr"])
            dv(lambda e: e.tensor_tensor(out=col(13), in0=col(8), in1=col(2), op=ALU.mult), ["li", "aim", "den2", "den"], ["tmp13"])
            dv(lambda e: e.tensor_tensor(out=col(11), in0=col(11), in1=col(13), op=ALU.add), ["cr", "tmp13"], ["cr"])
            dv(lambda e: e.tensor_tensor(out=col(11), in0=col(11), in1=col(9), op=ALU.mult), ["cr", "den"], ["cr"])
            dv(lambda e: e.tensor_tensor(out=col(12), in0=col(8), in1=col(1), op=ALU.mult), ["li", "are"], ["ci"])
            dv(lambda e: e.tensor_tensor(out=col(13), in0=col(10), in1=col(2), op=ALU.mult), ["nr", "aim", "cr"], ["tmp13"])
            dv(lambda e: e.tensor_tensor(out=col(12), in0=col(12), in1=col(13), op=ALU.subtract), ["ci", "tmp13"], ["ci"])
            dv(lambda e: e.tensor_tensor(out=col(12), in0=col(12), in1=col(9), op=ALU.mult), ["ci", "den"], ["ci"])
            dv(lambda e: e.tensor_tensor_scan(out=TAU[:, 0, :], data0=ONESW[:, 0:129],
                                              data1=ONESW[:, 0:129], initial=-1.0, op0=ALU.mult, op1=ALU.add),
               [], ["tau"])
            dv(lambda e: e.tensor_tensor(out=ARG[:], in0=TAU[:, 0:1, :].broadcast_to([128, 8, 129]),
                                         in1=col(4).unsqueeze(2).broadcast_to([128, 8, 129]), op=ALU.mult),
               ["tau", "th"], ["arg"])
            sincos(ARG[:], SINT[:], COST[:], TMP1[:], TMP2[:], ["arg"], "sc1")
            dv(lambda e: e.tensor_scalar(out=NS128[:, 0, :], in0=SINT[:, :, 128], scalar1=-1.0, scalar2=None, op0=ALU.mult),
               ["sc1s"], ["ns128"])
            for ri, nm in enumerate(("s_B_re", "s_B_im")):
                P.op("pool", (lambda e, ri=ri: e.memset(BN2[:, ri], 0.0)), w=[K_(("bn2", ri))])
                srcb = W[nm][l].rearrange("(gp gl) p h -> gl p gp h", gl=2)
                for gl in range(2):
                    for par in range(2):
                        P.dma("sp", (lambda e, ri=ri, gl=gl, par=par, srcb=srcb: e.dma_start(
                            out=BN2[gl * 64:(gl + 1) * 64, ri, par::2, par * 32 + gl * 16:par * 32 + (gl + 1) * 16],
                            in_=srcb[gl][:, par::2, :], allow_slow_non_contiguous=True)),
                            w=[K_(("bn2", ri))], sem="D_bn2_%d_%d_%d" % (ri, gl, par))
                P.op("pool", (lambda e, ri=ri: e.tensor_copy(out=BN2b[:, ri], in_=BN2[:, ri])),
                     r=[K_(("bn2", ri))], w=[K_(("bn2b", ri))])
                pbf = PSB[ri][:, :].bitcast(BF16)
                for gp in range(8):
                    pr = ((gp % 4) // 2) * 64
                    P.op("pe", (lambda e, ri=ri, gp=gp, pr=pr, pbf=pbf: e.transpose(
                        out=pbf[pr:pr + 64, gp * 128:(gp + 1) * 128], in_=BN2b[:, ri, gp, :], identity=identb[:])),
                        r=[K_(("bn2b", ri)), "identb"], w=[("ps", ri)])
                LB = LBR if ri == 0 else LBI
                for gp in range(8):
                    pr = ((gp % 4) // 2) * 64
                    P.op("act", (lambda e, LB=LB, gp=gp, pr=pr, pbf=pbf: e.copy(
                        out=LB[pr:pr + 64, gp, :], in_=pbf[pr:pr + 64, gp * 128:(gp + 1) * 128])),
                        r=[("ps", ri)], w=[K_(("lb", ri))])
            for ri, nm in enumerate(("s_C_re", "s_C_im")):
                P.op("pool", (lambda e, ri=ri: e.memset(CNat[0:32, ri], 0.0)), w=[K_(("cn", ri))])
                srcc = W[nm][l].rearrange("(gp gl) h p -> gl h gp p", gl=2)
                for gl in range(2):
                    P.dma("sp", (lambda e, ri=ri, gl=gl, srcc=srcc: e.dma_start(
                        out=CNat[gl * 16:(gl + 1) * 16, ri, :, gl * 64:(gl + 1) * 64], in_=srcc[gl],
                        allow_slow_non_contiguous=True)),
                        w=[K_(("cn", ri))], sem="D_cn_%d_%d" % (ri, gl))
                for gp in range(8):
                    P.op("pe", (lambda e, ri=ri, gp=gp: e.transpose(
                        out=PSB[2 + ri][:, gp * 32:(gp + 1) * 32], in_=CNat[0:32, ri, gp, :], identity=ident[0:32, 0:32])),
                        r=[K_(("cn", ri)), "ident"], w=[("ps", 2 + ri)])

            def b32(c):
                return col(c).unsqueeze(2).broadcast_to([128, 8, 32])

            def v32(ap):
                return ap.rearrange("p (g c) -> p g c", c=32)
            crt = v32(PSB[2][:, 0:256])
            cit = v32(PSB[3][:, 0:256])
            P.op("dve", lambda e: e.tensor_tensor(out=v32(CT[:, 0, :]), in0=crt, in1=b32(11), op=ALU.mult),
                 r=[("ps", 2), K_("cr")], w=[K_("ct0")])
            P.op("dve", lambda e: e.tensor_tensor(out=v32(CT[:, 1, :]), in0=cit, in1=b32(12), op=ALU.mult),
                 r=[("ps", 3), K_("ci")], w=[K_("ct1")])
            P.op("dve", lambda e: e.tensor_tensor(out=v32(CT[:, 2, :]), in0=crt, in1=b32(12), op=ALU.mult),
                 r=[("ps", 2), K_("ci")], w=[K_("ct2")])
            P.op("dve", lambda e: e.tensor_tensor(out=v32(CT[:, 3, :]), in0=cit, in1=b32(11), op=ALU.mult),
                 r=[("ps", 3), K_("cr")], w=[K_("ct3")])
            P.op("pool", lambda e: e.memset(LCR[:], 0.0), w=[K_("lcr")])
            P.op("pool", lambda e: e.memset(LCI[:], 0.0), w=[K_("lci")])
            dv(lambda e: e.tensor_tensor(out=CT[:, 2, :], in0=CT[:, 2, :], in1=CT[:, 3, :], op=ALU.add), ["ct2", "ct3"], ["ct2"])
            for par in range(2):
                dv(lambda e, par=par: e.tensor_tensor(
                    out=LCR[:, par::2, par * 32:(par + 1) * 32], in0=v32(CT[:, 0, :])[:, par::2, :],
                    in1=v32(CT[:, 1, :])[:, par::2, :], op=ALU.subtract), ["ct0", "ct1", "lcr"], ["lcr"])
                dv(lambda e, par=par: e.tensor_scalar(
                    out=LCI[:, par::2, par * 32:(par + 1) * 32], in0=v32(CT[:, 2, :])[:, par::2, :], scalar1=-1.0,
                    scalar2=None, op0=ALU.mult), ["ct2", "lci"], ["lci"])
            P.op("pool", lambda e: e.memset(CARR[:], 0.0), w=["CARR"])
            P.op("pool", lambda e: e.memset(CARI[:], 0.0), w=["CARI"])
            am.off = mark
            return dict(MAG=MAG, NS128=NS128, COST=COST, SINT=SINT, LBR=LBR, LBI=LBI, LCR=LCR, LCI=LCI, SDG=SDG,
                        GLW=GLW, CARR=CARR, CARI=CARI, K_=K_)

        def s5_block(tb, C5):
            am.off = base_off
            K_ = C5["K_"]
            MAG, NS128, COST, SINT = C5["MAG"], C5["NS128"], C5["COST"], C5["SINT"]
            LBR, LBI, LCR, LCI, SDG, GLW = C5["LBR"], C5["LBI"], C5["LCR"], C5["LCI"], C5["SDG"], C5["GLW"]
            CARR, CARI = C5["CARR"], C5["CARI"]
            US = am.alloc([2, 512], F32)
            USB = am.alloc([2, 512], BF16)
            T = am.alloc([8, 512], F32)
            WRI = am.alloc([4, 512], F32)
            ZRI = am.alloc([4, 512], F32)
            XRI = am.alloc([4, 512], BF16)
            YS = am.alloc([2, 512], F32)
            YG = am.alloc([2, 512], F32)
            YGB = am.alloc([2, 512], BF16)
            CTMP = am.alloc([1, 4], F32)
            i, S = load_wblock(win, 1544, 256)
            for ct in range(2):
                for kt in range(8):
                    P.op("pe", (lambda e, kt=kt, ct=ct, S=S: e.matmul(
                        PSB[ct][:, :], lhsT=S[:, kt, ct * 128:(ct + 1) * 128], rhs=HTB[:, kt, :],
                        start=(kt == 0), stop=(kt == 7))),
                        r=[("WS", i, kt // 4), ("HTB", kt, 0)], w=[("ps", ct)], nofence=True)
                P.op("act", (lambda e, ct=ct: e.copy(out=US[:, ct, :], in_=PSB[ct][:, :])), r=[("ps", ct)], w=[("US", ct)])
                P.op("pool", (lambda e, ct=ct: e.tensor_copy(out=USB[:, ct, :], in_=US[:, ct, :])),
                     r=[("US", ct)], w=[("USB", ct)])

            def b4(tab, gp):
                return tab[:, gp, 0:128].unsqueeze(1).broadcast_to([128, 4, 128])

            def v4(ap):
                return ap.rearrange("p (c t) -> p c t", t=128)

            def pair(gp):
                pb = gp % 2
                b2, b3 = (2, 3) if pb == 0 else (6, 7)
                ct = gp // 4
                pr = ((gp % 4) // 2) * 64
                P.op("pe", (lambda e, gp=gp, ct=ct, pr=pr: e.matmul(
                    PSB[b2][:, :], lhsT=LBR[pr:pr + 64, gp, :], rhs=USB[pr:pr + 64, ct, :], start=True, stop=True)),
                    r=[K_(("lb", 0)), ("USB", ct)], w=[("ps", b2)])
                P.op("pe", (lambda e, gp=gp, ct=ct, pr=pr: e.matmul(
                    PSB[b3][:, :], lhsT=LBI[pr:pr + 64, gp, :], rhs=USB[pr:pr + 64, ct, :], start=True, stop=True)),
                    r=[K_(("lb", 1)), ("USB", ct)], w=[("ps", b3)])
                cosb, sinb = b4(COST, gp), b4(SINT, gp)
                yield
                rk = [K_("sc1c"), K_("sc1s")]
                P.op("dve", (lambda e, cosb=cosb: e.tensor_tensor(out=v4(T[:, pb * 4 + 0, :]), in0=v4(PSB[b2][:, :]), in1=cosb, op=ALU.mult)),
                     r=[("ps", b2)] + rk, w=[("T", pb, 0)])
                P.op("dve", (lambda e, sinb=sinb: e.tensor_tensor(out=v4(T[:, pb * 4 + 1, :]), in0=v4(PSB[b3][:, :]), in1=sinb, op=ALU.mult)),
                     r=[("ps", b3)] + rk, w=[("T", pb, 1)])
                P.op("dve", (lambda e, cosb=cosb: e.tensor_tensor(out=v4(T[:, pb * 4 + 2, :]), in0=v4(PSB[b3][:, :]), in1=cosb, op=ALU.mult)),
                     r=[("ps", b3)] + rk, w=[("T", pb, 2)])
                P.op("dve", (lambda e, sinb=sinb: e.tensor_tensor(out=v4(T[:, pb * 4 + 3, :]), in0=v4(PSB[b2][:, :]), in1=sinb, op=ALU.mult)),
                     r=[("ps", b2)] + rk, w=[("T", pb, 3)])
                P.op("pool", lambda e: e.tensor_tensor(out=WRI[:, pb * 2 + 0, :], in0=T[:, pb * 4 + 0, :], in1=T[:, pb * 4 + 1, :], op=ALU.add),
                     r=[("T", pb, 0), ("T", pb, 1)], w=[("WRI", pb, 0)])
                P.op("pool", lambda e: e.tensor_tensor(out=WRI[:, pb * 2 + 1, :], in0=T[:, pb * 4 + 2, :], in1=T[:, pb * 4 + 3, :], op=ALU.subtract),
                     r=[("T", pb, 2), ("T", pb, 3)], w=[("WRI", pb, 1)])
                yield
                magb = MAG[:, 0, gp:gp + 1].broadcast_to([128, 128])
                for c in range(4):
                    cs = slice(c * 128, (c + 1) * 128)
                    first = (tb == 0 and c == 0)
                    for ri, CAR in ((0, CARR), (1, CARI)):
                        init = 0.0 if first else CAR[:, 0, gp:gp + 1]
                        P.op("dve", (lambda e, ri=ri, cs=cs, init=init, magb=magb: e.tensor_tensor_scan(
                            out=ZRI[:, pb * 2 + ri, cs], data0=magb, data1=WRI[:, pb * 2 + ri, cs], initial=init,
                            op0=ALU.mult, op1=ALU.add)),
                            r=[("WRI", pb, ri), K_("mag"), ("CAR", ri, gp)], w=[("ZRI", pb, ri, c)])
                    zr = ZRI[:, pb * 2, c * 128 + 127:c * 128 + 128]
                    zi = ZRI[:, pb * 2 + 1, c * 128 + 127:c * 128 + 128]
                    c128 = COST[:, gp, 128:129]
                    s128 = SINT[:, gp, 128:129]
                    ns128 = NS128[:, 0, gp:gp + 1]
                    P.op("act", (lambda e, zi=zi, ns128=ns128: e.activation(out=CTMP[:, 0, pb * 2:pb * 2 + 1], in_=zi, func=AF.Identity,
                                                                            scale=ns128)),
                         r=[("ZRI", pb, 1, c), K_("ns128")], w=[("CTMP0", pb)])
                    P.op("act", (lambda e, zr=zr, c128=c128, gp=gp: e.activation(out=CARR[:, 0, gp:gp + 1], in_=zr, func=AF.Identity,
                                                                                scale=c128, bias=CTMP[:, 0, pb * 2:pb * 2 + 1])),
                         r=[("ZRI", pb, 0, c), ("CTMP0", pb)] + rk, w=[("CAR", 0, gp)])
                    P.op("pool", (lambda e, zr=zr, s128=s128: e.tensor_scalar(out=CTMP[:, 0, pb * 2 + 1:pb * 2 + 2], in0=zr, scalar1=s128,
                                                                              scalar2=None, op0=ALU.mult)),
                         r=[("ZRI", pb, 0, c)] + rk, w=[("CTMP1", pb)])
                    P.op("pool", (lambda e, zi=zi, c128=c128, gp=gp: e.tensor_scalar(
                        out=CARI[:, 0, gp:gp + 1], in0=zi, scalar1=c128, scalar2=CTMP[:, 0, pb * 2 + 1:pb * 2 + 2],
                        op0=ALU.mult, op1=ALU.add)),
                         r=[("ZRI", pb, 1, c), ("CTMP1", pb)] + rk, w=[("CAR", 1, gp)])
                yield
                zk = [("ZRI", pb, 0, c_) for c_ in range(4)]
                zki = [("ZRI", pb, 1, c_) for c_ in range(4)]
                P.op("dve", (lambda e, cosb=cosb: e.tensor_tensor(out=v4(T[:, pb * 4 + 0, :]), in0=v4(ZRI[:, pb * 2, :]), in1=cosb, op=ALU.mult)),
                     r=zk + rk, w=[("T", pb, 0)])
                P.op("dve", (lambda e, sinb=sinb: e.tensor_tensor(out=v4(T[:, pb * 4 + 1, :]), in0=v4(ZRI[:, pb * 2 + 1, :]), in1=sinb, op=ALU.mult)),
                     r=zki + rk, w=[("T", pb, 1)])
                P.op("pool", (lambda e, sinb=sinb: e.tensor_tensor(out=v4(T[:, pb * 4 + 2, :]), in0=v4(ZRI[:, pb * 2, :]), in1=sinb, op=ALU.mult)),
                     r=zk + rk, w=[("T", pb, 2)])
                P.op("pool", (lambda e, cosb=cosb: e.tensor_tensor(out=v4(T[:, pb * 4 + 3, :]), in0=v4(ZRI[:, pb * 2 + 1, :]), in1=cosb, op=ALU.mult)),
                     r=zki + rk, w=[("T", pb, 3)])
                P.op("dve", lambda e: e.tensor_tensor(out=XRI[:, pb * 2 + 0, :], in0=T[:, pb * 4 + 0, :], in1=T[:, pb * 4 + 1, :], op=ALU.subtract),
                     r=[("T", pb, 0), ("T", pb, 1)], w=[("XRI", pb, 0)])
                P.op("pool", lambda e: e.tensor_tensor(out=XRI[:, pb * 2 + 1, :], in0=T[:, pb * 4 + 2, :], in1=T[:, pb * 4 + 3, :], op=ALU.add),
                     r=[("T", pb, 2), ("T", pb, 3)], w=[("XRI", pb, 1)])
                yield
                P.op("pe", (lambda e, gp=gp, ct=ct, pr=pr: e.matmul(
                    PSB[4 + ct][pr:pr + 64, :], lhsT=LCR[:, gp, :], rhs=XRI[:, pb * 2 + 0, :], start=(gp % 2 == 0), stop=False)),
                    r=[K_("lcr"), ("XRI", pb, 0)], w=[("ps", 4 + ct)])
                P.op("pe", (lambda e, gp=gp, ct=ct, pr=pr: e.matmul(
                    PSB[4 + ct][pr:pr + 64, :], lhsT=LCI[:, gp, :], rhs=XRI[:, pb * 2 + 1, :], start=False, stop=(gp % 2 == 1))),
                    r=[K_("lci"), ("XRI", pb, 1)], w=[("ps", 4 + ct)])
            run_pipelined([pair(g_) for g_ in range(8)], 2)
            for ct in range(2):
                P.op("dve", (lambda e, ct=ct: e.scalar_tensor_tensor(
                    out=YS[:, ct, :], in0=US[:, ct, :], scalar=SDG[:, 0, ct:ct + 1], in1=PSB[4 + ct][:, :],
                    op0=ALU.mult, op1=ALU.add)),
                    r=[("US", ct), ("ps", 4 + ct), K_("sd")], w=[("YS", ct)])
                P.op("pool", (lambda e, ct=ct: e.tensor_tensor(out=T[:, ct, :], in0=YS[:, ct, :], in1=YS[:, ct, :], op=ALU.mult)),
                     r=[("YS", ct)], w=[("T", 0, ct)])
                P.op("pool", (lambda e, ct=ct: e.tensor_scalar(out=T[:, ct, :], in0=T[:, ct, :], scalar1=0.044715, scalar2=1.0,
                                                               op0=ALU.mult, op1=ALU.add)),
                     r=[("T", 0, ct)], w=[("T", 0, ct)])
                P.op("pool", (lambda e, ct=ct: e.tensor_tensor(out=T[:, ct, :], in0=T[:, ct, :], in1=YS[:, ct, :], op=ALU.mult)),
                     r=[("T", 0, ct), ("YS", ct)], w=[("T", 0, ct)])
                P.op("act", (lambda e, ct=ct: e.activation(out=T[:, 2 + ct, :], in_=T[:, ct, :], func=AF.Sigmoid,
                                                           scale=1.5957691216057308)),
                     r=[("T", 0, ct)], w=[("T", 0, 2 + ct)])
                P.op("dve", (lambda e, ct=ct: e.tensor_tensor(out=YG[:, ct, :], in0=YS[:, ct, :], in1=T[:, 2 + ct, :], op=ALU.mult)),
                     r=[("YS", ct), ("T", 0, 2 + ct)], w=[("YG", ct)])
                P.op("pool", (lambda e, ct=ct: e.tensor_copy(out=YGB[:, ct, :], in_=YG[:, ct, :])),
                     r=[("YG", ct)], w=[("YGB", ct)])
            for mt in range(2):
                for kt in range(2):
                    P.op("pe", (lambda e, mt=mt, kt=kt: e.matmul(
                        PSB[6 + mt][:, :], lhsT=GLW[:, kt, mt * 128:(mt + 1) * 128], rhs=YGB[:, kt, :],
                        start=(kt == 0), stop=(kt == 1))),
                        r=[K_("glw"), ("YGB", kt)], w=[("ps", 6 + mt)])
                P.op("act", (lambda e, mt=mt: e.activation(out=T[:, mt, :], in_=PSB[6 + mt][:, :], func=AF.Sigmoid,
                                                           bias=SDG[:, 0, 2 + mt:3 + mt], scale=1.0)),
                     r=[("ps", 6 + mt), K_("glb")], w=[("T", 0, mt)])
                P.op("dve", (lambda e, mt=mt: e.tensor_tensor(out=YTB[:, 4 + mt, :], in0=YG[:, mt, :], in1=T[:, mt, :], op=ALU.mult)),
                     r=[("YG", mt), ("T", 0, mt)], w=[("YTB", 4 + mt)])

        R0 = 1800

        def rw_setup():
            RC = am.alloc([1, 32], F32)
            MUL = am.alloc([1, 2], F32)
            LW = am.alloc([1, 256], BF16)
            HST = am.alloc([2, 64], F32)
            HB0 = am.alloc([2, 64], BF16)
            PREV = am.alloc([1, 8], F32)
            mark_ = am.off
            LWF = am.alloc([1, 256], F32)
            am.off = mark_

            def K_(n):
                return ("rwc", n)
            mu = W["r_mu"][l]
            for qi in range(3):
                small_dma(RC[:, 0, qi * 2:(qi + 1) * 2], mu[qi * 256:(qi + 1) * 256].rearrange("(pr p) -> p pr", p=128), K_("mu"))
            small_dma(MUL[:, 0, 0:1], mu[768:896].rearrange("(p o) -> p o", o=1), K_("mul"))
            for nm, c0 in (("r_w0", 12), ("r_a0", 14), ("r_k_k", 16), ("r_k_a", 18), ("r_gn_w", 24), ("r_gn_b", 26)):
                small_dma(RC[:, 0, c0:c0 + 2], W[nm][l].rearrange("(pr p) -> p pr", p=128), K_("cols"))
            small_dma(RC[:, 0, 22:24], W["r_r_k"][l].rearrange("(pr hh) p -> (hh p) pr", hh=2), K_("cols"))
            P.op("dve", lambda e: e.tensor_scalar(out=RC[:, 0, 6:12], in0=RC[:, 0, 0:6], scalar1=-1.0, scalar2=1.0,
                                                  op0=ALU.mult, op1=ALU.add), r=[K_("mu")], w=[K_("omu")])
            P.op("dve", lambda e: e.tensor_scalar(out=RC[:, 0, 20:22], in0=RC[:, 0, 18:20], scalar1=-1.0, scalar2=1.0,
                                                  op0=ALU.mult, op1=ALU.add), r=[K_("cols")], w=[K_("omka")])
            P.op("dve", lambda e: e.tensor_scalar(out=MUL[:, 0, 1:2], in0=MUL[:, 0, 0:1], scalar1=-1.0, scalar2=1.0,
                                                  op0=ALU.mult, op1=ALU.add), r=[K_("mul")], w=[K_("omul")])
            small_dma(LWF[0:32, 0, :], W["r_w2"][l], K_("lwf"))
            small_dma(LWF[32:64, 0, :], W["r_a2"][l], K_("lwf"))
            small_dma(LWF[64:128, 0, :], W["r_g2"][l], K_("lwf"))
            P.op("pool", lambda e: e.tensor_copy(out=LW[:, 0, :], in_=LWF[:, 0, :]), r=[K_("lwf")], w=[K_("lw")])
            P.op("pool", lambda e: e.memset(HST[:], 0.0), w=["HST"])
            P.op("pool", lambda e: e.memset(HB0[:], 0.0), w=["HB0"])
            P.op("pool", lambda e: e.memset(PREV[:], 0.0), w=["PREV"])
            return dict(RC=RC, MUL=MUL, LW=LW, HST=HST, HB0=HB0, PREV=PREV, K_=K_)

        def rw_block(tb, CR):
            am.off = base_off
            K_ = CR["K_"]
            RC, MUL, LW, HST, HB0, PREV = CR["RC"], CR["MUL"], CR["LW"], CR["HST"], CR["HB0"], CR["PREV"]
            Fb = am.alloc([10, 512], F32)
            RAW = am.alloc([1, 516], F32)
            LRAW = am.alloc([1, 516], F32)
            FL = am.alloc([1, 512], F32)
            LB16 = am.alloc([1, 512], BF16)
            LK = am.alloc([8, 128], BF16)
            RB = am.alloc([8, 128], BF16)
            AH = am.alloc([8, 128], BF16)
            VB = am.alloc([1, 512], BF16)
            AB1 = am.alloc([4, 128], BF16)
            AB2 = am.alloc([4, 128], BF16)
            AB3 = am.alloc([4, 64], BF16)
            TM = am.alloc([4, 256], BF16)
            Zr = am.alloc([2, 4, 128], BF16)
            PPr = am.alloc([2, 4, 128], BF16)
            G0TS = am.alloc([8, 64], BF16)
            HINC = am.alloc([8, 64], F32)
            QTS = am.alloc([8, 64], BF16)
            Y0TS = am.alloc([1, 512], F32)
            PTc = am.alloc([1, 8], F32)
            HBs = am.alloc([9, 64], BF16)
            TMPH = am.alloc([1, 64], F32)

            def F(i):
                return Fb[:, i, :]

            def fk(i):
                return ("F", i)

            def col(c):
                return RC[:, 0, c:c + 1]
            cK = [K_("mu"), K_("omu"), K_("cols"), K_("omka")]

            def dve(fn, r, w):
                P.op("dve", fn, r=r, w=w)

            def act(fn, r, w):
                P.op("act", fn, r=r, w=w)

            def pool(fn, r, w):
                P.op("pool", fn, r=r, w=w)

            HB = (slice(0, 64), slice(64, 128))

            i, S = load_wblock(win, R0 + 768, 128)
            mm_fm(0, i, S, 128, HTB, "HTB")
            act(lambda e: e.copy(out=LRAW[:, 0, 1:513], in_=PSB[0][:, :]), [("ps", 0)], ["LRAW"])
            if tb == 0:
                pool(lambda e: e.memset(LRAW[:, 0, 0:1], 0.0), [], ["LRAW0"])
            else:
                pool(lambda e: e.tensor_copy(out=LRAW[:, 0, 0:1], in_=PREV[:, 0, 6:7]), ["PREVL"], ["LRAW0"])
            dve(lambda e: e.tensor_scalar(out=FL[:, 0, :], in0=LRAW[:, 0, 1:513], scalar1=MUL[:, 0, 1:2], scalar2=None,
                                          op0=ALU.mult), ["LRAW", K_("omul")], ["FL"])
            dve(lambda e: e.scalar_tensor_tensor(out=FL[:, 0, :], in0=LRAW[:, 0, 0:512], scalar=MUL[:, 0, 0:1], in1=FL[:, 0, :],
                                                 op0=ALU.mult, op1=ALU.add), ["LRAW", "LRAW0", "FL", K_("mul")], ["FL"])
            pool(lambda e: e.tensor_copy(out=PREV[:, 0, 6:7], in_=LRAW[:, 0, 512:513]), ["LRAW", "LRAW0"], ["PREVL"])
            act(lambda e: e.activation(out=LB16[0:32, 0, :], in_=FL[0:32, 0, :], func=AF.Tanh), ["FL"], ["LB16a"])
            act(lambda e: e.copy(out=LB16[32:64, 0, :], in_=FL[32:64, 0, :]), ["FL"], ["LB16b"])
            act(lambda e: e.activation(out=LB16[64:128, 0, :], in_=FL[64:128, 0, :], func=AF.Sigmoid), ["FL"], ["LB16c"])
            slots = [load_wblock(win, R0 + qi * 256, 256) for qi in range(3)]

            def do_pair(pr):
                ps_ = slice(pr * 128, (pr + 1) * 128)
                for qi in range(3):
                    iq, Sq = slots[qi]
                    bank = qi % 2
                    for kt in range(8):
                        P.op("pe", (lambda e, kt=kt, Sq=Sq, bank=bank: e.matmul(
                            PSB[bank][:, :], lhsT=Sq[:, kt, ps_], rhs=HTB[:, kt, :], start=(kt == 0), stop=(kt == 7))),
                            r=[("WS", iq, kt // 4), ("HTB", kt, 0)], w=[("ps", bank)], nofence=True)
                    act((lambda e, bank=bank: e.copy(out=RAW[:, 0, 1:513], in_=PSB[bank][:, :])), [("ps", bank)], ["RAW"])
                    pc = qi * 2 + pr
                    if tb == 0:
                        pool(lambda e: e.memset(RAW[:, 0, 0:1], 0.0), [], ["RAW0"])
                    else:
                        pool((lambda e, pc=pc: e.tensor_copy(out=RAW[:, 0, 0:1], in_=PREV[:, 0, pc:pc + 1])),
                             [("PREV", pc)], ["RAW0"])
                    dve((lambda e, qi=qi, pc=pc: e.tensor_scalar(out=F(qi), in0=RAW[:, 0, 1:513], scalar1=col(6 + pc),
                                                                 scalar2=None, op0=ALU.mult)), ["RAW"] + cK, [fk(qi)])
                    dve((lambda e, qi=qi, pc=pc: e.scalar_tensor_tensor(out=F(qi), in0=RAW[:, 0, 0:512], scalar=col(pc),
                                                                        in1=F(qi), op0=ALU.mult, op1=ALU.add)),
                        ["RAW", "RAW0", fk(qi)] + cK, [fk(qi)])
                    pool((lambda e, pc=pc: e.tensor_copy(out=PREV[:, 0, pc:pc + 1], in_=RAW[:, 0, 512:513])),
                         ["RAW", "RAW0"], [("PREV", pc)])
                pool(lambda e: e.tensor_copy(out=VB[:, 0, :], in_=F(2)), [fk(2)], ["VB"])
                P.op("pe", lambda e: e.matmul(PSB[2][:, :], lhsT=LW[0:32, 0, ps_], rhs=LB16[0:32, 0, :], start=True, stop=True),
                     r=[K_("lw"), "LB16a"], w=[("ps", 2)])
                act(lambda e: e.activation(out=F(3), in_=PSB[2][:, :], func=AF.Sigmoid, bias=col(12 + pr), scale=1.0),
                    [("ps", 2)] + cK, [fk(3)])
                dve(lambda e: e.tensor_scalar(out=F(3), in0=F(3), scalar1=-0.6065306597126334, scalar2=None, op0=ALU.mult),
                    [fk(3)], [fk(3)])
                P.op("pe", lambda e: e.matmul(PSB[3][:, :], lhsT=LW[32:64, 0, ps_], rhs=LB16[32:64, 0, :], start=True, stop=True),
                     r=[K_("lw"), "LB16b"], w=[("ps", 3)])
                act(lambda e: e.activation(out=F(4), in_=PSB[3][:, :], func=AF.Sigmoid, bias=col(14 + pr), scale=1.0),
                    [("ps", 3)] + cK, [fk(4)])
                P.op("pe", lambda e: e.matmul(PSB[2][:, :], lhsT=LW[64:128, 0, ps_], rhs=LB16[64:128, 0, :], start=True, stop=True),
                     r=[K_("lw"), "LB16c"], w=[("ps", 2)])
                act(lambda e: e.copy(out=F(5), in_=PSB[2][:, :]), [("ps", 2)], [fk(5)])
                dve(lambda e: e.tensor_scalar(out=F(6), in0=F(1), scalar1=col(16 + pr), scalar2=None, op0=ALU.mult),
                    [fk(1)] + cK, [fk(6)])
                pool(lambda e: e.tensor_tensor(out=F(7), in0=F(6), in1=F(6), op=ALU.mult), [fk(6)], [fk(7)])
                P.op("pe", lambda e: e.matmul(PSB[3][:, :], lhsT=OBD[:, :], rhs=F(7), start=True, stop=True),
                     r=["OBD", fk(7)], w=[("ps", 3)])
                act(lambda e: e.activation(out=F(7), in_=PSB[3][:, :], func=AF.Ln, bias=CST[:, 3:4], scale=1.0),
                    [("ps", 3), "cst"], [fk(7)])
                act(lambda e: e.activation(out=F(7), in_=F(7), func=AF.Exp, scale=-0.5), [fk(7)], [fk(7)])
                dve(lambda e: e.tensor_tensor(out=F(6), in0=F(6), in1=F(7), op=ALU.mult), [fk(6), fk(7)], [fk(6)])
                dve(lambda e: e.tensor_scalar(out=F(7), in0=F(4), scalar1=col(18 + pr), scalar2=col(20 + pr), op0=ALU.mult,
                                              op1=ALU.add), [fk(4)] + cK, [fk(7)])
                dve(lambda e: e.tensor_tensor(out=F(1), in0=F(1), in1=F(7), op=ALU.mult), [fk(1), fk(7)], [fk(1)])
                dve(lambda e: e.scalar_tensor_tensor(out=F(7), in0=F(0), scalar=col(22 + pr), in1=F(1), op0=ALU.mult,
                                                     op1=ALU.mult), [fk(0), fk(1)] + cK, [fk(7)])
                P.op("pe", lambda e: e.matmul(PSB[2][:, :], lhsT=OBD[:, :], rhs=F(7), start=True, stop=True),
                     r=["OBD", fk(7)], w=[("ps", 2)])
                dve(lambda e: e.tensor_tensor(out=F(8), in0=PSB[2][:, :], in1=F(2), op=ALU.mult), [("ps", 2), fk(2)], [fk(8)])
                dve(lambda e: e.tensor_tensor(out=F(7), in0=F(6), in1=F(4), op=ALU.mult), [fk(6), fk(4)], [fk(7)])
                for c in range(8):
                    cs = slice(c * 64, (c + 1) * 64)
                    dve((lambda e, cs=cs: e.tensor_tensor_scan(out=Fb[:, 9, cs], data0=ONESW[:, 0:64], data1=Fb[:, 3, cs],
                                                               initial=0.0, op0=ALU.mult, op1=ALU.add)),
                        [fk(3), "ONESW"], [fk(9)])
                dve(lambda e: e.tensor_tensor(out=F(3), in0=F(9), in1=F(3), op=ALU.subtract), [fk(9), fk(3)], [fk(3)])
                act(lambda e: e.activation(out=PTc[:, 0, :], in_=Fb[:, 9, 63::64], func=AF.Exp), [fk(9)], ["PTc"])
                v8c = lambda ap: ap.rearrange("p (c t) -> p c t", t=64)
                act(lambda e: e.activation(out=F(4), in_=F(9), func=AF.Exp), [fk(9), fk(4)], [fk(4)])
                dve(lambda e: e.tensor_tensor(out=RB[:, :, 64:128], in0=v8c(F(0)), in1=v8c(F(4)), op=ALU.mult),
                    [fk(0), fk(4)], ["RBr"])
                act(lambda e: e.activation(out=F(4), in_=F(3), func=AF.Exp), [fk(3), fk(4), "RBr"], [fk(4)])
                dve(lambda e: e.scalar_tensor_tensor(out=RB[:, :, 0:64], in0=v8c(F(6)), scalar=-1.0, in1=v8c(F(4)),
                                                     op0=ALU.mult, op1=ALU.mult), [fk(6), fk(4)], ["RBb"])
                act(lambda e: e.activation(out=F(4), in_=F(9), func=AF.Exp, scale=-1.0), [fk(9), fk(4), "RBb"], [fk(4)])
                dve(lambda e: e.tensor_tensor(out=LK[:, :, 0:64], in0=v8c(F(7)), in1=v8c(F(4)), op=ALU.mult),
                    [fk(7), fk(4)], ["LKa"])
                pool(lambda e: e.tensor_tensor(out=LK[:, :, 64:128], in0=v8c(F(1)), in1=v8c(F(4)), op=ALU.mult),
                     [fk(1), fk(4)], ["LKk"])
                for c in range(8):
                    cs = slice(c * 64, (c + 1) * 64)
                    act((lambda e, c=c, cs=cs: e.activation(out=Fb[:, 3, cs], in_=Fb[:, 9, cs], func=AF.Exp,
                                                            bias=Fb[:, 9, c * 64 + 63:c * 64 + 64], scale=-1.0)),
                        [fk(9), fk(3)], [fk(3)])
                dve(lambda e: e.tensor_tensor(out=AH[:, :, 0:64], in0=v8c(F(7)), in1=v8c(F(3)), op=ALU.mult),
                    [fk(7), fk(3)], ["AHa"])
                pool(lambda e: e.tensor_tensor(out=AH[:, :, 64:128], in0=v8c(F(1)), in1=v8c(F(3)), op=ALU.mult),
                     [fk(1), fk(3)], ["AHk"])

                def do_group(grp):
                    c0 = grp * 4
                    for j in range(4):
                        c = c0 + j
                        for hb in HB:
                            P.op("pe", (lambda e, c=c, j=j, hb=hb: e.matmul(PSB[0][hb, j * 128:(j + 1) * 128], lhsT=LK[hb, c, 0:64],
                                                                            rhs=RB[hb, c, :], start=True, stop=True)),
                                 r=["LKa", "RBr", "RBb"], w=[("ps", 0)])
                            P.op("pe", (lambda e, c=c, j=j, hb=hb: e.matmul(PSB[1][hb, j * 128:(j + 1) * 128], lhsT=LK[hb, c, 64:128],
                                                                            rhs=RB[hb, c, :], start=True, stop=True)),
                                 r=["LKk", "RBr", "RBb"], w=[("ps", 1)])
                            P.op("pe", (lambda e, c=c, j=j, hb=hb: e.matmul(PSB[2][hb, j * 64:(j + 1) * 64], lhsT=RB[hb, c, 0:64],
                                                                            rhs=LK[hb, c, 0:64], start=True, stop=True)),
                                 r=["LKa", "RBb"], w=[("ps", 2)])
                    v4 = lambda ap, w_: ap.rearrange("p (j x) -> p j x", x=w_)
                    m1 = MASK1[:, :].unsqueeze(1).broadcast_to([128, 4, 128])
                    dve(lambda e: e.tensor_tensor(out=AB1[:], in0=v4(PSB[0][:, :], 128), in1=m1, op=ALU.mult),
                        [("ps", 0), "MASK1"], ["AB1"])
                    dve(lambda e: e.tensor_tensor(out=AB2[:], in0=v4(PSB[1][:, :], 128), in1=m1, op=ALU.mult),
                        [("ps", 1), "MASK1"], ["AB2"])
                    dve(lambda e: e.tensor_tensor(out=AB3[:], in0=v4(PSB[2][:, 0:256], 64),
                                                  in1=MASKL[:, :].unsqueeze(1).broadcast_to([128, 4, 64]), op=ALU.mult),
                        [("ps", 2), "MASKL"], ["AB3"])
                    pt3 = PSB[3][:, :].bitcast(BF16)
                    for j in range(4):
                        c = c0 + j
                        for hb in HB:
                            srcs = (RB[hb, c, 0:64], VB[hb, 0, c * 64:(c + 1) * 64], AH[hb, c, 0:64], AH[hb, c, 64:128])
                            for q, src in enumerate(srcs):
                                P.op("pe", (lambda e, j=j, q=q, src=src, hb=hb: e.transpose(
                                    out=pt3[hb, j * 256 + q * 64:j * 256 + (q + 1) * 64], in_=src, identity=identb[hb, hb])),
                                    r=["RBb", "VB", "AHa", "AHk", "identb"], w=[("ps", 3)])
                    act(lambda e: e.copy(out=TM[:], in_=pt3.rearrange("p (j x) -> p j x", x=256)), [("ps", 3)], ["TM"])
                    for j in range(4):
                        for hb in HB:
                            P.op("pe", (lambda e, j=j, hb=hb: e.matmul(PSB[2][hb, 256 + j * 64:256 + (j + 1) * 64], lhsT=AB2[hb, j, 0:64],
                                                                       rhs=TM[hb, j, 64:128], start=True, stop=True)),
                                 r=["AB2", "TM"], w=[("ps", 2)])
                    pool(lambda e: e.tensor_copy(out=Zr[:, 0, :, 0:64], in_=TM[:, :, 0:64]), ["TM"], [("Z", 0)])
                    act(lambda e: e.copy(out=Zr[:, 0, :, 64:128], in_=v4(PSB[2][:, 256:512], 64)), [("ps", 2)], [("Z", 0)])
                    for jj in range(6):
                        zi, zo = jj % 2, (jj + 1) % 2
                        for j in range(4):
                            for hb in HB:
                                if jj == 0:
                                    Pm, PmT, pk = AB3[hb, j, :], AB1[hb, j, 0:64], ["AB3", "AB1"]
                                else:
                                    Pm, PmT = PPr[hb, jj % 2, j, 0:64], PPr[hb, jj % 2, j, 64:128]
                                    pk = [("PP", jj % 2)]
                                P.op("pe", (lambda e, j=j, PmT=PmT, zi=zi, hb=hb: e.matmul(
                                    PSB[5][hb, j * 128:(j + 1) * 128], lhsT=PmT, rhs=Zr[hb, zi, j, :], start=True, stop=True)),
                                    r=pk + [("Z", zi)], w=[("ps", 5)])
                                if jj < 5:
                                    P.op("pe", (lambda e, j=j, Pm=Pm, PmT=PmT, hb=hb: e.matmul(
                                        PSB[4][hb, j * 128:j * 128 + 64], lhsT=PmT, rhs=Pm, start=True, stop=True)),
                                        r=pk, w=[("ps", 4)])
                                    P.op("pe", (lambda e, j=j, Pm=Pm, PmT=PmT, hb=hb: e.matmul(
                                        PSB[4][hb, j * 128 + 64:(j + 1) * 128], lhsT=Pm, rhs=PmT, start=True, stop=True)),
                                        r=pk, w=[("ps", 4)])
                        dve((lambda e, zi=zi, zo=zo: e.tensor_tensor(out=Zr[:, zo], in0=Zr[:, zi],
                                                                   in1=v4(PSB[5][:, :], 128), op=ALU.add)),
                            [("Z", zi), ("ps", 5)], [("Z", zo)])
                        if jj < 5:
                            act((lambda e, jj=jj: e.copy(out=PPr[:, (jj + 1) % 2], in_=v4(PSB[4][:, :], 128))),
                                [("ps", 4)], [("PP", (jj + 1) % 2)])
                    for j in range(4):
                        for hb in HB:
                            ZF = Zr[hb, 0]
                            P.op("pe", (lambda e, j=j, hb=hb, ZF=ZF: e.matmul(PSB[6][hb, j * 64:(j + 1) * 64], lhsT=ZF[:, j, 0:64],
                                                                              rhs=TM[hb, j, 128:192], start=True, stop=True)),
                                 r=[("Z", 0), "TM"], w=[("ps", 6)])
                            P.op("pe", (lambda e, j=j, hb=hb, ZF=ZF: e.matmul(PSB[6][hb, 256 + j * 64:256 + (j + 1) * 64], lhsT=TM[hb, j, 128:192],
                                                                              rhs=ZF[:, j, 64:128], start=True, stop=False)),
                                 r=[("Z", 0), "TM"], w=[("ps", 6)])
                            P.op("pe", (lambda e, j=j, hb=hb: e.matmul(PSB[6][hb, 256 + j * 64:256 + (j + 1) * 64], lhsT=TM[hb, j, 192:256],
                                                                       rhs=TM[hb, j, 64:128], start=False, stop=True)),
                                 r=["TM"], w=[("ps", 6)])
                            P.op("pe", (lambda e, j=j, hb=hb, ZF=ZF: e.matmul(PSB[7][hb, j * 64:(j + 1) * 64], lhsT=ZF[:, j, 0:64],
                                                                              rhs=AB1[hb, j, 64:128], start=True, stop=True)),
                                 r=[("Z", 0), "AB1"], w=[("ps", 7)])
                            P.op("pe", (lambda e, j=j, hb=hb, ZF=ZF: e.matmul(PSB[7][hb, 256 + j * 64:256 + (j + 1) * 64], lhsT=ZF[:, j, 64:128],
                                                                              rhs=AB1[hb, j, 64:128], start=True, stop=False)),
                                 r=[("Z", 0), "AB1"], w=[("ps", 7)])
                            P.op("pe", (lambda e, j=j, hb=hb: e.matmul(PSB[7][hb, 256 + j * 64:256 + (j + 1) * 64], lhsT=TM[hb, j, 64:128],
                                                                       rhs=AB2[hb, j, 64:128], start=False, stop=True)),
                                 r=["TM", "AB2"], w=[("ps", 7)])
                    p6a, p6b = v4(PSB[6][:, 0:256], 64), v4(PSB[6][:, 256:512], 64)
                    p7a, p7b = v4(PSB[7][:, 0:256], 64), v4(PSB[7][:, 256:512], 64)
                    act(lambda e: e.copy(out=G0TS[:, c0:c0 + 4, :], in_=p6a), [("ps", 6)], [("G0TS", grp)])
                    act(lambda e: e.copy(out=HINC[:, c0:c0 + 4, :], in_=p6b), [("ps", 6)], [("HINC", grp)])
                    dve(lambda e: e.tensor_tensor(out=QTS[:, c0:c0 + 4, :], in0=p7a, in1=RB[:, c0:c0 + 4, 64:128],
                                                  op=ALU.add), [("ps", 7), "RBr"], [("QTS", grp)])
                    dve(lambda e: e.tensor_copy(out=v8c(Y0TS[:, 0, :])[:, c0:c0 + 4, :], in_=p7b), [("ps", 7)],
                        [("Y0TS", grp)])
                for grp_ in range(2):
                    do_group(grp_)
                pool(lambda e: e.tensor_copy(out=HBs[:, 0, :], in_=HB0[:, pr, :]), ["HB0"], [("HBs", 0)])
                for c in range(8):
                    grp = c // 4
                    for hb in HB:
                        P.op("pe", (lambda e, c=c, hb=hb: e.matmul(PSB[2][hb, 0:64], lhsT=G0TS[hb, c, :], rhs=HBs[hb, c, :],
                                                                   start=True, stop=True)),
                             r=[("G0TS", grp), ("HBs", c)], w=[("ps", 2)])
                    dve((lambda e, c=c: e.scalar_tensor_tensor(out=TMPH[:, 0, :], in0=HST[:, pr, :], scalar=PTc[:, 0, c:c + 1],
                                                               in1=HINC[:, c, :], op0=ALU.mult, op1=ALU.add)),
                        ["HST", "PTc", ("HINC", grp)], ["TMPH"])
                    dve(lambda e: e.tensor_tensor(out=HST[:, pr, :], in0=TMPH[:, 0, :], in1=PSB[2][:, 0:64], op=ALU.add),
                        ["TMPH", ("ps", 2)], ["HST"])
                    act((lambda e, c=c: e.copy(out=HBs[:, c + 1, :], in_=HST[:, pr, :])), ["HST"], [("HBs", c + 1)])
                pool(lambda e: e.tensor_copy(out=HB0[:, pr, :], in_=HBs[:, 8, :]), [("HBs", 8)], ["HB0"])
                for c in range(8):
                    for hb in HB:
                        P.op("pe", (lambda e, c=c, hb=hb: e.matmul(PSB[3][hb, c * 64:(c + 1) * 64], lhsT=HBs[hb, c, :], rhs=QTS[hb, c, :],
                                                                   start=True, stop=True)),
                             r=[("HBs", c), ("QTS", c // 4)], w=[("ps", 3)])
                dve(lambda e: e.tensor_tensor(out=F(0), in0=PSB[3][:, :], in1=Y0TS[:, 0, :], op=ALU.add),
                    [("ps", 3), ("Y0TS", 0), ("Y0TS", 1), fk(0), "RBr"], [fk(0)])
                P.op("pe", lambda e: e.matmul(PSB[0][:, :], lhsT=O64BD[:, :], rhs=F(0), start=True, stop=True),
                     r=["O64BD", fk(0)], w=[("ps", 0)])
                dve(lambda e: e.tensor_tensor(out=F(0), in0=F(0), in1=PSB[0][:, :], op=ALU.subtract), [fk(0), ("ps", 0)], [fk(0)])
                pool(lambda e: e.tensor_tensor(out=F(4), in0=F(0), in1=F(0), op=ALU.mult), [fk(0), fk(4), "LKa", "LKk"], [fk(4)])
                P.op("pe", lambda e: e.matmul(PSB[1][:, :], lhsT=O64BD[:, :], rhs=F(4), start=True, stop=True),
                     r=["O64BD", fk(4)], w=[("ps", 1)])
                act(lambda e: e.activation(out=F(4), in_=PSB[1][:, :], func=AF.Ln, bias=CST[:, 4:5], scale=1.0),
                    [("ps", 1), "cst"], [fk(4)])
                act(lambda e: e.activation(out=F(4), in_=F(4), func=AF.Exp, scale=-0.5), [fk(4)], [fk(4)])
                dve(lambda e: e.tensor_tensor(out=F(0), in0=F(0), in1=F(4), op=ALU.mult), [fk(0), fk(4)], [fk(0)])
                dve(lambda e: e.tensor_scalar(out=F(0), in0=F(0), scalar1=col(24 + pr), scalar2=col(26 + pr), op0=ALU.mult,
                                              op1=ALU.add), [fk(0)] + cK, [fk(0)])
                dve(lambda e: e.tensor_tensor(out=F(0), in0=F(0), in1=F(8), op=ALU.add), [fk(0), fk(8)], [fk(0)])
                dve(lambda e: e.tensor_tensor(out=YTB[:, 6 + pr, :], in0=F(0), in1=F(5), op=ALU.mult), [fk(0), fk(5)], [("YTB", 6 + pr)])
            for pr_ in range(2):
                do_pair(pr_)

        def outproj(t0):
            oslots = {0: load_wblock(wout, 0, 256)}
            for db in range(4):
                if db + 1 < 4:
                    oslots[db + 1] = load_wblock(wout, (db + 1) * 256, 256)
                i, S = oslots[db]
                for di in range(2):
                    dt_ = db * 2 + di
                    bank = 4 + di
                    for kt in range(8):
                        P.op("pe", (lambda e, kt=kt, di=di, bank=bank, S=S: e.matmul(
                            PSB[bank][:, :], lhsT=S[:, kt, di * 128:(di + 1) * 128], rhs=YTB[:, kt, :],
                            start=(kt == 0), stop=(kt == 7))),
                            r=[("WS", i, kt // 4), ("YTB", kt)], w=[("ps", bank)])
                    P.op("dve", (lambda e, dt_=dt_, bank=bank: e.tensor_tensor(
                        out=XT[:, dt_, t0:t0 + 512], in0=XT[:, dt_, t0:t0 + 512], in1=PSB[bank][:, :], op=ALU.add)),
                        r=[("ps", bank), ("XT", t0 // 512)], w=[("XT", t0 // 512)])

        if "s5" in mixers:
            C5 = s5_setup()
            barrier()
        CR = rw_setup() if "rwkv" in mixers else None
        barrier()
        base_off = am.off
        for tb in range(4):
            t0 = tb * 512
            norm_to(HTB, "HTB", t0, g, 0)
            if "ssd" in mixers:
                ssd_block(tb)
            if "s5" in mixers:
                P.set_fence()
                s5_block(tb, C5)
                P.set_fence()
            if "rwkv" in mixers:
                P.set_fence()
                rw_block(tb, CR)
                P.set_fence()
            if dbg:
                dst = ydbg[s, l].rearrange("(kt p) t -> p kt t", p=128)[:, :, t0:t0 + 512]
                P.dma("sp", (lambda e, dst=dst: e.dma_start(out=dst, in_=YTB[:])),
                      r=[("YTB", k_) for k_ in range(8)], sem="D_ydbg")
            outproj(t0)

    for s in range(NS):
        barrier()
        load_x(s)
        barrier()
        for l in range(NL):
            if do_ffn:
                ffn(l, "ffn1")
            if mixers:
                barrier()
                mixer_phase(s, l)
                barrier()
            if do_ffn:
                ffn(l, "ffn2")
        barrier()
        final_store(s)
    last = {}
    for s_, v in out_toks:
        last[s_] = max(last.get(s_, 0), v)
    P.final_wait("sp", list(last.items()))
    assert ARMAX[0] <= AR_WORDS, ("arena overflow: need words", ARMAX[0])
    P.emit(stack)
    stack.close()
    return nc


L_ = 2
WEIGHT_SHAPES = [
    ("ffn1_norm", (L_, 1024)), ("ffn1_wg", (L_, 1024, 2816)), ("ffn1_wu", (L_, 1024, 2816)),
    ("ffn1_wd", (L_, 2816, 1024)), ("mix_norm", (L_, 1024)), ("w_in", (L_, 1024, 2696)),
    ("w_out", (L_, 1024, 1024)), ("m_A_log", (L_, 8)), ("m_dt_bias", (L_, 8)),
    ("m_conv_w", (L_, 1024, 4)), ("m_conv_b", (L_, 1024)), ("m_D", (L_, 8)),
    ("m_norm_w", (L_, 512)), ("s_A_re", (L_, 16, 64)), ("s_A_im", (L_, 16, 64)),
    ("s_B_re", (L_, 16, 64, 16)), ("s_B_im", (L_, 16, 64, 16)), ("s_C_re", (L_, 16, 16, 64)),
    ("s_C_im", (L_, 16, 16, 64)), ("s_log_dt", (L_, 16)), ("s_D", (L_, 256)),
    ("s_glu_w", (L_, 256, 256)), ("s_glu_b", (L_, 256)), ("r_mu", (L_, 896)),
    ("r_w0", (L_, 256)), ("r_w2", (L_, 32, 256)), ("r_a0", (L_, 256)), ("r_a2", (L_, 32, 256)),
    ("r_g2", (L_, 64, 256)), ("r_k_k", (L_, 256)), ("r_k_a", (L_, 256)), ("r_r_k", (L_, 4, 64)),
    ("r_gn_w", (L_, 256)), ("r_gn_b", (L_, 256)), ("ffn2_norm", (L_, 1024)),
    ("ffn2_wg", (L_, 1024, 2816)), ("ffn2_wu", (L_, 1024, 2816)), ("ffn2_wd", (L_, 2816, 1024)),
    ("final_norm", (1024,)),
]

_CFG = {"nseq": 2, "nlayers": 2, "mix": True, "ffn": True}


def kernel(**inputs):
    n = 8
    nc = bass.Bass("TRN2", target_bir_lowering=False)
    build_program(nc, _CFG)
    x = np.ascontiguousarray(np.asarray(inputs["x"], dtype=np.float32))
    wts = {nm: np.ascontiguousarray(np.asarray(inputs[nm], dtype=np.float32)) for nm, _ in WEIGHT_SHAPES}
    in_maps = []
    for c in range(n):
        m = {"x": x[2 * c:2 * c + 2]}
        m.update(wts)
        in_maps.append(m)
    res = run_bass_kernel_spmd(nc, in_maps, core_ids=list(range(n)))
    return np.concatenate([r["out"] for r in res.results], axis=0)
```

```python
import numpy as np
import concourse.bass as bass
import concourse.mybir as mybir
from concourse.bass_utils import run_bass_kernel_spmd

F32 = mybir.dt.float32
BF16 = mybir.dt.bfloat16
I32 = mybir.dt.int32
ALU = mybir.AluOpType
AF = mybir.ActivationFunctionType
AX = mybir.AxisListType

D = 1024
SEQ = 2048
DFF = 2816
NFT = DFF // 128
DIN = 2696
EPS = 1e-5

ENGS = ("pe", "dve", "act", "pool", "sp")
ROLL = 12000
SELF_SYNC = {"pe": False, "dve": True, "act": True, "pool": True, "sp": True}


class Prog:
    def __init__(self, nc):
        self.nc = nc
        self.ops = {e: [] for e in ENGS}
        self.cnt = {e: 0 for e in ENGS}
        self.gen = {e: 0 for e in ENGS}
        self.seen = {e: {} for e in ENGS}
        self.lastw = {}
        self.readers = {}
        self.dmacnt = {}
        self.semnames = []
        self.fence = {}
        self.fence_done = {e: 0 for e in ENGS}
        self.fence_id = 0

    def _semname(self, n):
        if n not in self.semnames:
            self.semnames.append(n)
        return n

    def _deps(self, eng, reads, writes):
        deps = {}

        def add(tok):
            if tok is None:
                return
            s, v = tok
            if deps.get(s, 0) < v:
                deps[s] = v

        for k in reads:
            add(self.lastw.get(k))
            if isinstance(k, tuple) and k[0] == "ps":
                for tok in self.readers.get(k, ()):
                    if not tok[0].startswith("E_%s_" % eng):
                        add(tok)
        for k in writes:
            add(self.lastw.get(k))
            for tok in self.readers.get(k, ()):
                add(tok)
        waits = []
        seen = self.seen[eng]
        for s, v in deps.items():
            if seen.get(s, 0) >= v:
                continue
            if s.startswith("E_%s_" % eng) and not SELF_SYNC[eng]:
                continue
            seen[s] = v
            waits.append((s, v))
        return waits

    def set_fence(self):
        toks = {}
        for en in ENGS:
            if self.cnt[en] > 0:
                toks["E_%s_%d" % (en, self.gen[en])] = self.cnt[en]
        for sname, v in self.dmacnt.items():
            toks[sname] = v
        self.fence = toks
        self.fence_id += 1

    def _fence_waits(self, eng):
        if self.fence_done[eng] == self.fence_id:
            return []
        self.fence_done[eng] = self.fence_id
        out = []
        seen = self.seen[eng]
        for s, v in self.fence.items():
            if seen.get(s, 0) >= v:
                continue
            if s.startswith("E_%s_" % eng):
                continue
            seen[s] = v
            out.append((s, v))
        return out

    def op(self, eng, fn, r=(), w=(), nofence=False):
        waits = self._deps(eng, r, w)
        if not nofence:
            waits = self._fence_waits(eng) + waits
        if self.cnt[eng] >= ROLL:
            self.gen[eng] += 1
            self.cnt[eng] = 0
        self.cnt[eng] += 1
        sname = self._semname("E_%s_%d" % (eng, self.gen[eng]))
        tok = (sname, self.cnt[eng])
        self.ops[eng].append((waits, fn, sname, 1))
        for k in w:
            self.lastw[k] = tok
            self.readers[k] = []
        for k in r:
            self.readers.setdefault(k, []).append(tok)
        return tok

    def dma(self, q, fn, r=(), w=(), sem=None, nofence=False):
        waits = self._deps(q, r, w)
        if not nofence:
            waits = self._fence_waits(q) + waits
        if sem is None:
            sem = "D_" + str(w[0] if w else r[0])
        sname = self._semname(sem)
        self.dmacnt[sname] = self.dmacnt.get(sname, 0) + 16
        tok = (sname, self.dmacnt[sname])
        self.ops[q].append((waits, fn, sname, 16))
        for k in w:
            self.lastw[k] = tok
            self.readers[k] = []
        for k in r:
            self.readers.setdefault(k, []).append(tok)
        return tok

    def final_wait(self, eng, toks):
        waits = []
        for s, v in toks:
            waits.append((s, v))
        self.ops[eng].append((waits, None, None, 0))

    def emit(self, stack):
        nc = self.nc
        sems = {}
        for n in self.semnames:
            sems[n] = stack.enter_context(nc.semaphore(n))
        block = stack.enter_context(nc.Block())
        engmap = {"pe": block.tensor, "dve": block.vector, "act": block.scalar,
                  "pool": block.gpsimd, "sp": block.sync}

        def mk(elist):
            def body(e):
                for waits, fn, sname, inc in elist:
                    for s, v in waits:
                        e.wait_ge(sems[s], v)
                    if fn is not None:
                        ins = fn(e)
                        ins.then_inc(sems[sname], inc)
            return body

        for en in ENGS:
            if self.ops[en]:
                engmap[en](mk(self.ops[en]))


def build_program(nc, cfg):
    from contextlib import ExitStack
    NS = cfg.get("nseq", 2)
    NL = cfg.get("nlayers", 2)
    do_ffn = cfg.get("ffn", True)
    mixers = cfg.get("mixers", ("ssd", "s5", "rwkv"))
    dbg = cfg.get("dbg", False)

    def din(name, shape):
        return nc.dram_tensor(name, list(shape), F32, kind="ExternalInput").ap()

    x_d = din("x", [NS, SEQ, D])
    W = {}
    for nm, shp in WEIGHT_SHAPES:
        W[nm] = din(nm, shp)
    out_d = nc.dram_tensor("out", [NS, SEQ, D], F32, kind="ExternalOutput").ap()
    if dbg:
        ydbg = nc.dram_tensor("ydbg", [NS, NL, D, SEQ], BF16, kind="ExternalOutput").ap()

    P = Prog(nc)
    stack = ExitStack()

    def sb(name, shape, dt):
        return stack.enter_context(nc.sbuf_tensor(name, list(shape), dt))

    def ps(name, shape, dt):
        return stack.enter_context(nc.psum_tensor(name, list(shape), dt))

    XT = sb("XT", [128, 8, SEQ], F32)
    ident = sb("ident", [128, 128], F32)
    identb = sb("identb", [128, 128], BF16)
    onesf = sb("onesf", [128, 128], F32)
    onesb = sb("onesb", [128, 128], BF16)
    gains = sb("gains", [128, 3 * NL + 1, 8], F32)
    AR_WORDS = 24576
    ARENA = sb("ARENA", [128, AR_WORDS], F32)
    WGU = sb("WGU", [128, 2, 2, 8, 256], BF16)
    STG = sb("STG", [128, 4, 4, 256], F32)
    SQ = sb("SQ", [128, 2, 512], BF16)
    RSTD = sb("RSTD", [128, 512], F32)
    SG = sb("SG", [128, 2, 512], F32)
    CST = sb("CST", [128, 8], F32)
    PSB = [ps("psb%d" % i, [128, 512], F32) for i in range(8)]

    ARMAX = [0]
    cfg['_armax'] = ARMAX

    class Arena:
        def __init__(self):
            self.off = 0

        def alloc(self, shape, dt):
            n = 1
            for d_ in shape:
                n *= d_
            words = (n * (2 if dt == BF16 else 4) + 3) // 4
            words = (words + 7) // 8 * 8
            ARMAX[0] = max(ARMAX[0], self.off + words)
            o_ = self.off if self.off + words <= AR_WORDS else 0
            a = ARENA[:, o_:o_ + words]
            self.off += words
            if dt == BF16:
                a = a.bitcast(BF16)
            a = a[:, 0:n]
            if len(shape) == 2:
                return a.rearrange("p (a b) -> p a b", b=shape[1])
            if len(shape) == 3:
                return a.rearrange("p (a b c) -> p a b c", b=shape[1], c=shape[2])
            return a

    ar = Arena()
    IOB = ar.alloc([2, 1024], F32)
    ar = Arena()
    HT = ar.alloc([8, 1024], BF16)
    ATt = ar.alloc([NFT, 1024], BF16)
    WDb = ar.alloc([2, NFT, 256], BF16)

    P.op("pool", lambda e: e.memset(onesf[:], 1.0), w=["onesf"])
    P.op("pool", lambda e: e.memset(onesb[:], 1.0), w=["onesb"])
    P.op("pool", lambda e: e.memset(CST[:, 0:1], EPS), w=["cst"])
    P.op("pool", lambda e: e.memset(CST[:, 1:2], 1.0), w=["cst"])
    P.op("pool", lambda e: e.memset(CST[:, 2:3], 0.0), w=["cst"])
    P.op("pool", lambda e: e.affine_select(out=ident[:], in_=onesf[:], pattern=[[1, 128]],
                                            compare_op=ALU.is_equal, fill=0.0, base=0,
                                            channel_multiplier=-1),
         r=["onesf"], w=["ident"])
    P.op("pool", lambda e: e.tensor_copy(out=identb[:], in_=ident[:]), r=["ident"], w=["identb"])

    def small_dma(dst, src, key):
        P.dma("sp", (lambda e, dst=dst, src=src: e.dma_start(out=dst, in_=src, allow_slow_non_contiguous=True)),
              w=[key])

    gi = 0
    gidx = {}
    for l in range(NL):
        for nm in ("ffn1_norm", "mix_norm", "ffn2_norm"):
            small_dma(gains[:, gi, :], W[nm][l].rearrange("(kt p) -> p kt", p=128), ("gains", gi))
            gidx[(nm, l)] = gi
            gi += 1
    small_dma(gains[:, gi, :], W["final_norm"].rearrange("(kt p) -> p kt", p=128), ("gains", gi))
    gidx["final"] = gi

    def load_x(s):
        for tb in range(16):
            b = tb % 2
            src = x_d[s, tb * 128:(tb + 1) * 128, :]
            P.dma("sp", (lambda e, b=b, src=src: e.dma_start(out=IOB[:, b, :], in_=src)),
                  w=[("IOB", b)])
            for half in range(2):
                bank = 6 + half
                for j in range(4):
                    kt = half * 4 + j
                    P.op("pe", (lambda e, b=b, kt=kt, j=j, bank=bank: e.transpose(
                        out=PSB[bank][:, j * 128:(j + 1) * 128],
                        in_=IOB[:, b, kt * 128:(kt + 1) * 128], identity=ident[:])),
                        r=[("IOB", b), "ident"], w=[("ps", bank)])
                dst = XT[:, half * 4:(half + 1) * 4, tb * 128:(tb + 1) * 128]
                srcp = PSB[bank][:, :].rearrange("p (a b) -> p a b", b=128)
                if half == 0:
                    P.op("dve", (lambda e, dst=dst, srcp=srcp: e.tensor_copy(out=dst, in_=srcp)),
                         r=[("ps", bank)], w=[("XT", tb // 4)])
                else:
                    P.op("act", (lambda e, dst=dst, srcp=srcp: e.copy(out=dst, in_=srcp)),
                         r=[("ps", bank)], w=[("XT", tb // 4)])

    def rms_stats(tok0, bank=6):
        for kt in range(8):
            b = kt % 2
            P.op("act", (lambda e, b=b, kt=kt: e.activation(
                out=SQ[:, b, :], in_=XT[:, kt, tok0:tok0 + 512], func=AF.Square)),
                r=[("XT", tok0 // 512)], w=[("SQ", b)])
            P.op("pe", (lambda e, b=b, kt=kt: e.matmul(
                PSB[bank][:, :], lhsT=onesb[:], rhs=SQ[:, b, :], start=(kt == 0), stop=(kt == 7))),
                r=[("SQ", b), "onesb"], w=[("ps", bank)])
        P.op("act", (lambda e: e.activation(out=RSTD[:], in_=PSB[bank][:, :], func=AF.Ln,
                                            bias=CST[:, 0:1], scale=1.0 / D)),
             r=[("ps", bank), "cst"], w=["RSTD"])
        P.op("act", (lambda e: e.activation(out=RSTD[:], in_=RSTD[:], func=AF.Exp, scale=-0.5)),
             r=["RSTD"], w=["RSTD"])

    def norm_to(dst, dkey, tok0, g, hoff):
        rms_stats(tok0)
        for kt in range(8):
            P.op("dve", (lambda e, kt=kt: e.scalar_tensor_tensor(
                out=dst[:, kt, hoff:hoff + 512], in0=XT[:, kt, tok0:tok0 + 512],
                scalar=gains[:, g, kt:kt + 1], in1=RSTD[:], op0=ALU.mult, op1=ALU.mult)),
                r=[("XT", tok0 // 512), "RSTD", ("gains", g)], w=[(dkey, kt, hoff // 512)])

    wcount = {"gu": 0, "d": 0, "stg": 0, "slot": 0}

    def wload(dst, src, key, nofence=False, ceng="pool"):
        a = src.shape[1]
        wd_ = src.shape[2]
        sbuf_i = wcount["stg"] % 4
        wcount["stg"] += 1
        P.dma("sp", (lambda e: e.dma_start(out=STG[:, sbuf_i, 0:a, 0:wd_], in_=src)),
              w=[("STG", sbuf_i)], nofence=nofence)
        if ceng == "act":
            P.op("act", (lambda e: e.copy(out=dst, in_=STG[:, sbuf_i, 0:a, 0:wd_])),
                 r=[("STG", sbuf_i)], w=[key], nofence=nofence)
        else:
            P.op("pool", (lambda e: e.tensor_copy(out=dst, in_=STG[:, sbuf_i, 0:a, 0:wd_])),
                 r=[("STG", sbuf_i)], w=[key], nofence=nofence)

    def run_pipelined(gens, depth=2):
        active = []
        it = iter(gens)
        while True:
            while len(active) < depth:
                try:
                    active.append(next(it))
                except StopIteration:
                    break
            if not active:
                break
            for g_ in list(active):
                try:
                    next(g_)
                except StopIteration:
                    active.remove(g_)

    def barrier():
        toks = []
        for en in ENGS:
            if P.cnt[en] > 0:
                toks.append(("E_%s_%d" % (en, P.gen[en]), P.cnt[en]))
        for sname, v in P.dmacnt.items():
            toks.append((sname, v))
        for en in ENGS:
            w_ = []
            for s_, v in toks:
                if P.seen[en].get(s_, 0) < v:
                    P.seen[en][s_] = v
                    w_.append((s_, v))
            if w_:
                P.ops[en].append((w_, None, None, 0))

    def ffn(l, which):
        wg = W[which + "_wg"][l].rearrange("(kt p) f -> p kt f", p=128)
        wu = W[which + "_wu"][l].rearrange("(kt p) f -> p kt f", p=128)
        wd = W[which + "_wd"][l].rearrange("(ft p) d -> p ft d", p=128)
        g = gidx[(which + "_norm", l)]
        for tt in range(SEQ // 1024):
            t0 = tt * 1024
            for half in range(2):
                norm_to(HT, "HT", t0 + half * 512, g, half * 512)
            for fb in range(NFT // 2):
                wb = wcount["gu"] % 2
                wcount["gu"] += 1
                f0 = fb * 256
                for gu, wsrc in ((0, wg), (1, wu)):
                    for kh in range(2):
                        wload(WGU[:, wb, gu, kh * 4:(kh + 1) * 4, :], wsrc[:, kh * 4:(kh + 1) * 4, f0:f0 + 256],
                              ("WGU", wb, gu, kh))
                for fi in range(2):
                    ft = fb * 2 + fi
                    for half in range(2):
                        bg, bu = half * 2, half * 2 + 1
                        for gu, bank in ((0, bg), (1, bu)):
                            for kt in range(8):
                                P.op("pe", (lambda e, wb=wb, gu=gu, kt=kt, fi=fi, half=half, bank=bank: e.matmul(
                                    PSB[bank][:, :], lhsT=WGU[:, wb, gu, kt, fi * 128:(fi + 1) * 128],
                                    rhs=HT[:, kt, half * 512:(half + 1) * 512],
                                    start=(kt == 0), stop=(kt == 7))),
                                    r=[("WGU", wb, gu, kt // 4), ("HT", kt, half)], w=[("ps", bank)])
                        P.op("act", (lambda e, half=half, bg=bg: e.activation(
                            out=SG[:, half, :], in_=PSB[bg][:, :], func=AF.Silu)),
                            r=[("ps", bg)], w=[("SG", half)])
                        P.op("dve", (lambda e, half=half, bu=bu, ft=ft: e.tensor_tensor(
                            out=ATt[:, ft, half * 512:(half + 1) * 512], in0=SG[:, half, :],
                            in1=PSB[bu][:, :], op=ALU.mult)),
                            r=[("SG", half), ("ps", bu)], w=[("AT", ft, half)])
            for db in range(4):
                wb = wcount["d"] % 2
                wcount["d"] += 1
                d0 = db * 256
                for c4 in range(6):
                    a0, a1 = c4 * 4, min(c4 * 4 + 4, NFT)
                    wload(WDb[:, wb, a0:a1, :], wd[:, a0:a1, d0:d0 + 256], ("WD", wb, c4))
                for di in range(2):
                    dt_ = db * 2 + di
                    for half in range(2):
                        bank = 4 + half
                        for ft in range(NFT):
                            P.op("pe", (lambda e, wb=wb, ft=ft, di=di, half=half, bank=bank: e.matmul(
                                PSB[bank][:, :], lhsT=WDb[:, wb, ft, di * 128:(di + 1) * 128],
                                rhs=ATt[:, ft, half * 512:(half + 1) * 512],
                                start=(ft == 0), stop=(ft == NFT - 1))),
                                r=[("WD", wb, ft // 4), ("AT", ft, half)], w=[("ps", bank)])
                        tk = t0 + half * 512
                        P.op("dve", (lambda e, dt_=dt_, tk=tk, bank=bank: e.scalar_tensor_tensor(
                            out=XT[:, dt_, tk:tk + 512], in0=PSB[bank][:, :], scalar=0.5,
                            in1=XT[:, dt_, tk:tk + 512], op0=ALU.mult, op1=ALU.add)),
                            r=[("ps", bank), ("XT", tk // 512)], w=[("XT", tk // 512)])

    out_toks = []

    def final_store(s):
        g = gidx["final"]
        for q in range(4):
            tok0 = q * 512
            rms_stats(tok0)
            for kt in range(8):
                P.op("dve", (lambda e, kt=kt, tok0=tok0: e.scalar_tensor_tensor(
                    out=XT[:, kt, tok0:tok0 + 512], in0=XT[:, kt, tok0:tok0 + 512],
                    scalar=gains[:, g, kt:kt + 1], in1=RSTD[:], op0=ALU.mult, op1=ALU.mult)),
                    r=[("XT", q), "RSTD", ("gains", g)], w=[("XT", q)])
            for tb4 in range(4):
                tb = q * 4 + tb4
                b = tb % 2
                for half in range(2):
                    bank = 6 + half
                    for j in range(4):
                        kt = half * 4 + j
                        P.op("pe", (lambda e, kt=kt, j=j, tb=tb, bank=bank: e.transpose(
                            out=PSB[bank][:, j * 128:(j + 1) * 128],
                            in_=XT[:, kt, tb * 128:(tb + 1) * 128], identity=ident[:])),
                            r=[("XT", q), "ident"], w=[("ps", bank)])
                    if half == 0:
                        P.op("dve", (lambda e, b=b, bank=bank: e.tensor_copy(
                            out=IOB[:, b, 0:512], in_=PSB[bank][:, :])),
                            r=[("ps", bank)], w=[("IOB", b)])
                    else:
                        P.op("act", (lambda e, b=b, bank=bank: e.copy(
                            out=IOB[:, b, 512:1024], in_=PSB[bank][:, :])),
                            r=[("ps", bank)], w=[("IOB", b)])
                dst = out_d[s, tb * 128:(tb + 1) * 128, :]
                tok = P.dma("sp", (lambda e, b=b, dst=dst: e.dma_start(out=dst, in_=IOB[:, b, :])),
                            r=[("IOB", b)], sem="D_out%d" % b)
                out_toks.append(tok)

    CW = sb("CW", [128, NL, 8, 4], F32)
    CBs = sb("CBs", [128, NL, 8], F32)
    DTB = sb("DTB", [8, NL], F32)
    ANEG = sb("ANEG", [8, NL], F32)
    DBC = sb("DBC", [128, NL, 8], F32)
    EH = sb("EH", [8, 8], F32)
    NEH = sb("NEH", [8, 8], F32)
    NEGM = sb("NEGM", [128, 128], F32)
    SEL127 = sb("SEL127", [128, 128], F32)
    ONES8 = sb("ONES8", [8, 128], F32)
    ONESW = sb("ONESW", [128, 132], F32)
    for l in range(NL):
        small_dma(CW[:, l, :, :], W["m_conv_w"][l].rearrange("(t p) k -> p t k", p=128), ("CW", l))
        small_dma(CBs[:, l, :], W["m_conv_b"][l].rearrange("(t p) -> p t", p=128), ("CBs", l))
        small_dma(DTB[:, l:l + 1], W["m_dt_bias"][l].rearrange("(h o) -> h o", o=1), ("DTB", l))
        small_dma(ANEG[:, l:l + 1], W["m_A_log"][l].rearrange("(h o) -> h o", o=1), ("ANEG", l))
        small_dma(DBC[:, l, :], W["m_D"][l:l + 1, :].broadcast_to([128, 8]), ("DBC", l))
        P.op("act", (lambda e, l=l: e.activation(out=ANEG[:, l:l + 1], in_=ANEG[:, l:l + 1], func=AF.Exp)),
             r=[("ANEG", l)], w=[("ANEG", l)])
        P.op("dve", (lambda e, l=l: e.tensor_scalar(out=ANEG[:, l:l + 1], in0=ANEG[:, l:l + 1], scalar1=-1.0,
                                                    scalar2=None, op0=ALU.mult)),
             r=[("ANEG", l)], w=[("ANEG", l)])
    P.op("pool", lambda e: e.memset(ONES8[:], 1.0), w=["ONES8"])
    MASK1 = sb("MASK1", [128, 128], F32)
    MASKL = sb("MASKL", [128, 64], F32)
    OBD = sb("OBD", [128, 128], F32)
    O64BD = sb("O64BD", [128, 128], F32)
    P.op("pool", lambda e: e.memset(OBD[:], 0.0), w=["OBD"])
    P.op("pool", lambda e: e.memset(OBD[0:64, 0:64], 1.0), w=["OBD"])
    P.op("pool", lambda e: e.memset(OBD[64:128, 64:128], 1.0), w=["OBD"])
    P.op("pool", lambda e: e.tensor_scalar(out=O64BD[:], in0=OBD[:], scalar1=1.0 / 64, scalar2=None, op0=ALU.mult),
         r=["OBD"], w=["O64BD"])
    P.op("pool", lambda e: e.memset(CST[:, 3:4], 1e-30), w=["cst"])
    P.op("pool", lambda e: e.memset(CST[:, 4:5], 64e-5), w=["cst"])
    for hb_ in (slice(0, 64), slice(64, 128)):
        P.op("pool", (lambda e, hb_=hb_: e.affine_select(out=MASK1[hb_, 0:64], in_=onesf[hb_, 0:64], pattern=[[1, 64]],
                                                        compare_op=ALU.is_gt, fill=0.0, base=0, channel_multiplier=-1)),
             r=["onesf"], w=["MASK1"])
        P.op("pool", (lambda e, hb_=hb_: e.affine_select(out=MASK1[hb_, 64:128], in_=onesf[hb_, 0:64], pattern=[[1, 64]],
                                                        compare_op=ALU.is_ge, fill=0.0, base=0, channel_multiplier=-1)),
             r=["onesf"], w=["MASK1"])
        P.op("pool", (lambda e, hb_=hb_: e.affine_select(out=MASKL[hb_, :], in_=onesf[hb_, 0:64], pattern=[[-1, 64]],
                                                        compare_op=ALU.is_gt, fill=0.0, base=0, channel_multiplier=1)),
             r=["onesf"], w=["MASKL"])
    P.op("pool", lambda e: e.memset(ONESW[:], 1.0), w=["ONESW"])
    P.op("pool", lambda e: e.memset(EH[:], 1.0), w=["EH"])
    P.op("pool", lambda e: e.affine_select(out=EH[:], in_=EH[:], pattern=[[-1, 8]],
                                            compare_op=ALU.is_equal, fill=0.0, base=0, channel_multiplier=1),
         r=["EH"], w=["EH"])
    P.op("pool", lambda e: e.tensor_scalar(out=NEH[:], in0=EH[:], scalar1=-1.0, scalar2=None, op0=ALU.mult),
         r=["EH"], w=["NEH"])
    P.op("pool", lambda e: e.memset(NEGM[:], 0.0), w=["NEGM"])
    P.op("pool", lambda e: e.affine_select(out=NEGM[:], in_=NEGM[:], pattern=[[1, 128]],
                                            compare_op=ALU.is_ge, fill=-30000.0, base=0, channel_multiplier=-1),
         r=["NEGM"], w=["NEGM"])
    P.op("pool", lambda e: e.affine_select(out=SEL127[:], in_=onesf[:], pattern=[[0, 128]],
                                            compare_op=ALU.is_equal, fill=0.0, base=-127, channel_multiplier=1),
         r=["onesf"], w=["SEL127"])

    win_all = [W["w_in"][l].rearrange("(kt p) c -> p kt c", p=128) for l in range(NL)]
    wout_all = [W["w_out"][l].rearrange("(kt p) c -> p kt c", p=128) for l in range(NL)]

    def wslot():
        i = wcount["slot"] % 4
        wcount["slot"] += 1
        return i, WGU[:, i // 2, i % 2]

    def load_wblock(wsrc, c0, width):
        i, S = wslot()
        for kh in range(2):
            wload(S[:, kh * 4:(kh + 1) * 4, 0:width], wsrc[:, kh * 4:(kh + 1) * 4, c0:c0 + width], ("WS", i, kh), nofence=True, ceng="act")
        return i, S

    def mm_fm(bank, i, S, width, rhs_t, rkey, m0=0):
        for kt in range(8):
            P.op("pe", (lambda e, kt=kt: e.matmul(
                PSB[bank][m0:m0 + width, :], lhsT=S[:, kt, 0:width], rhs=rhs_t[:, kt, :],
                start=(kt == 0), stop=(kt == 7))),
                r=[("WS", i, kt // 4), (rkey, kt, 0)], w=[("ps", bank)], nofence=True)

    def mixer_phase(s, l):
        am = Arena()
        HTB = am.alloc([8, 512], BF16)
        YTB = am.alloc([8, 512], BF16)
        TAIL = am.alloc([8, 4], F32)
        STATE = am.alloc([8, 64], F32)
        STATEB = am.alloc([8, 64], BF16)
        NORMW = am.alloc([1, 512], F32)
        small_dma(NORMW[:, 0, :], W["m_norm_w"][l:l + 1, :].broadcast_to([128, 512]), ("NORMW", l))
        base_off = am.off
        g = gidx[("mix_norm", l)]
        win = win_all[l]
        wout = wout_all[l]

        P.op("pool", lambda e: e.memset(STATE[:], 0.0), w=["STATE"])
        P.op("pool", lambda e: e.memset(STATEB[:], 0.0), w=["STATEB"])
        P.op("pool", lambda e: e.memset(YTB[:], 0.0), w=[("YTB", k_) for k_ in range(8)])

        def ssd_block(tb):
            am.off = base_off
            XS = am.alloc([4, 512], F32)
            BC = am.alloc([4, 512], BF16)
            ACC = am.alloc([2, 512], F32)
            D8 = am.alloc([4, 512], F32)
            SM = am.alloc([2, 64], F32)
            XDT = am.alloc([2, 512], BF16)
            XDS = am.alloc([2, 512], BF16)
            XSD = am.alloc([2, 512], F32)
            BTM = am.alloc([2, 256], BF16)
            LT = am.alloc([2, 512], F32)
            GT = am.alloc([4, 512], BF16)
            Y1 = am.alloc([4, 512], F32)
            SZ = am.alloc([2, 512], F32)
            YTM = am.alloc([2, 512], BF16)
            SS = am.alloc([2, 8], F32)
            xslots = {0: load_wblock(win, 512, 128)}
            for j in range(8):
                bank = j % 2
                if j + 1 < 8:
                    xslots[j + 1] = load_wblock(win, 512 + 128 * (j + 1), 128)
                i, S = xslots[j]
                mm_fm(bank, i, S, 128, HTB, "HTB")
                a = j % 2
                pb = PSB[bank]
                P.op("dve", (lambda e, a=a, pb=pb, j=j: e.tensor_scalar(
                    out=ACC[:, a, :], in0=pb[:, :], scalar1=CW[:, l, j, 3:4], scalar2=CBs[:, l, j:j + 1],
                    op0=ALU.mult, op1=ALU.add)),
                    r=[("ps", bank), ("CW", l), ("CBs", l)], w=[("ACC", a)])
                for jj in range(3):
                    sh = 3 - jj
                    P.op("dve", (lambda e, a=a, pb=pb, j=j, jj=jj, sh=sh: e.scalar_tensor_tensor(
                        out=ACC[:, a, sh:512], in0=pb[:, 0:512 - sh], scalar=CW[:, l, j, jj:jj + 1],
                        in1=ACC[:, a, sh:512], op0=ALU.mult, op1=ALU.add)),
                        r=[("ps", bank), ("ACC", a)], w=[("ACC", a)])
                    if tb > 0:
                        P.op("dve", (lambda e, a=a, j=j, jj=jj, sh=sh: e.scalar_tensor_tensor(
                            out=ACC[:, a, 0:sh], in0=TAIL[:, j, 3 - sh:3], scalar=CW[:, l, j, jj:jj + 1],
                            in1=ACC[:, a, 0:sh], op0=ALU.mult, op1=ALU.add)),
                            r=[("TAIL", j), ("ACC", a)], w=[("ACC", a)])
                P.op("dve", (lambda e, pb=pb, j=j: e.tensor_copy(out=TAIL[:, j, 0:3], in_=pb[:, 509:512])),
                     r=[("ps", bank), ("ACC", a)], w=[("TAIL", j)])
                dst = XS[:, j, :] if j < 4 else BC[:, j - 4, :]
                dkey = ("XS", j) if j < 4 else ("BC", j - 4)
                P.op("act", (lambda e, a=a, dst=dst: e.activation(out=dst, in_=ACC[:, a, :], func=AF.Silu)),
                     r=[("ACC", a)], w=[dkey])
            i, S = load_wblock(win, 1536, 8)
            mm_fm(2, i, S, 8, HTB, "HTB")
            P.op("act", lambda e: e.activation(out=D8[0:8, 0, :], in_=PSB[2][0:8, :], func=AF.Exp,
                                               bias=DTB[:, l:l + 1], scale=1.0),
                 r=[("ps", 2), ("DTB", l)], w=["DTE"])
            P.op("act", lambda e: e.activation(out=D8[0:8, 1, :], in_=D8[0:8, 0, :], func=AF.Ln,
                                               bias=CST[0:8, 1:2], scale=1.0),
                 r=["DTE", "cst"], w=["DT"])
            P.op("dve", lambda e: e.tensor_scalar(out=D8[0:8, 2, :], in0=D8[0:8, 1, :], scalar1=ANEG[:, l:l + 1],
                                                  scalar2=None, op0=ALU.mult),
                 r=["DT", ("ANEG", l)], w=["DA"])
            for c in range(4):
                P.op("dve", (lambda e, c=c: e.tensor_tensor_scan(
                    out=D8[0:8, 3, c * 128:(c + 1) * 128], data0=ONES8[:, :], data1=D8[0:8, 2, c * 128:(c + 1) * 128],
                    initial=0.0, op0=ALU.mult, op1=ALU.add)),
                    r=["DA", "ONES8"], w=[("ACS", c)])
            iz0, SZ0 = load_wblock(win, 0, 256)
            iz1, SZ1 = load_wblock(win, 256, 256)
            def chunk(c):
                cs = slice(c * 128, (c + 1) * 128)
                b = c % 2
                P.op("pe", (lambda e, cs=cs: e.transpose(out=PSB[2][:, 0:8], in_=D8[0:8, 1, cs], identity=ident[0:8, 0:8])),
                     r=["DT", "ident"], w=[("ps", 2)])
                P.op("pe", (lambda e, cs=cs: e.transpose(out=PSB[2][:, 8:16], in_=D8[0:8, 3, cs], identity=ident[0:8, 0:8])),
                     r=[("ACS", c), "ident"], w=[("ps", 2)])
                P.op("dve", (lambda e, b=b: e.tensor_copy(out=SM[:, b, 0:16], in_=PSB[2][:, 0:16])),
                     r=[("ps", 2)], w=[("SM", b, 0)])
                P.op("pe", (lambda e, b=b: e.matmul(PSB[2][:, 16:24], lhsT=SEL127[:], rhs=SM[:, b, 8:16],
                                                   start=True, stop=True)),
                     r=[("SM", b, 0), "SEL127"], w=[("ps", 2)])
                P.op("dve", (lambda e, b=b: e.tensor_tensor(out=SM[:, b, 16:24], in0=PSB[2][:, 16:24],
                                                           in1=SM[:, b, 8:16], op=ALU.subtract)),
                     r=[("ps", 2), ("SM", b, 0)], w=[("SM", b, 1)])
                P.op("act", (lambda e, b=b: e.activation(out=SM[:, b, 24:32], in_=SM[:, b, 16:24], func=AF.Exp)),
                     r=[("SM", b, 1)], w=[("SM", b, 2)])
                P.op("act", (lambda e, b=b: e.activation(out=SM[:, b, 32:40], in_=PSB[2][:, 16:24], func=AF.Exp)),
                     r=[("ps", 2)], w=[("SM", b, 3)])
                P.op("act", (lambda e, b=b: e.activation(out=SM[:, b, 40:48], in_=SM[:, b, 8:16], func=AF.Exp)),
                     r=[("SM", b, 0)], w=[("SM", b, 4)])
                P.op("dve", (lambda e, b=b: e.tensor_tensor(out=SM[:, b, 48:56], in0=SM[:, b, 0:8],
                                                           in1=SM[:, b, 24:32], op=ALU.mult)),
                     r=[("SM", b, 0), ("SM", b, 2)], w=[("SM", b, 5)])
                P.op("dve", (lambda e, b=b: e.tensor_scalar(out=SM[:, b, 56:64], in0=SM[:, b, 8:16], scalar1=-1.0,
                                                           scalar2=None, op0=ALU.mult)),
                     r=[("SM", b, 0)], w=[("SM", b, 6)])

                def bc8(ap):
                    return ap.unsqueeze(2).broadcast_to([128, 8, 64])

                def v8(ap):
                    return ap.rearrange("p (h d) -> p h d", d=64)

                yield
                for i4 in range(4):
                    P.op("pe", (lambda e, i4=i4, cs=cs: e.transpose(out=PSB[0][:, i4 * 128:(i4 + 1) * 128],
                                                                    in_=XS[:, i4, cs], identity=ident[:])),
                         r=[("XS", i4), "ident"], w=[("ps", 0)])
                P.op("dve", (lambda e, b=b: e.tensor_tensor(out=v8(XDT[:, b, :]), in0=v8(PSB[0][:, :]),
                                                           in1=bc8(SM[:, b, 0:8]), op=ALU.mult)),
                     r=[("ps", 0), ("SM", b, 0)], w=[("XDT", b)])
                P.op("dve", (lambda e, b=b: e.tensor_tensor(out=v8(XDS[:, b, :]), in0=v8(PSB[0][:, :]),
                                                           in1=bc8(SM[:, b, 48:56]), op=ALU.mult)),
                     r=[("ps", 0), ("SM", b, 5)], w=[("XDS", b)])
                P.op("dve", (lambda e, b=b: e.tensor_tensor(out=v8(XSD[:, b, :]), in0=v8(PSB[0][:, :]),
                                                           in1=bc8(DBC[:, l, :]), op=ALU.mult)),
                     r=[("ps", 0), ("DBC", l)], w=[("XSD", b)])
                yield
                pbt = PSB[2][:, 256:384].bitcast(BF16)
                for g2 in range(2):
                    P.op("pe", (lambda e, g2=g2, cs=cs: e.transpose(out=pbt[:, g2 * 128:(g2 + 1) * 128],
                                                                    in_=BC[:, g2, cs], identity=identb[:])),
                         r=[("BC", g2), "identb"], w=[("ps", 2)])
                P.op("act", (lambda e, b=b: e.copy(out=BTM[:, b, :], in_=pbt)),
                     r=[("ps", 2)], w=[("BTM", b)])
                for g2 in range(2):
                    P.op("pe", (lambda e, g2=g2, cs=cs: e.matmul(PSB[3][:, b * 256 + g2 * 128:b * 256 + (g2 + 1) * 128],
                                                                 lhsT=BC[:, g2, cs], rhs=BC[:, 2 + g2, cs],
                                                                 start=True, stop=True)),
                         r=[("BC", g2), ("BC", 2 + g2)], w=[("ps", 3)])
                yield
                for g2 in range(2):
                    P.op("pe", (lambda e, g2=g2, cs=cs: e.matmul(
                        PSB[6][:, g2 * 256:(g2 + 1) * 256], lhsT=BC[:, 2 + g2, cs],
                        rhs=STATEB[:, g2 * 4:(g2 + 1) * 4, :].rearrange("p h d -> p (h d)"),
                        start=True, stop=True)),
                        r=[("BC", 2 + g2), "STATEB"], w=[("ps", 6)])
                P.op("dve", (lambda e, b=b: e.tensor_tensor(out=v8(Y1[:, b * 2, :]), in0=v8(PSB[6][:, :]),
                                                           in1=bc8(SM[:, b, 40:48]), op=ALU.mult)),
                     r=[("ps", 6), ("SM", b, 4)], w=[("Y1", b, 0)])
                for g2 in range(2):
                    P.op("pe", (lambda e, g2=g2, b=b: e.matmul(
                        PSB[7][:, g2 * 256:(g2 + 1) * 256], lhsT=BTM[:, b, g2 * 128:(g2 + 1) * 128],
                        rhs=XDS[:, b, g2 * 256:(g2 + 1) * 256], start=True, stop=True)),
                        r=[("BTM", b), ("XDS", b)], w=[("ps", 7)])
                P.op("dve", (lambda e, b=b: e.tensor_tensor(out=STATE[:], in0=STATE[:], in1=bc8(SM[:, b, 32:40]),
                                                           op=ALU.mult)),
                     r=["STATE", ("SM", b, 3)], w=["STATE"])
                P.op("dve", lambda e: e.tensor_tensor(out=STATE[:], in0=STATE[:], in1=v8(PSB[7][:, :]), op=ALU.add),
                     r=["STATE", ("ps", 7)], w=["STATE"])
                P.op("dve", lambda e: e.tensor_copy(out=STATEB[:], in_=STATE[:]),
                     r=["STATE"], w=["STATEB"])

                yield
                for g2 in range(2):
                    for hh in range(4):
                        h = g2 * 4 + hh
                        o = PSB[4][:, hh * 128:(hh + 1) * 128]
                        P.op("pe", (lambda e, o=o, h=h, cs=cs: e.matmul(o, lhsT=EH[:, h:h + 1].broadcast_to([8, 128]), rhs=D8[0:8, 3, cs],
                                                                        start=True, stop=False)),
                             r=["EH", ("ACS", c)], w=[("ps", 4)])
                        P.op("pe", (lambda e, o=o: e.matmul(o, lhsT=ident[:], rhs=NEGM[:], start=False, stop=True)),
                             r=["ident", "NEGM"], w=[("ps", 4)])
                    for hh in range(4):
                        h = g2 * 4 + hh
                        P.op("act", (lambda e, g2=g2, hh=hh, h=h, b=b: e.activation(
                            out=LT[:, g2, hh * 128:(hh + 1) * 128], in_=PSB[4][:, hh * 128:(hh + 1) * 128], func=AF.Exp,
                            bias=SM[:, b, 56 + h:57 + h], scale=1.0)),
                            r=[("ps", 4), ("SM", b, 6)], w=[("LT", g2)])
                    P.op("dve", (lambda e, g2=g2: e.tensor_tensor(
                        out=GT[:, b * 2 + g2, :].rearrange("p (h l) -> p h l", l=128),
                        in0=LT[:, g2, :].rearrange("p (h l) -> p h l", l=128),
                        in1=PSB[3][:, b * 256 + g2 * 128:b * 256 + (g2 + 1) * 128].unsqueeze(1).broadcast_to([128, 4, 128]),
                        op=ALU.mult)),
                        r=[("LT", g2), ("ps", 3)], w=[("GT", b, g2)])
                yield
                for h in range(8):
                    g2, hh = h // 4, h % 4
                    P.op("pe", (lambda e, h=h, g2=g2, hh=hh, b=b: e.matmul(
                        PSB[5][:, h * 64:(h + 1) * 64], lhsT=GT[:, b * 2 + g2, hh * 128:(hh + 1) * 128],
                        rhs=XDT[:, b, h * 64:(h + 1) * 64], start=True, stop=True)),
                        r=[("GT", b, g2), ("XDT", b)], w=[("ps", 5)])
                P.op("dve", lambda e: e.tensor_tensor(out=Y1[:, b * 2, :], in0=Y1[:, b * 2, :], in1=PSB[5][:, :], op=ALU.add),
                     r=[("ps", 5), ("Y1", b, 0)], w=[("Y1", b, 0)])
                yield
                for half, (iz, Sz) in enumerate(((iz0, SZ0), (iz1, SZ1))):
                    for kt in range(8):
                        P.op("pe", (lambda e, half=half, Sz=Sz, kt=kt, cs=cs: e.matmul(
                            PSB[1][:, half * 256:(half + 1) * 256], lhsT=HTB[:, kt, cs], rhs=Sz[:, kt, 0:256],
                            start=(kt == 0), stop=(kt == 7))),
                            r=[("WS", iz, kt // 4), ("HTB", kt, 0)], w=[("ps", 1)])
                P.op("act", lambda e: e.activation(out=SZ[:, b, :], in_=PSB[1][:, :], func=AF.Silu),
                     r=[("ps", 1)], w=[("SZ", b)])
                yield
                P.op("dve", (lambda e, b=b: e.tensor_tensor(out=Y1[:, b * 2, :], in0=Y1[:, b * 2, :], in1=XSD[:, b, :], op=ALU.add)),
                     r=[("XSD", b), ("Y1", b, 0)], w=[("Y1", b, 0)])
                P.op("dve", lambda e: e.tensor_tensor(out=Y1[:, b * 2, :], in0=Y1[:, b * 2, :], in1=SZ[:, b, :], op=ALU.mult),
                     r=[("SZ", b), ("Y1", b, 0)], w=[("Y1", b, 0)])
                P.op("act", lambda e: e.activation(out=Y1[:, b * 2 + 1, :], in_=Y1[:, b * 2, :], func=AF.Square,
                                                   accum_out=SS[:, b, 0:1]),
                     r=[("Y1", b, 0)], w=[("Y1", b, 1), ("SS", b)])
                P.op("act", lambda e: e.activation(out=SS[:, b, 1:2], in_=SS[:, b, 0:1], func=AF.Ln,
                                                   bias=CST[:, 0:1], scale=1.0 / 512),
                     r=[("SS", b), "cst"], w=[("SS1", b)])
                P.op("act", lambda e: e.activation(out=SS[:, b, 2:3], in_=SS[:, b, 1:2], func=AF.Exp, scale=-0.5),
                     r=[("SS1", b)], w=[("SS2", b)])
                P.op("dve", (lambda e, b=b: e.scalar_tensor_tensor(
                    out=YTM[:, b, :], in0=Y1[:, b * 2, :], scalar=SS[:, b, 2:3], in1=NORMW[:, 0, :],
                    op0=ALU.mult, op1=ALU.mult)),
                    r=[("Y1", b, 0), ("SS2", b), ("NORMW", l)], w=[("YTM", b)])
                pyt = PSB[7][:, 256:512].bitcast(BF16)
                for i4 in range(4):
                    P.op("pe", (lambda e, i4=i4, b=b: e.transpose(out=pyt[:, i4 * 128:(i4 + 1) * 128],
                                                                  in_=YTM[:, b, i4 * 128:(i4 + 1) * 128],
                                                                  identity=identb[:])),
                         r=[("YTM", b), "identb"], w=[("ps", 7)])
                P.op("act", (lambda e, cs=cs: e.copy(out=YTB[:, 0:4, cs], in_=pyt.rearrange("p (a t) -> p a t", t=128))),
                     r=[("ps", 7)], w=[("YTB", k_) for k_ in range(4)])
            run_pipelined([chunk(c_) for c_ in range(4)], 2)

        TWO_PI = 6.283185307179586
        PI = 3.141592653589793

        def s5_setup():
            SC = am.alloc([20, 8], F32)
            MAG = am.alloc([1, 8], F32)
            NS128 = am.alloc([1, 8], F32)
            COST = am.alloc([8, 129], F32)
            SINT = am.alloc([8, 129], F32)
            LBR = am.alloc([8, 128], BF16)
            LBI = am.alloc([8, 128], BF16)
            LCR = am.alloc([8, 64], BF16)
            LCI = am.alloc([8, 64], BF16)
            SDG = am.alloc([1, 4], F32)
            GLW = am.alloc([2, 256], BF16)
            CARR = am.alloc([1, 8], F32)
            CARI = am.alloc([1, 8], F32)
            mark = am.off
            TAU = am.alloc([1, 129], F32)
            ARG = am.alloc([8, 129], F32)
            TMP1 = am.alloc([8, 129], F32)
            TMP2 = am.alloc([8, 129], F32)
            BN2 = am.alloc([2, 8, 64], F32)
            BN2b = am.alloc([2, 8, 64], BF16)
            CNat = am.alloc([2, 8, 128], F32)
            CT = am.alloc([4, 256], F32)
            st = {"n": 0}

            def K_(name):
                return ("s5c", name)

            def col(c):
                return SC[:, c, :]

            log_dt = W["s_log_dt"][l].rearrange("(gp gl) -> gl gp", gl=2)
            for gl in range(2):
                small_dma(SC[gl * 64:(gl + 1) * 64, 0, :], log_dt[gl:gl + 1, :].broadcast_to([64, 8]), K_(("ldt", gl)))
            small_dma(SC[:, 1, :], W["s_A_re"][l].rearrange("(gp gl) p -> (gl p) gp", gl=2), K_("are"))
            small_dma(SC[:, 2, :], W["s_A_im"][l].rearrange("(gp gl) p -> (gl p) gp", gl=2), K_("aim"))
            small_dma(SDG[:, 0, 0:2], W["s_D"][l].rearrange("(t p) -> p t", p=128), K_("sd"))
            small_dma(SDG[:, 0, 2:4], W["s_glu_b"][l].rearrange("(t p) -> p t", p=128), K_("glb"))
            wload(GLW[:, :, :], W["s_glu_w"][l].rearrange("(kt p) c -> p kt c", p=128), K_("glw"))

            def dv(fn, r, w):
                P.op("dve", fn, r=[K_(x) for x in r], w=[K_(x) for x in w])

            def ac(fn, r, w):
                P.op("act", fn, r=[K_(x) for x in r], w=[K_(x) for x in w])

            ac(lambda e: e.activation(out=col(0), in_=col(0), func=AF.Exp), [("ldt", 0), ("ldt", 1)], ["dt"])
            dv(lambda e: e.tensor_tensor(out=col(3), in0=col(1), in1=col(0), op=ALU.mult), ["are", "dt"], ["ar"])
            dv(lambda e: e.tensor_tensor(out=col(4), in0=col(2), in1=col(0), op=ALU.mult), ["aim", "dt"], ["th"])
            ac(lambda e: e.activation(out=MAG[:, 0, :], in_=col(3), func=AF.Exp), ["ar"], ["mag"])

            def sincos(th, out_s, out_c, t1, t2, rk, wk):
                for phase, out in ((0.0, out_s), (PI / 2, out_c)):
                    tag = "s" if phase == 0.0 else "c"
                    dv(lambda e: e.tensor_scalar(out=t1, in0=th, scalar1=1.0 / TWO_PI, scalar2=phase / TWO_PI + 0.5,
                                                 op0=ALU.mult, op1=ALU.add), rk, [wk + "t1"])
                    t2i = t2.bitcast(I32)
                    dv(lambda e: e.tensor_copy(out=t2i, in_=t1), [wk + "t1"], [wk + "t2"])
                    dv(lambda e: e.tensor_copy(out=t1, in_=t2i), [wk + "t2"], [wk + "t1"])
                    dv(lambda e: e.scalar_tensor_tensor(out=t1, in0=t1, scalar=-TWO_PI, in1=th, op0=ALU.mult,
                                                        op1=ALU.add), [wk + "t1"] + rk, [wk + "t1"])
                    if phase != 0.0:
                        dv(lambda e: e.tensor_scalar(out=t1, in0=t1, scalar1=phase, scalar2=None, op0=ALU.add),
                           [wk + "t1"], [wk + "t1"])
                    dv(lambda e: e.tensor_scalar(out=t2, in0=t1, scalar1=PI, scalar2=-TWO_PI, op0=ALU.is_gt,
                                                 op1=ALU.mult), [wk + "t1"], [wk + "t2"])
                    dv(lambda e: e.tensor_tensor(out=t1, in0=t1, in1=t2, op=ALU.add), [wk + "t1", wk + "t2"], [wk + "t1"])
                    dv(lambda e: e.tensor_scalar(out=t2, in0=t1, scalar1=-PI, scalar2=TWO_PI, op0=ALU.is_lt,
                                                 op1=ALU.mult), [wk + "t1"], [wk + "t2"])
                    dv(lambda e: e.tensor_tensor(out=t1, in0=t1, in1=t2, op=ALU.add), [wk + "t1", wk + "t2"], [wk + "t1"])
                    ac(lambda e, out=out: e.activation(out=out, in_=t1, func=AF.Sin), [wk + "t1"], [wk + tag])

            sincos(col(4), col(5), col(6), col(18), col(19), ["th"], "sc0")
            dv(lambda e: e.tensor_tensor(out=col(7), in0=MAG[:, 0, :], in1=col(6), op=ALU.mult), ["mag", "sc0c"], ["lr"])
            dv(lambda e: e.tensor_tensor(out=col(8), in0=MAG[:, 0, :], in1=col(5), op=ALU.mult), ["mag", "sc0s"], ["li"])
            dv(lambda e: e.tensor_tensor(out=col(9), in0=col(1), in1=col(1), op=ALU.mult), ["are"], ["den"])
            dv(lambda e: e.tensor_tensor(out=col(13), in0=col(2), in1=col(2), op=ALU.mult), ["aim"], ["den2"])
            dv(lambda e: e.tensor_tensor(out=col(9), in0=col(9), in1=col(13), op=ALU.add), ["den", "den2"], ["den"])
            dv(lambda e: e.reciprocal(out=col(9), in_=col(9)), ["den"], ["den"])
            dv(lambda e: e.tensor_scalar(out=col(10), in0=col(7), scalar1=-1.0, scalar2=None, op0=ALU.add), ["lr"], ["nr"])
            dv(lambda e: e.tensor_tensor(out=col(11), in0=col(10), in1=col(1), op=ALU.mult), ["nr", "are"], ["cr"])
            dv(lambda e: e.tensor_tensor(out=col(13), in0=col(8), in1=col(2), op=ALU.mult), ["li", "aim", "den2", "den"], ["tmp13"])
            dv(lambda e: e.tensor_tensor(out=col(11), in0=col(11), in1=col(13), op=ALU.add), ["cr", "tmp13"], ["cr"])
            dv(lambda e: e.tensor_tensor(out=col(11), in0=col(11), in1=col(9), op=ALU.mult), ["cr", "den"], ["cr"])
            dv(lambda e: e.tensor_tensor(out=col(12), in0=col(8), in1=col(1), op=ALU.mult), ["li", "are"], ["ci"])
            dv(lambda e: e.tensor_tensor(out=col(13), in0=col(10), in1=col(2), op=ALU.mult), ["nr", "aim", "cr"], ["tmp13"])
            dv(lambda e: e.tensor_tensor(out=col(12), in0=col(12), in1=col(13), op=ALU.subtract), ["ci", "tmp13"], ["ci"])
            dv(lambda e: e.tensor_tensor(out=col(12), in0=col(12), in1=col(9), op=ALU.mult), ["ci", "den"], ["ci"])
            dv(lambda e: e.tensor_tensor_scan(out=TAU[:, 0, :], data0=ONESW[:, 0:129],
                                              data1=ONESW[:, 0:129], initial=-1.0, op0=ALU.mult, op1=ALU.add),
               [], ["tau"])
            dv(lambda e: e.tensor_tensor(out=ARG[:], in0=TAU[:, 0:1, :].broadcast_to([128, 8, 129]),
                                         in1=col(4).unsqueeze(2).broadcast_to([128, 8, 129]), op=ALU.mult),
               ["tau", "th"], ["arg"])
            sincos(ARG[:], SINT[:], COST[:], TMP1[:], TMP2[:], ["arg"], "sc1")
            dv(lambda e: e.tensor_scalar(out=NS128[:, 0, :], in0=SINT[:, :, 128], scalar1=-1.0, scalar2=None, op0=ALU.mult),
               ["sc1s"], ["ns128"])
            for ri, nm in enumerate(("s_B_re", "s_B_im")):
                P.op("pool", (lambda e, ri=ri: e.memset(BN2[:, ri], 0.0)), w=[K_(("bn2", ri))])
                srcb = W[nm][l].rearrange("(gp gl) p h -> gl p gp h", gl=2)
                for gl in range(2):
                    for par in range(2):
                        P.dma("sp", (lambda e, ri=ri, gl=gl, par=par, srcb=srcb: e.dma_start(
                            out=BN2[gl * 64:(gl + 1) * 64, ri, par::2, par * 32 + gl * 16:par * 32 + (gl + 1) * 16],
                            in_=srcb[gl][:, par::2, :], allow_slow_non_contiguous=True)),
                            w=[K_(("bn2", ri))], sem="D_bn2_%d_%d_%d" % (ri, gl, par))
                P.op("pool", (lambda e, ri=ri: e.tensor_copy(out=BN2b[:, ri], in_=BN2[:, ri])),
                     r=[K_(("bn2", ri))], w=[K_(("bn2b", ri))])
                pbf = PSB[ri][:, :].bitcast(BF16)
                for gp in range(8):
                    pr = ((gp % 4) // 2) * 64
                    P.op("pe", (lambda e, ri=ri, gp=gp, pr=pr, pbf=pbf: e.transpose(
                        out=pbf[pr:pr + 64, gp * 128:(gp + 1) * 128], in_=BN2b[:, ri, gp, :], identity=identb[:])),
                        r=[K_(("bn2b", ri)), "identb"], w=[("ps", ri)])
                LB = LBR if ri == 0 else LBI
                for gp in range(8):
                    pr = ((gp % 4) // 2) * 64
                    P.op("act", (lambda e, LB=LB, gp=gp, pr=pr, pbf=pbf: e.copy(
                        out=LB[pr:pr + 64, gp, :], in_=pbf[pr:pr + 64, gp * 128:(gp + 1) * 128])),
                        r=[("ps", ri)], w=[K_(("lb", ri))])
            for ri, nm in enumerate(("s_C_re", "s_C_im")):
                P.op("pool", (lambda e, ri=ri: e.memset(CNat[0:32, ri], 0.0)), w=[K_(("cn", ri))])
                srcc = W[nm][l].rearrange("(gp gl) h p -> gl h gp p", gl=2)
                for gl in range(2):
                    P.dma("sp", (lambda e, ri=ri, gl=gl, srcc=srcc: e.dma_start(
                        out=CNat[gl * 16:(gl + 1) * 16, ri, :, gl * 64:(gl + 1) * 64], in_=srcc[gl],
                        allow_slow_non_contiguous=True)),
                        w=[K_(("cn", ri))], sem="D_cn_%d_%d" % (ri, gl))
                for gp in range(8):
                    P.op("pe", (lambda e, ri=ri, gp=gp: e.transpose(
                        out=PSB[2 + ri][:, gp * 32:(gp + 1) * 32], in_=CNat[0:32, ri, gp, :], identity=ident[0:32, 0:32])),
                        r=[K_(("cn", ri)), "ident"], w=[("ps", 2 + ri)])

            def b32(c):
                return col(c).unsqueeze(2).broadcast_to([128, 8, 32])

            def v32(ap):
                return ap.rearrange("p (g c) -> p g c", c=32)
            crt = v32(PSB[2][:, 0:256])
            cit = v32(PSB[3][:, 0:256])
            P.op("dve", lambda e: e.tensor_tensor(out=v32(CT[:, 0, :]), in0=crt, in1=b32(11), op=ALU.mult),
                 r=[("ps", 2), K_("cr")], w=[K_("ct0")])
            P.op("dve", lambda e: e.tensor_tensor(out=v32(CT[:, 1, :]), in0=cit, in1=b32(12), op=ALU.mult),
                 r=[("ps", 3), K_("ci")], w=[K_("ct1")])
            P.op("dve", lambda e: e.tensor_tensor(out=v32(CT[:, 2, :]), in0=crt, in1=b32(12), op=ALU.mult),
                 r=[("ps", 2), K_("ci")], w=[K_("ct2")])
            P.op("dve", lambda e: e.tensor_tensor(out=v32(CT[:, 3, :]), in0=cit, in1=b32(11), op=ALU.mult),
                 r=[("ps", 3), K_("cr")], w=[K_("ct3")])
            P.op("pool", lambda e: e.memset(LCR[:], 0.0), w=[K_("lcr")])
            P.op("pool", lambda e: e.memset(LCI[:], 0.0), w=[K_("lci")])
            dv(lambda e: e.tensor_tensor(out=CT[:, 2, :], in0=CT[:, 2, :], in1=CT[:, 3, :], op=ALU.add), ["ct2", "ct3"], ["ct2"])
            for par in range(2):
                dv(lambda e, par=par: e.tensor_tensor(
                    out=LCR[:, par::2, par * 32:(par + 1) * 32], in0=v32(CT[:, 0, :])[:, par::2, :],
                    in1=v32(CT[:, 1, :])[:, par::2, :], op=ALU.subtract), ["ct0", "ct1", "lcr"], ["lcr"])
                dv(lambda e, par=par: e.tensor_scalar(
                    out=LCI[:, par::2, par * 32:(par + 1) * 32], in0=v32(CT[:, 2, :])[:, par::2, :], scalar1=-1.0,
                    scalar2=None, op0=ALU.mult), ["ct2", "lci"], ["lci"])
            P.op("pool", lambda e: e.memset(CARR[:], 0.0), w=["CARR"])
            P.op("pool", lambda e: e.memset(CARI[:], 0.0), w=["CARI"])
            am.off = mark
            return dict(MAG=MAG, NS128=NS128, COST=COST, SINT=SINT, LBR=LBR, LBI=LBI, LCR=LCR, LCI=LCI, SDG=SDG,
                        GLW=GLW, CARR=CARR, CARI=CARI, K_=K_)

        def s5_block(tb, C5):
            am.off = base_off
            K_ = C5["K_"]
            MAG, NS128, COST, SINT = C5["MAG"], C5["NS128"], C5["COST"], C5["SINT"]
            LBR, LBI, LCR, LCI, SDG, GLW = C5["LBR"], C5["LBI"], C5["LCR"], C5["LCI"], C5["SDG"], C5["GLW"]
            CARR, CARI = C5["CARR"], C5["CARI"]
            US = am.alloc([2, 512], F32)
            USB = am.alloc([2, 512], BF16)
            T = am.alloc([8, 512], F32)
            WRI = am.alloc([4, 512], F32)
            ZRI = am.alloc([4, 512], F32)
            XRI = am.alloc([4, 512], BF16)
            YS = am.alloc([2, 512], F32)
            YG = am.alloc([2, 512], F32)
            YGB = am.alloc([2, 512], BF16)
            CTMP = am.alloc([1, 4], F32)
            i, S = load_wblock(win, 1544, 256)
            for ct in range(2):
                for kt in range(8):
                    P.op("pe", (lambda e, kt=kt, ct=ct, S=S: e.matmul(
                        PSB[ct][:, :], lhsT=S[:, kt, ct * 128:(ct + 1) * 128], rhs=HTB[:, kt, :],
                        start=(kt == 0), stop=(kt == 7))),
                        r=[("WS", i, kt // 4), ("HTB", kt, 0)], w=[("ps", ct)], nofence=True)
                P.op("act", (lambda e, ct=ct: e.copy(out=US[:, ct, :], in_=PSB[ct][:, :])), r=[("ps", ct)], w=[("US", ct)])
                P.op("pool", (lambda e, ct=ct: e.tensor_copy(out=USB[:, ct, :], in_=US[:, ct, :])),
                     r=[("US", ct)], w=[("USB", ct)])

            def b4(tab, gp):
                return tab[:, gp, 0:128].unsqueeze(1).broadcast_to([128, 4, 128])

            def v4(ap):
                return ap.rearrange("p (c t) -> p c t", t=128)

            def pair(gp):
                pb = gp % 2
                b2, b3 = (2, 3) if pb == 0 else (6, 7)
                ct = gp // 4
                pr = ((gp % 4) // 2) * 64
                P.op("pe", (lambda e, gp=gp, ct=ct, pr=pr: e.matmul(
                    PSB[b2][:, :], lhsT=LBR[pr:pr + 64, gp, :], rhs=USB[pr:pr + 64, ct, :], start=True, stop=True)),
                    r=[K_(("lb", 0)), ("USB", ct)], w=[("ps", b2)])
                P.op("pe", (lambda e, gp=gp, ct=ct, pr=pr: e.matmul(
                    PSB[b3][:, :], lhsT=LBI[pr:pr + 64, gp, :], rhs=USB[pr:pr + 64, ct, :], start=True, stop=True)),
                    r=[K_(("lb", 1)), ("USB", ct)], w=[("ps", b3)])
                cosb, sinb = b4(COST, gp), b4(SINT, gp)
                yield
                rk = [K_("sc1c"), K_("sc1s")]
                P.op("dve", (lambda e, cosb=cosb: e.tensor_tensor(out=v4(T[:, pb * 4 + 0, :]), in0=v4(PSB[b2][:, :]), in1=cosb, op=ALU.mult)),
                     r=[("ps", b2)] + rk, w=[("T", pb, 0)])
                P.op("dve", (lambda e, sinb=sinb: e.tensor_tensor(out=v4(T[:, pb * 4 + 1, :]), in0=v4(PSB[b3][:, :]), in1=sinb, op=ALU.mult)),
                     r=[("ps", b3)] + rk, w=[("T", pb, 1)])
                P.op("dve", (lambda e, cosb=cosb: e.tensor_tensor(out=v4(T[:, pb * 4 + 2, :]), in0=v4(PSB[b3][:, :]), in1=cosb, op=ALU.mult)),
                     r=[("ps", b3)] + rk, w=[("T", pb, 2)])
                P.op("dve", (lambda e, sinb=sinb: e.tensor_tensor(out=v4(T[:, pb * 4 + 3, :]), in0=v4(PSB[b2][:, :]), in1=sinb, op=ALU.mult)),
                     r=[("ps", b2)] + rk, w=[("T", pb, 3)])
                P.op("pool", lambda e: e.tensor_tensor(out=WRI[:, pb * 2 + 0, :], in0=T[:, pb * 4 + 0, :], in1=T[:, pb * 4 + 1, :], op=ALU.add),
                     r=[("T", pb, 0), ("T", pb, 1)], w=[("WRI", pb, 0)])
                P.op("pool", lambda e: e.tensor_tensor(out=WRI[:, pb * 2 + 1, :], in0=T[:, pb * 4 + 2, :], in1=T[:, pb * 4 + 3, :], op=ALU.subtract),
                     r=[("T", pb, 2), ("T", pb, 3)], w=[("WRI", pb, 1)])
                yield
                magb = MAG[:, 0, gp:gp + 1].broadcast_to([128, 128])
                for c in range(4):
                    cs = slice(c * 128, (c + 1) * 128)
                    first = (tb == 0 and c == 0)
                    for ri, CAR in ((0, CARR), (1, CARI)):
                        init = 0.0 if first else CAR[:, 0, gp:gp + 1]
                        P.op("dve", (lambda e, ri=ri, cs=cs, init=init, magb=magb: e.tensor_tensor_scan(
                            out=ZRI[:, pb * 2 + ri, cs], data0=magb, data1=WRI[:, pb * 2 + ri, cs], initial=init,
                            op0=ALU.mult, op1=ALU.add)),
                            r=[("WRI", pb, ri), K_("mag"), ("CAR", ri, gp)], w=[("ZRI", pb, ri, c)])
                    zr = ZRI[:, pb * 2, c * 128 + 127:c * 128 + 128]
                    zi = ZRI[:, pb * 2 + 1, c * 128 + 127:c * 128 + 128]
                    c128 = COST[:, gp, 128:129]
                    s128 = SINT[:, gp, 128:129]
                    ns128 = NS128[:, 0, gp:gp + 1]
                    P.op("act", (lambda e, zi=zi, ns128=ns128: e.activation(out=CTMP[:, 0, pb * 2:pb * 2 + 1], in_=zi, func=AF.Identity,
                                                                            scale=ns128)),
                         r=[("ZRI", pb, 1, c), K_("ns128")], w=[("CTMP0", pb)])
                    P.op("act", (lambda e, zr=zr, c128=c128, gp=gp: e.activation(out=CARR[:, 0, gp:gp + 1], in_=zr, func=AF.Identity,
                                                                                scale=c128, bias=CTMP[:, 0, pb * 2:pb * 2 + 1])),
                         r=[("ZRI", pb, 0, c), ("CTMP0", pb)] + rk, w=[("CAR", 0, gp)])
                    P.op("pool", (lambda e, zr=zr, s128=s128: e.tensor_scalar(out=CTMP[:, 0, pb * 2 + 1:pb * 2 + 2], in0=zr, scalar1=s128,
                                                                              scalar2=None, op0=ALU.mult)),
                         r=[("ZRI", pb, 0, c)] + rk, w=[("CTMP1", pb)])
                    P.op("pool", (lambda e, zi=zi, c128=c128, gp=gp: e.tensor_scalar(
                        out=CARI[:, 0, gp:gp + 1], in0=zi, scalar1=c128, scalar2=CTMP[:, 0, pb * 2 + 1:pb * 2 + 2],
                        op0=ALU.mult, op1=ALU.add)),
                         r=[("ZRI", pb, 1, c), ("CTMP1", pb)] + rk, w=[("CAR", 1, gp)])
                yield
                zk = [("ZRI", pb, 0, c_) for c_ in range(4)]
                zki = [("ZRI", pb, 1, c_) for c_ in range(4)]
                P.op("dve", (lambda e, cosb=cosb: e.tensor_tensor(out=v4(T[:, pb * 4 + 0, :]), in0=v4(ZRI[:, pb * 2, :]), in1=cosb, op=ALU.mult)),
                     r=zk + rk, w=[("T", pb, 0)])
                P.op("dve", (lambda e, sinb=sinb: e.tensor_tensor(out=v4(T[:, pb * 4 + 1, :]), in0=v4(ZRI[:, pb * 2 + 1, :]), in1=sinb, op=ALU.mult)),
                     r=zki + rk, w=[("T", pb, 1)])
                P.op("pool", (lambda e, sinb=sinb: e.tensor_tensor(out=v4(T[:, pb * 4 + 2, :]), in0=v4(ZRI[:, pb * 2, :]), in1=sinb, op=ALU.mult)),
                     r=zk + rk, w=[("T", pb, 2)])
                P.op("pool", (lambda e, cosb=cosb: e.tensor_tensor(out=v4(T[:, pb * 4 + 3, :]), in0=v4(ZRI[:, pb * 2 + 1, :]), in1=cosb, op=ALU.mult)),
                     r=zki + rk, w=[("T", pb, 3)])
                P.op("dve", lambda e: e.tensor_tensor(out=XRI[:, pb * 2 + 0, :], in0=T[:, pb * 4 + 0, :], in1=T[:, pb * 4 + 1, :], op=ALU.subtract),
                     r=[("T", pb, 0), ("T", pb, 1)], w=[("XRI", pb, 0)])
                P.op("pool", lambda e: e.tensor_tensor(out=XRI[:, pb * 2 + 1, :], in0=T[:, pb * 4 + 2, :], in1=T[:, pb * 4 + 3, :], op=ALU.add),
                     r=[("T", pb, 2), ("T", pb, 3)], w=[("XRI", pb, 1)])
                yield
                P.op("pe", (lambda e, gp=gp, ct=ct, pr=pr: e.matmul(
                    PSB[4 + ct][pr:pr + 64, :], lhsT=LCR[:, gp, :], rhs=XRI[:, pb * 2 + 0, :], start=(gp % 2 == 0), stop=False)),
                    r=[K_("lcr"), ("XRI", pb, 0)], w=[("ps", 4 + ct)])
                P.op("pe", (lambda e, gp=gp, ct=ct, pr=pr: e.matmul(
                    PSB[4 + ct][pr:pr + 64, :], lhsT=LCI[:, gp, :], rhs=XRI[:, pb * 2 + 1, :], start=False, stop=(gp % 2 == 1))),
                    r=[K_("lci"), ("XRI", pb, 1)], w=[("ps", 4 + ct)])
            run_pipelined([pair(g_) for g_ in range(8)], 2)
            for ct in range(2):
                P.op("dve", (lambda e, ct=ct: e.scalar_tensor_tensor(
                    out=YS[:, ct, :], in0=US[:, ct, :], scalar=SDG[:, 0, ct:ct + 1], in1=PSB[4 + ct][:, :],
                    op0=ALU.mult, op1=ALU.add)),
                    r=[("US", ct), ("ps", 4 + ct), K_("sd")], w=[("YS", ct)])
                P.op("pool", (lambda e, ct=ct: e.tensor_tensor(out=T[:, ct, :], in0=YS[:, ct, :], in1=YS[:, ct, :], op=ALU.mult)),
                     r=[("YS", ct)], w=[("T", 0, ct)])
                P.op("pool", (lambda e, ct=ct: e.tensor_scalar(out=T[:, ct, :], in0=T[:, ct, :], scalar1=0.044715, scalar2=1.0,
                                                               op0=ALU.mult, op1=ALU.add)),
                     r=[("T", 0, ct)], w=[("T", 0, ct)])
                P.op("pool", (lambda e, ct=ct: e.tensor_tensor(out=T[:, ct, :], in0=T[:, ct, :], in1=YS[:, ct, :], op=ALU.mult)),
                     r=[("T", 0, ct), ("YS", ct)], w=[("T", 0, ct)])
                P.op("act", (lambda e, ct=ct: e.activation(out=T[:, 2 + ct, :], in_=T[:, ct, :], func=AF.Sigmoid,
                                                           scale=1.5957691216057308)),
                     r=[("T", 0, ct)], w=[("T", 0, 2 + ct)])
                P.op("dve", (lambda e, ct=ct: e.tensor_tensor(out=YG[:, ct, :], in0=YS[:, ct, :], in1=T[:, 2 + ct, :], op=ALU.mult)),
                     r=[("YS", ct), ("T", 0, 2 + ct)], w=[("YG", ct)])
                P.op("pool", (lambda e, ct=ct: e.tensor_copy(out=YGB[:, ct, :], in_=YG[:, ct, :])),
                     r=[("YG", ct)], w=[("YGB", ct)])
            for mt in range(2):
                for kt in range(2):
                    P.op("pe", (lambda e, mt=mt, kt=kt: e.matmul(
                        PSB[6 + mt][:, :], lhsT=GLW[:, kt, mt * 128:(mt + 1) * 128], rhs=YGB[:, kt, :],
                        start=(kt == 0), stop=(kt == 1))),
                        r=[K_("glw"), ("YGB", kt)], w=[("ps", 6 + mt)])
                P.op("act", (lambda e, mt=mt: e.activation(out=T[:, mt, :], in_=PSB[6 + mt][:, :], func=AF.Sigmoid,
                                                           bias=SDG[:, 0, 2 + mt:3 + mt], scale=1.0)),
                     r=[("ps", 6 + mt), K_("glb")], w=[("T", 0, mt)])
                P.op("dve", (lambda e, mt=mt: e.tensor_tensor(out=YTB[:, 4 + mt, :], in0=YG[:, mt, :], in1=T[:, mt, :], op=ALU.mult)),
                     r=[("YG", mt), ("T", 0, mt)], w=[("YTB", 4 + mt)])

        R0 = 1800

        def rw_setup():
            RC = am.alloc([1, 32], F32)
            MUL = am.alloc([1, 2], F32)
            LW = am.alloc([1, 256], BF16)
            HST = am.alloc([2, 64], F32)
            HB0 = am.alloc([2, 64], BF16)
            PREV = am.alloc([1, 8], F32)
            mark_ = am.off
            LWF = am.alloc([1, 256], F32)
            am.off = mark_

            def K_(n):
                return ("rwc", n)
            mu = W["r_mu"][l]
            for qi in range(3):
                small_dma(RC[:, 0, qi * 2:(qi + 1) * 2], mu[qi * 256:(qi + 1) * 256].rearrange("(pr p) -> p pr", p=128), K_("mu"))
            small_dma(MUL[:, 0, 0:1], mu[768:896].rearrange("(p o) -> p o", o=1), K_("mul"))
            for nm, c0 in (("r_w0", 12), ("r_a0", 14), ("r_k_k", 16), ("r_k_a", 18), ("r_gn_w", 24), ("r_gn_b", 26)):
                small_dma(RC[:, 0, c0:c0 + 2], W[nm][l].rearrange("(pr p) -> p pr", p=128), K_("cols"))
            small_dma(RC[:, 0, 22:24], W["r_r_k"][l].rearrange("(pr hh) p -> (hh p) pr", hh=2), K_("cols"))
            P.op("dve", lambda e: e.tensor_scalar(out=RC[:, 0, 6:12], in0=RC[:, 0, 0:6], scalar1=-1.0, scalar2=1.0,
                                                  op0=ALU.mult, op1=ALU.add), r=[K_("mu")], w=[K_("omu")])
            P.op("dve", lambda e: e.tensor_scalar(out=RC[:, 0, 20:22], in0=RC[:, 0, 18:20], scalar1=-1.0, scalar2=1.0,
                                                  op0=ALU.mult, op1=ALU.add), r=[K_("cols")], w=[K_("omka")])
            P.op("dve", lambda e: e.tensor_scalar(out=MUL[:, 0, 1:2], in0=MUL[:, 0, 0:1], scalar1=-1.0, scalar2=1.0,
                                                  op0=ALU.mult, op1=ALU.add), r=[K_("mul")], w=[K_("omul")])
            small_dma(LWF[0:32, 0, :], W["r_w2"][l], K_("lwf"))
            small_dma(LWF[32:64, 0, :], W["r_a2"][l], K_("lwf"))
            small_dma(LWF[64:128, 0, :], W["r_g2"][l], K_("lwf"))
            P.op("pool", lambda e: e.tensor_copy(out=LW[:, 0, :], in_=LWF[:, 0, :]), r=[K_("lwf")], w=[K_("lw")])
            P.op("pool", lambda e: e.memset(HST[:], 0.0), w=["HST"])
            P.op("pool", lambda e: e.memset(HB0[:], 0.0), w=["HB0"])
            P.op("pool", lambda e: e.memset(PREV[:], 0.0), w=["PREV"])
            return dict(RC=RC, MUL=MUL, LW=LW, HST=HST, HB0=HB0, PREV=PREV, K_=K_)

        def rw_block(tb, CR):
            am.off = base_off
            K_ = CR["K_"]
            RC, MUL, LW, HST, HB0, PREV = CR["RC"], CR["MUL"], CR["LW"], CR["HST"], CR["HB0"], CR["PREV"]
            Fb = am.alloc([10, 512], F32)
            RAW = am.alloc([2, 516], F32)
            LRAW = am.alloc([1, 516], F32)
            FL = am.alloc([1, 512], F32)
            LB16 = am.alloc([1, 512], BF16)
            LK = am.alloc([8, 128], BF16)
            RB = am.alloc([8, 128], BF16)
            AH = am.alloc([8, 128], BF16)
            VB = am.alloc([1, 512], BF16)
            AB1 = am.alloc([4, 128], BF16)
            AB2 = am.alloc([4, 128], BF16)
            AB3 = am.alloc([4, 64], BF16)
            TM = am.alloc([4, 256], BF16)
            Zr = am.alloc([2, 4, 128], BF16)
            PPr = am.alloc([2, 4, 128], BF16)
            G0TS = am.alloc([8, 64], BF16)
            HINC = am.alloc([8, 64], F32)
            QTS = am.alloc([8, 64], BF16)
            Y0TS = am.alloc([1, 512], F32)
            PTc = am.alloc([1, 8], F32)
            HBs = am.alloc([9, 64], BF16)
            TMPH = am.alloc([1, 64], F32)

            def F(i):
                return Fb[:, i, :]

            def fk(i):
                return ("F", i)

            def col(c):
                return RC[:, 0, c:c + 1]
            cK = [K_("mu"), K_("omu"), K_("cols"), K_("omka")]

            def dve(fn, r, w):
                P.op("dve", fn, r=r, w=w)

            def act(fn, r, w):
                P.op("act", fn, r=r, w=w)

            def pool(fn, r, w):
                P.op("pool", fn, r=r, w=w)

            HB = (slice(0, 64), slice(64, 128))

            i, S = load_wblock(win, R0 + 768, 128)
            mm_fm(0, i, S, 128, HTB, "HTB")
            act(lambda e: e.copy(out=LRAW[:, 0, 1:513], in_=PSB[0][:, :]), [("ps", 0)], ["LRAW"])
            if tb == 0:
                pool(lambda e: e.memset(LRAW[:, 0, 0:1], 0.0), [], ["LRAW0"])
            else:
                pool(lambda e: e.tensor_copy(out=LRAW[:, 0, 0:1], in_=PREV[:, 0, 6:7]), ["PREVL"], ["LRAW0"])
            dve(lambda e: e.tensor_scalar(out=FL[:, 0, :], in0=LRAW[:, 0, 1:513], scalar1=MUL[:, 0, 1:2], scalar2=None,
                                          op0=ALU.mult), ["LRAW", K_("omul")], ["FL"])
            dve(lambda e: e.scalar_tensor_tensor(out=FL[:, 0, :], in0=LRAW[:, 0, 0:512], scalar=MUL[:, 0, 0:1], in1=FL[:, 0, :],
                                                 op0=ALU.mult, op1=ALU.add), ["LRAW", "LRAW0", "FL", K_("mul")], ["FL"])
            pool(lambda e: e.tensor_copy(out=PREV[:, 0, 6:7], in_=LRAW[:, 0, 512:513]), ["LRAW", "LRAW0"], ["PREVL"])
            act(lambda e: e.activation(out=LB16[0:32, 0, :], in_=FL[0:32, 0, :], func=AF.Tanh), ["FL"], ["LB16a"])
            act(lambda e: e.copy(out=LB16[32:64, 0, :], in_=FL[32:64, 0, :]), ["FL"], ["LB16b"])
            act(lambda e: e.activation(out=LB16[64:128, 0, :], in_=FL[64:128, 0, :], func=AF.Sigmoid), ["FL"], ["LB16c"])
            slots = [load_wblock(win, R0 + qi * 256, 256) for qi in range(3)]

            def do_pair(pr):
                ps_ = slice(pr * 128, (pr + 1) * 128)
                for qi in range(3):
                    iq, Sq = slots[qi]
                    bank = qi % 2
                    for kt in range(8):
                        P.op("pe", (lambda e, kt=kt, Sq=Sq, bank=bank: e.matmul(
                            PSB[bank][:, :], lhsT=Sq[:, kt, ps_], rhs=HTB[:, kt, :], start=(kt == 0), stop=(kt == 7))),
                            r=[("WS", iq, kt // 4), ("HTB", kt, 0)], w=[("ps", bank)], nofence=True)
                    act((lambda e, bank=bank, qi=qi: e.copy(out=RAW[:, qi % 2, 1:513], in_=PSB[bank][:, :])), [("ps", bank)], [("RAW", qi % 2)])
                    pc = qi * 2 + pr
                    if tb == 0:
                        pool((lambda e, qi=qi: e.memset(RAW[:, qi % 2, 0:1], 0.0)), [], [("RAW0", qi % 2)])
                    else:
                        pool((lambda e, pc=pc, qi=qi: e.tensor_copy(out=RAW[:, qi % 2, 0:1], in_=PREV[:, 0, pc:pc + 1])),
                             [("PREV", pc)], [("RAW0", qi % 2)])
                    dve((lambda e, qi=qi, pc=pc: e.tensor_scalar(out=F(qi), in0=RAW[:, qi % 2, 1:513], scalar1=col(6 + pc),
                                                                 scalar2=None, op0=ALU.mult)), [("RAW", qi % 2)] + cK, [fk(qi)])
                    dve((lambda e, qi=qi, pc=pc: e.scalar_tensor_tensor(out=F(qi), in0=RAW[:, qi % 2, 0:512], scalar=col(pc),
                                                                        in1=F(qi), op0=ALU.mult, op1=ALU.add)),
                        [("RAW", qi % 2), ("RAW0", qi % 2), fk(qi)] + cK, [fk(qi)])
                    pool((lambda e, pc=pc, qi=qi: e.tensor_copy(out=PREV[:, 0, pc:pc + 1], in_=RAW[:, qi % 2, 512:513])),
                         [("RAW", qi % 2), ("RAW0", qi % 2)], [("PREV", pc)])
                pool(lambda e: e.tensor_copy(out=VB[:, 0, :], in_=F(2)), [fk(2)], ["VB"])
                P.op("pe", lambda e: e.matmul(PSB[2][:, :], lhsT=LW[0:32, 0, ps_], rhs=LB16[0:32, 0, :], start=True, stop=True),
                     r=[K_("lw"), "LB16a"], w=[("ps", 2)])
                act(lambda e: e.activation(out=F(3), in_=PSB[2][:, :], func=AF.Sigmoid, bias=col(12 + pr), scale=1.0),
                    [("ps", 2)] + cK, [fk(3)])
                dve(lambda e: e.tensor_scalar(out=F(3), in0=F(3), scalar1=-0.6065306597126334, scalar2=None, op0=ALU.mult),
                    [fk(3)], [fk(3)])
                P.op("pe", lambda e: e.matmul(PSB[3][:, :], lhsT=LW[32:64, 0, ps_], rhs=LB16[32:64, 0, :], start=True, stop=True),
                     r=[K_("lw"), "LB16b"], w=[("ps", 3)])
                act(lambda e: e.activation(out=F(4), in_=PSB[3][:, :], func=AF.Sigmoid, bias=col(14 + pr), scale=1.0),
                    [("ps", 3)] + cK, [fk(4)])
                P.op("pe", lambda e: e.matmul(PSB[2][:, :], lhsT=LW[64:128, 0, ps_], rhs=LB16[64:128, 0, :], start=True, stop=True),
                     r=[K_("lw"), "LB16c"], w=[("ps", 2)])
                act(lambda e: e.copy(out=F(5), in_=PSB[2][:, :]), [("ps", 2)], [fk(5)])
                dve(lambda e: e.tensor_scalar(out=F(6), in0=F(1), scalar1=col(16 + pr), scalar2=None, op0=ALU.mult),
                    [fk(1)] + cK, [fk(6)])
                pool(lambda e: e.tensor_tensor(out=F(7), in0=F(6), in1=F(6), op=ALU.mult), [fk(6)], [fk(7)])
                P.op("pe", lambda e: e.matmul(PSB[3][:, :], lhsT=OBD[:, :], rhs=F(7), start=True, stop=True),
                     r=["OBD", fk(7)], w=[("ps", 3)])
                act(lambda e: e.activation(out=F(7), in_=PSB[3][:, :], func=AF.Ln, bias=CST[:, 3:4], scale=1.0),
                    [("ps", 3), "cst"], [fk(7)])
                act(lambda e: e.activation(out=F(7), in_=F(7), func=AF.Exp, scale=-0.5), [fk(7)], [fk(7)])
                dve(lambda e: e.tensor_tensor(out=F(6), in0=F(6), in1=F(7), op=ALU.mult), [fk(6), fk(7)], [fk(6)])
                dve(lambda e: e.tensor_scalar(out=F(7), in0=F(4), scalar1=col(18 + pr), scalar2=col(20 + pr), op0=ALU.mult,
                                              op1=ALU.add), [fk(4)] + cK, [fk(7)])
                dve(lambda e: e.tensor_tensor(out=F(1), in0=F(1), in1=F(7), op=ALU.mult), [fk(1), fk(7)], [fk(1)])
                dve(lambda e: e.scalar_tensor_tensor(out=F(7), in0=F(0), scalar=col(22 + pr), in1=F(1), op0=ALU.mult,
                                                     op1=ALU.mult), [fk(0), fk(1)] + cK, [fk(7)])
                P.op("pe", lambda e: e.matmul(PSB[2][:, :], lhsT=OBD[:, :], rhs=F(7), start=True, stop=True),
                     r=["OBD", fk(7)], w=[("ps", 2)])
                dve(lambda e: e.tensor_tensor(out=F(8), in0=PSB[2][:, :], in1=F(2), op=ALU.mult), [("ps", 2), fk(2)], [fk(8)])
                dve(lambda e: e.tensor_tensor(out=F(7), in0=F(6), in1=F(4), op=ALU.mult), [fk(6), fk(4)], [fk(7)])
                for c in range(8):
                    cs = slice(c * 64, (c + 1) * 64)
                    dve((lambda e, cs=cs: e.tensor_tensor_scan(out=Fb[:, 9, cs], data0=ONESW[:, 0:64], data1=Fb[:, 3, cs],
                                                               initial=0.0, op0=ALU.mult, op1=ALU.add)),
                        [fk(3), "ONESW"], [fk(9)])
                dve(lambda e: e.tensor_tensor(out=F(3), in0=F(9), in1=F(3), op=ALU.subtract), [fk(9), fk(3)], [fk(3)])
                act(lambda e: e.activation(out=PTc[:, 0, :], in_=Fb[:, 9, 63::64], func=AF.Exp), [fk(9)], ["PTc"])
                v8c = lambda ap: ap.rearrange("p (c t) -> p c t", t=64)
                act(lambda e: e.activation(out=F(4), in_=F(9), func=AF.Exp), [fk(9), fk(4)], [fk(4)])
                dve(lambda e: e.tensor_tensor(out=RB[:, :, 64:128], in0=v8c(F(0)), in1=v8c(F(4)), op=ALU.mult),
                    [fk(0), fk(4)], ["RBr"])
                act(lambda e: e.activation(out=F(4), in_=F(3), func=AF.Exp), [fk(3), fk(4), "RBr"], [fk(4)])
                dve(lambda e: e.scalar_tensor_tensor(out=RB[:, :, 0:64], in0=v8c(F(6)), scalar=-1.0, in1=v8c(F(4)),
                                                     op0=ALU.mult, op1=ALU.mult), [fk(6), fk(4)], ["RBb"])
                act(lambda e: e.activation(out=F(4), in_=F(9), func=AF.Exp, scale=-1.0), [fk(9), fk(4), "RBb"], [fk(4)])
                dve(lambda e: e.tensor_tensor(out=LK[:, :, 0:64], in0=v8c(F(7)), in1=v8c(F(4)), op=ALU.mult),
                    [fk(7), fk(4)], ["LKa"])
                pool(lambda e: e.tensor_tensor(out=LK[:, :, 64:128], in0=v8c(F(1)), in1=v8c(F(4)), op=ALU.mult),
                     [fk(1), fk(4)], ["LKk"])
                for c in range(8):
                    cs = slice(c * 64, (c + 1) * 64)
                    act((lambda e, c=c, cs=cs: e.activation(out=Fb[:, 3, cs], in_=Fb[:, 9, cs], func=AF.Exp,
                                                            bias=Fb[:, 9, c * 64 + 63:c * 64 + 64], scale=-1.0)),
                        [fk(9), fk(3)], [fk(3)])
                dve(lambda e: e.tensor_tensor(out=AH[:, :, 0:64], in0=v8c(F(7)), in1=v8c(F(3)), op=ALU.mult),
                    [fk(7), fk(3)], ["AHa"])
                pool(lambda e: e.tensor_tensor(out=AH[:, :, 64:128], in0=v8c(F(1)), in1=v8c(F(3)), op=ALU.mult),
                     [fk(1), fk(3)], ["AHk"])

                def do_group(grp):
                    c0 = grp * 4
                    for j in range(4):
                        c = c0 + j
                        for hb in HB:
                            P.op("pe", (lambda e, c=c, j=j, hb=hb: e.matmul(PSB[0][hb, j * 128:(j + 1) * 128], lhsT=LK[hb, c, 0:64],
                                                                            rhs=RB[hb, c, :], start=True, stop=True)),
                                 r=["LKa", "RBr", "RBb"], w=[("ps", 0)])
                            P.op("pe", (lambda e, c=c, j=j, hb=hb: e.matmul(PSB[1][hb, j * 128:(j + 1) * 128], lhsT=LK[hb, c, 64:128],
                                                                            rhs=RB[hb, c, :], start=True, stop=True)),
                                 r=["LKk", "RBr", "RBb"], w=[("ps", 1)])
                            P.op("pe", (lambda e, c=c, j=j, hb=hb: e.matmul(PSB[2][hb, j * 64:(j + 1) * 64], lhsT=RB[hb, c, 0:64],
                                                                            rhs=LK[hb, c, 0:64], start=True, stop=True)),
                                 r=["LKa", "RBb"], w=[("ps", 2)])
                    v4 = lambda ap, w_: ap.rearrange("p (j x) -> p j x", x=w_)
                    m1 = MASK1[:, :].unsqueeze(1).broadcast_to([128, 4, 128])
                    dve(lambda e: e.tensor_tensor(out=AB1[:], in0=v4(PSB[0][:, :], 128), in1=m1, op=ALU.mult),
                        [("ps", 0), "MASK1"], ["AB1"])
                    dve(lambda e: e.tensor_tensor(out=AB2[:], in0=v4(PSB[1][:, :], 128), in1=m1, op=ALU.mult),
                        [("ps", 1), "MASK1"], ["AB2"])
                    dve(lambda e: e.tensor_tensor(out=AB3[:], in0=v4(PSB[2][:, 0:256], 64),
                                                  in1=MASKL[:, :].unsqueeze(1).broadcast_to([128, 4, 64]), op=ALU.mult),
                        [("ps", 2), "MASKL"], ["AB3"])
                    pt3 = PSB[3][:, :].bitcast(BF16)
                    for j in range(4):
                        c = c0 + j
                        for hb in HB:
                            srcs = (RB[hb, c, 0:64], VB[hb, 0, c * 64:(c + 1) * 64], AH[hb, c, 0:64], AH[hb, c, 64:128])
                            for q, src in enumerate(srcs):
                                P.op("pe", (lambda e, j=j, q=q, src=src, hb=hb: e.transpose(
                                    out=pt3[hb, j * 256 + q * 64:j * 256 + (q + 1) * 64], in_=src, identity=identb[hb, hb])),
                                    r=["RBb", "VB", "AHa", "AHk", "identb"], w=[("ps", 3)])
                    act(lambda e: e.copy(out=TM[:], in_=pt3.rearrange("p (j x) -> p j x", x=256)), [("ps", 3)], ["TM"])
                    for j in range(4):
                        for hb in HB:
                            P.op("pe", (lambda e, j=j, hb=hb: e.matmul(PSB[2][hb, 256 + j * 64:256 + (j + 1) * 64], lhsT=AB2[hb, j, 0:64],
                                                                       rhs=TM[hb, j, 64:128], start=True, stop=True)),
                                 r=["AB2", "TM"], w=[("ps", 2)])
                    pool(lambda e: e.tensor_copy(out=Zr[:, 0, :, 0:64], in_=TM[:, :, 0:64]), ["TM"], [("Z", 0)])
                    act(lambda e: e.copy(out=Zr[:, 0, :, 64:128], in_=v4(PSB[2][:, 256:512], 64)), [("ps", 2)], [("Z", 0)])
                    for jj in range(6):
                        zi, zo = jj % 2, (jj + 1) % 2
                        for j in range(4):
                            for hb in HB:
                                if jj == 0:
                                    Pm, PmT, pk = AB3[hb, j, :], AB1[hb, j, 0:64], ["AB3", "AB1"]
                                else:
                                    Pm, PmT = PPr[hb, jj % 2, j, 0:64], PPr[hb, jj % 2, j, 64:128]
                                    pk = [("PP", jj % 2)]
                                P.op("pe", (lambda e, j=j, PmT=PmT, zi=zi, hb=hb: e.matmul(
                                    PSB[5][hb, j * 128:(j + 1) * 128], lhsT=PmT, rhs=Zr[hb, zi, j, :], start=True, stop=True)),
                                    r=pk + [("Z", zi)], w=[("ps", 5)])
                                if jj < 5:
                                    P.op("pe", (lambda e, j=j, Pm=Pm, PmT=PmT, hb=hb: e.matmul(
                                        PSB[4][hb, j * 128:j * 128 + 64], lhsT=PmT, rhs=Pm, start=True, stop=True)),
                                        r=pk, w=[("ps", 4)])
                                    P.op("pe", (lambda e, j=j, Pm=Pm, PmT=PmT, hb=hb: e.matmul(
                                        PSB[4][hb, j * 128 + 64:(j + 1) * 128], lhsT=Pm, rhs=PmT, start=True, stop=True)),
                                        r=pk, w=[("ps", 4)])
                        dve((lambda e, zi=zi, zo=zo: e.tensor_tensor(out=Zr[:, zo], in0=Zr[:, zi],
                                                                   in1=v4(PSB[5][:, :], 128), op=ALU.add)),
                            [("Z", zi), ("ps", 5)], [("Z", zo)])
                        if jj < 5:
                            act((lambda e, jj=jj: e.copy(out=PPr[:, (jj + 1) % 2], in_=v4(PSB[4][:, :], 128))),
                                [("ps", 4)], [("PP", (jj + 1) % 2)])
                    for j in range(4):
                        for hb in HB:
                            ZF = Zr[hb, 0]
                            P.op("pe", (lambda e, j=j, hb=hb, ZF=ZF: e.matmul(PSB[6][hb, j * 64:(j + 1) * 64], lhsT=ZF[:, j, 0:64],
                                                                              rhs=TM[hb, j, 128:192], start=True, stop=True)),
                                 r=[("Z", 0), "TM"], w=[("ps", 6)])
                            P.op("pe", (lambda e, j=j, hb=hb, ZF=ZF: e.matmul(PSB[6][hb, 256 + j * 64:256 + (j + 1) * 64], lhsT=TM[hb, j, 128:192],
                                                                              rhs=ZF[:, j, 64:128], start=True, stop=False)),
                                 r=[("Z", 0), "TM"], w=[("ps", 6)])
                            P.op("pe", (lambda e, j=j, hb=hb: e.matmul(PSB[6][hb, 256 + j * 64:256 + (j + 1) * 64], lhsT=TM[hb, j, 192:256],
                                                                       rhs=TM[hb, j, 64:128], start=False, stop=True)),
                                 r=["TM"], w=[("ps", 6)])
                            P.op("pe", (lambda e, j=j, hb=hb, ZF=ZF: e.matmul(PSB[7][hb, j * 64:(j + 1) * 64], lhsT=ZF[:, j, 0:64],
                                                                              rhs=AB1[hb, j, 64:128], start=True, stop=True)),
                                 r=[("Z", 0), "AB1"], w=[("ps", 7)])
                            P.op("pe", (lambda e, j=j, hb=hb, ZF=ZF: e.matmul(PSB[7][hb, 256 + j * 64:256 + (j + 1) * 64], lhsT=ZF[:, j, 64:128],
                                                                              rhs=AB1[hb, j, 64:128], start=True, stop=False)),
                                 r=[("Z", 0), "AB1"], w=[("ps", 7)])
                            P.op("pe", (lambda e, j=j, hb=hb: e.matmul(PSB[7][hb, 256 + j * 64:256 + (j + 1) * 64], lhsT=TM[hb, j, 64:128],
                                                                       rhs=AB2[hb, j, 64:128], start=False, stop=True)),
                                 r=["TM", "AB2"], w=[("ps", 7)])
                    p6a, p6b = v4(PSB[6][:, 0:256], 64), v4(PSB[6][:, 256:512], 64)
                    p7a, p7b = v4(PSB[7][:, 0:256], 64), v4(PSB[7][:, 256:512], 64)
                    act(lambda e: e.copy(out=G0TS[:, c0:c0 + 4, :], in_=p6a), [("ps", 6)], [("G0TS", grp)])
                    act(lambda e: e.copy(out=HINC[:, c0:c0 + 4, :], in_=p6b), [("ps", 6)], [("HINC", grp)])
                    dve(lambda e: e.tensor_tensor(out=QTS[:, c0:c0 + 4, :], in0=p7a, in1=RB[:, c0:c0 + 4, 64:128],
                                                  op=ALU.add), [("ps", 7), "RBr"], [("QTS", grp)])
                    dve(lambda e: e.tensor_copy(out=v8c(Y0TS[:, 0, :])[:, c0:c0 + 4, :], in_=p7b), [("ps", 7)],
                        [("Y0TS", grp)])
                for grp_ in range(2):
                    do_group(grp_)
                pool(lambda e: e.tensor_copy(out=HBs[:, 0, :], in_=HB0[:, pr, :]), ["HB0"], [("HBs", 0)])
                for c in range(8):
                    grp = c // 4
                    for hb in HB:
                        P.op("pe", (lambda e, c=c, hb=hb: e.matmul(PSB[2][hb, 0:64], lhsT=G0TS[hb, c, :], rhs=HBs[hb, c, :],
                                                                   start=True, stop=True)),
                             r=[("G0TS", grp), ("HBs", c)], w=[("ps", 2)])
                    dve((lambda e, c=c: e.scalar_tensor_tensor(out=TMPH[:, 0, :], in0=HST[:, pr, :], scalar=PTc[:, 0, c:c + 1],
                                                               in1=HINC[:, c, :], op0=ALU.mult, op1=ALU.add)),
                        ["HST", "PTc", ("HINC", grp)], ["TMPH"])
                    dve(lambda e: e.tensor_tensor(out=HST[:, pr, :], in0=TMPH[:, 0, :], in1=PSB[2][:, 0:64], op=ALU.add),
                        ["TMPH", ("ps", 2)], ["HST"])
                    act((lambda e, c=c: e.copy(out=HBs[:, c + 1, :], in_=HST[:, pr, :])), ["HST"], [("HBs", c + 1)])
                pool(lambda e: e.tensor_copy(out=HB0[:, pr, :], in_=HBs[:, 8, :]), [("HBs", 8)], ["HB0"])
                for c in range(8):
                    for hb in HB:
                        P.op("pe", (lambda e, c=c, hb=hb: e.matmul(PSB[3][hb, c * 64:(c + 1) * 64], lhsT=HBs[hb, c, :], rhs=QTS[hb, c, :],
                                                                   start=True, stop=True)),
                             r=[("HBs", c), ("QTS", c // 4)], w=[("ps", 3)])
                dve(lambda e: e.tensor_tensor(out=F(0), in0=PSB[3][:, :], in1=Y0TS[:, 0, :], op=ALU.add),
                    [("ps", 3), ("Y0TS", 0), ("Y0TS", 1), fk(0), "RBr"], [fk(0)])
                P.op("pe", lambda e: e.matmul(PSB[0][:, :], lhsT=O64BD[:, :], rhs=F(0), start=True, stop=True),
                     r=["O64BD", fk(0)], w=[("ps", 0)])
                dve(lambda e: e.tensor_tensor(out=F(0), in0=F(0), in1=PSB[0][:, :], op=ALU.subtract), [fk(0), ("ps", 0)], [fk(0)])
                pool(lambda e: e.tensor_tensor(out=F(4), in0=F(0), in1=F(0), op=ALU.mult), [fk(0), fk(4), "LKa", "LKk"], [fk(4)])
                P.op("pe", lambda e: e.matmul(PSB[1][:, :], lhsT=O64BD[:, :], rhs=F(4), start=True, stop=True),
                     r=["O64BD", fk(4)], w=[("ps", 1)])
                act(lambda e: e.activation(out=F(4), in_=PSB[1][:, :], func=AF.Ln, bias=CST[:, 4:5], scale=1.0),
                    [("ps", 1), "cst"], [fk(4)])
                act(lambda e: e.activation(out=F(4), in_=F(4), func=AF.Exp, scale=-0.5), [fk(4)], [fk(4)])
                dve(lambda e: e.tensor_tensor(out=F(0), in0=F(0), in1=F(4), op=ALU.mult), [fk(0), fk(4)], [fk(0)])
                dve(lambda e: e.tensor_scalar(out=F(0), in0=F(0), scalar1=col(24 + pr), scalar2=col(26 + pr), op0=ALU.mult,
                                              op1=ALU.add), [fk(0)] + cK, [fk(0)])
                dve(lambda e: e.tensor_tensor(out=F(0), in0=F(0), in1=F(8), op=ALU.add), [fk(0), fk(8)], [fk(0)])
                dve(lambda e: e.tensor_tensor(out=YTB[:, 6 + pr, :], in0=F(0), in1=F(5), op=ALU.mult), [fk(0), fk(5)], [("YTB", 6 + pr)])
            for pr_ in range(2):
                do_pair(pr_)

        def outproj(t0):
            oslots = {0: load_wblock(wout, 0, 256)}
            for db in range(4):
                if db + 1 < 4:
                    oslots[db + 1] = load_wblock(wout, (db + 1) * 256, 256)
                i, S = oslots[db]
                for di in range(2):
                    dt_ = db * 2 + di
                    bank = 4 + di
                    for kt in range(8):
                        P.op("pe", (lambda e, kt=kt, di=di, bank=bank, S=S: e.matmul(
                            PSB[bank][:, :], lhsT=S[:, kt, di * 128:(di + 1) * 128], rhs=YTB[:, kt, :],
                            start=(kt == 0), stop=(kt == 7))),
                            r=[("WS", i, kt // 4), ("YTB", kt)], w=[("ps", bank)])
                    P.op("dve", (lambda e, dt_=dt_, bank=bank: e.tensor_tensor(
                        out=XT[:, dt_, t0:t0 + 512], in0=XT[:, dt_, t0:t0 + 512], in1=PSB[bank][:, :], op=ALU.add)),
                        r=[("ps", bank), ("XT", t0 // 512)], w=[("XT", t0 // 512)])

        if "s5" in mixers:
            C5 = s5_setup()
            barrier()
        CR = rw_setup() if "rwkv" in mixers else None
        barrier()
        base_off = am.off
        for tb in range(4):
            t0 = tb * 512
            norm_to(HTB, "HTB", t0, g, 0)
            if "ssd" in mixers:
                ssd_block(tb)
            if "s5" in mixers:
                P.set_fence()
                s5_block(tb, C5)
                P.set_fence()
            if "rwkv" in mixers:
                P.set_fence()
                rw_block(tb, CR)
                P.set_fence()
            if dbg:
                dst = ydbg[s, l].rearrange("(kt p) t -> p kt t", p=128)[:, :, t0:t0 + 512]
                P.dma("sp", (lambda e, dst=dst: e.dma_start(out=dst, in_=YTB[:])),
                      r=[("YTB", k_) for k_ in range(8)], sem="D_ydbg")
            outproj(t0)

    for s in range(NS):
        barrier()
        load_x(s)
        barrier()
        for l in range(NL):
            if do_ffn:
                ffn(l, "ffn1")
            if mixers:
                barrier()
                mixer_phase(s, l)
                barrier()
            if do_ffn:
                ffn(l, "ffn2")
        barrier()
        final_store(s)
    last = {}
    for s_, v in out_toks:
        last[s_] = max(last.get(s_, 0), v)
    P.final_wait("sp", list(last.items()))
    assert ARMAX[0] <= AR_WORDS, ("arena overflow: need words", ARMAX[0])
    P.emit(stack)
    stack.close()
    return nc


L_ = 2
WEIGHT_SHAPES = [
    ("ffn1_norm", (L_, 1024)), ("ffn1_wg", (L_, 1024, 2816)), ("ffn1_wu", (L_, 1024, 2816)),
    ("ffn1_wd", (L_, 2816, 1024)), ("mix_norm", (L_, 1024)), ("w_in", (L_, 1024, 2696)),
    ("w_out", (L_, 1024, 1024)), ("m_A_log", (L_, 8)), ("m_dt_bias", (L_, 8)),
    ("m_conv_w", (L_, 1024, 4)), ("m_conv_b", (L_, 1024)), ("m_D", (L_, 8)),
    ("m_norm_w", (L_, 512)), ("s_A_re", (L_, 16, 64)), ("s_A_im", (L_, 16, 64)),
    ("s_B_re", (L_, 16, 64, 16)), ("s_B_im", (L_, 16, 64, 16)), ("s_C_re", (L_, 16, 16, 64)),
    ("s_C_im", (L_, 16, 16, 64)), ("s_log_dt", (L_, 16)), ("s_D", (L_, 256)),
    ("s_glu_w", (L_, 256, 256)), ("s_glu_b", (L_, 256)), ("r_mu", (L_, 896)),
    ("r_w0", (L_, 256)), ("r_w2", (L_, 32, 256)), ("r_a0", (L_, 256)), ("r_a2", (L_, 32, 256)),
    ("r_g2", (L_, 64, 256)), ("r_k_k", (L_, 256)), ("r_k_a", (L_, 256)), ("r_r_k", (L_, 4, 64)),
    ("r_gn_w", (L_, 256)), ("r_gn_b", (L_, 256)), ("ffn2_norm", (L_, 1024)),
    ("ffn2_wg", (L_, 1024, 2816)), ("ffn2_wu", (L_, 1024, 2816)), ("ffn2_wd", (L_, 2816, 1024)),
    ("final_norm", (1024,)),
]

_CFG = {"nseq": 2, "nlayers": 2, "mix": True, "ffn": True}


def kernel(**inputs):
    n = 8
    nc = bass.Bass("TRN2", target_bir_lowering=False)
    build_program(nc, _CFG)
    x = np.ascontiguousarray(np.asarray(inputs["x"], dtype=np.float32))
    wts = {nm: np.ascontiguousarray(np.asarray(inputs[nm], dtype=np.float32)) for nm, _ in WEIGHT_SHAPES}
    in_maps = []
    for c in range(n):
        m = {"x": x[2 * c:2 * c + 2]}
        m.update(wts)
        in_maps.append(m)
    res = run_bass_kernel_spmd(nc, in_maps, core_ids=list(range(n)))
    return np.concatenate([r["out"] for r in res.results], axis=0)
```

```python
import numpy as np
import concourse.bass as bass
import concourse.mybir as mybir
from concourse.bass_utils import run_bass_kernel_spmd

F32 = mybir.dt.float32
BF16 = mybir.dt.bfloat16
I32 = mybir.dt.int32
ALU = mybir.AluOpType
AF = mybir.ActivationFunctionType
AX = mybir.AxisListType

D = 1024
SEQ = 2048
DFF = 2816
NFT = DFF // 128
DIN = 2696
EPS = 1e-5

ENGS = ("pe", "dve", "act", "pool", "sp")
ROLL = 12000
SELF_SYNC = {"pe": False, "dve": True, "act": True, "pool": True, "sp": True}


class Prog:
    def __init__(self, nc):
        self.nc = nc
        self.ops = {e: [] for e in ENGS}
        self.cnt = {e: 0 for e in ENGS}
        self.gen = {e: 0 for e in ENGS}
        self.seen = {e: {} for e in ENGS}
        self.lastw = {}
        self.readers = {}
        self.dmacnt = {}
        self.semnames = []
        self.fence = {}
        self.fence_done = {e: 0 for e in ENGS}
        self.fence_id = 0

    def _semname(self, n):
        if n not in self.semnames:
            self.semnames.append(n)
        return n

    def _deps(self, eng, reads, writes):
        deps = {}

        def add(tok):
            if tok is None:
                return
            s, v = tok
            if deps.get(s, 0) < v:
                deps[s] = v

        for k in reads:
            add(self.lastw.get(k))
            if isinstance(k, tuple) and k[0] == "ps":
                for tok in self.readers.get(k, ()):
                    if not tok[0].startswith("E_%s_" % eng):
                        add(tok)
        for k in writes:
            add(self.lastw.get(k))
            for tok in self.readers.get(k, ()):
                add(tok)
        waits = []
        seen = self.seen[eng]
        for s, v in deps.items():
            if seen.get(s, 0) >= v:
                continue
            if s.startswith("E_%s_" % eng) and not SELF_SYNC[eng]:
                continue
            seen[s] = v
            waits.append((s, v))
        return waits

    def set_fence(self):
        toks = {}
        for en in ENGS:
            if self.cnt[en] > 0:
                toks["E_%s_%d" % (en, self.gen[en])] = self.cnt[en]
        for sname, v in self.dmacnt.items():
            toks[sname] = v
        self.fence = toks
        self.fence_id += 1

    def _fence_waits(self, eng):
        if self.fence_done[eng] == self.fence_id:
            return []
        self.fence_done[eng] = self.fence_id
        out = []
        seen = self.seen[eng]
        for s, v in self.fence.items():
            if seen.get(s, 0) >= v:
                continue
            if s.startswith("E_%s_" % eng):
                continue
            seen[s] = v
            out.append((s, v))
        return out

    def op(self, eng, fn, r=(), w=(), nofence=False):
        waits = self._deps(eng, r, w)
        if not nofence:
            waits = self._fence_waits(eng) + waits
        if self.cnt[eng] >= ROLL:
            self.gen[eng] += 1
            self.cnt[eng] = 0
        self.cnt[eng] += 1
        sname = self._semname("E_%s_%d" % (eng, self.gen[eng]))
        tok = (sname, self.cnt[eng])
        self.ops[eng].append((waits, fn, sname, 1))
        for k in w:
            self.lastw[k] = tok
            self.readers[k] = []
        for k in r:
            self.readers.setdefault(k, []).append(tok)
        return tok

    def dma(self, q, fn, r=(), w=(), sem=None, nofence=False):
        waits = self._deps(q, r, w)
        if not nofence:
            waits = self._fence_waits(q) + waits
        if sem is None:
            sem = "D_" + str(w[0] if w else r[0])
        sname = self._semname(sem)
        self.dmacnt[sname] = self.dmacnt.get(sname, 0) + 16
        tok = (sname, self.dmacnt[sname])
        self.ops[q].append((waits, fn, sname, 16))
        for k in w:
            self.lastw[k] = tok
            self.readers[k] = []
        for k in r:
            self.readers.setdefault(k, []).append(tok)
        return tok

    def final_wait(self, eng, toks):
        waits = []
        for s, v in toks:
            waits.append((s, v))
        self.ops[eng].append((waits, None, None, 0))

    def emit(self, stack):
        nc = self.nc
        sems = {}
        for n in self.semnames:
            sems[n] = stack.enter_context(nc.semaphore(n))
        block = stack.enter_context(nc.Block())
        engmap = {"pe": block.tensor, "dve": block.vector, "act": block.scalar,
                  "pool": block.gpsimd, "sp": block.sync}

        def mk(elist):
            def body(e):
                for waits, fn, sname, inc in elist:
                    for s, v in waits:
                        e.wait_ge(sems[s], v)
                    if fn is not None:
                        ins = fn(e)
                        ins.then_inc(sems[sname], inc)
            return body

        for en in ENGS:
            if self.ops[en]:
                engmap[en](mk(self.ops[en]))


def build_program(nc, cfg):
    from contextlib import ExitStack
    NS = cfg.get("nseq", 2)
    NL = cfg.get("nlayers", 2)
    do_ffn = cfg.get("ffn", True)
    mixers = cfg.get("mixers", ("ssd", "s5", "rwkv"))
    dbg = cfg.get("dbg", False)

    def din(name, shape):
        return nc.dram_tensor(name, list(shape), F32, kind="ExternalInput").ap()

    x_d = din("x", [NS, SEQ, D])
    W = {}
    for nm, shp in WEIGHT_SHAPES:
        W[nm] = din(nm, shp)
    out_d = nc.dram_tensor("out", [NS, SEQ, D], F32, kind="ExternalOutput").ap()
    if dbg:
        ydbg = nc.dram_tensor("ydbg", [NS, NL, D, SEQ], BF16, kind="ExternalOutput").ap()

    P = Prog(nc)
    stack = ExitStack()

    def sb(name, shape, dt):
        return stack.enter_context(nc.sbuf_tensor(name, list(shape), dt))

    def ps(name, shape, dt):
        return stack.enter_context(nc.psum_tensor(name, list(shape), dt))

    XT = sb("XT", [128, 8, SEQ], F32)
    ident = sb("ident", [128, 128], F32)
    identb = sb("identb", [128, 128], BF16)
    onesf = sb("onesf", [128, 128], F32)
    onesb = sb("onesb", [128, 128], BF16)
    gains = sb("gains", [128, 3 * NL + 1, 8], F32)
    AR_WORDS = 24576
    ARENA = sb("ARENA", [128, AR_WORDS], F32)
    WGU = sb("WGU", [128, 2, 2, 8, 256], BF16)
    STG = sb("STG", [128, 4, 4, 256], F32)
    SQ = sb("SQ", [128, 2, 512], BF16)
    RSTD = sb("RSTD", [128, 512], F32)
    SG = sb("SG", [128, 2, 512], F32)
    CST = sb("CST", [128, 8], F32)
    PSB = [ps("psb%d" % i, [128, 512], F32) for i in range(8)]

    ARMAX = [0]
    cfg['_armax'] = ARMAX

    class Arena:
        def __init__(self):
            self.off = 0

        def alloc(self, shape, dt):
            n = 1
            for d_ in shape:
                n *= d_
            words = (n * (2 if dt == BF16 else 4) + 3) // 4
            words = (words + 7) // 8 * 8
            ARMAX[0] = max(ARMAX[0], self.off + words)
            o_ = self.off if self.off + words <= AR_WORDS else 0
            a = ARENA[:, o_:o_ + words]
            self.off += words
            if dt == BF16:
                a = a.bitcast(BF16)
            a = a[:, 0:n]
            if len(shape) == 2:
                return a.rearrange("p (a b) -> p a b", b=shape[1])
            if len(shape) == 3:
                return a.rearrange("p (a b c) -> p a b c", b=shape[1], c=shape[2])
            return a

    ar = Arena()
    IOB = ar.alloc([2, 1024], F32)
    ar = Arena()
    HT = ar.alloc([8, 1024], BF16)
    ATt = ar.alloc([NFT, 1024], BF16)
    WDb = ar.alloc([2, NFT, 256], BF16)

    P.op("pool", lambda e: e.memset(onesf[:], 1.0), w=["onesf"])
    P.op("pool", lambda e: e.memset(onesb[:], 1.0), w=["onesb"])
    P.op("pool", lambda e: e.memset(CST[:, 0:1], EPS), w=["cst"])
    P.op("pool", lambda e: e.memset(CST[:, 1:2], 1.0), w=["cst"])
    P.op("pool", lambda e: e.memset(CST[:, 2:3], 0.0), w=["cst"])
    P.op("pool", lambda e: e.affine_select(out=ident[:], in_=onesf[:], pattern=[[1, 128]],
                                            compare_op=ALU.is_equal, fill=0.0, base=0,
                                            channel_multiplier=-1),
         r=["onesf"], w=["ident"])
    P.op("pool", lambda e: e.tensor_copy(out=identb[:], in_=ident[:]), r=["ident"], w=["identb"])

    def small_dma(dst, src, key):
        P.dma("sp", (lambda e, dst=dst, src=src: e.dma_start(out=dst, in_=src, allow_slow_non_contiguous=True)),
              w=[key])

    gi = 0
    gidx = {}
    for l in range(NL):
        for nm in ("ffn1_norm", "mix_norm", "ffn2_norm"):
            small_dma(gains[:, gi, :], W[nm][l].rearrange("(kt p) -> p kt", p=128), ("gains", gi))
            gidx[(nm, l)] = gi
            gi += 1
    small_dma(gains[:, gi, :], W["final_norm"].rearrange("(kt p) -> p kt", p=128), ("gains", gi))
    gidx["final"] = gi

    def load_x(s):
        for tb in range(16):
            b = tb % 2
            src = x_d[s, tb * 128:(tb + 1) * 128, :]
            P.dma("sp", (lambda e, b=b, src=src: e.dma_start(out=IOB[:, b, :], in_=src)),
                  w=[("IOB", b)])
            for half in range(2):
                bank = 6 + half
                for j in range(4):
                    kt = half * 4 + j
                    P.op("pe", (lambda e, b=b, kt=kt, j=j, bank=bank: e.transpose(
                        out=PSB[bank][:, j * 128:(j + 1) * 128],
                        in_=IOB[:, b, kt * 128:(kt + 1) * 128], identity=ident[:])),
                        r=[("IOB", b), "ident"], w=[("ps", bank)])
                dst = XT[:, half * 4:(half + 1) * 4, tb * 128:(tb + 1) * 128]
                srcp = PSB[bank][:, :].rearrange("p (a b) -> p a b", b=128)
                if half == 0:
                    P.op("dve", (lambda e, dst=dst, srcp=srcp: e.tensor_copy(out=dst, in_=srcp)),
                         r=[("ps", bank)], w=[("XT", tb // 4)])
                else:
                    P.op("act", (lambda e, dst=dst, srcp=srcp: e.copy(out=dst, in_=srcp)),
                         r=[("ps", bank)], w=[("XT", tb // 4)])

    def rms_stats(tok0, bank=6):
        for kt in range(8):
            b = kt % 2
            P.op("act", (lambda e, b=b, kt=kt: e.activation(
                out=SQ[:, b, :], in_=XT[:, kt, tok0:tok0 + 512], func=AF.Square)),
                r=[("XT", tok0 // 512)], w=[("SQ", b)])
            P.op("pe", (lambda e, b=b, kt=kt: e.matmul(
                PSB[bank][:, :], lhsT=onesb[:], rhs=SQ[:, b, :], start=(kt == 0), stop=(kt == 7))),
                r=[("SQ", b), "onesb"], w=[("ps", bank)])
        P.op("act", (lambda e: e.activation(out=RSTD[:], in_=PSB[bank][:, :], func=AF.Ln,
                                            bias=CST[:, 0:1], scale=1.0 / D)),
             r=[("ps", bank), "cst"], w=["RSTD"])
        P.op("act", (lambda e: e.activation(out=RSTD[:], in_=RSTD[:], func=AF.Exp, scale=-0.5)),
             r=["RSTD"], w=["RSTD"])

    def norm_to(dst, dkey, tok0, g, hoff):
        rms_stats(tok0)
        for kt in range(8):
            P.op("dve", (lambda e, kt=kt: e.scalar_tensor_tensor(
                out=dst[:, kt, hoff:hoff + 512], in0=XT[:, kt, tok0:tok0 + 512],
                scalar=gains[:, g, kt:kt + 1], in1=RSTD[:], op0=ALU.mult, op1=ALU.mult)),
                r=[("XT", tok0 // 512), "RSTD", ("gains", g)], w=[(dkey, kt, hoff // 512)])

    wcount = {"gu": 0, "d": 0, "stg": 0, "slot": 0}

    def wload(dst, src, key, nofence=False, ceng="pool"):
        a = src.shape[1]
        wd_ = src.shape[2]
        sbuf_i = wcount["stg"] % 4
        wcount["stg"] += 1
        P.dma("sp", (lambda e: e.dma_start(out=STG[:, sbuf_i, 0:a, 0:wd_], in_=src)),
              w=[("STG", sbuf_i)], nofence=nofence)
        if ceng == "act":
            P.op("act", (lambda e: e.copy(out=dst, in_=STG[:, sbuf_i, 0:a, 0:wd_])),
                 r=[("STG", sbuf_i)], w=[key], nofence=nofence)
        else:
            P.op("pool", (lambda e: e.tensor_copy(out=dst, in_=STG[:, sbuf_i, 0:a, 0:wd_])),
                 r=[("STG", sbuf_i)], w=[key], nofence=nofence)

    def run_pipelined(gens, depth=2):
        active = []
        it = iter(gens)
        while True:
            while len(active) < depth:
                try:
                    active.append(next(it))
                except StopIteration:
                    break
            if not active:
                break
            for g_ in list(active):
                try:
                    next(g_)
                except StopIteration:
                    active.remove(g_)

    def barrier():
        toks = []
        for en in ENGS:
            if P.cnt[en] > 0:
                toks.append(("E_%s_%d" % (en, P.gen[en]), P.cnt[en]))
        for sname, v in P.dmacnt.items():
            toks.append((sname, v))
        for en in ENGS:
            w_ = []
            for s_, v in toks:
                if P.seen[en].get(s_, 0) < v:
                    P.seen[en][s_] = v
                    w_.append((s_, v))
            if w_:
                P.ops[en].append((w_, None, None, 0))

    def ffn(l, which):
        wg = W[which + "_wg"][l].rearrange("(kt p) f -> p kt f", p=128)
        wu = W[which + "_wu"][l].rearrange("(kt p) f -> p kt f", p=128)
        wd = W[which + "_wd"][l].rearrange("(ft p) d -> p ft d", p=128)
        g = gidx[(which + "_norm", l)]
        for tt in range(SEQ // 1024):
            t0 = tt * 1024
            for half in range(2):
                norm_to(HT, "HT", t0 + half * 512, g, half * 512)
            for fb in range(NFT // 2):
                wb = wcount["gu"] % 2
                wcount["gu"] += 1
                f0 = fb * 256
                for gu, wsrc in ((0, wg), (1, wu)):
                    for kh in range(2):
                        wload(WGU[:, wb, gu, kh * 4:(kh + 1) * 4, :], wsrc[:, kh * 4:(kh + 1) * 4, f0:f0 + 256],
                              ("WGU", wb, gu, kh))
                for fi in range(2):
                    ft = fb * 2 + fi
                    for half in range(2):
                        bg, bu = half * 2, half * 2 + 1
                        for gu, bank in ((0, bg), (1, bu)):
                            for kt in range(8):
                                P.op("pe", (lambda e, wb=wb, gu=gu, kt=kt, fi=fi, half=half, bank=bank: e.matmul(
                                    PSB[bank][:, :], lhsT=WGU[:, wb, gu, kt, fi * 128:(fi + 1) * 128],
                                    rhs=HT[:, kt, half * 512:(half + 1) * 512],
                                    start=(kt == 0), stop=(kt == 7))),
                                    r=[("WGU", wb, gu, kt // 4), ("HT", kt, half)], w=[("ps", bank)])
                        P.op("act", (lambda e, half=half, bg=bg: e.activation(
                            out=SG[:, half, :], in_=PSB[bg][:, :], func=AF.Silu)),
                            r=[("ps", bg)], w=[("SG", half)])
                        P.op("dve", (lambda e, half=half, bu=bu, ft=ft: e.tensor_tensor(
                            out=ATt[:, ft, half * 512:(half + 1) * 512], in0=SG[:, half, :],
                            in1=PSB[bu][:, :], op=ALU.mult)),
                            r=[("SG", half), ("ps", bu)], w=[("AT", ft, half)])
            for db in range(4):
                wb = wcount["d"] % 2
                wcount["d"] += 1
                d0 = db * 256
                for c4 in range(6):
                    a0, a1 = c4 * 4, min(c4 * 4 + 4, NFT)
                    wload(WDb[:, wb, a0:a1, :], wd[:, a0:a1, d0:d0 + 256], ("WD", wb, c4))
                for di in range(2):
                    dt_ = db * 2 + di
                    for half in range(2):
                        bank = 4 + half
                        for ft in range(NFT):
                            P.op("pe", (lambda e, wb=wb, ft=ft, di=di, half=half, bank=bank: e.matmul(
                                PSB[bank][:, :], lhsT=WDb[:, wb, ft, di * 128:(di + 1) * 128],
                                rhs=ATt[:, ft, half * 512:(half + 1) * 512],
                                start=(ft == 0), stop=(ft == NFT - 1))),
                                r=[("WD", wb, ft // 4), ("AT", ft, half)], w=[("ps", bank)])
                        tk = t0 + half * 512
                        P.op("dve", (lambda e, dt_=dt_, tk=tk, bank=bank: e.scalar_tensor_tensor(
                            out=XT[:, dt_, tk:tk + 512], in0=PSB[bank][:, :], scalar=0.5,
                            in1=XT[:, dt_, tk:tk + 512], op0=ALU.mult, op1=ALU.add)),
                            r=[("ps", bank), ("XT", tk // 512)], w=[("XT", tk // 512)])

    out_toks = []

    def final_store(s):
        g = gidx["final"]
        for q in range(4):
            tok0 = q * 512
            rms_stats(tok0)
            for kt in range(8):
                P.op("dve", (lambda e, kt=kt, tok0=tok0: e.scalar_tensor_tensor(
                    out=XT[:, kt, tok0:tok0 + 512], in0=XT[:, kt, tok0:tok0 + 512],
                    scalar=gains[:, g, kt:kt + 1], in1=RSTD[:], op0=ALU.mult, op1=ALU.mult)),
                    r=[("XT", q), "RSTD", ("gains", g)], w=[("XT", q)])
            for tb4 in range(4):
                tb = q * 4 + tb4
                b = tb % 2
                for half in range(2):
                    bank = 6 + half
                    for j in range(4):
                        kt = half * 4 + j
                        P.op("pe", (lambda e, kt=kt, j=j, tb=tb, bank=bank: e.transpose(
                            out=PSB[bank][:, j * 128:(j + 1) * 128],
                            in_=XT[:, kt, tb * 128:(tb + 1) * 128], identity=ident[:])),
                            r=[("XT", q), "ident"], w=[("ps", bank)])
                    if half == 0:
                        P.op("dve", (lambda e, b=b, bank=bank: e.tensor_copy(
                            out=IOB[:, b, 0:512], in_=PSB[bank][:, :])),
                            r=[("ps", bank)], w=[("IOB", b)])
                    else:
                        P.op("act", (lambda e, b=b, bank=bank: e.copy(
                            out=IOB[:, b, 512:1024], in_=PSB[bank][:, :])),
                            r=[("ps", bank)], w=[("IOB", b)])
                dst = out_d[s, tb * 128:(tb + 1) * 128, :]
                tok = P.dma("sp", (lambda e, b=b, dst=dst: e.dma_start(out=dst, in_=IOB[:, b, :])),
                            r=[("IOB", b)], sem="D_out%d" % b)
                out_toks.append(tok)

    CW = sb("CW", [128, NL, 8, 4], F32)
    CBs = sb("CBs", [128, NL, 8], F32)
    DTB = sb("DTB", [8, NL], F32)
    ANEG = sb("ANEG", [8, NL], F32)
    DBC = sb("DBC", [128, NL, 8], F32)
    EH = sb("EH", [8, 8], F32)
    NEH = sb("NEH", [8, 8], F32)
    NEGM = sb("NEGM", [128, 128], F32)
    SEL127 = sb("SEL127", [128, 128], F32)
    ONES8 = sb("ONES8", [8, 128], F32)
    ONESW = sb("ONESW", [128, 132], F32)
    for l in range(NL):
        small_dma(CW[:, l, :, :], W["m_conv_w"][l].rearrange("(t p) k -> p t k", p=128), ("CW", l))
        small_dma(CBs[:, l, :], W["m_conv_b"][l].rearrange("(t p) -> p t", p=128), ("CBs", l))
        small_dma(DTB[:, l:l + 1], W["m_dt_bias"][l].rearrange("(h o) -> h o", o=1), ("DTB", l))
        small_dma(ANEG[:, l:l + 1], W["m_A_log"][l].rearrange("(h o) -> h o", o=1), ("ANEG", l))
        small_dma(DBC[:, l, :], W["m_D"][l:l + 1, :].broadcast_to([128, 8]), ("DBC", l))
        P.op("act", (lambda e, l=l: e.activation(out=ANEG[:, l:l + 1], in_=ANEG[:, l:l + 1], func=AF.Exp)),
             r=[("ANEG", l)], w=[("ANEG", l)])
        P.op("dve", (lambda e, l=l: e.tensor_scalar(out=ANEG[:, l:l + 1], in0=ANEG[:, l:l + 1], scalar1=-1.0,
                                                    scalar2=None, op0=ALU.mult)),
             r=[("ANEG", l)], w=[("ANEG", l)])
    P.op("pool", lambda e: e.memset(ONES8[:], 1.0), w=["ONES8"])
    MASK1 = sb("MASK1", [128, 128], F32)
    MASKL = sb("MASKL", [128, 64], F32)
    OBD = sb("OBD", [128, 128], F32)
    O64BD = sb("O64BD", [128, 128], F32)
    P.op("pool", lambda e: e.memset(OBD[:], 0.0), w=["OBD"])
    P.op("pool", lambda e: e.memset(OBD[0:64, 0:64], 1.0), w=["OBD"])
    P.op("pool", lambda e: e.memset(OBD[64:128, 64:128], 1.0), w=["OBD"])
    P.op("pool", lambda e: e.tensor_scalar(out=O64BD[:], in0=OBD[:], scalar1=1.0 / 64, scalar2=None, op0=ALU.mult),
         r=["OBD"], w=["O64BD"])
    P.op("pool", lambda e: e.memset(CST[:, 3:4], 1e-30), w=["cst"])
    P.op("pool", lambda e: e.memset(CST[:, 4:5], 64e-5), w=["cst"])
    for hb_ in (slice(0, 64), slice(64, 128)):
        P.op("pool", (lambda e, hb_=hb_: e.affine_select(out=MASK1[hb_, 0:64], in_=onesf[hb_, 0:64], pattern=[[1, 64]],
                                                        compare_op=ALU.is_gt, fill=0.0, base=0, channel_multiplier=-1)),
             r=["onesf"], w=["MASK1"])
        P.op("pool", (lambda e, hb_=hb_: e.affine_select(out=MASK1[hb_, 64:128], in_=onesf[hb_, 0:64], pattern=[[1, 64]],
                                                        compare_op=ALU.is_ge, fill=0.0, base=0, channel_multiplier=-1)),
             r=["onesf"], w=["MASK1"])
        P.op("pool", (lambda e, hb_=hb_: e.affine_select(out=MASKL[hb_, :], in_=onesf[hb_, 0:64], pattern=[[-1, 64]],
                                                        compare_op=ALU.is_gt, fill=0.0, base=0, channel_multiplier=1)),
             r=["onesf"], w=["MASKL"])
    P.op("pool", lambda e: e.memset(ONESW[:], 1.0), w=["ONESW"])
    P.op("pool", lambda e: e.memset(EH[:], 1.0), w=["EH"])
    P.op("pool", lambda e: e.affine_select(out=EH[:], in_=EH[:], pattern=[[-1, 8]],
                                            compare_op=ALU.is_equal, fill=0.0, base=0, channel_multiplier=1),
         r=["EH"], w=["EH"])
    P.op("pool", lambda e: e.tensor_scalar(out=NEH[:], in0=EH[:], scalar1=-1.0, scalar2=None, op0=ALU.mult),
         r=["EH"], w=["NEH"])
    P.op("pool", lambda e: e.memset(NEGM[:], 0.0), w=["NEGM"])
    P.op("pool", lambda e: e.affine_select(out=NEGM[:], in_=NEGM[:], pattern=[[1, 128]],
                                            compare_op=ALU.is_ge, fill=-30000.0, base=0, channel_multiplier=-1),
         r=["NEGM"], w=["NEGM"])
    P.op("pool", lambda e: e.affine_select(out=SEL127[:], in_=onesf[:], pattern=[[0, 128]],
                                            compare_op=ALU.is_equal, fill=0.0, base=-127, channel_multiplier=1),
         r=["onesf"], w=["SEL127"])

    win_all = [W["w_in"][l].rearrange("(kt p) c -> p kt c", p=128) for l in range(NL)]
    wout_all = [W["w_out"][l].rearrange("(kt p) c -> p kt c", p=128) for l in range(NL)]

    def wslot():
        i = wcount["slot"] % 4
        wcount["slot"] += 1
        return i, WGU[:, i // 2, i % 2]

    def load_wblock(wsrc, c0, width):
        i, S = wslot()
        for kh in range(2):
            wload(S[:, kh * 4:(kh + 1) * 4, 0:width], wsrc[:, kh * 4:(kh + 1) * 4, c0:c0 + width], ("WS", i, kh), nofence=True, ceng="act")
        return i, S

    def mm_fm(bank, i, S, width, rhs_t, rkey, m0=0):
        for kt in range(8):
            P.op("pe", (lambda e, kt=kt: e.matmul(
                PSB[bank][m0:m0 + width, :], lhsT=S[:, kt, 0:width], rhs=rhs_t[:, kt, :],
                start=(kt == 0), stop=(kt == 7))),
                r=[("WS", i, kt // 4), (rkey, kt, 0)], w=[("ps", bank)], nofence=True)

    def mixer_phase(s, l):
        am = Arena()
        HTB = am.alloc([8, 512], BF16)
        YTB = am.alloc([8, 512], BF16)
        TAIL = am.alloc([8, 4], F32)
        STATE = am.alloc([8, 64], F32)
        STATEB = am.alloc([8, 64], BF16)
        NORMW = am.alloc([1, 512], F32)
        small_dma(NORMW[:, 0, :], W["m_norm_w"][l:l + 1, :].broadcast_to([128, 512]), ("NORMW", l))
        base_off = am.off
        g = gidx[("mix_norm", l)]
        win = win_all[l]
        wout = wout_all[l]

        P.op("pool", lambda e: e.memset(STATE[:], 0.0), w=["STATE"])
        P.op("pool", lambda e: e.memset(STATEB[:], 0.0), w=["STATEB"])
        P.op("pool", lambda e: e.memset(YTB[:], 0.0), w=[("YTB", k_) for k_ in range(8)])

        def ssd_block(tb):
            am.off = base_off
            XS = am.alloc([4, 512], F32)
            BC = am.alloc([4, 512], BF16)
            ACC = am.alloc([2, 512], F32)
            D8 = am.alloc([4, 512], F32)
            SM = am.alloc([2, 64], F32)
            XDT = am.alloc([2, 512], BF16)
            XDS = am.alloc([2, 512], BF16)
            XSD = am.alloc([2, 512], F32)
            BTM = am.alloc([2, 256], BF16)
            LT = am.alloc([2, 512], F32)
            GT = am.alloc([4, 512], BF16)
            Y1 = am.alloc([4, 512], F32)
            SZ = am.alloc([2, 512], F32)
            YTM = am.alloc([2, 512], BF16)
            SS = am.alloc([2, 8], F32)
            xslots = {0: load_wblock(win, 512, 128)}
            for j in range(8):
                bank = j % 2
                if j + 1 < 8:
                    xslots[j + 1] = load_wblock(win, 512 + 128 * (j + 1), 128)
                i, S = xslots[j]
                mm_fm(bank, i, S, 128, HTB, "HTB")
                a = j % 2
                pb = PSB[bank]
                P.op("dve", (lambda e, a=a, pb=pb, j=j: e.tensor_scalar(
                    out=ACC[:, a, :], in0=pb[:, :], scalar1=CW[:, l, j, 3:4], scalar2=CBs[:, l, j:j + 1],
                    op0=ALU.mult, op1=ALU.add)),
                    r=[("ps", bank), ("CW", l), ("CBs", l)], w=[("ACC", a)])
                for jj in range(3):
                    sh = 3 - jj
                    P.op("dve", (lambda e, a=a, pb=pb, j=j, jj=jj, sh=sh: e.scalar_tensor_tensor(
                        out=ACC[:, a, sh:512], in0=pb[:, 0:512 - sh], scalar=CW[:, l, j, jj:jj + 1],
                        in1=ACC[:, a, sh:512], op0=ALU.mult, op1=ALU.add)),
                        r=[("ps", bank), ("ACC", a)], w=[("ACC", a)])
                    if tb > 0:
                        P.op("dve", (lambda e, a=a, j=j, jj=jj, sh=sh: e.scalar_tensor_tensor(
                            out=ACC[:, a, 0:sh], in0=TAIL[:, j, 3 - sh:3], scalar=CW[:, l, j, jj:jj + 1],
                            in1=ACC[:, a, 0:sh], op0=ALU.mult, op1=ALU.add)),
                            r=[("TAIL", j), ("ACC", a)], w=[("ACC", a)])
                P.op("dve", (lambda e, pb=pb, j=j: e.tensor_copy(out=TAIL[:, j, 0:3], in_=pb[:, 509:512])),
                     r=[("ps", bank), ("ACC", a)], w=[("TAIL", j)])
                dst = XS[:, j, :] if j < 4 else BC[:, j - 4, :]
                dkey = ("XS", j) if j < 4 else ("BC", j - 4)
                P.op("act", (lambda e, a=a, dst=dst: e.activation(out=dst, in_=ACC[:, a, :], func=AF.Silu)),
                     r=[("ACC", a)], w=[dkey])
            i, S = load_wblock(win, 1536, 8)
            mm_fm(2, i, S, 8, HTB, "HTB")
            P.op("act", lambda e: e.activation(out=D8[0:8, 0, :], in_=PSB[2][0:8, :], func=AF.Exp,
                                               bias=DTB[:, l:l + 1], scale=1.0),
                 r=[("ps", 2), ("DTB", l)], w=["DTE"])
            P.op("act", lambda e: e.activation(out=D8[0:8, 1, :], in_=D8[0:8, 0, :], func=AF.Ln,
                                               bias=CST[0:8, 1:2], scale=1.0),
                 r=["DTE", "cst"], w=["DT"])
            P.op("dve", lambda e: e.tensor_scalar(out=D8[0:8, 2, :], in0=D8[0:8, 1, :], scalar1=ANEG[:, l:l + 1],
                                                  scalar2=None, op0=ALU.mult),
                 r=["DT", ("ANEG", l)], w=["DA"])
            for c in range(4):
                P.op("dve", (lambda e, c=c: e.tensor_tensor_scan(
                    out=D8[0:8, 3, c * 128:(c + 1) * 128], data0=ONES8[:, :], data1=D8[0:8, 2, c * 128:(c + 1) * 128],
                    initial=0.0, op0=ALU.mult, op1=ALU.add)),
                    r=["DA", "ONES8"], w=[("ACS", c)])
            iz0, SZ0 = load_wblock(win, 0, 256)
            iz1, SZ1 = load_wblock(win, 256, 256)
            def chunk(c):
                cs = slice(c * 128, (c + 1) * 128)
                b = c % 2
                P.op("pe", (lambda e, cs=cs: e.transpose(out=PSB[2][:, 0:8], in_=D8[0:8, 1, cs], identity=ident[0:8, 0:8])),
                     r=["DT", "ident"], w=[("ps", 2)])
                P.op("pe", (lambda e, cs=cs: e.transpose(out=PSB[2][:, 8:16], in_=D8[0:8, 3, cs], identity=ident[0:8, 0:8])),
                     r=[("ACS", c), "ident"], w=[("ps", 2)])
                P.op("dve", (lambda e, b=b: e.tensor_copy(out=SM[:, b, 0:16], in_=PSB[2][:, 0:16])),
                     r=[("ps", 2)], w=[("SM", b, 0)])
                P.op("pe", (lambda e, b=b: e.matmul(PSB[2][:, 16:24], lhsT=SEL127[:], rhs=SM[:, b, 8:16],
                                                   start=True, stop=True)),
                     r=[("SM", b, 0), "SEL127"], w=[("ps", 2)])
                P.op("dve", (lambda e, b=b: e.tensor_tensor(out=SM[:, b, 16:24], in0=PSB[2][:, 16:24],
                                                           in1=SM[:, b, 8:16], op=ALU.subtract)),
                     r=[("ps", 2), ("SM", b, 0)], w=[("SM", b, 1)])
                P.op("act", (lambda e, b=b: e.activation(out=SM[:, b, 24:32], in_=SM[:, b, 16:24], func=AF.Exp)),
                     r=[("SM", b, 1)], w=[("SM", b, 2)])
                P.op("act", (lambda e, b=b: e.activation(out=SM[:, b, 32:40], in_=PSB[2][:, 16:24], func=AF.Exp)),
                     r=[("ps", 2)], w=[("SM", b, 3)])
                P.op("act", (lambda e, b=b: e.activation(out=SM[:, b, 40:48], in_=SM[:, b, 8:16], func=AF.Exp)),
                     r=[("SM", b, 0)], w=[("SM", b, 4)])
                P.op("dve", (lambda e, b=b: e.tensor_tensor(out=SM[:, b, 48:56], in0=SM[:, b, 0:8],
                                                           in1=SM[:, b, 24:32], op=ALU.mult)),
                     r=[("SM", b, 0), ("SM", b, 2)], w=[("SM", b, 5)])
                P.op("dve", (lambda e, b=b: e.tensor_scalar(out=SM[:, b, 56:64], in0=SM[:, b, 8:16], scalar1=-1.0,
                                                           scalar2=None, op0=ALU.mult)),
                     r=[("SM", b, 0)], w=[("SM", b, 6)])

                def bc8(ap):
                    return ap.unsqueeze(2).broadcast_to([128, 8, 64])

                def v8(ap):
                    return ap.rearrange("p (h d) -> p h d", d=64)

                yield
                for i4 in range(4):
                    P.op("pe", (lambda e, i4=i4, cs=cs: e.transpose(out=PSB[0][:, i4 * 128:(i4 + 1) * 128],
                                                                    in_=XS[:, i4, cs], identity=ident[:])),
                         r=[("XS", i4), "ident"], w=[("ps", 0)])
                P.op("dve", (lambda e, b=b: e.tensor_tensor(out=v8(XDT[:, b, :]), in0=v8(PSB[0][:, :]),
                                                           in1=bc8(SM[:, b, 0:8]), op=ALU.mult)),
                     r=[("ps", 0), ("SM", b, 0)], w=[("XDT", b)])
                P.op("dve", (lambda e, b=b: e.tensor_tensor(out=v8(XDS[:, b, :]), in0=v8(PSB[0][:, :]),
                                                           in1=bc8(SM[:, b, 48:56]), op=ALU.mult)),
                     r=[("ps", 0), ("SM", b, 5)], w=[("XDS", b)])
                P.op("dve", (lambda e, b=b: e.tensor_tensor(out=v8(XSD[:, b, :]), in0=v8(PSB[0][:, :]),
                                                           in1=bc8(DBC[:, l, :]), op=ALU.mult)),
                     r=[("ps", 0), ("DBC", l)], w=[("XSD", b)])
                yield
                pbt = PSB[2][:, 256:384].bitcast(BF16)
                for g2 in range(2):
                    P.op("pe", (lambda e, g2=g2, cs=cs: e.transpose(out=pbt[:, g2 * 128:(g2 + 1) * 128],
                                                                    in_=BC[:, g2, cs], identity=identb[:])),
                         r=[("BC", g2), "identb"], w=[("ps", 2)])
                P.op("act", (lambda e, b=b: e.copy(out=BTM[:, b, :], in_=pbt)),
                     r=[("ps", 2)], w=[("BTM", b)])
                for g2 in range(2):
                    P.op("pe", (lambda e, g2=g2, cs=cs: e.matmul(PSB[3][:, b * 256 + g2 * 128:b * 256 + (g2 + 1) * 128],
                                                                 lhsT=BC[:, g2, cs], rhs=BC[:, 2 + g2, cs],
                                                                 start=True, stop=True)),
                         r=[("BC", g2), ("BC", 2 + g2)], w=[("ps", 3)])
                yield
                for g2 in range(2):
                    P.op("pe", (lambda e, g2=g2, cs=cs: e.matmul(
                        PSB[6][:, g2 * 256:(g2 + 1) * 256], lhsT=BC[:, 2 + g2, cs],
                        rhs=STATEB[:, g2 * 4:(g2 + 1) * 4, :].rearrange("p h d -> p (h d)"),
                        start=True, stop=True)),
                        r=[("BC", 2 + g2), "STATEB"], w=[("ps", 6)])
                P.op("dve", (lambda e, b=b: e.tensor_tensor(out=v8(Y1[:, b * 2, :]), in0=v8(PSB[6][:, :]),
                                                           in1=bc8(SM[:, b, 40:48]), op=ALU.mult)),
                     r=[("ps", 6), ("SM", b, 4)], w=[("Y1", b, 0)])
                for g2 in range(2):
                    P.op("pe", (lambda e, g2=g2, b=b: e.matmul(
                        PSB[7][:, g2 * 256:(g2 + 1) * 256], lhsT=BTM[:, b, g2 * 128:(g2 + 1) * 128],
                        rhs=XDS[:, b, g2 * 256:(g2 + 1) * 256], start=True, stop=True)),
                        r=[("BTM", b), ("XDS", b)], w=[("ps", 7)])
                P.op("dve", (lambda e, b=b: e.tensor_tensor(out=STATE[:], in0=STATE[:], in1=bc8(SM[:, b, 32:40]),
                                                           op=ALU.mult)),
                     r=["STATE", ("SM", b, 3)], w=["STATE"])
                P.op("dve", lambda e: e.tensor_tensor(out=STATE[:], in0=STATE[:], in1=v8(PSB[7][:, :]), op=ALU.add),
                     r=["STATE", ("ps", 7)], w=["STATE"])
                P.op("dve", lambda e: e.tensor_copy(out=STATEB[:], in_=STATE[:]),
                     r=["STATE"], w=["STATEB"])

                yield
                for g2 in range(2):
                    for hh in range(4):
                        h = g2 * 4 + hh
                        o = PSB[4][:, hh * 128:(hh + 1) * 128]
                        P.op("pe", (lambda e, o=o, h=h, cs=cs: e.matmul(o, lhsT=EH[:, h:h + 1].broadcast_to([8, 128]), rhs=D8[0:8, 3, cs],
                                                                        start=True, stop=False)),
                             r=["EH", ("ACS", c)], w=[("ps", 4)])
                        P.op("pe", (lambda e, o=o: e.matmul(o, lhsT=ident[:], rhs=NEGM[:], start=False, stop=True)),
                             r=["ident", "NEGM"], w=[("ps", 4)])
                    for hh in range(4):
                        h = g2 * 4 + hh
                        P.op("act", (lambda e, g2=g2, hh=hh, h=h, b=b: e.activation(
                            out=LT[:, g2, hh * 128:(hh + 1) * 128], in_=PSB[4][:, hh * 128:(hh + 1) * 128], func=AF.Exp,
                            bias=SM[:, b, 56 + h:57 + h], scale=1.0)),
                            r=[("ps", 4), ("SM", b, 6)], w=[("LT", g2)])
                    P.op("dve", (lambda e, g2=g2: e.tensor_tensor(
                        out=GT[:, b * 2 + g2, :].rearrange("p (h l) -> p h l", l=128),
                        in0=LT[:, g2, :].rearrange("p (h l) -> p h l", l=128),
                        in1=PSB[3][:, b * 256 + g2 * 128:b * 256 + (g2 + 1) * 128].unsqueeze(1).broadcast_to([128, 4, 128]),
                        op=ALU.mult)),
                        r=[("LT", g2), ("ps", 3)], w=[("GT", b, g2)])
                yield
                for h in range(8):
                    g2, hh = h // 4, h % 4
                    P.op("pe", (lambda e, h=h, g2=g2, hh=hh, b=b: e.matmul(
                        PSB[5][:, h * 64:(h + 1) * 64], lhsT=GT[:, b * 2 + g2, hh * 128:(hh + 1) * 128],
                        rhs=XDT[:, b, h * 64:(h + 1) * 64], start=True, stop=True)),
                        r=[("GT", b, g2), ("XDT", b)], w=[("ps", 5)])
                P.op("dve", lambda e: e.tensor_tensor(out=Y1[:, b * 2, :], in0=Y1[:, b * 2, :], in1=PSB[5][:, :], op=ALU.add),
                     r=[("ps", 5), ("Y1", b, 0)], w=[("Y1", b, 0)])
                yield
                for half, (iz, Sz) in enumerate(((iz0, SZ0), (iz1, SZ1))):
                    for kt in range(8):
                        P.op("pe", (lambda e, half=half, Sz=Sz, kt=kt, cs=cs: e.matmul(
                            PSB[1][:, half * 256:(half + 1) * 256], lhsT=HTB[:, kt, cs], rhs=Sz[:, kt, 0:256],
                            start=(kt == 0), stop=(kt == 7))),
                            r=[("WS", iz, kt // 4), ("HTB", kt, 0)], w=[("ps", 1)])
                P.op("act", lambda e: e.activation(out=SZ[:, b, :], in_=PSB[1][:, :], func=AF.Silu),
                     r=[("ps", 1)], w=[("SZ", b)])
                yield
                P.op("dve", (lambda e, b=b: e.tensor_tensor(out=Y1[:, b * 2, :], in0=Y1[:, b * 2, :], in1=XSD[:, b, :], op=ALU.add)),
                     r=[("XSD", b), ("Y1", b, 0)], w=[("Y1", b, 0)])
                P.op("dve", lambda e: e.tensor_tensor(out=Y1[:, b * 2, :], in0=Y1[:, b * 2, :], in1=SZ[:, b, :], op=ALU.mult),
                     r=[("SZ", b), ("Y1", b, 0)], w=[("Y1", b, 0)])
                P.op("act", lambda e: e.activation(out=Y1[:, b * 2 + 1, :], in_=Y1[:, b * 2, :], func=AF.Square,
                                                   accum_out=SS[:, b, 0:1]),
                     r=[("Y1", b, 0)], w=[("Y1", b, 1), ("SS", b)])
                P.op("act", lambda e: e.activation(out=SS[:, b, 1:2], in_=SS[:, b, 0:1], func=AF.Ln,
                                                   bias=CST[:, 0:1], scale=1.0 / 512),
                     r=[("SS", b), "cst"], w=[("SS1", b)])
                P.op("act", lambda e: e.activation(out=SS[:, b, 2:3], in_=SS[:, b, 1:2], func=AF.Exp, scale=-0.5),
                     r=[("SS1", b)], w=[("SS2", b)])
                P.op("dve", (lambda e, b=b: e.scalar_tensor_tensor(
                    out=YTM[:, b, :], in0=Y1[:, b * 2, :], scalar=SS[:, b, 2:3], in1=NORMW[:, 0, :],
                    op0=ALU.mult, op1=ALU.mult)),
                    r=[("Y1", b, 0), ("SS2", b), ("NORMW", l)], w=[("YTM", b)])
                pyt = PSB[7][:, 256:512].bitcast(BF16)
                for i4 in range(4):
                    P.op("pe", (lambda e, i4=i4, b=b: e.transpose(out=pyt[:, i4 * 128:(i4 + 1) * 128],
                                                                  in_=YTM[:, b, i4 * 128:(i4 + 1) * 128],
                                                                  identity=identb[:])),
                         r=[("YTM", b), "identb"], w=[("ps", 7)])
                P.op("act", (lambda e, cs=cs: e.copy(out=YTB[:, 0:4, cs], in_=pyt.rearrange("p (a t) -> p a t", t=128))),
                     r=[("ps", 7)], w=[("YTB", k_) for k_ in range(4)])
            run_pipelined([chunk(c_) for c_ in range(4)], 2)

        RWPRE = {}

        def rw_prefetch():
            if "rwkv" in mixers:
                RWPRE["lora"] = load_wblock(win, 1800 + 768, 128)
                RWPRE["rkv"] = [load_wblock(win, 1800 + qi * 256, 256) for qi in range(3)]

        TWO_PI = 6.283185307179586
        PI = 3.141592653589793

        def s5_setup():
            SC = am.alloc([20, 8], F32)
            MAG = am.alloc([1, 8], F32)
            NS128 = am.alloc([1, 8], F32)
            COST = am.alloc([8, 129], F32)
            SINT = am.alloc([8, 129], F32)
            LBR = am.alloc([8, 128], BF16)
            LBI = am.alloc([8, 128], BF16)
            LCR = am.alloc([8, 64], BF16)
            LCI = am.alloc([8, 64], BF16)
            SDG = am.alloc([1, 4], F32)
            GLW = am.alloc([2, 256], BF16)
            CARR = am.alloc([1, 8], F32)
            CARI = am.alloc([1, 8], F32)
            mark = am.off
            TAU = am.alloc([1, 129], F32)
            ARG = am.alloc([8, 129], F32)
            TMP1 = am.alloc([8, 129], F32)
            TMP2 = am.alloc([8, 129], F32)
            BN2 = am.alloc([2, 8, 64], F32)
            BN2b = am.alloc([2, 8, 64], BF16)
            CNat = am.alloc([2, 8, 128], F32)
            CT = am.alloc([4, 256], F32)
            st = {"n": 0}

            def K_(name):
                return ("s5c", name)

            def col(c):
                return SC[:, c, :]

            log_dt = W["s_log_dt"][l].rearrange("(gp gl) -> gl gp", gl=2)
            for gl in range(2):
                small_dma(SC[gl * 64:(gl + 1) * 64, 0, :], log_dt[gl:gl + 1, :].broadcast_to([64, 8]), K_(("ldt", gl)))
            small_dma(SC[:, 1, :], W["s_A_re"][l].rearrange("(gp gl) p -> (gl p) gp", gl=2), K_("are"))
            small_dma(SC[:, 2, :], W["s_A_im"][l].rearrange("(gp gl) p -> (gl p) gp", gl=2), K_("aim"))
            small_dma(SDG[:, 0, 0:2], W["s_D"][l].rearrange("(t p) -> p t", p=128), K_("sd"))
            small_dma(SDG[:, 0, 2:4], W["s_glu_b"][l].rearrange("(t p) -> p t", p=128), K_("glb"))
            wload(GLW[:, :, :], W["s_glu_w"][l].rearrange("(kt p) c -> p kt c", p=128), K_("glw"))

            def dv(fn, r, w):
                P.op("dve", fn, r=[K_(x) for x in r], w=[K_(x) for x in w])

            def ac(fn, r, w):
                P.op("act", fn, r=[K_(x) for x in r], w=[K_(x) for x in w])

            ac(lambda e: e.activation(out=col(0), in_=col(0), func=AF.Exp), [("ldt", 0), ("ldt", 1)], ["dt"])
            dv(lambda e: e.tensor_tensor(out=col(3), in0=col(1), in1=col(0), op=ALU.mult), ["are", "dt"], ["ar"])
            dv(lambda e: e.tensor_tensor(out=col(4), in0=col(2), in1=col(0), op=ALU.mult), ["aim", "dt"], ["th"])
            ac(lambda e: e.activation(out=MAG[:, 0, :], in_=col(3), func=AF.Exp), ["ar"], ["mag"])

            def sincos(th, out_s, out_c, t1, t2, rk, wk):
                for phase, out in ((0.0, out_s), (PI / 2, out_c)):
                    tag = "s" if phase == 0.0 else "c"
                    dv(lambda e: e.tensor_scalar(out=t1, in0=th, scalar1=1.0 / TWO_PI, scalar2=phase / TWO_PI + 0.5,
                                                 op0=ALU.mult, op1=ALU.add), rk, [wk + "t1"])
                    t2i = t2.bitcast(I32)
                    dv(lambda e: e.tensor_copy(out=t2i, in_=t1), [wk + "t1"], [wk + "t2"])
                    dv(lambda e: e.tensor_copy(out=t1, in_=t2i), [wk + "t2"], [wk + "t1"])
                    dv(lambda e: e.scalar_tensor_tensor(out=t1, in0=t1, scalar=-TWO_PI, in1=th, op0=ALU.mult,
                                                        op1=ALU.add), [wk + "t1"] + rk, [wk + "t1"])
                    if phase != 0.0:
                        dv(lambda e: e.tensor_scalar(out=t1, in0=t1, scalar1=phase, scalar2=None, op0=ALU.add),
                           [wk + "t1"], [wk + "t1"])
                    dv(lambda e: e.tensor_scalar(out=t2, in0=t1, scalar1=PI, scalar2=-TWO_PI, op0=ALU.is_gt,
                                                 op1=ALU.mult), [wk + "t1"], [wk + "t2"])
                    dv(lambda e: e.tensor_tensor(out=t1, in0=t1, in1=t2, op=ALU.add), [wk + "t1", wk + "t2"], [wk + "t1"])
                    dv(lambda e: e.tensor_scalar(out=t2, in0=t1, scalar1=-PI, scalar2=TWO_PI, op0=ALU.is_lt,
                                                 op1=ALU.mult), [wk + "t1"], [wk + "t2"])
                    dv(lambda e: e.tensor_tensor(out=t1, in0=t1, in1=t2, op=ALU.add), [wk + "t1", wk + "t2"], [wk + "t1"])
                    ac(lambda e, out=out: e.activation(out=out, in_=t1, func=AF.Sin), [wk + "t1"], [wk + tag])

            sincos(col(4), col(5), col(6), col(18), col(19), ["th"], "sc0")
            dv(lambda e: e.tensor_tensor(out=col(7), in0=MAG[:, 0, :], in1=col(6), op=ALU.mult), ["mag", "sc0c"], ["lr"])
            dv(lambda e: e.tensor_tensor(out=col(8), in0=MAG[:, 0, :], in1=col(5), op=ALU.mult), ["mag", "sc0s"], ["li"])
            dv(lambda e: e.tensor_tensor(out=col(9), in0=col(1), in1=col(1), op=ALU.mult), ["are"], ["den"])
            dv(lambda e: e.tensor_tensor(out=col(13), in0=col(2), in1=col(2), op=ALU.mult), ["aim"], ["den2"])
            dv(lambda e: e.tensor_tensor(out=col(9), in0=col(9), in1=col(13), op=ALU.add), ["den", "den2"], ["den"])
            dv(lambda e: e.reciprocal(out=col(9), in_=col(9)), ["den"], ["den"])
            dv(lambda e: e.tensor_scalar(out=col(10), in0=col(7), scalar1=-1.0, scalar2=None, op0=ALU.add), ["lr"], ["nr"])
            dv(lambda e: e.tensor_tensor(out=col(11), in0=col(10), in1=col(1), op=ALU.mult), ["nr", "are"], ["cr"])
            dv(lambda e: e.tensor_tensor(out=col(13), in0=col(8), in1=col(2), op=ALU.mult), ["li", "aim", "den2", "den"], ["tmp13"])
            dv(lambda e: e.tensor_tensor(out=col(11), in0=col(11), in1=col(13), op=ALU.add), ["cr", "tmp13"], ["cr"])
            dv(lambda e: e.tensor_tensor(out=col(11), in0=col(11), in1=col(9), op=ALU.mult), ["cr", "den"], ["cr"])
            dv(lambda e: e.tensor_tensor(out=col(12), in0=col(8), in1=col(1), op=ALU.mult), ["li", "are"], ["ci"])
            dv(lambda e: e.tensor_tensor(out=col(13), in0=col(10), in1=col(2), op=ALU.mult), ["nr", "aim", "cr"], ["tmp13"])
            dv(lambda e: e.tensor_tensor(out=col(12), in0=col(12), in1=col(13), op=ALU.subtract), ["ci", "tmp13"], ["ci"])
            dv(lambda e: e.tensor_tensor(out=col(12), in0=col(12), in1=col(9), op=ALU.mult), ["ci", "den"], ["ci"])
            dv(lambda e: e.tensor_tensor_scan(out=TAU[:, 0, :], data0=ONESW[:, 0:129],
                                              data1=ONESW[:, 0:129], initial=-1.0, op0=ALU.mult, op1=ALU.add),
               [], ["tau"])
            dv(lambda e: e.tensor_tensor(out=ARG[:], in0=TAU[:, 0:1, :].broadcast_to([128, 8, 129]),
                                         in1=col(4).unsqueeze(2).broadcast_to([128, 8, 129]), op=ALU.mult),
               ["tau", "th"], ["arg"])
            sincos(ARG[:], SINT[:], COST[:], TMP1[:], TMP2[:], ["arg"], "sc1")
            dv(lambda e: e.tensor_scalar(out=NS128[:, 0, :], in0=SINT[:, :, 128], scalar1=-1.0, scalar2=None, op0=ALU.mult),
               ["sc1s"], ["ns128"])
            for ri, nm in enumerate(("s_B_re", "s_B_im")):
                P.op("pool", (lambda e, ri=ri: e.memset(BN2[:, ri], 0.0)), w=[K_(("bn2", ri))])
                srcb = W[nm][l].rearrange("(gp gl) p h -> gl p gp h", gl=2)
                for gl in range(2):
                    for par in range(2):
                        P.dma("sp", (lambda e, ri=ri, gl=gl, par=par, srcb=srcb: e.dma_start(
                            out=BN2[gl * 64:(gl + 1) * 64, ri, par::2, par * 32 + gl * 16:par * 32 + (gl + 1) * 16],
                            in_=srcb[gl][:, par::2, :], allow_slow_non_contiguous=True)),
                            w=[K_(("bn2", ri))], sem="D_bn2_%d_%d_%d" % (ri, gl, par))
                P.op("pool", (lambda e, ri=ri: e.tensor_copy(out=BN2b[:, ri], in_=BN2[:, ri])),
                     r=[K_(("bn2", ri))], w=[K_(("bn2b", ri))])
                pbf = PSB[ri][:, :].bitcast(BF16)
                for gp in range(8):
                    pr = ((gp % 4) // 2) * 64
                    P.op("pe", (lambda e, ri=ri, gp=gp, pr=pr, pbf=pbf: e.transpose(
                        out=pbf[pr:pr + 64, gp * 128:(gp + 1) * 128], in_=BN2b[:, ri, gp, :], identity=identb[:])),
                        r=[K_(("bn2b", ri)), "identb"], w=[("ps", ri)])
                LB = LBR if ri == 0 else LBI
                for gp in range(8):
                    pr = ((gp % 4) // 2) * 64
                    P.op("act", (lambda e, LB=LB, gp=gp, pr=pr, pbf=pbf: e.copy(
                        out=LB[pr:pr + 64, gp, :], in_=pbf[pr:pr + 64, gp * 128:(gp + 1) * 128])),
                        r=[("ps", ri)], w=[K_(("lb", ri))])
            for ri, nm in enumerate(("s_C_re", "s_C_im")):
                P.op("pool", (lambda e, ri=ri: e.memset(CNat[0:32, ri], 0.0)), w=[K_(("cn", ri))])
                srcc = W[nm][l].rearrange("(gp gl) h p -> gl h gp p", gl=2)
                for gl in range(2):
                    P.dma("sp", (lambda e, ri=ri, gl=gl, srcc=srcc: e.dma_start(
                        out=CNat[gl * 16:(gl + 1) * 16, ri, :, gl * 64:(gl + 1) * 64], in_=srcc[gl],
                        allow_slow_non_contiguous=True)),
                        w=[K_(("cn", ri))], sem="D_cn_%d_%d" % (ri, gl))
                for gp in range(8):
                    P.op("pe", (lambda e, ri=ri, gp=gp: e.transpose(
                        out=PSB[2 + ri][:, gp * 32:(gp + 1) * 32], in_=CNat[0:32, ri, gp, :], identity=ident[0:32, 0:32])),
                        r=[K_(("cn", ri)), "ident"], w=[("ps", 2 + ri)])

            def b32(c):
                return col(c).unsqueeze(2).broadcast_to([128, 8, 32])

            def v32(ap):
                return ap.rearrange("p (g c) -> p g c", c=32)
            crt = v32(PSB[2][:, 0:256])
            cit = v32(PSB[3][:, 0:256])
            P.op("dve", lambda e: e.tensor_tensor(out=v32(CT[:, 0, :]), in0=crt, in1=b32(11), op=ALU.mult),
                 r=[("ps", 2), K_("cr")], w=[K_("ct0")])
            P.op("dve", lambda e: e.tensor_tensor(out=v32(CT[:, 1, :]), in0=cit, in1=b32(12), op=ALU.mult),
                 r=[("ps", 3), K_("ci")], w=[K_("ct1")])
            P.op("dve", lambda e: e.tensor_tensor(out=v32(CT[:, 2, :]), in0=crt, in1=b32(12), op=ALU.mult),
                 r=[("ps", 2), K_("ci")], w=[K_("ct2")])
            P.op("dve", lambda e: e.tensor_tensor(out=v32(CT[:, 3, :]), in0=cit, in1=b32(11), op=ALU.mult),
                 r=[("ps", 3), K_("cr")], w=[K_("ct3")])
            P.op("pool", lambda e: e.memset(LCR[:], 0.0), w=[K_("lcr")])
            P.op("pool", lambda e: e.memset(LCI[:], 0.0), w=[K_("lci")])
            dv(lambda e: e.tensor_tensor(out=CT[:, 2, :], in0=CT[:, 2, :], in1=CT[:, 3, :], op=ALU.add), ["ct2", "ct3"], ["ct2"])
            for par in range(2):
                dv(lambda e, par=par: e.tensor_tensor(
                    out=LCR[:, par::2, par * 32:(par + 1) * 32], in0=v32(CT[:, 0, :])[:, par::2, :],
                    in1=v32(CT[:, 1, :])[:, par::2, :], op=ALU.subtract), ["ct0", "ct1", "lcr"], ["lcr"])
                dv(lambda e, par=par: e.tensor_scalar(
                    out=LCI[:, par::2, par * 32:(par + 1) * 32], in0=v32(CT[:, 2, :])[:, par::2, :], scalar1=-1.0,
                    scalar2=None, op0=ALU.mult), ["ct2", "lci"], ["lci"])
            P.op("pool", lambda e: e.memset(CARR[:], 0.0), w=["CARR"])
            P.op("pool", lambda e: e.memset(CARI[:], 0.0), w=["CARI"])
            am.off = mark
            return dict(MAG=MAG, NS128=NS128, COST=COST, SINT=SINT, LBR=LBR, LBI=LBI, LCR=LCR, LCI=LCI, SDG=SDG,
                        GLW=GLW, CARR=CARR, CARI=CARI, K_=K_)

        def s5_block(tb, C5):
            am.off = base_off
            K_ = C5["K_"]
            MAG, NS128, COST, SINT = C5["MAG"], C5["NS128"], C5["COST"], C5["SINT"]
            LBR, LBI, LCR, LCI, SDG, GLW = C5["LBR"], C5["LBI"], C5["LCR"], C5["LCI"], C5["SDG"], C5["GLW"]
            CARR, CARI = C5["CARR"], C5["CARI"]
            US = am.alloc([2, 512], F32)
            USB = am.alloc([2, 512], BF16)
            T = am.alloc([8, 512], F32)
            WRI = am.alloc([4, 512], F32)
            ZRI = am.alloc([4, 512], F32)
            XRI = am.alloc([4, 512], BF16)
            YS = am.alloc([2, 512], F32)
            YG = am.alloc([2, 512], F32)
            YGB = am.alloc([2, 512], BF16)
            CTMP = am.alloc([1, 4], F32)
            i, S = load_wblock(win, 1544, 256)
            for ct in range(2):
                for kt in range(8):
                    P.op("pe", (lambda e, kt=kt, ct=ct, S=S: e.matmul(
                        PSB[ct][:, :], lhsT=S[:, kt, ct * 128:(ct + 1) * 128], rhs=HTB[:, kt, :],
                        start=(kt == 0), stop=(kt == 7))),
                        r=[("WS", i, kt // 4), ("HTB", kt, 0)], w=[("ps", ct)], nofence=True)
                P.op("act", (lambda e, ct=ct: e.copy(out=US[:, ct, :], in_=PSB[ct][:, :])), r=[("ps", ct)], w=[("US", ct)])
                P.op("pool", (lambda e, ct=ct: e.tensor_copy(out=USB[:, ct, :], in_=US[:, ct, :])),
                     r=[("US", ct)], w=[("USB", ct)])

            rw_prefetch()

            def b4(tab, gp):
                return tab[:, gp, 0:128].unsqueeze(1).broadcast_to([128, 4, 128])

            def v4(ap):
                return ap.rearrange("p (c t) -> p c t", t=128)

            def pair(gp):
                pb = gp % 2
                b2, b3 = (2, 3) if pb == 0 else (6, 7)
                ct = gp // 4
                pr = ((gp % 4) // 2) * 64
                P.op("pe", (lambda e, gp=gp, ct=ct, pr=pr: e.matmul(
                    PSB[b2][:, :], lhsT=LBR[pr:pr + 64, gp, :], rhs=USB[pr:pr + 64, ct, :], start=True, stop=True)),
                    r=[K_(("lb", 0)), ("USB", ct)], w=[("ps", b2)])
                P.op("pe", (lambda e, gp=gp, ct=ct, pr=pr: e.matmul(
                    PSB[b3][:, :], lhsT=LBI[pr:pr + 64, gp, :], rhs=USB[pr:pr + 64, ct, :], start=True, stop=True)),
                    r=[K_(("lb", 1)), ("USB", ct)], w=[("ps", b3)])
                cosb, sinb = b4(COST, gp), b4(SINT, gp)
                yield
                rk = [K_("sc1c"), K_("sc1s")]
                P.op("dve", (lambda e, cosb=cosb: e.tensor_tensor(out=v4(T[:, pb * 4 + 0, :]), in0=v4(PSB[b2][:, :]), in1=cosb, op=ALU.mult)),
                     r=[("ps", b2)] + rk, w=[("T", pb, 0)])
                P.op("dve", (lambda e, sinb=sinb: e.tensor_tensor(out=v4(T[:, pb * 4 + 1, :]), in0=v4(PSB[b3][:, :]), in1=sinb, op=ALU.mult)),
                     r=[("ps", b3)] + rk, w=[("T", pb, 1)])
                P.op("dve", (lambda e, cosb=cosb: e.tensor_tensor(out=v4(T[:, pb * 4 + 2, :]), in0=v4(PSB[b3][:, :]), in1=cosb, op=ALU.mult)),
                     r=[("ps", b3)] + rk, w=[("T", pb, 2)])
                P.op("dve", (lambda e, sinb=sinb: e.tensor_tensor(out=v4(T[:, pb * 4 + 3, :]), in0=v4(PSB[b2][:, :]), in1=sinb, op=ALU.mult)),
                     r=[("ps", b2)] + rk, w=[("T", pb, 3)])
                P.op("pool", lambda e: e.tensor_tensor(out=WRI[:, pb * 2 + 0, :], in0=T[:, pb * 4 + 0, :], in1=T[:, pb * 4 + 1, :], op=ALU.add),
                     r=[("T", pb, 0), ("T", pb, 1)], w=[("WRI", pb, 0)])
                P.op("pool", lambda e: e.tensor_tensor(out=WRI[:, pb * 2 + 1, :], in0=T[:, pb * 4 + 2, :], in1=T[:, pb * 4 + 3, :], op=ALU.subtract),
                     r=[("T", pb, 2), ("T", pb, 3)], w=[("WRI", pb, 1)])
                yield
                magb = MAG[:, 0, gp:gp + 1].broadcast_to([128, 128])
                for c in range(4):
                    cs = slice(c * 128, (c + 1) * 128)
                    first = (tb == 0 and c == 0)
                    for ri, CAR in ((0, CARR), (1, CARI)):
                        init = 0.0 if first else CAR[:, 0, gp:gp + 1]
                        P.op("dve", (lambda e, ri=ri, cs=cs, init=init, magb=magb: e.tensor_tensor_scan(
                            out=ZRI[:, pb * 2 + ri, cs], data0=magb, data1=WRI[:, pb * 2 + ri, cs], initial=init,
                            op0=ALU.mult, op1=ALU.add)),
                            r=[("WRI", pb, ri), K_("mag"), ("CAR", ri, gp)], w=[("ZRI", pb, ri, c)])
                    zr = ZRI[:, pb * 2, c * 128 + 127:c * 128 + 128]
                    zi = ZRI[:, pb * 2 + 1, c * 128 + 127:c * 128 + 128]
                    c128 = COST[:, gp, 128:129]
                    s128 = SINT[:, gp, 128:129]
                    ns128 = NS128[:, 0, gp:gp + 1]
                    P.op("act", (lambda e, zi=zi, ns128=ns128: e.activation(out=CTMP[:, 0, pb * 2:pb * 2 + 1], in_=zi, func=AF.Identity,
                                                                            scale=ns128)),
                         r=[("ZRI", pb, 1, c), K_("ns128")], w=[("CTMP0", pb)])
                    P.op("act", (lambda e, zr=zr, c128=c128, gp=gp: e.activation(out=CARR[:, 0, gp:gp + 1], in_=zr, func=AF.Identity,
                                                                                scale=c128, bias=CTMP[:, 0, pb * 2:pb * 2 + 1])),
                         r=[("ZRI", pb, 0, c), ("CTMP0", pb)] + rk, w=[("CAR", 0, gp)])
                    P.op("pool", (lambda e, zr=zr, s128=s128: e.tensor_scalar(out=CTMP[:, 0, pb * 2 + 1:pb * 2 + 2], in0=zr, scalar1=s128,
                                                                              scalar2=None, op0=ALU.mult)),
                         r=[("ZRI", pb, 0, c)] + rk, w=[("CTMP1", pb)])
                    P.op("pool", (lambda e, zi=zi, c128=c128, gp=gp: e.tensor_scalar(
                        out=CARI[:, 0, gp:gp + 1], in0=zi, scalar1=c128, scalar2=CTMP[:, 0, pb * 2 + 1:pb * 2 + 2],
                        op0=ALU.mult, op1=ALU.add)),
                         r=[("ZRI", pb, 1, c), ("CTMP1", pb)] + rk, w=[("CAR", 1, gp)])
                yield
                zk = [("ZRI", pb, 0, c_) for c_ in range(4)]
                zki = [("ZRI", pb, 1, c_) for c_ in range(4)]
                P.op("dve", (lambda e, cosb=cosb: e.tensor_tensor(out=v4(T[:, pb * 4 + 0, :]), in0=v4(ZRI[:, pb * 2, :]), in1=cosb, op=ALU.mult)),
                     r=zk + rk, w=[("T", pb, 0)])
                P.op("dve", (lambda e, sinb=sinb: e.tensor_tensor(out=v4(T[:, pb * 4 + 1, :]), in0=v4(ZRI[:, pb * 2 + 1, :]), in1=sinb, op=ALU.mult)),
                     r=zki + rk, w=[("T", pb, 1)])
                P.op("pool", (lambda e, sinb=sinb: e.tensor_tensor(out=v4(T[:, pb * 4 + 2, :]), in0=v4(ZRI[:, pb * 2, :]), in1=sinb, op=ALU.mult)),
                     r=zk + rk, w=[("T", pb, 2)])
                P.op("pool", (lambda e, cosb=cosb: e.tensor_tensor(out=v4(T[:, pb * 4 + 3, :]), in0=v4(ZRI[:, pb * 2 + 1, :]), in1=cosb, op=ALU.mult)),
                     r=zki + rk, w=[("T", pb, 3)])
                P.op("dve", lambda e: e.tensor_tensor(out=XRI[:, pb * 2 + 0, :], in0=T[:, pb * 4 + 0, :], in1=T[:, pb * 4 + 1, :], op=ALU.subtract),
                     r=[("T", pb, 0), ("T", pb, 1)], w=[("XRI", pb, 0)])
                P.op("pool", lambda e: e.tensor_tensor(out=XRI[:, pb * 2 + 1, :], in0=T[:, pb * 4 + 2, :], in1=T[:, pb * 4 + 3, :], op=ALU.add),
                     r=[("T", pb, 2), ("T", pb, 3)], w=[("XRI", pb, 1)])
                yield
                P.op("pe", (lambda e, gp=gp, ct=ct, pr=pr: e.matmul(
                    PSB[4 + ct][pr:pr + 64, :], lhsT=LCR[:, gp, :], rhs=XRI[:, pb * 2 + 0, :], start=(gp % 2 == 0), stop=False)),
                    r=[K_("lcr"), ("XRI", pb, 0)], w=[("ps", 4 + ct)])
                P.op("pe", (lambda e, gp=gp, ct=ct, pr=pr: e.matmul(
                    PSB[4 + ct][pr:pr + 64, :], lhsT=LCI[:, gp, :], rhs=XRI[:, pb * 2 + 1, :], start=False, stop=(gp % 2 == 1))),
                    r=[K_("lci"), ("XRI", pb, 1)], w=[("ps", 4 + ct)])
            run_pipelined([pair(g_) for g_ in range(8)], 2)
            for ct in range(2):
                P.op("dve", (lambda e, ct=ct: e.scalar_tensor_tensor(
                    out=YS[:, ct, :], in0=US[:, ct, :], scalar=SDG[:, 0, ct:ct + 1], in1=PSB[4 + ct][:, :],
                    op0=ALU.mult, op1=ALU.add)),
                    r=[("US", ct), ("ps", 4 + ct), K_("sd")], w=[("YS", ct)])
                P.op("pool", (lambda e, ct=ct: e.tensor_tensor(out=T[:, ct, :], in0=YS[:, ct, :], in1=YS[:, ct, :], op=ALU.mult)),
                     r=[("YS", ct)], w=[("T", 0, ct)])
                P.op("pool", (lambda e, ct=ct: e.tensor_scalar(out=T[:, ct, :], in0=T[:, ct, :], scalar1=0.044715, scalar2=1.0,
                                                               op0=ALU.mult, op1=ALU.add)),
                     r=[("T", 0, ct)], w=[("T", 0, ct)])
                P.op("pool", (lambda e, ct=ct: e.tensor_tensor(out=T[:, ct, :], in0=T[:, ct, :], in1=YS[:, ct, :], op=ALU.mult)),
                     r=[("T", 0, ct), ("YS", ct)], w=[("T", 0, ct)])
                P.op("act", (lambda e, ct=ct: e.activation(out=T[:, 2 + ct, :], in_=T[:, ct, :], func=AF.Sigmoid,
                                                           scale=1.5957691216057308)),
                     r=[("T", 0, ct)], w=[("T", 0, 2 + ct)])
                P.op("dve", (lambda e, ct=ct: e.tensor_tensor(out=YG[:, ct, :], in0=YS[:, ct, :], in1=T[:, 2 + ct, :], op=ALU.mult)),
                     r=[("YS", ct), ("T", 0, 2 + ct)], w=[("YG", ct)])
                P.op("pool", (lambda e, ct=ct: e.tensor_copy(out=YGB[:, ct, :], in_=YG[:, ct, :])),
                     r=[("YG", ct)], w=[("YGB", ct)])
            for mt in range(2):
                for kt in range(2):
                    P.op("pe", (lambda e, mt=mt, kt=kt: e.matmul(
                        PSB[6 + mt][:, :], lhsT=GLW[:, kt, mt * 128:(mt + 1) * 128], rhs=YGB[:, kt, :],
                        start=(kt == 0), stop=(kt == 1))),
                        r=[K_("glw"), ("YGB", kt)], w=[("ps", 6 + mt)])
                P.op("act", (lambda e, mt=mt: e.activation(out=T[:, mt, :], in_=PSB[6 + mt][:, :], func=AF.Sigmoid,
                                                           bias=SDG[:, 0, 2 + mt:3 + mt], scale=1.0)),
                     r=[("ps", 6 + mt), K_("glb")], w=[("T", 0, mt)])
                P.op("dve", (lambda e, mt=mt: e.tensor_tensor(out=YTB[:, 4 + mt, :], in0=YG[:, mt, :], in1=T[:, mt, :], op=ALU.mult)),
                     r=[("YG", mt), ("T", 0, mt)], w=[("YTB", 4 + mt)])

        R0 = 1800

        def rw_setup():
            RC = am.alloc([1, 32], F32)
            MUL = am.alloc([1, 2], F32)
            LW = am.alloc([1, 256], BF16)
            HST = am.alloc([2, 64], F32)
            HB0 = am.alloc([2, 64], BF16)
            PREV = am.alloc([1, 8], F32)
            mark_ = am.off
            LWF = am.alloc([1, 256], F32)
            am.off = mark_

            def K_(n):
                return ("rwc", n)
            mu = W["r_mu"][l]
            for qi in range(3):
                small_dma(RC[:, 0, qi * 2:(qi + 1) * 2], mu[qi * 256:(qi + 1) * 256].rearrange("(pr p) -> p pr", p=128), K_("mu"))
            small_dma(MUL[:, 0, 0:1], mu[768:896].rearrange("(p o) -> p o", o=1), K_("mul"))
            for nm, c0 in (("r_w0", 12), ("r_a0", 14), ("r_k_k", 16), ("r_k_a", 18), ("r_gn_w", 24), ("r_gn_b", 26)):
                small_dma(RC[:, 0, c0:c0 + 2], W[nm][l].rearrange("(pr p) -> p pr", p=128), K_("cols"))
            small_dma(RC[:, 0, 22:24], W["r_r_k"][l].rearrange("(pr hh) p -> (hh p) pr", hh=2), K_("cols"))
            P.op("dve", lambda e: e.tensor_scalar(out=RC[:, 0, 6:12], in0=RC[:, 0, 0:6], scalar1=-1.0, scalar2=1.0,
                                                  op0=ALU.mult, op1=ALU.add), r=[K_("mu")], w=[K_("omu")])
            P.op("dve", lambda e: e.tensor_scalar(out=RC[:, 0, 20:22], in0=RC[:, 0, 18:20], scalar1=-1.0, scalar2=1.0,
                                                  op0=ALU.mult, op1=ALU.add), r=[K_("cols")], w=[K_("omka")])
            P.op("dve", lambda e: e.tensor_scalar(out=MUL[:, 0, 1:2], in0=MUL[:, 0, 0:1], scalar1=-1.0, scalar2=1.0,
                                                  op0=ALU.mult, op1=ALU.add), r=[K_("mul")], w=[K_("omul")])
            small_dma(LWF[0:32, 0, :], W["r_w2"][l], K_("lwf"))
            small_dma(LWF[32:64, 0, :], W["r_a2"][l], K_("lwf"))
            small_dma(LWF[64:128, 0, :], W["r_g2"][l], K_("lwf"))
            P.op("pool", lambda e: e.tensor_copy(out=LW[:, 0, :], in_=LWF[:, 0, :]), r=[K_("lwf")], w=[K_("lw")])
            P.op("pool", lambda e: e.memset(HST[:], 0.0), w=["HST"])
            P.op("pool", lambda e: e.memset(HB0[:], 0.0), w=["HB0"])
            P.op("pool", lambda e: e.memset(PREV[:], 0.0), w=["PREV"])
            return dict(RC=RC, MUL=MUL, LW=LW, HST=HST, HB0=HB0, PREV=PREV, K_=K_)

        def rw_block(tb, CR):
            am.off = base_off
            K_ = CR["K_"]
            RC, MUL, LW, HST, HB0, PREV = CR["RC"], CR["MUL"], CR["LW"], CR["HST"], CR["HB0"], CR["PREV"]
            Fb = am.alloc([10, 512], F32)
            RAW = am.alloc([2, 516], F32)
            LRAW = am.alloc([1, 516], F32)
            FL = am.alloc([1, 512], F32)
            LB16 = am.alloc([1, 512], BF16)
            LK = am.alloc([8, 128], BF16)
            RB = am.alloc([8, 128], BF16)
            AH = am.alloc([8, 128], BF16)
            VB = am.alloc([1, 512], BF16)
            AB1 = am.alloc([4, 128], BF16)
            AB2 = am.alloc([4, 128], BF16)
            AB3 = am.alloc([4, 64], BF16)
            TM = am.alloc([4, 256], BF16)
            Zr = am.alloc([2, 4, 128], BF16)
            PPr = am.alloc([2, 4, 128], BF16)
            G0TS = am.alloc([8, 64], BF16)
            HINC = am.alloc([8, 64], F32)
            QTS = am.alloc([8, 64], BF16)
            Y0TS = am.alloc([1, 512], F32)
            PTc = am.alloc([1, 8], F32)
            HBs = am.alloc([9, 64], BF16)
            TMPH = am.alloc([1, 64], F32)

            def F(i):
                return Fb[:, i, :]

            def fk(i):
                return ("F", i)

            def col(c):
                return RC[:, 0, c:c + 1]
            cK = [K_("mu"), K_("omu"), K_("cols"), K_("omka")]

            def dve(fn, r, w):
                P.op("dve", fn, r=r, w=w)

            def act(fn, r, w):
                P.op("act", fn, r=r, w=w)

            def pool(fn, r, w):
                P.op("pool", fn, r=r, w=w)

            HB = (slice(0, 64), slice(64, 128))

            if "lora" in RWPRE:
                i, S = RWPRE.pop("lora")
            else:
                i, S = load_wblock(win, R0 + 768, 128)
            mm_fm(0, i, S, 128, HTB, "HTB")
            act(lambda e: e.copy(out=LRAW[:, 0, 1:513], in_=PSB[0][:, :]), [("ps", 0)], ["LRAW"])
            if tb == 0:
                pool(lambda e: e.memset(LRAW[:, 0, 0:1], 0.0), [], ["LRAW0"])
            else:
                pool(lambda e: e.tensor_copy(out=LRAW[:, 0, 0:1], in_=PREV[:, 0, 6:7]), ["PREVL"], ["LRAW0"])
            dve(lambda e: e.tensor_scalar(out=FL[:, 0, :], in0=LRAW[:, 0, 1:513], scalar1=MUL[:, 0, 1:2], scalar2=None,
                                          op0=ALU.mult), ["LRAW", K_("omul")], ["FL"])
            dve(lambda e: e.scalar_tensor_tensor(out=FL[:, 0, :], in0=LRAW[:, 0, 0:512], scalar=MUL[:, 0, 0:1], in1=FL[:, 0, :],
                                                 op0=ALU.mult, op1=ALU.add), ["LRAW", "LRAW0", "FL", K_("mul")], ["FL"])
            pool(lambda e: e.tensor_copy(out=PREV[:, 0, 6:7], in_=LRAW[:, 0, 512:513]), ["LRAW", "LRAW0"], ["PREVL"])
            act(lambda e: e.activation(out=LB16[0:32, 0, :], in_=FL[0:32, 0, :], func=AF.Tanh), ["FL"], ["LB16a"])
            act(lambda e: e.copy(out=LB16[32:64, 0, :], in_=FL[32:64, 0, :]), ["FL"], ["LB16b"])
            act(lambda e: e.activation(out=LB16[64:128, 0, :], in_=FL[64:128, 0, :], func=AF.Sigmoid), ["FL"], ["LB16c"])
            if "rkv" in RWPRE:
                slots = RWPRE.pop("rkv")
            else:
                slots = [load_wblock(win, R0 + qi * 256, 256) for qi in range(3)]

            def do_pair(pr):
                ps_ = slice(pr * 128, (pr + 1) * 128)
                for qi in range(3):
                    iq, Sq = slots[qi]
                    bank = qi % 2
                    for kt in range(8):
                        P.op("pe", (lambda e, kt=kt, Sq=Sq, bank=bank: e.matmul(
                            PSB[bank][:, :], lhsT=Sq[:, kt, ps_], rhs=HTB[:, kt, :], start=(kt == 0), stop=(kt == 7))),
                            r=[("WS", iq, kt // 4), ("HTB", kt, 0)], w=[("ps", bank)], nofence=True)
                    act((lambda e, bank=bank, qi=qi: e.copy(out=RAW[:, qi % 2, 1:513], in_=PSB[bank][:, :])), [("ps", bank)], [("RAW", qi % 2)])
                    pc = qi * 2 + pr
                    if tb == 0:
                        pool((lambda e, qi=qi: e.memset(RAW[:, qi % 2, 0:1], 0.0)), [], [("RAW0", qi % 2)])
                    else:
                        pool((lambda e, pc=pc, qi=qi: e.tensor_copy(out=RAW[:, qi % 2, 0:1], in_=PREV[:, 0, pc:pc + 1])),
                             [("PREV", pc)], [("RAW0", qi % 2)])
                    dve((lambda e, qi=qi, pc=pc: e.tensor_scalar(out=F(qi), in0=RAW[:, qi % 2, 1:513], scalar1=col(6 + pc),
                                                                 scalar2=None, op0=ALU.mult)), [("RAW", qi % 2)] + cK, [fk(qi)])
                    dve((lambda e, qi=qi, pc=pc: e.scalar_tensor_tensor(out=F(qi), in0=RAW[:, qi % 2, 0:512], scalar=col(pc),
                                                                        in1=F(qi), op0=ALU.mult, op1=ALU.add)),
                        [("RAW", qi % 2), ("RAW0", qi % 2), fk(qi)] + cK, [fk(qi)])
                    pool((lambda e, pc=pc, qi=qi: e.tensor_copy(out=PREV[:, 0, pc:pc + 1], in_=RAW[:, qi % 2, 512:513])),
                         [("RAW", qi % 2), ("RAW0", qi % 2)], [("PREV", pc)])
                pool(lambda e: e.tensor_copy(out=VB[:, 0, :], in_=F(2)), [fk(2)], ["VB"])
                P.op("pe", lambda e: e.matmul(PSB[2][:, :], lhsT=LW[0:32, 0, ps_], rhs=LB16[0:32, 0, :], start=True, stop=True),
                     r=[K_("lw"), "LB16a"], w=[("ps", 2)])
                act(lambda e: e.activation(out=F(3), in_=PSB[2][:, :], func=AF.Sigmoid, bias=col(12 + pr), scale=1.0),
                    [("ps", 2)] + cK, [fk(3)])
                dve(lambda e: e.tensor_scalar(out=F(3), in0=F(3), scalar1=-0.6065306597126334, scalar2=None, op0=ALU.mult),
                    [fk(3)], [fk(3)])
                P.op("pe", lambda e: e.matmul(PSB[3][:, :], lhsT=LW[32:64, 0, ps_], rhs=LB16[32:64, 0, :], start=True, stop=True),
                     r=[K_("lw"), "LB16b"], w=[("ps", 3)])
                act(lambda e: e.activation(out=F(4), in_=PSB[3][:, :], func=AF.Sigmoid, bias=col(14 + pr), scale=1.0),
                    [("ps", 3)] + cK, [fk(4)])
                P.op("pe", lambda e: e.matmul(PSB[2][:, :], lhsT=LW[64:128, 0, ps_], rhs=LB16[64:128, 0, :], start=True, stop=True),
                     r=[K_("lw"), "LB16c"], w=[("ps", 2)])
                act(lambda e: e.copy(out=F(5), in_=PSB[2][:, :]), [("ps", 2)], [fk(5)])
                dve(lambda e: e.tensor_scalar(out=F(6), in0=F(1), scalar1=col(16 + pr), scalar2=None, op0=ALU.mult),
                    [fk(1)] + cK, [fk(6)])
                pool(lambda e: e.tensor_tensor(out=F(7), in0=F(6), in1=F(6), op=ALU.mult), [fk(6)], [fk(7)])
                P.op("pe", lambda e: e.matmul(PSB[3][:, :], lhsT=OBD[:, :], rhs=F(7), start=True, stop=True),
                     r=["OBD", fk(7)], w=[("ps", 3)])
                act(lambda e: e.activation(out=F(7), in_=PSB[3][:, :], func=AF.Ln, bias=CST[:, 3:4], scale=1.0),
                    [("ps", 3), "cst"], [fk(7)])
                act(lambda e: e.activation(out=F(7), in_=F(7), func=AF.Exp, scale=-0.5), [fk(7)], [fk(7)])
                dve(lambda e: e.tensor_tensor(out=F(6), in0=F(6), in1=F(7), op=ALU.mult), [fk(6), fk(7)], [fk(6)])
                dve(lambda e: e.tensor_scalar(out=F(7), in0=F(4), scalar1=col(18 + pr), scalar2=col(20 + pr), op0=ALU.mult,
                                              op1=ALU.add), [fk(4)] + cK, [fk(7)])
                dve(lambda e: e.tensor_tensor(out=F(1), in0=F(1), in1=F(7), op=ALU.mult), [fk(1), fk(7)], [fk(1)])
                dve(lambda e: e.scalar_tensor_tensor(out=F(7), in0=F(0), scalar=col(22 + pr), in1=F(1), op0=ALU.mult,
                                                     op1=ALU.mult), [fk(0), fk(1)] + cK, [fk(7)])
                P.op("pe", lambda e: e.matmul(PSB[2][:, :], lhsT=OBD[:, :], rhs=F(7), start=True, stop=True),
                     r=["OBD", fk(7)], w=[("ps", 2)])
                dve(lambda e: e.tensor_tensor(out=F(8), in0=PSB[2][:, :], in1=F(2), op=ALU.mult), [("ps", 2), fk(2)], [fk(8)])
                dve(lambda e: e.tensor_tensor(out=F(7), in0=F(6), in1=F(4), op=ALU.mult), [fk(6), fk(4)], [fk(7)])
                for c in range(8):
                    cs = slice(c * 64, (c + 1) * 64)
                    dve((lambda e, cs=cs: e.tensor_tensor_scan(out=Fb[:, 9, cs], data0=ONESW[:, 0:64], data1=Fb[:, 3, cs],
                                                               initial=0.0, op0=ALU.mult, op1=ALU.add)),
                        [fk(3), "ONESW"], [fk(9)])
                dve(lambda e: e.tensor_tensor(out=F(3), in0=F(9), in1=F(3), op=ALU.subtract), [fk(9), fk(3)], [fk(3)])
                act(lambda e: e.activation(out=PTc[:, 0, :], in_=Fb[:, 9, 63::64], func=AF.Exp), [fk(9)], ["PTc"])
                v8c = lambda ap: ap.rearrange("p (c t) -> p c t", t=64)
                act(lambda e: e.activation(out=F(4), in_=F(9), func=AF.Exp), [fk(9), fk(4)], [fk(4)])
                dve(lambda e: e.tensor_tensor(out=RB[:, :, 64:128], in0=v8c(F(0)), in1=v8c(F(4)), op=ALU.mult),
                    [fk(0), fk(4)], ["RBr"])
                act(lambda e: e.activation(out=F(4), in_=F(3), func=AF.Exp), [fk(3), fk(4), "RBr"], [fk(4)])
                dve(lambda e: e.scalar_tensor_tensor(out=RB[:, :, 0:64], in0=v8c(F(6)), scalar=-1.0, in1=v8c(F(4)),
                                                     op0=ALU.mult, op1=ALU.mult), [fk(6), fk(4)], ["RBb"])
                act(lambda e: e.activation(out=F(4), in_=F(9), func=AF.Exp, scale=-1.0), [fk(9), fk(4), "RBb"], [fk(4)])
                dve(lambda e: e.tensor_tensor(out=LK[:, :, 0:64], in0=v8c(F(7)), in1=v8c(F(4)), op=ALU.mult),
                    [fk(7), fk(4)], ["LKa"])
                pool(lambda e: e.tensor_tensor(out=LK[:, :, 64:128], in0=v8c(F(1)), in1=v8c(F(4)), op=ALU.mult),
                     [fk(1), fk(4)], ["LKk"])
                for c in range(8):
                    cs = slice(c * 64, (c + 1) * 64)
                    act((lambda e, c=c, cs=cs: e.activation(out=Fb[:, 3, cs], in_=Fb[:, 9, cs], func=AF.Exp,
                                                            bias=Fb[:, 9, c * 64 + 63:c * 64 + 64], scale=-1.0)),
                        [fk(9), fk(3)], [fk(3)])
                dve(lambda e: e.tensor_tensor(out=AH[:, :, 0:64], in0=v8c(F(7)), in1=v8c(F(3)), op=ALU.mult),
                    [fk(7), fk(3)], ["AHa"])
                pool(lambda e: e.tensor_tensor(out=AH[:, :, 64:128], in0=v8c(F(1)), in1=v8c(F(3)), op=ALU.mult),
                     [fk(1), fk(3)], ["AHk"])

                def do_group(grp):
                    c0 = grp * 4
                    for j in range(4):
                        c = c0 + j
                        for hb in HB:
                            P.op("pe", (lambda e, c=c, j=j, hb=hb: e.matmul(PSB[0][hb, j * 128:(j + 1) * 128], lhsT=LK[hb, c, 0:64],
                                                                            rhs=RB[hb, c, :], start=True, stop=True)),
                                 r=["LKa", "RBr", "RBb"], w=[("ps", 0)])
                            P.op("pe", (lambda e, c=c, j=j, hb=hb: e.matmul(PSB[1][hb, j * 128:(j + 1) * 128], lhsT=LK[hb, c, 64:128],
                                                                            rhs=RB[hb, c, :], start=True, stop=True)),
                                 r=["LKk", "RBr", "RBb"], w=[("ps", 1)])
                            P.op("pe", (lambda e, c=c, j=j, hb=hb: e.matmul(PSB[2][hb, j * 64:(j + 1) * 64], lhsT=RB[hb, c, 0:64],
                                                                            rhs=LK[hb, c, 0:64], start=True, stop=True)),
                                 r=["LKa", "RBb"], w=[("ps", 2)])
                    v4 = lambda ap, w_: ap.rearrange("p (j x) -> p j x", x=w_)
                    m1 = MASK1[:, :].unsqueeze(1).broadcast_to([128, 4, 128])
                    dve(lambda e: e.tensor_tensor(out=AB1[:], in0=v4(PSB[0][:, :], 128), in1=m1, op=ALU.mult),
                        [("ps", 0), "MASK1"], ["AB1"])
                    dve(lambda e: e.tensor_tensor(out=AB2[:], in0=v4(PSB[1][:, :], 128), in1=m1, op=ALU.mult),
                        [("ps", 1), "MASK1"], ["AB2"])
                    dve(lambda e: e.tensor_tensor(out=AB3[:], in0=v4(PSB[2][:, 0:256], 64),
                                                  in1=MASKL[:, :].unsqueeze(1).broadcast_to([128, 4, 64]), op=ALU.mult),
                        [("ps", 2), "MASKL"], ["AB3"])
                    pt3 = PSB[3][:, :].bitcast(BF16)
                    for j in range(4):
                        c = c0 + j
                        for hb in HB:
                            srcs = (RB[hb, c, 0:64], VB[hb, 0, c * 64:(c + 1) * 64], AH[hb, c, 0:64], AH[hb, c, 64:128])
                            for q, src in enumerate(srcs):
                                P.op("pe", (lambda e, j=j, q=q, src=src, hb=hb: e.transpose(
                                    out=pt3[hb, j * 256 + q * 64:j * 256 + (q + 1) * 64], in_=src, identity=identb[hb, hb])),
                                    r=["RBb", "VB", "AHa", "AHk", "identb"], w=[("ps", 3)])
                    act(lambda e: e.copy(out=TM[:], in_=pt3.rearrange("p (j x) -> p j x", x=256)), [("ps", 3)], ["TM"])
                    for j in range(4):
                        for hb in HB:
                            P.op("pe", (lambda e, j=j, hb=hb: e.matmul(PSB[2][hb, 256 + j * 64:256 + (j + 1) * 64], lhsT=AB2[hb, j, 0:64],
                                                                       rhs=TM[hb, j, 64:128], start=True, stop=True)),
                                 r=["AB2", "TM"], w=[("ps", 2)])
                    pool(lambda e: e.tensor_copy(out=Zr[:, 0, :, 0:64], in_=TM[:, :, 0:64]), ["TM"], [("Z", 0)])
                    act(lambda e: e.copy(out=Zr[:, 0, :, 64:128], in_=v4(PSB[2][:, 256:512], 64)), [("ps", 2)], [("Z", 0)])
                    for jj in range(6):
                        zi, zo = jj % 2, (jj + 1) % 2
                        for j in range(4):
                            for hb in HB:
                                if jj == 0:
                                    Pm, PmT, pk = AB3[hb, j, :], AB1[hb, j, 0:64], ["AB3", "AB1"]
                                else:
                                    Pm, PmT = PPr[hb, jj % 2, j, 0:64], PPr[hb, jj % 2, j, 64:128]
                                    pk = [("PP", jj % 2)]
                                P.op("pe", (lambda e, j=j, PmT=PmT, zi=zi, hb=hb: e.matmul(
                                    PSB[5][hb, j * 128:(j + 1) * 128], lhsT=PmT, rhs=Zr[hb, zi, j, :], start=True, stop=True)),
                                    r=pk + [("Z", zi)], w=[("ps", 5)])
                                if jj < 5:
                                    P.op("pe", (lambda e, j=j, Pm=Pm, PmT=PmT, hb=hb: e.matmul(
                                        PSB[4][hb, j * 128:j * 128 + 64], lhsT=PmT, rhs=Pm, start=True, stop=True)),
                                        r=pk, w=[("ps", 4)])
                                    P.op("pe", (lambda e, j=j, Pm=Pm, PmT=PmT, hb=hb: e.matmul(
                                        PSB[4][hb, j * 128 + 64:(j + 1) * 128], lhsT=Pm, rhs=PmT, start=True, stop=True)),
                                        r=pk, w=[("ps", 4)])
                        dve((lambda e, zi=zi, zo=zo: e.tensor_tensor(out=Zr[:, zo], in0=Zr[:, zi],
                                                                   in1=v4(PSB[5][:, :], 128), op=ALU.add)),
                            [("Z", zi), ("ps", 5)], [("Z", zo)])
                        if jj < 5:
                            act((lambda e, jj=jj: e.copy(out=PPr[:, (jj + 1) % 2], in_=v4(PSB[4][:, :], 128))),
                                [("ps", 4)], [("PP", (jj + 1) % 2)])
                    for j in range(4):
                        for hb in HB:
                            ZF = Zr[hb, 0]
                            P.op("pe", (lambda e, j=j, hb=hb, ZF=ZF: e.matmul(PSB[6][hb, j * 64:(j + 1) * 64], lhsT=ZF[:, j, 0:64],
                                                                              rhs=TM[hb, j, 128:192], start=True, stop=True)),
                                 r=[("Z", 0), "TM"], w=[("ps", 6)])
                            P.op("pe", (lambda e, j=j, hb=hb, ZF=ZF: e.matmul(PSB[6][hb, 256 + j * 64:256 + (j + 1) * 64], lhsT=TM[hb, j, 128:192],
                                                                              rhs=ZF[:, j, 64:128], start=True, stop=False)),
                                 r=[("Z", 0), "TM"], w=[("ps", 6)])
                            P.op("pe", (lambda e, j=j, hb=hb: e.matmul(PSB[6][hb, 256 + j * 64:256 + (j + 1) * 64], lhsT=TM[hb, j, 192:256],
                                                                       rhs=TM[hb, j, 64:128], start=False, stop=True)),
                                 r=["TM"], w=[("ps", 6)])
                            P.op("pe", (lambda e, j=j, hb=hb, ZF=ZF: e.matmul(PSB[7][hb, j * 64:(j + 1) * 64], lhsT=ZF[:, j, 0:64],
                                                                              rhs=AB1[hb, j, 64:128], start=True, stop=True)),
                                 r=[("Z", 0), "AB1"], w=[("ps", 7)])
                            P.op("pe", (lambda e, j=j, hb=hb, ZF=ZF: e.matmul(PSB[7][hb, 256 + j * 64:256 + (j + 1) * 64], lhsT=ZF[:, j, 64:128],
                                                                              rhs=AB1[hb, j, 64:128], start=True, stop=False)),
                                 r=[("Z", 0), "AB1"], w=[("ps", 7)])
                            P.op("pe", (lambda e, j=j, hb=hb: e.matmul(PSB[7][hb, 256 + j * 64:256 + (j + 1) * 64], lhsT=TM[hb, j, 64:128],
                                                                       rhs=AB2[hb, j, 64:128], start=False, stop=True)),
                                 r=["TM", "AB2"], w=[("ps", 7)])
                    p6a, p6b = v4(PSB[6][:, 0:256], 64), v4(PSB[6][:, 256:512], 64)
                    p7a, p7b = v4(PSB[7][:, 0:256], 64), v4(PSB[7][:, 256:512], 64)
                    act(lambda e: e.copy(out=G0TS[:, c0:c0 + 4, :], in_=p6a), [("ps", 6)], [("G0TS", grp)])
                    act(lambda e: e.copy(out=HINC[:, c0:c0 + 4, :], in_=p6b), [("ps", 6)], [("HINC", grp)])
                    dve(lambda e: e.tensor_tensor(out=QTS[:, c0:c0 + 4, :], in0=p7a, in1=RB[:, c0:c0 + 4, 64:128],
                                                  op=ALU.add), [("ps", 7), "RBr"], [("QTS", grp)])
                    dve(lambda e: e.tensor_copy(out=v8c(Y0TS[:, 0, :])[:, c0:c0 + 4, :], in_=p7b), [("ps", 7)],
                        [("Y0TS", grp)])
                for grp_ in range(2):
                    do_group(grp_)
                pool(lambda e: e.tensor_copy(out=HBs[:, 0, :], in_=HB0[:, pr, :]), ["HB0"], [("HBs", 0)])
                for c in range(8):
                    grp = c // 4
                    for hb in HB:
                        P.op("pe", (lambda e, c=c, hb=hb: e.matmul(PSB[2][hb, 0:64], lhsT=G0TS[hb, c, :], rhs=HBs[hb, c, :],
                                                                   start=True, stop=True)),
                             r=[("G0TS", grp), ("HBs", c)], w=[("ps", 2)])
                    dve((lambda e, c=c: e.scalar_tensor_tensor(out=TMPH[:, 0, :], in0=HST[:, pr, :], scalar=PTc[:, 0, c:c + 1],
                                                               in1=HINC[:, c, :], op0=ALU.mult, op1=ALU.add)),
                        ["HST", "PTc", ("HINC", grp)], ["TMPH"])
                    dve(lambda e: e.tensor_tensor(out=HST[:, pr, :], in0=TMPH[:, 0, :], in1=PSB[2][:, 0:64], op=ALU.add),
                        ["TMPH", ("ps", 2)], ["HST"])
                    act((lambda e, c=c: e.copy(out=HBs[:, c + 1, :], in_=HST[:, pr, :])), ["HST"], [("HBs", c + 1)])
                pool(lambda e: e.tensor_copy(out=HB0[:, pr, :], in_=HBs[:, 8, :]), [("HBs", 8)], ["HB0"])
                for c in range(8):
                    for hb in HB:
                        P.op("pe", (lambda e, c=c, hb=hb: e.matmul(PSB[3][hb, c * 64:(c + 1) * 64], lhsT=HBs[hb, c, :], rhs=QTS[hb, c, :],
                                                                   start=True, stop=True)),
                             r=[("HBs", c), ("QTS", c // 4)], w=[("ps", 3)])
                dve(lambda e: e.tensor_tensor(out=F(0), in0=PSB[3][:, :], in1=Y0TS[:, 0, :], op=ALU.add),
                    [("ps", 3), ("Y0TS", 0), ("Y0TS", 1), fk(0), "RBr"], [fk(0)])
                P.op("pe", lambda e: e.matmul(PSB[0][:, :], lhsT=O64BD[:, :], rhs=F(0), start=True, stop=True),
                     r=["O64BD", fk(0)], w=[("ps", 0)])
                dve(lambda e: e.tensor_tensor(out=F(0), in0=F(0), in1=PSB[0][:, :], op=ALU.subtract), [fk(0), ("ps", 0)], [fk(0)])
                pool(lambda e: e.tensor_tensor(out=F(4), in0=F(0), in1=F(0), op=ALU.mult), [fk(0), fk(4), "LKa", "LKk"], [fk(4)])
                P.op("pe", lambda e: e.matmul(PSB[1][:, :], lhsT=O64BD[:, :], rhs=F(4), start=True, stop=True),
                     r=["O64BD", fk(4)], w=[("ps", 1)])
                act(lambda e: e.activation(out=F(4), in_=PSB[1][:, :], func=AF.Ln, bias=CST[:, 4:5], scale=1.0),
                    [("ps", 1), "cst"], [fk(4)])
                act(lambda e: e.activation(out=F(4), in_=F(4), func=AF.Exp, scale=-0.5), [fk(4)], [fk(4)])
                dve(lambda e: e.tensor_tensor(out=F(0), in0=F(0), in1=F(4), op=ALU.mult), [fk(0), fk(4)], [fk(0)])
                dve(lambda e: e.tensor_scalar(out=F(0), in0=F(0), scalar1=col(24 + pr), scalar2=col(26 + pr), op0=ALU.mult,
                                              op1=ALU.add), [fk(0)] + cK, [fk(0)])
                dve(lambda e: e.tensor_tensor(out=F(0), in0=F(0), in1=F(8), op=ALU.add), [fk(0), fk(8)], [fk(0)])
                dve(lambda e: e.tensor_tensor(out=YTB[:, 6 + pr, :], in0=F(0), in1=F(5), op=ALU.mult), [fk(0), fk(5)], [("YTB", 6 + pr)])
            for pr_ in range(2):
                do_pair(pr_)

        def outproj(t0):
            oslots = {0: load_wblock(wout, 0, 256)}
            for db in range(4):
                if db + 1 < 4:
                    oslots[db + 1] = load_wblock(wout, (db + 1) * 256, 256)
                i, S = oslots[db]
                for di in range(2):
                    dt_ = db * 2 + di
                    bank = 4 + di
                    for kt in range(8):
                        P.op("pe", (lambda e, kt=kt, di=di, bank=bank, S=S: e.matmul(
                            PSB[bank][:, :], lhsT=S[:, kt, di * 128:(di + 1) * 128], rhs=YTB[:, kt, :],
                            start=(kt == 0), stop=(kt == 7))),
                            r=[("WS", i, kt // 4), ("YTB", kt)], w=[("ps", bank)])
                    P.op("dve", (lambda e, dt_=dt_, bank=bank: e.tensor_tensor(
                        out=XT[:, dt_, t0:t0 + 512], in0=XT[:, dt_, t0:t0 + 512], in1=PSB[bank][:, :], op=ALU.add)),
                        r=[("ps", bank), ("XT", t0 // 512)], w=[("XT", t0 // 512)])

        if "s5" in mixers:
            C5 = s5_setup()
            barrier()
        CR = rw_setup() if "rwkv" in mixers else None
        barrier()
        base_off = am.off
        for tb in range(4):
            t0 = tb * 512
            norm_to(HTB, "HTB", t0, g, 0)
            if "ssd" in mixers:
                ssd_block(tb)
            if "s5" in mixers:
                P.set_fence()
                s5_block(tb, C5)
                P.set_fence()
            if "rwkv" in mixers:
                P.set_fence()
                rw_block(tb, CR)
                P.set_fence()
            if dbg:
                dst = ydbg[s, l].rearrange("(kt p) t -> p kt t", p=128)[:, :, t0:t0 + 512]
                P.dma("sp", (lambda e, dst=dst: e.dma_start(out=dst, in_=YTB[:])),
                      r=[("YTB", k_) for k_ in range(8)], sem="D_ydbg")
            outproj(t0)

    for s in range(NS):
        barrier()
        load_x(s)
        barrier()
        for l in range(NL):
            if do_ffn:
                ffn(l, "ffn1")
            if mixers:
                barrier()
                mixer_phase(s, l)
                barrier()
            if do_ffn:
                ffn(l, "ffn2")
        barrier()
        final_store(s)
    last = {}
    for s_, v in out_toks:
        last[s_] = max(last.get(s_, 0), v)
    P.final_wait("sp", list(last.items()))
    assert ARMAX[0] <= AR_WORDS, ("arena overflow: need words", ARMAX[0])
    P.emit(stack)
    stack.close()
    return nc


L_ = 2
WEIGHT_SHAPES = [
    ("ffn1_norm", (L_, 1024)), ("ffn1_wg", (L_, 1024, 2816)), ("ffn1_wu", (L_, 1024, 2816)),
    ("ffn1_wd", (L_, 2816, 1024)), ("mix_norm", (L_, 1024)), ("w_in", (L_, 1024, 2696)),
    ("w_out", (L_, 1024, 1024)), ("m_A_log", (L_, 8)), ("m_dt_bias", (L_, 8)),
    ("m_conv_w", (L_, 1024, 4)), ("m_conv_b", (L_, 1024)), ("m_D", (L_, 8)),
    ("m_norm_w", (L_, 512)), ("s_A_re", (L_, 16, 64)), ("s_A_im", (L_, 16, 64)),
    ("s_B_re", (L_, 16, 64, 16)), ("s_B_im", (L_, 16, 64, 16)), ("s_C_re", (L_, 16, 16, 64)),
    ("s_C_im", (L_, 16, 16, 64)), ("s_log_dt", (L_, 16)), ("s_D", (L_, 256)),
    ("s_glu_w", (L_, 256, 256)), ("s_glu_b", (L_, 256)), ("r_mu", (L_, 896)),
    ("r_w0", (L_, 256)), ("r_w2", (L_, 32, 256)), ("r_a0", (L_, 256)), ("r_a2", (L_, 32, 256)),
    ("r_g2", (L_, 64, 256)), ("r_k_k", (L_, 256)), ("r_k_a", (L_, 256)), ("r_r_k", (L_, 4, 64)),
    ("r_gn_w", (L_, 256)), ("r_gn_b", (L_, 256)), ("ffn2_norm", (L_, 1024)),
    ("ffn2_wg", (L_, 1024, 2816)), ("ffn2_wu", (L_, 1024, 2816)), ("ffn2_wd", (L_, 2816, 1024)),
    ("final_norm", (1024,)),
]

_CFG = {"nseq": 2, "nlayers": 2, "mix": True, "ffn": True}


def kernel(**inputs):
    n = 8
    nc = bass.Bass("TRN2", target_bir_lowering=False)
    build_program(nc, _CFG)
    x = np.ascontiguousarray(np.asarray(inputs["x"], dtype=np.float32))
    wts = {nm: np.ascontiguousarray(np.asarray(inputs[nm], dtype=np.float32)) for nm, _ in WEIGHT_SHAPES}
    in_maps = []
    for c in range(n):
        m = {"x": x[2 * c:2 * c + 2]}
        m.update(wts)
        in_maps.append(m)
    res = run_bass_kernel_spmd(nc, in_maps, core_ids=list(range(n)))
    return np.concatenate([r["out"] for r in res.results], axis=0)
```
